# Optimizing a Trainium2 kernel written in Bass

```python
import math
import jax
import jax.numpy as jnp
from jax import lax
import numpy as np

D_MODEL = 1024
BATCH = 1
SEQ = 16384
DEPTH = 4

N_MIXERS = 3
N_SSD_LAYERS = (DEPTH + N_MIXERS - 1) // N_MIXERS
N_GDN_LAYERS = (DEPTH + N_MIXERS - 2) // N_MIXERS
N_S5_LAYERS = DEPTH // N_MIXERS
NORM_EPS = 1e-6
CONV_WIDTH = 5
D_FF = ((8 * D_MODEL + 3 * 256 - 1) // (3 * 256)) * 256

SSD_D_INNER = 2 * D_MODEL
SSD_HEAD_DIM = 64
SSD_HEADS = SSD_D_INNER // SSD_HEAD_DIM
SSD_GROUPS = 8
SSD_STATE = 128
SSD_CHUNK = 128
SSD_GN = SSD_GROUPS * SSD_STATE
SSD_CONV_CH = SSD_D_INNER + 2 * SSD_GN
SSD_IN = 2 * SSD_D_INNER + 2 * SSD_GN + 2 * SSD_HEADS

GDN_HEAD_DIM = 128
GDN_QK_HEADS = D_MODEL // GDN_HEAD_DIM
GDN_V_HEADS = 2 * GDN_QK_HEADS
GDN_KEY_DIM = GDN_QK_HEADS * GDN_HEAD_DIM
GDN_VALUE_DIM = GDN_V_HEADS * GDN_HEAD_DIM
GDN_CHUNK = 64
GDN_CONV_CH = 2 * GDN_KEY_DIM + GDN_VALUE_DIM
GDN_IN = GDN_CONV_CH + GDN_VALUE_DIM + 4 * GDN_V_HEADS

S5_GROUP = 16
S5_GROUPS = D_MODEL // S5_GROUP
S5_STATE = 64

kernel_name = 'hybrid_bidir_ssd_gdn_s5_encoder'


def rms_norm(x, w):
    xf = x.astype(jnp.float32)
    y = xf * lax.rsqrt(jnp.mean(xf * xf, axis=-1, keepdims=True) + NORM_EPS)
    return (y * w.astype(jnp.float32)).astype(x.dtype)


def l2_normalize(x):
    return x * lax.rsqrt(jnp.sum(x * x, axis=-1, keepdims=True) + NORM_EPS)


def _flip(t):
    return jnp.flip(t, axis=1)


def centred_depthwise_conv(x, w, b):
    k, ch = w.shape
    pad = k // 2
    y = lax.conv_general_dilated(x, w[:, None, :].astype(x.dtype), window_strides=(1,),
                                 padding=[(pad, pad)], dimension_numbers=('NWC', 'WIO', 'NWC'),
                                 feature_group_count=ch)
    return y + b


def _exp_segsum(a_cs):
    l = a_cs.shape[-1]
    diff = a_cs[..., :, None] - a_cs[..., None, :]
    mask = jnp.tril(jnp.ones((l, l), dtype=bool))
    return jnp.exp(jnp.where(mask, diff, -jnp.inf))


def ssd_chunked(x, dt, a, b_in, c_out):
    f32 = jnp.float32
    bt, L, h, p = x.shape
    g, n = b_in.shape[-2:]
    r = h // g
    q = SSD_CHUNK
    nc = L // q
    x, dt, b_in, c_out = x.astype(f32), dt.astype(f32), b_in.astype(f32), c_out.astype(f32)
    xdt = (x * dt[..., None]).reshape(bt, nc, q, g, r, p)
    da = jnp.moveaxis((dt * a).reshape(bt, nc, q, g, r), 2, -1)
    da_cs = jnp.cumsum(da, axis=-1)
    bc = b_in.reshape(bt, nc, q, g, n)
    cc = c_out.reshape(bt, nc, q, g, n)
    lmat = _exp_segsum(da_cs)
    cb = jnp.einsum('bclgn,bcsgn->bcgls', cc, bc)
    y_diag = jnp.einsum('bcgls,bcgrls,bcsgrp->bclgrp', cb, lmat, xdt)
    decay_to_end = jnp.exp(da_cs[..., -1:] - da_cs)
    chunk_states = jnp.einsum('bclgn,bcgrl,bclgrp->bcgrpn', bc, decay_to_end, xdt)
    chunk_decay = jnp.exp(da_cs[..., -1])

    def step(state, inp):
        s_c, d_c = inp
        return state * d_c[..., None, None] + s_c, state

    init = jnp.zeros_like(chunk_states[:, 0])
    _, prev = lax.scan(step, init, (jnp.moveaxis(chunk_states, 1, 0), jnp.moveaxis(chunk_decay, 1, 0)))
    prev = jnp.moveaxis(prev, 0, 1)
    y_off = jnp.einsum('bclgn,bcgrpn,bcgrl->bclgrp', cc, prev, jnp.exp(da_cs))
    return (y_diag + y_off).reshape(bt, L, h, p)


def gated_group_rms_norm(y, z, w, groups):
    yz = y * jax.nn.silu(z)
    shp = yz.shape
    yz = yz.reshape(shp[:-1] + (groups, shp[-1] // groups))
    return rms_norm(yz, w.reshape(groups, -1)).reshape(shp)


def mamba2_mixer(h, w_in, conv_w, conv_b, dt_bias, a_log, d_skip, norm_w, w_out):
    f32 = jnp.float32
    bt, L, _ = h.shape
    proj = h @ w_in
    z = proj[..., :SSD_D_INNER]
    xbc = proj[..., SSD_D_INNER:SSD_D_INNER + SSD_CONV_CH]
    dt_raw = proj[..., SSD_D_INNER + SSD_CONV_CH:]
    xbc = jax.nn.silu(centred_depthwise_conv(xbc, conv_w, conv_b))
    xs = xbc[..., :SSD_D_INNER].reshape(bt, L, SSD_HEADS, SSD_HEAD_DIM)
    b_in = xbc[..., SSD_D_INNER:SSD_D_INNER + SSD_GN].reshape(bt, L, SSD_GROUPS, SSD_STATE)
    c_out = xbc[..., SSD_D_INNER + SSD_GN:].reshape(bt, L, SSD_GROUPS, SSD_STATE)
    dt = jax.nn.softplus(dt_raw.astype(f32).reshape(bt, L, 2, SSD_HEADS) + dt_bias.astype(f32))
    a = -jnp.exp(a_log.astype(f32))
    y_fwd = ssd_chunked(xs, dt[:, :, 0], a[0], b_in, c_out)
    y_bwd = _flip(ssd_chunked(_flip(xs), _flip(dt[:, :, 1]), a[1], _flip(b_in), _flip(c_out)))
    y = y_fwd + y_bwd + xs.astype(f32) * d_skip.astype(f32)[:, None]
    y = y.reshape(bt, L, SSD_D_INNER).astype(h.dtype)
    y = gated_group_rms_norm(y, z, norm_w, SSD_GROUPS)
    return y @ w_out


def gated_delta_chunked(q, k, v, g, beta):
    bt, L, h, dk = q.shape
    dv = v.shape[-1]
    c = GDN_CHUNK
    nc = L // c

    def chunks(t):
        return jnp.swapaxes(t.reshape((bt, nc, c) + t.shape[2:]), 2, 3)

    q, k, v, g, beta = chunks(q), chunks(k), chunks(v), chunks(g), chunks(beta)
    g_cs = jnp.cumsum(g, axis=-1)
    k_beta = k * beta[..., None]
    v_beta = v * beta[..., None]
    decay = _exp_segsum(g_cs)
    strict = jnp.tril(jnp.ones((c, c), dtype=bool), -1)
    a_mat = jnp.where(strict, jnp.einsum('bnhid,bnhjd->bnhij', k_beta, k) * decay, 0.0)
    rhs = jnp.concatenate([v_beta, k_beta * jnp.exp(g_cs)[..., None]], axis=-1)
    sol = lax.linalg.triangular_solve(a_mat, rhs, left_side=True, lower=True, unit_diagonal=True)
    u, w = sol[..., :dv], sol[..., dv:]
    qk = jnp.einsum('bnhid,bnhjd->bnhij', q, k) * decay

    def step(s, inp):
        q_i, k_i, u_i, w_i, qk_i, gcs_i = inp
        v_new = u_i - jnp.einsum('bhcd,bhde->bhce', w_i, s)
        o_i = (jnp.einsum('bhcd,bhde->bhce', q_i * jnp.exp(gcs_i)[..., None], s)
               + jnp.einsum('bhij,bhje->bhie', qk_i, v_new))
        g_last = gcs_i[..., -1]
        k_dec = k_i * jnp.exp(g_last[..., None] - gcs_i)[..., None]
        s = s * jnp.exp(g_last)[..., None, None] + jnp.einsum('bhcd,bhce->bhde', k_dec, v_new)
        return s, o_i

    init = jnp.zeros((bt, h, dk, dv), dtype=q.dtype)
    mv = lambda t: jnp.moveaxis(t, 1, 0)
    _, o = lax.scan(step, init, (mv(q), mv(k), mv(u), mv(w), mv(qk), mv(g_cs)))
    o = jnp.swapaxes(jnp.moveaxis(o, 0, 1), 2, 3)
    return o.reshape(bt, L, h, dv)


def gdn_mixer(h, w_in, conv_w, conv_b, dt_bias, a_log, norm_w, w_out):
    f32 = jnp.float32
    bt, L, _ = h.shape
    proj = h @ w_in
    qkv = jax.nn.silu(centred_depthwise_conv(proj[..., :GDN_CONV_CH], conv_w, conv_b)).astype(f32)
    o0 = GDN_CONV_CH + GDN_VALUE_DIM
    z = proj[..., GDN_CONV_CH:o0]
    a_raw = proj[..., o0:o0 + 2 * GDN_V_HEADS]
    b_raw = proj[..., o0 + 2 * GDN_V_HEADS:]
    rep = GDN_V_HEADS // GDN_QK_HEADS
    q = l2_normalize(qkv[..., :GDN_KEY_DIM].reshape(bt, L, GDN_QK_HEADS, GDN_HEAD_DIM)) * (GDN_HEAD_DIM ** -0.5)
    k = l2_normalize(qkv[..., GDN_KEY_DIM:2 * GDN_KEY_DIM].reshape(bt, L, GDN_QK_HEADS, GDN_HEAD_DIM))
    q = jnp.repeat(q, rep, axis=2)
    k = jnp.repeat(k, rep, axis=2)
    v = qkv[..., 2 * GDN_KEY_DIM:].reshape(bt, L, GDN_V_HEADS, GDN_HEAD_DIM)
    g = -jnp.exp(a_log.astype(f32)) * jax.nn.softplus(
        a_raw.astype(f32).reshape(bt, L, 2, GDN_V_HEADS) + dt_bias.astype(f32))
    beta = jax.nn.sigmoid(b_raw.astype(f32).reshape(bt, L, 2, GDN_V_HEADS))
    o_fwd = gated_delta_chunked(q, k, v, g[:, :, 0], beta[:, :, 0])
    o_bwd = _flip(gated_delta_chunked(_flip(q), _flip(k), _flip(v), _flip(g[:, :, 1]), _flip(beta[:, :, 1])))
    o = rms_norm(o_fwd + o_bwd, norm_w) * jax.nn.silu(z.astype(f32).reshape(bt, L, GDN_V_HEADS, GDN_HEAD_DIM))
    return o.reshape(bt, L, GDN_VALUE_DIM).astype(h.dtype) @ w_out


def _cmul(ar, ai, br, bi):
    return ar * br - ai * bi, ar * bi + ai * br


def _s5_combine(e1, e2):
    a1r, a1i, b1r, b1i = e1
    a2r, a2i, b2r, b2i = e2
    ar, ai = _cmul(a2r, a2i, a1r, a1i)
    br, bi = _cmul(a2r, a2i, b1r, b1i)
    return ar, ai, br + b2r, bi + b2i


def s5_direction(u, lam_re, lam_im, log_step, b_re, b_im, c_re, c_im):
    L = u.shape[0]
    step = jnp.exp(log_step)[:, None]
    mag = jnp.exp(lam_re * step)
    ang = lam_im * step
    lbar_r, lbar_i = mag * jnp.cos(ang), mag * jnp.sin(ang)
    inv_den = 1.0 / (lam_re * lam_re + lam_im * lam_im)
    zr, zi = _cmul(lbar_r - 1.0, lbar_i, lam_re * inv_den, -lam_im * inv_den)
    bb_r, bb_i = _cmul(zr[..., None], zi[..., None], b_re, b_im)
    xr = jnp.einsum('lbgc,gnc->lbgn', u, bb_r)
    xi = jnp.einsum('lbgc,gnc->lbgn', u, bb_i)
    ar = jnp.broadcast_to(lbar_r, (L, 1) + lbar_r.shape)
    ai = jnp.broadcast_to(lbar_i, (L, 1) + lbar_i.shape)
    _, _, sr, si = lax.associative_scan(_s5_combine, (ar, ai, xr, xi), axis=0)
    return jnp.einsum('lbgn,gcn->lbgc', sr, c_re) - jnp.einsum('lbgn,gcn->lbgc', si, c_im)


def s5_mixer(h, lam_re, lam_im, log_step, b_re, b_im, c_re, c_im, d_skip, w_glu, b_glu):
    f32 = jnp.float32
    bt, L, d = h.shape
    lam_re, lam_im, log_step = lam_re.astype(f32), lam_im.astype(f32), log_step.astype(f32)
    b_re, b_im, c_re, c_im = b_re.astype(f32), b_im.astype(f32), c_re.astype(f32), c_im.astype(f32)
    u = jnp.swapaxes(h.astype(f32).reshape(bt, L, S5_GROUPS, S5_GROUP), 0, 1)
    y_fwd = s5_direction(u, lam_re[0], lam_im[0], log_step[0], b_re, b_im, c_re[0], c_im[0])
    y_bwd = jnp.flip(s5_direction(jnp.flip(u, axis=0), lam_re[1], lam_im[1], log_step[1],
                                  b_re, b_im, c_re[1], c_im[1]), axis=0)
    y = jnp.swapaxes(y_fwd + y_bwd, 0, 1).reshape(bt, L, d).astype(h.dtype) + d_skip * h
    y = jax.nn.gelu(y)
    val, gate = jnp.split(y @ w_glu + b_glu, 2, axis=-1)
    return val * jax.nn.sigmoid(gate)


def swiglu_ffn(h, w_gate_up, w_down):
    gate, up = jnp.split(h @ w_gate_up, 2, axis=-1)
    return (jax.nn.silu(gate) * up) @ w_down


def setup_inputs(seed: int = 0) -> dict:
    key = jax.random.key(seed)
    keys = jax.random.split(key, 64)
    counter = [0]
    f32 = jnp.float32

    def nk():
        kk = keys[counter[0]]
        counter[0] += 1
        return kk

    def normal(shape, scale):
        return jax.random.normal(nk(), shape, f32) * scale

    def gain(shape):
        return 1.0 + 0.01 * jax.random.normal(nk(), shape, f32)

    def small(shape):
        return 0.01 * jax.random.normal(nk(), shape, f32)

    def dt_bias(shape):
        dt = jnp.exp(jax.random.uniform(nk(), shape, f32, math.log(1e-3), math.log(1e-1)))
        return dt + jnp.log(-jnp.expm1(-dt))

    def a_log(shape):
        return jnp.log(jax.random.uniform(nk(), shape, f32, 1.0, 16.0))

    na, nb, nc = N_SSD_LAYERS, N_GDN_LAYERS, N_S5_LAYERS
    n_idx = jnp.arange(S5_STATE, dtype=f32)
    lam_im0 = jnp.broadcast_to(math.pi * n_idx, (nc, 2, S5_GROUPS, S5_STATE))
    return {
        'x': jax.random.normal(nk(), (BATCH, SEQ, D_MODEL), f32),
        'norm_w': gain((DEPTH, 4, D_MODEL)),
        'ssd_w_in': normal((na, D_MODEL, SSD_IN), D_MODEL ** -0.5),
        'ssd_conv_w': normal((na, CONV_WIDTH, SSD_CONV_CH), CONV_WIDTH ** -0.5),
        'ssd_conv_b': small((na, SSD_CONV_CH)),
        'ssd_dt_bias': dt_bias((na, 2, SSD_HEADS)),
        'ssd_a_log': a_log((na, 2, SSD_HEADS)),
        'ssd_d': gain((na, SSD_HEADS)),
        'ssd_norm_w': gain((na, SSD_D_INNER)),
        'ssd_w_out': normal((na, SSD_D_INNER, D_MODEL), SSD_D_INNER ** -0.5),
        'gdn_w_in': normal((nb, D_MODEL, GDN_IN), D_MODEL ** -0.5),
        'gdn_conv_w': normal((nb, CONV_WIDTH, GDN_CONV_CH), CONV_WIDTH ** -0.5),
        'gdn_conv_b': small((nb, GDN_CONV_CH)),
        'gdn_dt_bias': dt_bias((nb, 2, GDN_V_HEADS)),
        'gdn_a_log': a_log((nb, 2, GDN_V_HEADS)),
        'gdn_norm_w': gain((nb, GDN_HEAD_DIM)),
        'gdn_w_out': normal((nb, GDN_VALUE_DIM, D_MODEL), GDN_VALUE_DIM ** -0.5),
        's5_lam_re': -0.5 + small((nc, 2, S5_GROUPS, S5_STATE)),
        's5_lam_im': lam_im0 + small((nc, 2, S5_GROUPS, S5_STATE)),
        's5_log_step': jax.random.uniform(nk(), (nc, 2, S5_GROUPS), f32, math.log(1e-3), math.log(1e-1)),
        's5_b_re': normal((nc, S5_GROUPS, S5_STATE, S5_GROUP), (2 * S5_GROUP) ** -0.5),
        's5_b_im': normal((nc, S5_GROUPS, S5_STATE, S5_GROUP), (2 * S5_GROUP) ** -0.5),
        's5_c_re': normal((nc, 2, S5_GROUPS, S5_GROUP, S5_STATE), S5_STATE ** -0.5),
        's5_c_im': normal((nc, 2, S5_GROUPS, S5_GROUP, S5_STATE), S5_STATE ** -0.5),
        's5_d': normal((nc, D_MODEL), 1.0),
        's5_w_glu': normal((nc, D_MODEL, 2 * D_MODEL), D_MODEL ** -0.5),
        's5_b_glu': small((nc, 2 * D_MODEL)),
        'ffn_w_gate_up': normal((DEPTH, D_MODEL, 2 * D_FF), D_MODEL ** -0.5),
        'ffn_w_down': normal((DEPTH, D_FF, D_MODEL), D_FF ** -0.5),
    }


def reference(x, norm_w, ssd_w_in, ssd_conv_w, ssd_conv_b, ssd_dt_bias, ssd_a_log, ssd_d,
              ssd_norm_w, ssd_w_out, gdn_w_in, gdn_conv_w, gdn_conv_b, gdn_dt_bias, gdn_a_log,
              gdn_norm_w, gdn_w_out, s5_lam_re, s5_lam_im, s5_log_step, s5_b_re, s5_b_im,
              s5_c_re, s5_c_im, s5_d, s5_w_glu, s5_b_glu, ffn_w_gate_up, ffn_w_down):
    h = x
    for i in range(DEPTH):
        kind, j = i % N_MIXERS, i // N_MIXERS
        hn = rms_norm(h, norm_w[i, 0])
        if kind == 0:
            m = mamba2_mixer(hn, ssd_w_in[j], ssd_conv_w[j], ssd_conv_b[j], ssd_dt_bias[j],
                             ssd_a_log[j], ssd_d[j], ssd_norm_w[j], ssd_w_out[j])
        elif kind == 1:
            m = gdn_mixer(hn, gdn_w_in[j], gdn_conv_w[j], gdn_conv_b[j], gdn_dt_bias[j],
                          gdn_a_log[j], gdn_norm_w[j], gdn_w_out[j])
        else:
            m = s5_mixer(hn, s5_lam_re[j], s5_lam_im[j], s5_log_step[j], s5_b_re[j], s5_b_im[j],
                         s5_c_re[j], s5_c_im[j], s5_d[j], s5_w_glu[j], s5_b_glu[j])
        h = h + rms_norm(m, norm_w[i, 1])
        f = swiglu_ffn(rms_norm(h, norm_w[i, 2]), ffn_w_gate_up[i], ffn_w_down[i])
        h = h + rms_norm(f, norm_w[i, 3])
    return h
```

```python
import math


import numpy as np
import concourse.bass as bass
import concourse.mybir as mybir
from concourse.bass_utils import run_bass_kernel_spmd

F32 = mybir.dt.float32
BF16 = mybir.dt.bfloat16
I32 = mybir.dt.int32
AF = mybir.ActivationFunctionType
ALU = mybir.AluOpType
AX = mybir.AxisListType


def _is_psum(k):
    n = k[0] if isinstance(k, tuple) else k
    return isinstance(n, str) and n.startswith("ps_")


class Prog:
    NDMA = 4

    def __init__(self, nc):
        self.nc = nc
        self.eng = {"pe": nc.tensor, "dve": nc.vector, "act": nc.scalar,
                    "pool": nc.gpsimd, "sp": nc.sync}
        self.sem = {}
        self.cnt = {}
        for e in ("pe", "dve", "act", "pool"):
            self.sem[e] = nc.alloc_semaphore(name="s_" + e)
            self.cnt[e] = 0
        self.dq = {}
        for q in ("sp", "act", "pool"):
            sems = []
            for i in range(self.NDMA):
                k = "d_%s%d" % (q, i)
                self.sem[k] = nc.alloc_semaphore(name=k)
                self.cnt[k] = 0
                sems.append(k)
            self.dq[q] = [sems, 0]
        self.seen = {e: {} for e in self.eng}
        self.last_w = {}
        self.readers = {}
        self.out_tokens = []

    def _wait(self, e, needs):
        eng = self.eng[e]
        for sk, val in needs.items():
            if e == "pe" and sk == "pe":
                continue
            if self.seen[e].get(sk, 0) >= val:
                continue
            eng.wait_ge(self.sem[sk], val)
            self.seen[e][sk] = val

    def _needs(self, reads, writes, e=None):
        needs = {}

        def add(tok):
            if tok is None:
                return
            sk, v = tok
            if needs.get(sk, 0) < v:
                needs[sk] = v
        for k in reads:
            add(self.last_w.get(k))
            if _is_psum(k):
                for t in self.readers.get(k, ()):
                    if t[0] != e:
                        add(t)
        for k in writes:
            add(self.last_w.get(k))
            for t in self.readers.get(k, ()):
                add(t)
        return needs

    def _commit(self, tok, reads, writes):
        for k in writes:
            self.last_w[k] = tok
            self.readers[k] = []
        for k in reads:
            if k in writes:
                continue
            self.readers.setdefault(k, []).append(tok)
            if len(self.readers[k]) > 12:
                best = {}
                for sk, v in self.readers[k]:
                    if best.get(sk, 0) < v:
                        best[sk] = v
                self.readers[k] = list(best.items())

    def op(self, e, fn, reads=(), writes=()):
        self._wait(e, self._needs(reads, writes, e))
        ins = fn()
        self.cnt[e] += 1
        ins.then_inc(self.sem[e], 1)
        tok = (e, self.cnt[e])
        self._commit(tok, reads, writes)
        return tok

    def dma(self, q, out, in_, reads=(), writes=(), is_output=False, **kw):
        sems, n = self.dq[q]
        sk = sems[n % self.NDMA]
        self.dq[q][1] = n + 1
        needs = self._needs(reads, writes)
        if self.cnt[sk] > 0:
            needs[sk] = max(needs.get(sk, 0), self.cnt[sk])
        self._wait(q, needs)
        ins = self.eng[q].dma_start(out=out, in_=in_, **kw)
        self.cnt[sk] += 16
        ins.then_inc(self.sem[sk], 16)
        tok = (sk, self.cnt[sk])
        self._commit(tok, reads, writes)
        if is_output:
            self.out_tokens.append(tok)
        return tok

    def finish(self):
        needs = {}
        for sk, v in self.out_tokens:
            needs[sk] = max(needs.get(sk, 0), v)
        self._wait("sp", needs)
        needs = {e: self.cnt[e] for e in ("pe", "dve", "act", "pool") if self.cnt[e] > 0}
        for q in self.dq:
            for sk in self.dq[q][0]:
                if self.cnt[sk] > 0:
                    needs[sk] = self.cnt[sk]
        self._wait("sp", needs)


EPS = 1e-6
TG = 512
NTG = 4
D = 1024
DT = 8
DFF = 2816
FT = 22
GELU_C = 0.7978845608028654


def build_dense(variant, has_ffn, next_kind):
    nc = bass.Bass("TRN2", target_bir_lowering=False)
    p = Prog(nc)
    NTOK = TG * NTG

    def din(name, shape):
        return nc.dram_tensor(name, shape, F32, kind="ExternalInput").ap()

    def dout(name, shape):
        return nc.dram_tensor(name, shape, F32, kind="ExternalOutput").ap()

    hT_in = din("hT", [D, NTOK])
    if variant in ("ssd", "gdn"):
        mf = din("mf", [2048, NTOK])
        mb = din("mb", [2048, NTOK])
        zT = din("zT", [2048, NTOK])
        w_out = din("w_out", [DT, 128, 16 * 128])
        mnw = din("mnw", [128, 16])
    if variant == "s5":
        mf = din("mf", [D, NTOK])
        mb = din("mb", [D, NTOK])
        hn_in = din("hnT", [D, NTOK])
        w_glu = din("w_glu", [16, 128, DT * 128])
        b_glu = din("b_glu", [128, 16])
        s5d = din("s5d", [128, DT])
    if has_ffn:
        nws = din("nws", [128, 3 * DT])
        wgu = din("wgu", [FT, 128, 2 * DT * 128])
        wd = din("wd", [DT, 128, FT * 128])
        hT_out = dout("hT_out", [D, NTOK])
    if next_kind is not None:
        nw0 = din("nw0", [128, DT])
    if next_kind == "proj":
        w_in = din("w_in", [49, 128, DT * 128])
        projT = dout("projT", [49 * 128, NTOK])
    if next_kind == "hn":
        hn_out = dout("hn_out", [D, NTOK])

    sb = lambda name, shape, dt=F32: nc.alloc_sbuf_tensor(name, shape, dt)
    hT = sb("hT_sb", [128, DT, TG])
    mT = sb("mT_sb", [128, DT, TG])
    xnT = sb("xnT", [128, DT, TG], BF16)
    lhsA = sb("lhsA", [128, 16, TG], BF16)
    actT = sb("actT", [128, FT, TG], BF16)
    NW = 3
    wbuf = [sb("wbuf%d" % i, [128, FT * 128], BF16) for i in range(NW)]
    stg = [[sb("stg%d_%d" % (i, s), [128, TG]) for s in range(2)] for i in range(3)]
    tmpA = [sb("tmpA%d" % i, [128, TG]) for i in range(2)]
    tmpB = [sb("tmpB%d" % i, [128, TG]) for i in range(2)]
    yzb = [sb("yzb%d" % i, [128, TG]) for i in range(4)]
    sqb = [sb("sqb%d" % i, [128, TG], BF16) for i in range(2)]
    rstd = sb("rstd", [128, TG])
    ostg = [sb("ostg%d" % i, [128, TG]) for i in range(2)]
    ones = sb("ones", [128, 128], BF16)
    cols = sb("cols", [128, 64])
    ps_ss = nc.alloc_psum_tensor("ps_ss", [128, TG], F32)
    ps_acc = [nc.alloc_psum_tensor("ps_acc%d" % i, [128, TG], F32) for i in range(3)]
    ps_g = [nc.alloc_psum_tensor("ps_g%d" % i, [128, TG], F32) for i in range(2)]
    ps_u = [nc.alloc_psum_tensor("ps_u%d" % i, [128, TG], F32) for i in range(2)]

    p.op("pool", lambda: nc.gpsimd.memset(ones[:], 1.0), writes=["ones"])
    if has_ffn:
        p.dma("sp", cols[:, 0:24], nws, writes=["cols"])
    if next_kind is not None:
        p.dma("sp", cols[:, 24:32], nw0, writes=["cols"])
    if variant in ("ssd", "gdn"):
        p.dma("sp", cols[:, 32:48], mnw, writes=["cols"])
    if variant == "s5":
        p.dma("sp", cols[:, 32:48], b_glu, writes=["cols"])
        p.dma("sp", cols[:, 48:56], s5d, writes=["cols"])

    cnt = {"w": 0, "acc": 0, "gu": 0, "o": 0, "ev": 0}

    def load_w(src, n):
        i = cnt["w"] % NW
        cnt["w"] += 1
        p.dma("pool", wbuf[i][:, 0:n], src, writes=[("w", i)], max_dma_last_dim=4096)
        return wbuf[i], ("w", i)

    def rms_rstd(tiles, n_feat):
        last = len(tiles) - 1
        for i, (ap, key) in enumerate(tiles):
            s = i % 2
            p.op("act", lambda ap=ap, s=s: nc.scalar.activation(out=sqb[s][:], in_=ap, func=AF.Square),
                 reads=[key], writes=[("sq", s)])
            p.op("pe", lambda s=s, i=i: nc.tensor.matmul(ps_ss[:], lhsT=ones[:], rhs=sqb[s][:],
                                                       start=(i == 0), stop=(i == last)),
                 reads=[("sq", s), "ones"], writes=["ps_ss"])
        p.op("act", lambda: nc.scalar.activation(out=rstd[:], in_=ps_ss[:], func=AF.Ln,
                                                 scale=1.0 / n_feat, bias=EPS),
             reads=["ps_ss"], writes=["rstd"])
        p.op("act", lambda: nc.scalar.activation(out=rstd[:], in_=rstd[:], func=AF.Exp, scale=-0.5),
             reads=["rstd"], writes=["rstd"])

    def evac(dst, dkey, src, skey):
        cnt["ev"] += 1
        if cnt["ev"] % 2:
            p.op("act", lambda: nc.scalar.copy(out=dst, in_=src), reads=[skey], writes=[dkey])
        else:
            p.op("dve", lambda: nc.vector.tensor_copy(out=dst, in_=src), reads=[skey], writes=[dkey])

    def proj_fm(wsrc, nk, rhs_fn, rhs_keys, consume):
        wt, wkey = load_w(wsrc, nk * 128)
        a = cnt["acc"] % 3
        cnt["acc"] += 1
        for k in range(nk):
            p.op("pe", lambda k=k: nc.tensor.matmul(ps_acc[a][:], lhsT=wt[:, k * 128:(k + 1) * 128],
                                                    rhs=rhs_fn(k), start=(k == 0), stop=(k == nk - 1)),
                 reads=[wkey] + rhs_keys, writes=[("acc", a)])
        consume(ps_acc[a][:], ("acc", a))

    def add_norm_into_h(nwoff):
        rms_rstd([(mT[:, j, :], ("mT", j)) for j in range(DT)], D)
        for j in range(DT):
            s = j % 2
            p.op("dve", lambda j=j, s=s: nc.vector.scalar_tensor_tensor(
                out=tmpA[s][:], in0=mT[:, j, :], scalar=cols[:, nwoff + j:nwoff + j + 1], in1=rstd[:],
                op0=ALU.mult, op1=ALU.mult), reads=[("mT", j), "cols", "rstd"], writes=[("tmpA", s)])
            p.op("pool", lambda j=j, s=s: nc.gpsimd.tensor_tensor(out=hT[:, j, :], in0=hT[:, j, :], in1=tmpA[s][:],
                                                                  op=ALU.add),
                 reads=[("hT", j), ("tmpA", s)], writes=[("hT", j)])

    def norm_h_to(dst_fn, dkey_fn, nwoff):
        rms_rstd([(hT[:, j, :], ("hT", j)) for j in range(DT)], D)
        for j in range(DT):
            p.op("dve", lambda j=j: nc.vector.scalar_tensor_tensor(
                out=dst_fn(j), in0=hT[:, j, :], scalar=cols[:, nwoff + j:nwoff + j + 1], in1=rstd[:],
                op0=ALU.mult, op1=ALU.mult), reads=[("hT", j), "cols", "rstd"], writes=[dkey_fn(j)])

    for tg in range(NTG):
        tok = slice(tg * TG, (tg + 1) * TG)
        for j in range(DT):
            p.dma("sp", hT[:, j, :], hT_in[j * 128:(j + 1) * 128, tok], writes=[("hT", j)])

        if variant in ("ssd", "gdn"):
            gsz = 2 if variant == "ssd" else 1
            for G in range(16 // gsz):
                tl = []
                for t in range(gsz):
                    ft = G * gsz + t
                    s = ft % 2
                    rows = slice(ft * 128, (ft + 1) * 128)
                    p.dma("sp", stg[0][s][:], mf[rows, tok], writes=[("stg0", s)])
                    p.dma("act", stg[1][s][:], mb[rows, tok], writes=[("stg1", s)])
                    p.dma("sp", stg[2][s][:], zT[rows, tok], writes=[("stg2", s)])
                    yb = yzb[ft % 4]
                    ykey = ("yz", ft % 4)
                    p.op("pool", lambda s=s, yb=yb: nc.gpsimd.tensor_tensor(out=yb[:], in0=stg[0][s][:], in1=stg[1][s][:],
                                                                            op=ALU.add),
                         reads=[("stg0", s), ("stg1", s)], writes=[ykey])
                    p.op("act", lambda s=s: nc.scalar.activation(out=tmpB[s][:], in_=stg[2][s][:], func=AF.Silu),
                         reads=[("stg2", s)], writes=[("tmpB", s)])
                    if variant == "ssd":
                        p.op("dve", lambda s=s, yb=yb: nc.vector.tensor_tensor(out=yb[:], in0=yb[:], in1=tmpB[s][:],
                                                                               op=ALU.mult),
                             reads=[ykey, ("tmpB", s)], writes=[ykey])
                    tl.append((yb, ykey, ft, s))
                rms_rstd([(yb[:], ykey) for (yb, ykey, ft, s) in tl], 128 * gsz)
                for (yb, ykey, ft, s) in tl:
                    if variant == "ssd":
                        p.op("dve", lambda yb=yb, ft=ft: nc.vector.scalar_tensor_tensor(
                            out=lhsA[:, ft, :], in0=yb[:], scalar=cols[:, 32 + ft:33 + ft], in1=rstd[:],
                            op0=ALU.mult, op1=ALU.mult), reads=[ykey, "cols", "rstd"], writes=[("lhsA", ft)])
                    else:
                        p.op("dve", lambda yb=yb: nc.vector.scalar_tensor_tensor(
                            out=yb[:], in0=yb[:], scalar=cols[:, 32:33], in1=rstd[:],
                            op0=ALU.mult, op1=ALU.mult), reads=[ykey, "cols", "rstd"], writes=[ykey])
                        p.op("dve", lambda yb=yb, ft=ft, s=s: nc.vector.tensor_tensor(
                            out=lhsA[:, ft, :], in0=yb[:], in1=tmpB[s][:], op=ALU.mult),
                            reads=[ykey, ("tmpB", s)], writes=[("lhsA", ft)])
            for j in range(DT):
                proj_fm(w_out[j], 16, lambda k: lhsA[:, k, :], [("lhsA", k) for k in range(16)],
                        lambda ps, pk, j=j: evac(mT[:, j, :], ("mT", j), ps, pk))
        if variant == "s5":
            for ft in range(DT):
                s = ft % 2
                rows = slice(ft * 128, (ft + 1) * 128)
                p.dma("sp", stg[0][s][:], mf[rows, tok], writes=[("stg0", s)])
                p.dma("act", stg[1][s][:], mb[rows, tok], writes=[("stg1", s)])
                p.dma("sp", stg[2][s][:], hn_in[rows, tok], writes=[("stg2", s)])
                yb = yzb[ft % 4]
                ykey = ("yz", ft % 4)
                p.op("pool", lambda s=s, yb=yb: nc.gpsimd.tensor_tensor(out=yb[:], in0=stg[0][s][:], in1=stg[1][s][:],
                                                                        op=ALU.add),
                     reads=[("stg0", s), ("stg1", s)], writes=[ykey])
                p.op("dve", lambda s=s, yb=yb, ft=ft: nc.vector.scalar_tensor_tensor(
                    out=yb[:], in0=stg[2][s][:], scalar=cols[:, 48 + ft:49 + ft], in1=yb[:],
                    op0=ALU.mult, op1=ALU.add), reads=[("stg2", s), "cols", ykey], writes=[ykey])
                p.op("act", lambda s=s, yb=yb: nc.scalar.activation(out=tmpB[s][:], in_=yb[:], func=AF.Square),
                     reads=[ykey], writes=[("tmpB", s)])
                p.op("dve", lambda s=s: nc.vector.tensor_scalar(out=tmpB[s][:], in0=tmpB[s][:], scalar1=0.044715,
                                                                scalar2=1.0, op0=ALU.mult, op1=ALU.add),
                     reads=[("tmpB", s)], writes=[("tmpB", s)])
                p.op("dve", lambda s=s, yb=yb: nc.vector.tensor_tensor(out=tmpB[s][:], in0=tmpB[s][:], in1=yb[:],
                                                                       op=ALU.mult),
                     reads=[("tmpB", s), ykey], writes=[("tmpB", s)])
                p.op("act", lambda s=s: nc.scalar.activation(out=tmpB[s][:], in_=tmpB[s][:], func=AF.Sigmoid,
                                                             scale=2.0 * GELU_C),
                     reads=[("tmpB", s)], writes=[("tmpB", s)])
                p.op("dve", lambda s=s, yb=yb, ft=ft: nc.vector.tensor_tensor(out=lhsA[:, ft, :], in0=yb[:],
                                                                              in1=tmpB[s][:], op=ALU.mult),
                     reads=[ykey, ("tmpB", s)], writes=[("lhsA", ft)])
            for j in range(DT):
                def cons_gate(ps, pk, j=j):
                    s = j % 2
                    p.op("act", lambda: nc.scalar.activation(out=tmpA[s][:], in_=ps, func=AF.Sigmoid,
                                                             bias=cols[:, 40 + j:41 + j]),
                         reads=[pk, "cols"], writes=[("tmpA", s)])

                def cons_val(ps, pk, j=j):
                    s = j % 2
                    p.op("dve", lambda: nc.vector.scalar_tensor_tensor(
                        out=mT[:, j, :], in0=ps, scalar=cols[:, 32 + j:33 + j], in1=tmpA[s][:],
                        op0=ALU.add, op1=ALU.mult), reads=[pk, "cols", ("tmpA", s)], writes=[("mT", j)])
                lk = [("lhsA", k) for k in range(DT)]
                proj_fm(w_glu[8 + j], DT, lambda k: lhsA[:, k, :], lk, cons_gate)
                proj_fm(w_glu[j], DT, lambda k: lhsA[:, k, :], lk, cons_val)

        if has_ffn:
            add_norm_into_h(0)
            norm_h_to(lambda j: xnT[:, j, :], lambda j: ("xnT", j), 8)
            xk = [("xnT", k) for k in range(DT)]
            for f in range(FT):
                wt, wkey = load_w(wgu[f], 2 * DT * 128)
                g = cnt["gu"] % 2
                cnt["gu"] += 1
                for half, pst, nm in ((0, ps_g, "g"), (1, ps_u, "u")):
                    for k in range(DT):
                        p.op("pe", lambda k=k, half=half, pst=pst: nc.tensor.matmul(
                            pst[g][:], lhsT=wt[:, (half * DT + k) * 128:(half * DT + k + 1) * 128],
                            rhs=xnT[:, k, :], start=(k == 0), stop=(k == DT - 1)),
                            reads=[wkey] + xk, writes=[(nm, g)])
                p.op("act", lambda g=g: nc.scalar.activation(out=tmpB[g][:], in_=ps_g[g][:], func=AF.Silu),
                     reads=[("g", g)], writes=[("tmpB", g)])
                p.op("dve", lambda g=g, f=f: nc.vector.tensor_tensor(out=actT[:, f, :], in0=tmpB[g][:],
                                                                     in1=ps_u[g][:], op=ALU.mult),
                     reads=[("tmpB", g), ("u", g)], writes=[("actT", f)])
            ak = [("actT", k) for k in range(FT)]
            for j in range(DT):
                proj_fm(wd[j], FT, lambda k: actT[:, k, :], ak,
                        lambda ps, pk, j=j: evac(mT[:, j, :], ("mT", j), ps, pk))
            add_norm_into_h(16)
            for j in range(DT):
                p.dma("sp", hT_out[j * 128:(j + 1) * 128, tok], hT[:, j, :], reads=[("hT", j)], is_output=True)

        if next_kind == "proj":
            norm_h_to(lambda j: xnT[:, j, :], lambda j: ("xnT", j), 24)
            xk = [("xnT", k) for k in range(DT)]
            for j in range(49):
                def cons(ps, pk, j=j):
                    o = cnt["o"] % 2
                    cnt["o"] += 1
                    evac(ostg[o][:], ("ostg", o), ps, pk)
                    p.dma("sp" if j % 2 else "act", projT[j * 128:(j + 1) * 128, tok], ostg[o][:],
                          reads=[("ostg", o)], is_output=True)
                proj_fm(w_in[j], DT, lambda k: xnT[:, k, :], xk, cons)
        if next_kind == "hn":
            for j in range(DT):
                pass
            norm_h_to(lambda j: mT[:, j, :], lambda j: ("mT", j), 24)
            for j in range(DT):
                p.dma("sp", hn_out[j * 128:(j + 1) * 128, tok], mT[:, j, :], reads=[("mT", j)], is_output=True)
    p.finish()
    return nc


def arrange_w(w, n_out_tiles=None):
    K, N = w.shape
    nk = K // 128
    nj = (N + 127) // 128
    if N % 128:
        w = np.concatenate([w, np.zeros((K, nj * 128 - N), w.dtype)], axis=1)
    r = w.reshape(nk, 128, nj, 128).transpose(2, 1, 0, 3).reshape(nj, 128, nk * 128)
    return np.ascontiguousarray(r)


def col_tiles(v):
    return np.ascontiguousarray(v.reshape(-1, 128).T)


T_SEQ = 16384
TB = 512


def make_consts(nc, p):
    io = nc.alloc_sbuf_tensor("c_iota", [128, 128], F32)
    ident = nc.alloc_sbuf_tensor("c_ident", [128, 128], F32)
    triu = nc.alloc_sbuf_tensor("c_triu", [128, 128], F32)
    p.op("pool", lambda: nc.gpsimd.iota(io[:], pattern=[[1, 128]], base=0, channel_multiplier=-1,
                                        allow_small_or_imprecise_dtypes=True), writes=["c_iota"])
    p.op("dve", lambda: nc.vector.tensor_single_scalar(out=ident[:], in_=io[:], scalar=0.0, op=ALU.is_equal),
         reads=["c_iota"], writes=["c_ident"])
    p.op("dve", lambda: nc.vector.tensor_single_scalar(out=triu[:], in_=io[:], scalar=0.0, op=ALU.is_ge),
         reads=["c_iota"], writes=["c_triu"])
    return dict(iota=io, ident=ident, triu=triu)


def conv_silu_block(nc, p, raw, rawkey_fn, cw, cb, cT, ckey_fn, ntiles, tmp, tmpkey):
    for t in range(ntiles):
        p.op("act", lambda t=t: nc.scalar.activation(out=tmp[:], in_=raw[:, t, 0:TB], func=AF.Identity,
                                                     scale=cw[:, t, 0:1]),
             reads=[rawkey_fn(t), "cw"], writes=[tmpkey])
        for k in range(1, 5):
            p.op("dve", lambda t=t, k=k: nc.vector.scalar_tensor_tensor(
                out=tmp[:], in0=raw[:, t, k:k + TB], scalar=cw[:, t, k:k + 1], in1=tmp[:],
                op0=ALU.mult, op1=ALU.add), reads=[rawkey_fn(t), "cw", tmpkey], writes=[tmpkey])
        p.op("act", lambda t=t: nc.scalar.activation(out=cT[:, t, :], in_=tmp[:], func=AF.Silu, bias=cb[:, t:t + 1]),
             reads=[tmpkey, "cb"], writes=[ckey_fn(t)])


def build_ssd(n_blocks=T_SEQ // TB):
    nc = bass.Bass("TRN2", target_bir_lowering=False)
    p = Prog(nc)
    T = n_blocks * TB
    din = lambda name, shape: nc.dram_tensor(name, shape, F32, kind="ExternalInput").ap()
    xbc = din("xbc", [2, 512, T + 4])
    dtr = din("dtr", [2, 4, T])
    cwd = din("cw", [2, 128, 20])
    cbd = din("cb", [128, 4])
    dtbd = din("dtb", [2, 4, 1])
    alogd = din("alog", [2, 4, 1])
    dskd = din("dsk", [128, 2])
    yT = nc.dram_tensor("yT", [2, 256, T], F32, kind="ExternalOutput").ap()

    sb = lambda name, shape, dt=F32: nc.alloc_sbuf_tensor(name, shape, dt)
    C = make_consts(nc, p)
    ident, triu = C["ident"], C["triu"]
    sel = sb("sel", [4, 4, 128])
    p.op("pool", lambda: nc.gpsimd.iota(sel[:], pattern=[[1, 4], [0, 128]], base=0, channel_multiplier=-1,
                                        allow_small_or_imprecise_dtypes=True), writes=["sel"])
    p.op("dve", lambda: nc.vector.tensor_single_scalar(out=sel[:], in_=sel[:], scalar=0.0, op=ALU.is_equal),
         reads=["sel"], writes=["sel"])
    ones4 = sb("ones4", [4, 128])
    p.op("pool", lambda: nc.gpsimd.memset(ones4[:], 1.0), writes=["ones4"])

    cw = sb("cw_sb", [128, 4, 5])
    cb = sb("cb_sb", [128, 4])
    dsk = sb("dsk_sb", [128, 2])
    dtb = sb("dtb_sb", [4, 1])
    acol = sb("acol", [4, 1])
    p.dma("sp", cb[:], cbd, writes=["cb"])
    p.dma("sp", dsk[:], dskd, writes=["dsk"])

    raw = [sb("raw%d" % i, [128, 4, TB + 4]) for i in range(2)]
    cT = [sb("cT%d" % i, [128, 4, TB]) for i in range(2)]
    ctmp = sb("ctmp", [128, TB])
    dtraw = sb("dtraw", [4, TB])
    dtT = [sb("dtT%d" % i, [4, TB]) for i in range(2)]
    daT = sb("daT", [4, TB])
    csT = [sb("csT%d" % i, [4, TB]) for i in range(2)]
    yTo = [sb("yTo%d" % i, [128, 2, TB]) for i in range(2)]
    S = sb("S", [128, 256])
    S_bf = sb("S_bf", [128, 256], BF16)

    def two(name, shape, dt=F32):
        return [sb("%s%d" % (name, i), shape, dt) for i in range(2)]
    x_tok = two("x_tok", [128, 256])
    B_tok = two("B_tok", [128, 128], BF16)
    sm = two("sm", [128, 8])
    CBm = two("CBm", [128, 128])
    Dm = two("Dm", [128, 4, 128])
    G = two("G", [128, 4, 128], BF16)
    eFb = two("eFb", [128, 4, 128])
    CsT = two("CsT", [128, 4, 128], BF16)
    w4 = two("w4", [128, 8])
    xdt = two("xdt", [128, 4, 64], BF16)
    xdtd = two("xdtd", [128, 4, 64], BF16)
    y_tok = two("y_tok", [128, 256])

    ps = lambda name: nc.alloc_psum_tensor(name, [128, 512], F32)
    ps_tr = [ps("ps_tr0"), ps("ps_tr1")]
    ps_fb = [ps("ps_fb0"), ps("ps_fb1")]
    ps_cb = ps("ps_cb")
    ps_y = ps("ps_y")
    ps_st = ps("ps_st")
    ps_yT = ps("ps_yT")

    for inst in range(2):
        p.dma("sp", cw[:].rearrange("p t k -> p (t k)"), cwd[inst], writes=["cw"])
        p.dma("sp", dtb[:], dtbd[inst], writes=["dtb"])
        p.dma("sp", acol[:], alogd[inst], writes=["acol"])
        p.op("act", lambda: nc.scalar.activation(out=acol[:], in_=acol[:], func=AF.Exp), reads=["acol"], writes=["acol"])
        p.op("dve", lambda: nc.vector.tensor_scalar(out=acol[:], in0=acol[:], scalar1=-1.0, scalar2=None, op0=ALU.mult),
             reads=["acol"], writes=["acol"])
        p.op("dve", lambda: nc.vector.memset(S[:], 0.0), writes=["S"])
        p.op("dve", lambda: nc.vector.memset(S_bf[:], 0.0), writes=["S_bf"])
        for b in range(n_blocks):
            bp = b % 2
            col0 = b * TB
            for t in range(4):
                p.dma("sp" if t % 2 else "act", raw[bp][:, t, :], xbc[inst, t * 128:(t + 1) * 128, col0:col0 + TB + 4],
                      writes=[("raw", bp, t)])
            p.dma("sp", dtraw[:], dtr[inst, :, col0:col0 + TB], writes=["dtraw"])
            conv_silu_block(nc, p, raw[bp], lambda t: ("raw", bp, t), cw, cb, cT[bp], lambda t: ("cT", bp, t), 4, ctmp, "ctmp")
            p.op("act", lambda: nc.scalar.activation(out=daT[:], in_=dtraw[:], func=AF.Exp, bias=dtb[:, 0:1]),
                 reads=["dtraw", "dtb"], writes=["daT"])
            p.op("act", lambda: nc.scalar.activation(out=dtT[bp][:], in_=daT[:], func=AF.Ln, bias=1.0),
                 reads=["daT"], writes=[("dtT", bp)])
            p.op("dve", lambda: nc.vector.tensor_scalar(out=daT[:], in0=dtT[bp][:], scalar1=acol[:, 0:1], scalar2=None,
                                                        op0=ALU.mult),
                 reads=[("dtT", bp), "acol"], writes=["daT"])
            for j in range(4):
                cs = slice(j * 128, (j + 1) * 128)
                p.op("dve", lambda cs=cs: nc.vector.tensor_tensor_scan(
                    out=csT[bp][:, cs], data0=ones4[:], data1=daT[:, cs], initial=0.0, op0=ALU.mult, op1=ALU.add),
                    reads=["daT", "ones4"], writes=[("csT", bp)])
            for j in range(4):
                cs = slice(j * 128, (j + 1) * 128)
                c = b * 4 + j
                q = c % 2
                ctk = [("cT", bp, t) for t in range(4)]
                for t in range(3):
                    p.op("pe", lambda t=t: nc.tensor.transpose(out=ps_tr[q][:, t * 128:(t + 1) * 128],
                                                               in_=cT[bp][:, t, cs], identity=ident[:]),
                         reads=[("cT", bp, t), "c_ident"], writes=[("ps_tr", q)])
                p.op("pe", lambda: nc.tensor.transpose(out=ps_tr[q][:, 384:388], in_=dtT[bp][:, cs],
                                                       identity=ident[0:4, 0:4]),
                     reads=[("dtT", bp), "c_ident"], writes=[("ps_tr", q)])
                p.op("pe", lambda: nc.tensor.transpose(out=ps_tr[q][:, 388:392], in_=csT[bp][:, cs],
                                                       identity=ident[0:4, 0:4]),
                     reads=[("csT", bp), "c_ident"], writes=[("ps_tr", q)])
                p.op("act", lambda: nc.scalar.copy(out=x_tok[q][:], in_=ps_tr[q][:, 0:256]),
                     reads=[("ps_tr", q)], writes=[("x_tok", q)])
                p.op("dve", lambda: nc.vector.tensor_copy(out=B_tok[q][:], in_=ps_tr[q][:, 256:384]),
                     reads=[("ps_tr", q)], writes=[("B_tok", q)])
                p.op("dve", lambda: nc.vector.tensor_copy(out=sm[q][:], in_=ps_tr[q][:, 384:392]),
                     reads=[("ps_tr", q)], writes=[("sm", q)])
                for h in range(4):
                    p.op("pe", lambda h=h: nc.tensor.matmul(ps_fb[q][:, h * 128:(h + 1) * 128], lhsT=sel[:, h, :],
                                                            rhs=csT[bp][:, cs], start=True, stop=True),
                         reads=["sel", ("csT", bp)], writes=[("ps_fb", q)])
                p.op("pe", lambda: nc.tensor.matmul(ps_cb[:, 0:128], lhsT=cT[bp][:, 2, cs], rhs=cT[bp][:, 3, cs],
                                                    start=True, stop=True),
                     reads=[("cT", bp, 2), ("cT", bp, 3)], writes=["ps_cb"])
                p.op("dve", lambda: nc.vector.tensor_tensor(out=CBm[q][:], in0=ps_cb[:, 0:128], in1=triu[:], op=ALU.mult),
                     reads=["ps_cb", "c_triu"], writes=[("CBm", q)])
                fb3 = ps_fb[q][:].rearrange("p (h l) -> p h l", h=4)
                Ftok = sm[q][:, 4:8]
                p.op("dve", lambda: nc.vector.tensor_tensor(out=Dm[q][:], in0=fb3,
                                                            in1=Ftok.unsqueeze(2).broadcast_to([128, 4, 128]),
                                                            op=ALU.subtract),
                     reads=[("ps_fb", q), ("sm", q)], writes=[("Dm", q)])
                p.op("dve", lambda: nc.vector.tensor_scalar(out=Dm[q][:], in0=Dm[q][:], scalar1=0.0, scalar2=None, op0=ALU.min),
                     reads=[("Dm", q)], writes=[("Dm", q)])
                p.op("act", lambda: nc.scalar.activation(out=Dm[q][:], in_=Dm[q][:], func=AF.Exp),
                     reads=[("Dm", q)], writes=[("Dm", q)])
                p.op("dve", lambda: nc.vector.scalar_tensor_tensor(
                    out=G[q][:], in0=Dm[q][:], scalar=1.0, in1=CBm[q][:].unsqueeze(1).broadcast_to([128, 4, 128]),
                    op0=ALU.min, op1=ALU.mult), reads=[("Dm", q), ("CBm", q)], writes=[("G", q)])
                p.op("act", lambda: nc.scalar.activation(out=eFb[q][:], in_=fb3, func=AF.Exp),
                     reads=[("ps_fb", q)], writes=[("eFb", q)])
                p.op("pool", lambda: nc.gpsimd.tensor_tensor(
                    out=CsT[q][:], in0=eFb[q][:], in1=cT[bp][:, 3, cs].unsqueeze(1).broadcast_to([128, 4, 128]),
                    op=ALU.mult), reads=[("eFb", q), ("cT", bp, 3)], writes=[("CsT", q)])
                p.op("dve", lambda: nc.vector.tensor_tensor(out=w4[q][:, 0:4], in0=fb3[:, :, 127], in1=Ftok,
                                                            op=ALU.subtract),
                     reads=[("ps_fb", q), ("sm", q)], writes=[("w4", q)])
                p.op("act", lambda: nc.scalar.activation(out=w4[q][:, 0:4], in_=w4[q][:, 0:4], func=AF.Exp),
                     reads=[("w4", q)], writes=[("w4", q)])
                p.op("dve", lambda: nc.vector.tensor_tensor(out=w4[q][:, 4:8], in0=w4[q][:, 0:4], in1=sm[q][:, 0:4],
                                                            op=ALU.mult),
                     reads=[("w4", q), ("sm", q)], writes=[("w4", q)])
                x3 = x_tok[q][:].rearrange("p (h e) -> p h e", h=4)
                p.op("dve", lambda: nc.vector.tensor_tensor(out=xdt[q][:], in0=x3,
                                                            in1=sm[q][:, 0:4].unsqueeze(2).broadcast_to([128, 4, 64]),
                                                            op=ALU.mult),
                     reads=[("x_tok", q), ("sm", q)], writes=[("xdt", q)])
                p.op("pool", lambda: nc.gpsimd.tensor_tensor(out=xdtd[q][:], in0=x3,
                                                             in1=w4[q][:, 4:8].unsqueeze(2).broadcast_to([128, 4, 64]),
                                                             op=ALU.mult),
                     reads=[("x_tok", q), ("w4", q)], writes=[("xdtd", q)])
                for h in range(4):
                    hs = slice(h * 64, (h + 1) * 64)
                    p.op("pe", lambda h=h, hs=hs: nc.tensor.matmul(ps_y[:, hs], lhsT=G[q][:, h, :], rhs=xdt[q][:, h, :],
                                                                  start=True, stop=False),
                         reads=[("G", q), ("xdt", q)], writes=["ps_y"])
                    p.op("pe", lambda h=h, hs=hs: nc.tensor.matmul(ps_y[:, hs], lhsT=CsT[q][:, h, :], rhs=S_bf[:, hs],
                                                                  start=False, stop=True),
                         reads=[("CsT", q), "S_bf"], writes=["ps_y"])
                p.op("pe", lambda: nc.tensor.matmul(ps_st[:, 0:256], lhsT=B_tok[q][:],
                                                    rhs=xdtd[q][:].rearrange("p h e -> p (h e)"), start=True, stop=True),
                     reads=[("B_tok", q), ("xdtd", q)], writes=["ps_st"])
                for h in range(4):
                    hs = slice(h * 64, (h + 1) * 64)
                    p.op("dve", lambda h=h, hs=hs: nc.vector.scalar_tensor_tensor(
                        out=S[:, hs], in0=S[:, hs], scalar=eFb[q][:, h, 127:128], in1=ps_st[:, hs],
                        op0=ALU.mult, op1=ALU.add), reads=["S", ("eFb", q), "ps_st"], writes=["S"])
                p.op("act", lambda: nc.scalar.copy(out=S_bf[:], in_=S[:]), reads=["S"], writes=["S_bf"])
                p.op("act", lambda: nc.scalar.copy(out=y_tok[q][:], in_=ps_y[:, 0:256]), reads=["ps_y"],
                     writes=[("y_tok", q)])
                for t in range(2):
                    p.op("pe", lambda t=t: nc.tensor.transpose(out=ps_yT[:, t * 128:(t + 1) * 128],
                                                               in_=y_tok[q][:, t * 128:(t + 1) * 128], identity=ident[:]),
                         reads=[("y_tok", q), "c_ident"], writes=["ps_yT"])
                for t in range(2):
                    if inst == 0:
                        p.op("dve", lambda t=t: nc.vector.scalar_tensor_tensor(
                            out=yTo[bp][:, t, cs], in0=cT[bp][:, t, cs], scalar=dsk[:, t:t + 1],
                            in1=ps_yT[:, t * 128:(t + 1) * 128], op0=ALU.mult, op1=ALU.add),
                            reads=[("cT", bp, t), "dsk", "ps_yT"], writes=[("yTo", bp)])
                    else:
                        p.op("dve", lambda t=t: nc.vector.tensor_copy(out=yTo[bp][:, t, cs],
                                                                      in_=ps_yT[:, t * 128:(t + 1) * 128]),
                             reads=["ps_yT"], writes=[("yTo", bp)])
            for t in range(2):
                p.dma("sp", yT[inst, t * 128:(t + 1) * 128, col0:col0 + TB], yTo[bp][:, t, :],
                      reads=[("yTo", bp)], is_output=True)
    p.finish()
    return nc


CH = 64
EPS = 1e-6


def build_gdn(n_blocks=T_SEQ // TB):
    nc = bass.Bass("TRN2", target_bir_lowering=False)
    p = Prog(nc)
    T = n_blocks * TB
    din = lambda name, shape: nc.dram_tensor(name, shape, F32, kind="ExternalInput").ap()
    qkv = din("qkv", [2, 512, T + 4])
    abr = din("abr", [2, 2, 2, T])
    cwd = din("cw", [2, 128, 20])
    cbd = din("cb", [128, 4])
    dtbd = din("dtb", [2, 2, 1])
    alogd = din("alog", [2, 2, 1])
    oT = nc.dram_tensor("oT", [2, 256, T], F32, kind="ExternalOutput").ap()

    sb = lambda name, shape, dt=F32: nc.alloc_sbuf_tensor(name, shape, dt)
    C = make_consts(nc, p)
    ident, io = C["ident"], C["iota"]

    def blockdiag(B, name):
        nb = 128 // B
        E = sb("E" + name, [nb, 128])
        E2 = sb("E2" + name, [nb, 128])
        p.op("pool", lambda: nc.gpsimd.iota(E[:], pattern=[[1, 128]], base=0, channel_multiplier=-B,
                                            allow_small_or_imprecise_dtypes=True), writes=["E" + name])
        p.op("dve", lambda: nc.vector.tensor_single_scalar(out=E2[:], in_=E[:], scalar=float(B), op=ALU.is_lt),
             reads=["E" + name], writes=["E2" + name])
        p.op("dve", lambda: nc.vector.tensor_single_scalar(out=E[:], in_=E[:], scalar=0.0, op=ALU.is_ge),
             reads=["E" + name], writes=["E" + name])
        p.op("dve", lambda: nc.vector.tensor_tensor(out=E[:], in0=E[:], in1=E2[:], op=ALU.mult),
             reads=["E" + name, "E2" + name], writes=["E" + name])
        m = sb("bd" + name, [128, 128])
        pst = ps_n[0]
        p.op("pe", lambda: nc.tensor.matmul(pst[:, 0:128], lhsT=E[:], rhs=E[:], start=True, stop=True),
             reads=["E" + name], writes=["ps_n0"])
        p.op("dve", lambda: nc.vector.tensor_copy(out=m[:], in_=pst[:, 0:128]), reads=["ps_n0"], writes=["bd" + name])
        return m

    ps = lambda name: nc.alloc_psum_tensor(name, [128, 512], F32)
    ps_n = [ps("ps_n0"), ps("ps_n1")]
    ps_stack = ps("ps_stack")
    ps_small = ps("ps_small")
    ps_row = ps("ps_row")
    ps_kq = ps("ps_kq")
    ps_s1 = ps("ps_s1")
    ps_s2 = ps("ps_s2")

    bd16 = blockdiag(16, "16")
    bd32 = blockdiag(32, "32")
    bd64 = blockdiag(64, "64")
    m32 = sb("m32", [128, 128])
    m64 = sb("m64", [128, 128])
    MUi = sb("MUi", [128, 128])
    MLs = sb("MLs", [128, 128])
    p.op("dve", lambda: nc.vector.tensor_tensor(out=m32[:], in0=bd32[:], in1=bd16[:], op=ALU.subtract),
         reads=["bd32", "bd16"], writes=["m32"])
    p.op("dve", lambda: nc.vector.tensor_tensor(out=m64[:], in0=bd64[:], in1=bd32[:], op=ALU.subtract),
         reads=["bd64", "bd32"], writes=["m64"])
    p.op("dve", lambda: nc.vector.tensor_single_scalar(out=MUi[:], in_=io[:], scalar=0.0, op=ALU.is_ge),
         reads=["c_iota"], writes=["MUi"])
    p.op("dve", lambda: nc.vector.tensor_tensor(out=MUi[:], in0=MUi[:], in1=bd64[:], op=ALU.mult),
         reads=["MUi", "bd64"], writes=["MUi"])
    p.op("dve", lambda: nc.vector.tensor_single_scalar(out=MLs[:], in_=io[:], scalar=0.0, op=ALU.is_lt),
         reads=["c_iota"], writes=["MLs"])
    p.op("dve", lambda: nc.vector.tensor_tensor(out=MLs[:], in0=MLs[:], in1=bd64[:], op=ALU.mult),
         reads=["MLs", "bd64"], writes=["MLs"])
    ones2 = sb("ones2", [2, 128])
    p.op("pool", lambda: nc.gpsimd.memset(ones2[:], 1.0), writes=["ones2"])
    onesb = sb("onesb", [128, 128], BF16)
    p.op("pool", lambda: nc.gpsimd.memset(onesb[:], 1.0), writes=["onesb"])

    cw = sb("cw_sb", [128, 4, 5])
    cb = sb("cb_sb", [128, 4])
    dtb = sb("dtb_sb", [2, 1])
    acol = sb("acol", [2, 1])
    p.dma("sp", cb[:], cbd, writes=["cb"])

    raw = [sb("raw%d" % i, [128, 4, TB + 4]) for i in range(2)]
    cT = [sb("cT%d" % i, [128, 4, TB]) for i in range(2)]
    ctmp = sb("ctmp", [128, TB])
    sqb = sb("sqb", [128, TB], BF16)
    rs = sb("rs", [128, TB])
    qT2 = [sb("qT2_%d" % i, [128, TB // CH, 2, CH]) for i in range(2)]
    kT2 = [sb("kT2_%d" % i, [128, TB // CH, 2, CH]) for i in range(2)]
    vT2 = [sb("vT2_%d" % i, [128, TB // CH, 2, CH]) for i in range(2)]
    araw = sb("araw", [2, TB])
    braw = sb("braw", [2, TB])
    gT = sb("gT", [2, TB])
    gcsT = sb("gcsT", [2, TB])
    betaT = sb("betaT", [2, TB])
    glT = sb("glT", [2, TB])
    Mg = [sb("Mg%d" % i, [2, TB // CH, 2, CH]) for i in range(2)]
    Mb = [sb("Mb%d" % i, [2, TB // CH, 2, CH]) for i in range(2)]
    Ml = [sb("Ml%d" % i, [2, TB // CH, 2, CH]) for i in range(2)]
    oTo = [sb("oTo%d" % i, [128, 2, TB]) for i in range(2)]
    S = sb("S", [128, 256])

    def two(name, shape, dt=F32):
        return [sb("%s%d" % (name, i), shape, dt) for i in range(2)]
    colsb = two("colsb", [128, 8])
    egrow = two("egrow", [128, 128])
    D1 = two("D1", [128, 128])
    D2 = two("D2", [128, 128])
    qkT = two("qkT", [128, 128])
    Am = two("Am", [128, 128])
    ATm = two("ATm", [128, 128])
    X = two("X", [128, 128]); Y = two("Y", [128, 128])
    Ao32 = two("Ao32", [128, 128]); Ao32T = two("Ao32T", [128, 128]); Ao64 = two("Ao64", [128, 128])
    Tm = two("Tm", [128, 128]); Um = two("Um", [128, 128])
    X2 = two("X2", [128, 128]); Y2 = two("Y2", [128, 128])
    Rm = two("Rm", [128, 128]); Pm = two("Pm", [128, 128])
    RHSv = two("RHSv", [128, 128]); RHSw = two("RHSw", [128, 128]); kdec = two("kdec", [128, 2, 128])
    qgT = two("qgT", [128, 128])
    u_sb = two("u_sb", [128, 128]); wT_sb = two("wT_sb", [128, 128])
    vnew = two("vnew", [128, 128])

    ncnt = [0]

    def mmN(lhsT, rhs, rkeys):
        i = ncnt[0] % 2
        ncnt[0] += 1
        key = "ps_n%d" % i
        p.op("pe", lambda: nc.tensor.matmul(ps_n[i][:, 0:128], lhsT=lhsT, rhs=rhs, start=True, stop=True),
             reads=rkeys, writes=[key])
        return ps_n[i][:, 0:128], key

    def ev_copy(dst, dkey, src, skey):
        p.op("act", lambda: nc.scalar.copy(out=dst, in_=src), reads=[skey], writes=[dkey])

    def ev_comb(dst, dkey, a, akey, src, skey, op):
        p.op("dve", lambda: nc.vector.tensor_tensor(out=dst, in0=a, in1=src, op=op), reads=[akey, skey], writes=[dkey])

    for inst in range(2):
        p.dma("sp", cw[:].rearrange("p t k -> p (t k)"), cwd[inst], writes=["cw"])
        p.dma("sp", dtb[:], dtbd[inst], writes=["dtb"])
        p.dma("sp", acol[:], alogd[inst], writes=["acol"])
        p.op("act", lambda: nc.scalar.activation(out=acol[:], in_=acol[:], func=AF.Exp), reads=["acol"], writes=["acol"])
        p.op("dve", lambda: nc.vector.tensor_scalar(out=acol[:], in0=acol[:], scalar1=-1.0, scalar2=None, op0=ALU.mult),
             reads=["acol"], writes=["acol"])
        p.op("dve", lambda: nc.vector.memset(S[:], 0.0), writes=["S"])
        for b in range(n_blocks):
            bp = b % 2
            col0 = b * TB
            for t in range(4):
                p.dma("sp" if t % 2 else "act", raw[bp][:, t, :], qkv[inst, t * 128:(t + 1) * 128, col0:col0 + TB + 4],
                      writes=[("raw", bp, t)])
            p.dma("sp", araw[:], abr[inst, 0, :, col0:col0 + TB], writes=["araw"])
            p.dma("sp", braw[:], abr[inst, 1, :, col0:col0 + TB], writes=["braw"])
            conv_silu_block(nc, p, raw[bp], lambda t: ("raw", bp, t), cw, cb, cT[bp], lambda t: ("cT", bp, t), 4, ctmp, "ctmp")
            for t, dst, dname, sc in ((0, qT2[bp], "qT2", 128.0 ** -0.5), (1, kT2[bp], "kT2", 1.0)):
                p.op("act", lambda t=t: nc.scalar.activation(out=sqb[:], in_=cT[bp][:, t, :], func=AF.Square),
                     reads=[("cT", bp, t)], writes=["sqb"])
                p.op("pe", lambda: nc.tensor.matmul(ps_n[0][:], lhsT=onesb[:], rhs=sqb[:], start=True, stop=True),
                     reads=["onesb", "sqb"], writes=["ps_n0"])
                p.op("act", lambda: nc.scalar.activation(out=rs[:], in_=ps_n[0][:], func=AF.Ln, bias=EPS),
                     reads=["ps_n0"], writes=["rs"])
                p.op("act", lambda: nc.scalar.activation(out=rs[:], in_=rs[:], func=AF.Exp, scale=-0.5),
                     reads=["rs"], writes=["rs"])
                for vh in range(2):
                    p.op("dve", lambda t=t, dst=dst, sc=sc, vh=vh: nc.vector.scalar_tensor_tensor(
                        out=dst[:, :, vh, :], in0=cT[bp][:, t, :].rearrange("p (c i) -> p c i", i=CH), scalar=sc,
                        in1=rs[:].rearrange("p (c i) -> p c i", i=CH), op0=ALU.mult, op1=ALU.mult),
                        reads=[("cT", bp, t), "rs"], writes=[(dname, bp)])
            for vh in range(2):
                p.op("pool", lambda vh=vh: nc.gpsimd.tensor_copy(
                    out=vT2[bp][:, :, vh, :], in_=cT[bp][:, 2 + vh, :].rearrange("p (c i) -> p c i", i=CH)),
                    reads=[("cT", bp, 2 + vh)], writes=[("vT2", bp)])
            p.op("act", lambda: nc.scalar.activation(out=gT[:], in_=araw[:], func=AF.Exp, bias=dtb[:, 0:1]),
                 reads=["araw", "dtb"], writes=["gT"])
            p.op("act", lambda: nc.scalar.activation(out=gT[:], in_=gT[:], func=AF.Ln, bias=1.0),
                 reads=["gT"], writes=["gT"])
            p.op("dve", lambda: nc.vector.tensor_scalar(out=gT[:], in0=gT[:], scalar1=acol[:, 0:1], scalar2=None,
                                                        op0=ALU.mult), reads=["gT", "acol"], writes=["gT"])
            p.op("act", lambda: nc.scalar.activation(out=betaT[:], in_=braw[:], func=AF.Sigmoid),
                 reads=["braw"], writes=["betaT"])
            for j in range(TB // CH):
                cs = slice(j * CH, (j + 1) * CH)
                p.op("dve", lambda cs=cs: nc.vector.tensor_tensor_scan(
                    out=gcsT[:, cs], data0=ones2[:, 0:CH], data1=gT[:, cs], initial=0.0, op0=ALU.mult, op1=ALU.add),
                    reads=["gT", "ones2"], writes=["gcsT"])
            for j in range(TB // CH):
                cs = slice(j * CH, (j + 1) * CH)
                e = (j + 1) * CH - 1
                p.op("pool", lambda cs=cs, e=e: nc.gpsimd.tensor_copy(out=glT[:, cs],
                                                                     in_=gcsT[:, e:e + 1].broadcast_to([2, CH])),
                     reads=["gcsT"], writes=["glT"])
            for src, dst, nm in ((gcsT, Mg[bp], "Mg"), (betaT, Mb[bp], "Mb"), (glT, Ml[bp], "Ml")):
                for vh in range(2):
                    p.op("dve", lambda src=src, dst=dst, vh=vh: nc.vector.tensor_scalar(
                        out=dst[:, :, vh, :], in0=src[:].rearrange("p (c i) -> p c i", i=CH), scalar1=ident[0:2, vh:vh + 1],
                        scalar2=None, op0=ALU.mult),
                        reads=[src is gcsT and "gcsT" or (src is betaT and "betaT" or "glT"), "c_ident"],
                        writes=[(nm, bp)])

            for j in range(TB // CH):
                cs = slice(j * CH, (j + 1) * CH)
                c = b * (TB // CH) + j
                q = c % 2
                for i, (M, nm) in enumerate(((Mg[bp], "Mg"), (Mb[bp], "Mb"), (Ml[bp], "Ml"))):
                    p.op("pe", lambda i=i, M=M: nc.tensor.matmul(ps_small[:, 2 * i:2 * i + 2], lhsT=M[:, j, :, :].rearrange("p v i -> p (v i)"), rhs=ones2[:, 0:2],
                                                                 start=True, stop=True),
                         reads=[(nm, bp), "ones2"], writes=["ps_small"])
                p.op("pe", lambda: nc.tensor.matmul(ps_row[:, 0:128], lhsT=ones2[:], rhs=Mg[bp][:, j, :, :].rearrange("p v i -> p (v i)"),
                                                    start=True, stop=True),
                     reads=[("Mg", bp), "ones2"], writes=["ps_row"])
                cq = colsb[q]
                ck = ("colsb", q)
                p.op("dve", lambda: nc.vector.tensor_copy(out=cq[:, 0:3], in_=ps_small[:, 0:6].rearrange("p (a b) -> p a b", b=2)[:, :, 0]),
                     reads=["ps_small"], writes=[ck])
                p.op("act", lambda: nc.scalar.activation(out=cq[:, 3:4], in_=cq[:, 0:1], func=AF.Exp), reads=[ck], writes=[ck])
                p.op("dve", lambda: nc.vector.tensor_tensor(out=cq[:, 4:5], in0=cq[:, 1:2], in1=cq[:, 3:4], op=ALU.mult),
                     reads=[ck], writes=[ck])
                p.op("dve", lambda: nc.vector.tensor_tensor(out=cq[:, 5:6], in0=cq[:, 2:3], in1=cq[:, 0:1], op=ALU.subtract),
                     reads=[ck], writes=[ck])
                p.op("act", lambda: nc.scalar.activation(out=cq[:, 5:6], in_=cq[:, 5:6], func=AF.Exp), reads=[ck], writes=[ck])
                p.op("dve", lambda: nc.vector.tensor_scalar(out=cq[:, 6:7], in0=cq[:, 0:1], scalar1=-1.0, scalar2=None,
                                                            op0=ALU.mult), reads=[ck], writes=[ck])
                p.op("act", lambda: nc.scalar.activation(out=egrow[q][:], in_=ps_row[:, 0:128], func=AF.Exp),
                     reads=["ps_row"], writes=[("egrow", q)])
                p.op("dve", lambda: nc.vector.tensor_scalar(out=D1[q][:], in0=ps_row[:, 0:128], scalar1=cq[:, 0:1],
                                                            scalar2=0.0, op0=ALU.subtract, op1=ALU.max),
                     reads=["ps_row", ck], writes=[("D1", q)])
                p.op("dve", lambda: nc.vector.tensor_scalar(out=D2[q][:], in0=ps_row[:, 0:128], scalar1=cq[:, 0:1],
                                                            scalar2=0.0, op0=ALU.subtract, op1=ALU.min),
                     reads=["ps_row", ck], writes=[("D2", q)])
                p.op("act", lambda: nc.scalar.activation(out=D1[q][:], in_=D1[q][:], func=AF.Exp, scale=-1.0),
                     reads=[("D1", q)], writes=[("D1", q)])
                p.op("act", lambda: nc.scalar.activation(out=D2[q][:], in_=D2[q][:], func=AF.Exp),
                     reads=[("D2", q)], writes=[("D2", q)])
                p.op("dve", lambda: nc.vector.scalar_tensor_tensor(out=D1[q][:], in0=D1[q][:], scalar=1.0, in1=MLs[:],
                                                                   op0=ALU.min, op1=ALU.mult),
                     reads=[("D1", q), "MLs"], writes=[("D1", q)])
                p.op("dve", lambda: nc.vector.scalar_tensor_tensor(out=D2[q][:], in0=D2[q][:], scalar=1.0, in1=MUi[:],
                                                                   op0=ALU.min, op1=ALU.mult),
                     reads=[("D2", q), "MUi"], writes=[("D2", q)])
                kk = kT2[bp][:, j, :, :].rearrange("p v i -> p (v i)")
                qq = qT2[bp][:, j, :, :].rearrange("p v i -> p (v i)")
                p.op("pe", lambda: nc.tensor.matmul(ps_kq[:, 0:128], lhsT=kk, rhs=kk, start=True, stop=True),
                     reads=[("kT2", bp)], writes=["ps_kq"])
                p.op("pe", lambda: nc.tensor.matmul(ps_kq[:, 128:256], lhsT=kk, rhs=qq, start=True, stop=True),
                     reads=[("kT2", bp), ("qT2", bp)], writes=["ps_kq"])
                p.op("dve", lambda: nc.vector.scalar_tensor_tensor(out=Am[q][:], in0=D1[q][:], scalar=cq[:, 1:2],
                                                                   in1=ps_kq[:, 0:128], op0=ALU.mult, op1=ALU.mult),
                     reads=[("D1", q), ck, "ps_kq"], writes=[("Am", q)])
                p.op("dve", lambda: nc.vector.tensor_tensor(out=qkT[q][:], in0=D2[q][:], in1=ps_kq[:, 128:256], op=ALU.mult),
                     reads=[("D2", q), "ps_kq"], writes=[("qkT", q)])
                p.op("pe", lambda: nc.tensor.matmul(ps_stack[:, 0:128], lhsT=vT2[bp][:, j, :, :].rearrange("p v i -> p (v i)"), rhs=ident[:],
                                                    start=True, stop=True),
                     reads=[("vT2", bp), "c_ident"], writes=["ps_stack"])
                p.op("pe", lambda: nc.tensor.matmul(ps_stack[:, 128:256], lhsT=kk, rhs=ident[:], start=True, stop=True),
                     reads=[("kT2", bp), "c_ident"], writes=["ps_stack"])
                p.op("dve", lambda: nc.vector.tensor_scalar(out=RHSv[q][:], in0=ps_stack[:, 0:128], scalar1=cq[:, 1:2],
                                                            scalar2=None, op0=ALU.mult),
                     reads=["ps_stack", ck], writes=[("RHSv", q)])
                p.op("dve", lambda: nc.vector.tensor_scalar(out=RHSw[q][:], in0=ps_stack[:, 128:256], scalar1=cq[:, 4:5],
                                                            scalar2=None, op0=ALU.mult),
                     reads=["ps_stack", ck], writes=[("RHSw", q)])
                for vh in range(2):
                    p.op("dve", lambda vh=vh: nc.vector.tensor_scalar(
                        out=kdec[q][:, vh, :], in0=ps_stack[:, 128:256], scalar1=cq[:, 5:6],
                        scalar2=bd64[:, 64 * vh:64 * vh + 1], op0=ALU.mult, op1=ALU.mult),
                        reads=["ps_stack", ck, "bd64"], writes=[("kdec", q)])
                p.op("pool", lambda: nc.gpsimd.tensor_tensor(
                    out=qgT[q][:], in0=qq, in1=egrow[q][:], op=ALU.mult),
                     reads=[("qT2", bp), ("egrow", q)], writes=[("qgT", q)])
                pa, pk = mmN(Am[q][:], ident[:], [("Am", q), "c_ident"])
                ev_copy(ATm[q][:], ("ATm", q), pa, pk)
                for dst, nm, src, snm, msk, mnm in ((X, "X", Am, "Am", bd16, "bd16"), (Y, "Y", ATm, "ATm", bd16, "bd16"),
                                                    (Ao32, "Ao32", Am, "Am", m32, "m32"),
                                                    (Ao32T, "Ao32T", ATm, "ATm", m32, "m32"),
                                                    (Ao64, "Ao64", Am, "Am", m64, "m64")):
                    p.op("pool", lambda dst=dst, src=src, msk=msk: nc.gpsimd.tensor_tensor(out=dst[q][:], in0=src[q][:],
                                                                                          in1=msk[:], op=ALU.mult),
                         reads=[(snm, q), mnm], writes=[(nm, q)])
                Tq, Uq = Tm[q], Um[q]
                tk, uk = ("Tm", q), ("Um", q)
                p.op("dve", lambda: nc.vector.tensor_tensor(out=Tq[:], in0=ident[:], in1=X[q][:], op=ALU.subtract),
                     reads=["c_ident", ("X", q)], writes=[tk])
                p.op("dve", lambda: nc.vector.tensor_tensor(out=Uq[:], in0=ident[:], in1=Y[q][:], op=ALU.subtract),
                     reads=["c_ident", ("Y", q)], writes=[uk])
                xa, ya, xk_, yk_ = X[q], Y[q], ("X", q), ("Y", q)
                xb, yb, xbk, ybk = X2[q], Y2[q], ("X2", q), ("Y2", q)
                for lvl in range(3):
                    pa, pk = mmN(ya[:], xa[:], [yk_, xk_])
                    ev_copy(xb[:], xbk, pa, pk)
                    if lvl < 2:
                        pa, pk = mmN(xa[:], ya[:], [xk_, yk_])
                        ev_copy(yb[:], ybk, pa, pk)
                    pa, pk = mmN(Uq[:], xb[:], [uk, xbk])
                    pb, pkb = mmN(xb[:], Uq[:], [xbk, uk])
                    ev_comb(Tq[:], tk, Tq[:], tk, pa, pk, ALU.add)
                    ev_comb(Uq[:], uk, Uq[:], uk, pb, pkb, ALU.add)
                    xa, ya, xk_, yk_, xb, yb, xbk, ybk = xb, yb, xbk, ybk, xa, ya, xk_, yk_
                pa, pk = mmN(Ao32T[q][:], Tq[:], [("Ao32T", q), tk])
                ev_copy(Rm[q][:], ("Rm", q), pa, pk)
                pa, pk = mmN(Ao32[q][:], Uq[:], [("Ao32", q), uk])
                ev_copy(Pm[q][:], ("Pm", q), pa, pk)
                pa, pk = mmN(Uq[:], Rm[q][:], [uk, ("Rm", q)])
                pb, pkb = mmN(Tq[:], Pm[q][:], [tk, ("Pm", q)])
                ev_comb(Tq[:], tk, Tq[:], tk, pa, pk, ALU.subtract)
                ev_comb(Uq[:], uk, Uq[:], uk, pb, pkb, ALU.subtract)
                pa, pk = mmN(Ao64[q][:], Uq[:], [("Ao64", q), uk])
                ev_copy(Pm[q][:], ("Pm", q), pa, pk)
                pb, pkb = mmN(Tq[:], Pm[q][:], [tk, ("Pm", q)])
                ev_comb(Uq[:], uk, Uq[:], uk, pb, pkb, ALU.subtract)
                pa, pk = mmN(Uq[:], RHSv[q][:], [uk, ("RHSv", q)])
                ev_copy(u_sb[q][:], ("u_sb", q), pa, pk)
                pa, pk = mmN(RHSw[q][:], Uq[:], [("RHSw", q), uk])
                ev_copy(wT_sb[q][:], ("wT_sb", q), pa, pk)
                for vh in range(2):
                    hs = slice(vh * 64, (vh + 1) * 64)
                    vs = slice(vh * 128, (vh + 1) * 128)
                    p.op("pe", lambda hs=hs, vs=vs: nc.tensor.matmul(ps_s1[:, vs], lhsT=wT_sb[q][:], rhs=S[:, vs],
                                                                    start=True, stop=True),
                         reads=[("wT_sb", q), "S"], writes=["ps_s1"])
                for vh in range(2):
                    hs = slice(vh * 64, (vh + 1) * 64)
                    vs = slice(vh * 128, (vh + 1) * 128)
                    p.op("dve", lambda hs=hs, vs=vs: nc.vector.tensor_tensor(out=vnew[q][hs, :], in0=u_sb[q][hs, :],
                                                                            in1=ps_s1[hs, vs], op=ALU.subtract),
                         reads=[("u_sb", q), "ps_s1"], writes=[("vnew", q)])
                for vh in range(2):
                    hs = slice(vh * 64, (vh + 1) * 64)
                    vs = slice(vh * 128, (vh + 1) * 128)
                    oc = slice(256 + vh * 64, 256 + (vh + 1) * 64)
                    p.op("pe", lambda hs=hs, vs=vs, oc=oc: nc.tensor.matmul(ps_s1[:, oc], lhsT=S[:, vs], rhs=qgT[q][:, hs],
                                                                           start=True, stop=False),
                         reads=["S", ("qgT", q)], writes=["ps_s1"])
                    p.op("pe", lambda hs=hs, oc=oc: nc.tensor.matmul(ps_s1[:, oc], lhsT=vnew[q][:], rhs=qkT[q][:, hs],
                                                                    start=False, stop=True),
                         reads=[("vnew", q), ("qkT", q)], writes=["ps_s1"])
                p.op("act", lambda: nc.scalar.copy(out=oTo[bp][:, :, cs],
                                                   in_=ps_s1[:, 256:384].rearrange("p (v i) -> p v i", v=2)),
                     reads=["ps_s1"], writes=[("oTo", bp)])
                for vh in range(2):
                    hs = slice(vh * 64, (vh + 1) * 64)
                    vs = slice(vh * 128, (vh + 1) * 128)
                    p.op("pe", lambda hs=hs, vs=vs, vh=vh: nc.tensor.matmul(ps_s2[:, vs], lhsT=kdec[q][:, vh, :], rhs=vnew[q][:],
                                                                    start=True, stop=True),
                         reads=[("kdec", q), ("vnew", q)], writes=["ps_s2"])
                for vh in range(2):
                    vs = slice(vh * 128, (vh + 1) * 128)
                    e = vh * 64 + 63
                    p.op("dve", lambda vs=vs, e=e: nc.vector.scalar_tensor_tensor(
                        out=S[:, vs], in0=S[:, vs], scalar=egrow[q][:, e:e + 1], in1=ps_s2[:, vs],
                        op0=ALU.mult, op1=ALU.add), reads=["S", ("egrow", q), "ps_s2"], writes=["S"])
            for vh in range(2):
                p.dma("sp", oT[inst, vh * 128:(vh + 1) * 128, col0:col0 + TB], oTo[bp][:, vh, :],
                      reads=[("oTo", bp)], is_output=True)
    p.finish()
    return nc


T_SEQ = 16384
TC = 512
NP = 4


def build_s5(n_blocks=T_SEQ // TC):
    nc = bass.Bass("TRN2", target_bir_lowering=False)
    p = Prog(nc)
    T = n_blocks * TC
    din = lambda name, shape: nc.dram_tensor(name, shape, F32, kind="ExternalInput").ap()
    uT = din("uT", [2, 128, T])
    lre_d = din("lam_re", [2, 128, NP])
    lim_d = din("lam_im", [2, 128, NP])
    lst_d = din("log_step", [2, 128, NP])
    bre_d = din("b_re", [128, NP * 128])
    bim_d = din("b_im", [128, NP * 128])
    cre_d = din("c_re", [2, 128, NP * 128])
    cim_d = din("c_im", [2, 128, NP * 128])
    yT = nc.dram_tensor("yT", [2, 128, T], F32, kind="ExternalOutput").ap()

    sb = lambda name, shape, dt=F32: nc.alloc_sbuf_tensor(name, shape, dt)
    tpr = sb("tpr", [128, TC])
    p.op("pool", lambda: nc.gpsimd.iota(tpr[:], pattern=[[1, TC]], base=0, channel_multiplier=0,
                                        allow_small_or_imprecise_dtypes=True), writes=["tpr"])
    Bre = sb("Bre", [128, NP * 128]); Bim = sb("Bim", [128, NP * 128])
    Cre = sb("Cre", [128, NP * 128]); Cim = sb("Cim", [128, NP * 128])
    p.dma("sp", Bre[:], bre_d, writes=["Bre"])
    p.dma("sp", Bim[:], bim_d, writes=["Bim"])
    lre = sb("lre", [128, NP]); lim = sb("lim", [128, NP]); stp = sb("stp", [128, NP])
    rr = sb("rr", [128, NP]); thn = sb("thn", [128, NP])
    sc = {n: sb("sc_" + n, [128, NP]) for n in ("cos", "sin", "rc", "rs", "zr", "zi", "nzr", "den", "t1", "t2", "y")}
    sy = sb("sy", [128, TC]); sk = sb("sk", [128, TC], I32); sf = sb("sf", [128, TC])
    s2 = sb("s2", [128, TC]); q4 = sb("q4", [128, TC]); c2 = sb("c2", [128, TC])
    tS = sb("tS", [128, TC]); tCo = sb("tCo", [128, TC])
    Ezr = sb("Ezr", [128, NP, TC]); Ezi = sb("Ezi", [128, NP, TC])
    Fr = sb("Fr", [128, NP, TC]); Fi = sb("Fi", [128, NP, TC])
    carry = sb("carry", [128, NP, 2])
    ctmp = sb("carry_tmp", [128, 4])
    ub = [sb("ub%d" % i, [128, TC]) for i in range(2)]
    yo = [sb("yo%d" % i, [128, TC]) for i in range(2)]

    def two(name):
        return [sb("%s%d" % (name, i), [128, TC]) for i in range(2)]
    m1, m2, m3, m4 = two("m1"), two("m2"), two("m3"), two("m4")
    xr_, xi_ = two("xr_"), two("xi_")
    sr_, si_ = two("sr_"), two("si_")
    d1, d2, d3, d4 = two("d1"), two("d2"), two("d3"), two("d4")
    or_, oi_ = two("or_"), two("oi_")
    ps = lambda name: nc.alloc_psum_tensor(name, [128, 512], F32)
    ps_xr = [ps("ps_xr0"), ps("ps_xr1")]
    ps_xi = [ps("ps_xi0"), ps("ps_xi1")]
    ps_y = [ps("ps_y0"), ps("ps_y1")]

    def sincos(y_ap, n, out_s, out_c, ykeys, okeys):
        K = "sincos_tmp"
        p.op("dve", lambda: nc.vector.tensor_copy(out=sk[:, 0:n], in_=y_ap), reads=ykeys, writes=[K])
        p.op("dve", lambda: nc.vector.tensor_copy(out=sf[:, 0:n], in_=sk[:, 0:n]), reads=[K], writes=[K])
        p.op("dve", lambda: nc.vector.tensor_tensor(out=sf[:, 0:n], in0=y_ap, in1=sf[:, 0:n], op=ALU.subtract),
             reads=ykeys + [K], writes=[K])
        p.op("act", lambda: nc.scalar.activation(out=s2[:, 0:n], in_=sf[:, 0:n], func=AF.Sin, scale=math.pi),
             reads=[K], writes=[K])
        p.op("act", lambda: nc.scalar.activation(out=q4[:, 0:n], in_=sf[:, 0:n], func=AF.Sin, scale=math.pi / 2),
             reads=[K], writes=[K])
        p.op("dve", lambda: nc.vector.tensor_tensor(out=c2[:, 0:n], in0=q4[:, 0:n], in1=q4[:, 0:n], op=ALU.mult),
             reads=[K], writes=[K])
        p.op("dve", lambda: nc.vector.tensor_scalar(out=c2[:, 0:n], in0=c2[:, 0:n], scalar1=-2.0, scalar2=1.0,
                                                    op0=ALU.mult, op1=ALU.add), reads=[K], writes=[K])
        p.op("dve", lambda: nc.vector.scalar_tensor_tensor(out=out_s, in0=s2[:, 0:n], scalar=2.0, in1=c2[:, 0:n],
                                                           op0=ALU.mult, op1=ALU.mult), reads=[K], writes=okeys)
        p.op("dve", lambda: nc.vector.tensor_tensor(out=c2[:, 0:n], in0=s2[:, 0:n], in1=s2[:, 0:n], op=ALU.mult),
             reads=[K], writes=[K])
        p.op("dve", lambda: nc.vector.tensor_scalar(out=out_c, in0=c2[:, 0:n], scalar1=-2.0, scalar2=1.0,
                                                    op0=ALU.mult, op1=ALU.add), reads=[K], writes=okeys)

    for inst in range(2):
        p.dma("sp", lre[:], lre_d[inst], writes=["prm"])
        p.dma("sp", lim[:], lim_d[inst], writes=["prm"])
        p.dma("sp", stp[:], lst_d[inst], writes=["prm"])
        p.dma("sp", Cre[:], cre_d[inst], writes=["Cre"])
        p.dma("sp", Cim[:], cim_d[inst], writes=["Cim"])
        P = ["prm"]
        p.op("act", lambda: nc.scalar.activation(out=stp[:], in_=stp[:], func=AF.Exp), reads=P, writes=P)
        p.op("dve", lambda: nc.vector.tensor_tensor(out=rr[:], in0=lre[:], in1=stp[:], op=ALU.mult), reads=P, writes=P)
        p.op("act", lambda: nc.scalar.activation(out=rr[:], in_=rr[:], func=AF.Exp), reads=P, writes=P)
        p.op("dve", lambda: nc.vector.tensor_tensor(out=thn[:], in0=lim[:], in1=stp[:], op=ALU.mult), reads=P, writes=P)
        p.op("dve", lambda: nc.vector.tensor_scalar(out=thn[:], in0=thn[:], scalar1=1.0 / (2 * math.pi), scalar2=None,
                                                    op0=ALU.mult), reads=P, writes=P)
        sincos(thn[:], NP, sc["sin"][:], sc["cos"][:], P, P)
        p.op("dve", lambda: nc.vector.tensor_scalar(out=sc["y"][:], in0=thn[:], scalar1=float(TC), scalar2=None,
                                                    op0=ALU.mult), reads=P, writes=P)
        sincos(sc["y"][:], NP, sc["rs"][:], sc["rc"][:], P, P)
        tt = lambda o, a, b, op: p.op("dve", lambda: nc.vector.tensor_tensor(out=o, in0=a, in1=b, op=op), reads=P, writes=P)
        tt(sc["cos"][:], sc["cos"][:], rr[:], ALU.mult)
        tt(sc["sin"][:], sc["sin"][:], rr[:], ALU.mult)
        p.op("dve", lambda: nc.vector.tensor_scalar(out=sc["cos"][:], in0=sc["cos"][:], scalar1=-1.0, scalar2=None,
                                                    op0=ALU.add), reads=P, writes=P)
        tt(sc["t1"][:], lre[:], lre[:], ALU.mult)
        tt(sc["t2"][:], lim[:], lim[:], ALU.mult)
        tt(sc["den"][:], sc["t1"][:], sc["t2"][:], ALU.add)
        p.op("dve", lambda: nc.vector.reciprocal(out=sc["den"][:], in_=sc["den"][:]), reads=P, writes=P)
        tt(sc["t1"][:], sc["cos"][:], lre[:], ALU.mult)
        tt(sc["t2"][:], sc["sin"][:], lim[:], ALU.mult)
        tt(sc["zr"][:], sc["t1"][:], sc["t2"][:], ALU.add)
        tt(sc["zr"][:], sc["zr"][:], sc["den"][:], ALU.mult)
        tt(sc["t1"][:], sc["sin"][:], lre[:], ALU.mult)
        tt(sc["t2"][:], sc["cos"][:], lim[:], ALU.mult)
        tt(sc["zi"][:], sc["t1"][:], sc["t2"][:], ALU.subtract)
        tt(sc["zi"][:], sc["zi"][:], sc["den"][:], ALU.mult)
        p.op("dve", lambda: nc.vector.tensor_scalar(out=sc["nzr"][:], in0=sc["zr"][:], scalar1=-1.0, scalar2=None,
                                                    op0=ALU.mult), reads=P, writes=P)
        for pr in range(NP):
            p.op("dve", lambda pr=pr: nc.vector.tensor_scalar(out=sy[:], in0=tpr[:], scalar1=thn[:, pr:pr + 1], scalar2=None,
                                                              op0=ALU.mult), reads=["tpr"] + P, writes=["sy"])
            sincos(sy[:], TC, tS[:], tCo[:], ["sy"], ["tSC"])
            tk = ("tab", pr)
            p.op("pool", lambda pr=pr: nc.gpsimd.tensor_copy(out=Fr[:, pr, :], in_=tCo[:]), reads=["tSC"], writes=[tk])
            p.op("pool", lambda pr=pr: nc.gpsimd.tensor_copy(out=Fi[:, pr, :], in_=tS[:]), reads=["tSC"], writes=[tk])
            p.op("dve", lambda pr=pr: nc.vector.tensor_scalar(out=Ezr[:, pr, :], in0=tCo[:], scalar1=sc["zr"][:, pr:pr + 1],
                                                              scalar2=None, op0=ALU.mult), reads=["tSC"] + P, writes=[tk])
            p.op("dve", lambda pr=pr: nc.vector.scalar_tensor_tensor(
                out=Ezr[:, pr, :], in0=tS[:], scalar=sc["zi"][:, pr:pr + 1], in1=Ezr[:, pr, :], op0=ALU.mult, op1=ALU.add),
                reads=["tSC", tk] + P, writes=[tk])
            p.op("dve", lambda pr=pr: nc.vector.tensor_scalar(out=Ezi[:, pr, :], in0=tCo[:], scalar1=sc["zi"][:, pr:pr + 1],
                                                              scalar2=None, op0=ALU.mult), reads=["tSC"] + P, writes=[tk])
            p.op("dve", lambda pr=pr: nc.vector.scalar_tensor_tensor(
                out=Ezi[:, pr, :], in0=tS[:], scalar=sc["nzr"][:, pr:pr + 1], in1=Ezi[:, pr, :], op0=ALU.mult, op1=ALU.add),
                reads=["tSC", tk] + P, writes=[tk])
        p.op("dve", lambda: nc.vector.memset(carry[:], 0.0), writes=[("carry", pr) for pr in range(NP)])

        it = 0
        for b in range(n_blocks):
            bp = b % 2
            col0 = b * TC
            p.dma("sp", ub[bp][:], uT[inst, :, col0:col0 + TC], writes=[("ub", bp)])
            for pr in range(NP):
                q = it % 2
                it += 1
                tk = ("tab", pr)
                ck = ("carry", pr)
                ws = slice(pr * 128, (pr + 1) * 128)
                p.op("pe", lambda: nc.tensor.matmul(ps_xr[q][:], lhsT=Bre[:, ws], rhs=ub[bp][:], start=True, stop=True),
                     reads=["Bre", ("ub", bp)], writes=[("ps_xr", q)])
                p.op("pe", lambda: nc.tensor.matmul(ps_xi[q][:], lhsT=Bim[:, ws], rhs=ub[bp][:], start=True, stop=True),
                     reads=["Bim", ("ub", bp)], writes=[("ps_xi", q)])
                for o, onm, a, anm, tab in ((m1, "m1", ps_xr, "ps_xr", Ezr), (m2, "m2", ps_xi, "ps_xi", Ezi),
                                            (m3, "m3", ps_xr, "ps_xr", Ezi), (m4, "m4", ps_xi, "ps_xi", Ezr)):
                    p.op("dve", lambda o=o, a=a, tab=tab: nc.vector.tensor_tensor(out=o[q][:], in0=a[q][:], in1=tab[:, pr, :],
                                                                                 op=ALU.mult),
                         reads=[(anm, q), tk], writes=[(onm, q)])
                p.op("pool", lambda: nc.gpsimd.tensor_tensor(out=xr_[q][:], in0=m1[q][:], in1=m2[q][:], op=ALU.subtract),
                     reads=[("m1", q), ("m2", q)], writes=[("xr_", q)])
                p.op("pool", lambda: nc.gpsimd.tensor_tensor(out=xi_[q][:], in0=m3[q][:], in1=m4[q][:], op=ALU.add),
                     reads=[("m3", q), ("m4", q)], writes=[("xi_", q)])
                rb = rr[:, pr:pr + 1].broadcast_to([128, TC])
                p.op("dve", lambda: nc.vector.tensor_tensor_scan(out=sr_[q][:], data0=rb, data1=xr_[q][:],
                                                                 initial=carry[:, pr, 0:1], op0=ALU.mult, op1=ALU.add),
                     reads=["prm", ("xr_", q), ck], writes=[("sr_", q)])
                p.op("dve", lambda: nc.vector.tensor_tensor_scan(out=si_[q][:], data0=rb, data1=xi_[q][:],
                                                                 initial=carry[:, pr, 1:2], op0=ALU.mult, op1=ALU.add),
                     reads=["prm", ("xi_", q), ck], writes=[("si_", q)])
                lr, li = sr_[q][:, TC - 1:TC], si_[q][:, TC - 1:TC]
                p.op("dve", lambda: nc.vector.tensor_tensor(out=ctmp[:, 0:1], in0=li, in1=sc["rs"][:, pr:pr + 1], op=ALU.mult),
                     reads=[("si_", q), "prm"], writes=["ctmp"])
                p.op("dve", lambda: nc.vector.tensor_tensor(out=ctmp[:, 1:2], in0=li, in1=sc["rc"][:, pr:pr + 1], op=ALU.mult),
                     reads=[("si_", q), "prm"], writes=["ctmp"])
                p.op("dve", lambda: nc.vector.scalar_tensor_tensor(out=carry[:, pr, 0:1], in0=lr, scalar=sc["rc"][:, pr:pr + 1],
                                                                   in1=ctmp[:, 0:1], op0=ALU.mult, op1=ALU.subtract),
                     reads=[("sr_", q), "prm", "ctmp"], writes=[ck])
                p.op("dve", lambda: nc.vector.scalar_tensor_tensor(out=carry[:, pr, 1:2], in0=lr, scalar=sc["rs"][:, pr:pr + 1],
                                                                   in1=ctmp[:, 1:2], op0=ALU.mult, op1=ALU.add),
                     reads=[("sr_", q), "prm", "ctmp"], writes=[ck])
                for o, onm, a, anm, tab in ((d1, "d1", sr_, "sr_", Fr), (d2, "d2", si_, "si_", Fi),
                                            (d3, "d3", sr_, "sr_", Fi), (d4, "d4", si_, "si_", Fr)):
                    p.op("pool", lambda o=o, a=a, tab=tab: nc.gpsimd.tensor_tensor(out=o[q][:], in0=a[q][:], in1=tab[:, pr, :],
                                                                                  op=ALU.mult),
                         reads=[(anm, q), tk], writes=[(onm, q)])
                p.op("dve", lambda: nc.vector.tensor_tensor(out=or_[q][:], in0=d1[q][:], in1=d2[q][:], op=ALU.subtract),
                     reads=[("d1", q), ("d2", q)], writes=[("or_", q)])
                p.op("dve", lambda: nc.vector.scalar_tensor_tensor(out=oi_[q][:], in0=d3[q][:], scalar=-1.0, in1=d4[q][:],
                                                                   op0=ALU.mult, op1=ALU.subtract),
                     reads=[("d3", q), ("d4", q)], writes=[("oi_", q)])
                p.op("pe", lambda: nc.tensor.matmul(ps_y[bp][:], lhsT=Cre[:, ws], rhs=or_[q][:], start=(pr == 0), stop=False),
                     reads=["Cre", ("or_", q)], writes=[("ps_y", bp)])
                p.op("pe", lambda: nc.tensor.matmul(ps_y[bp][:], lhsT=Cim[:, ws], rhs=oi_[q][:], start=False,
                                                    stop=(pr == NP - 1)),
                     reads=["Cim", ("oi_", q)], writes=[("ps_y", bp)])
            p.op("act", lambda: nc.scalar.copy(out=yo[bp][:], in_=ps_y[bp][:]), reads=[("ps_y", bp)], writes=[("yo", bp)])
            p.dma("sp", yT[inst, :, col0:col0 + TC], yo[bp][:], reads=[("yo", bp)], is_output=True)
    p.finish()
    return nc


NCORES = 8
SEQ = 16384
TOKC = SEQ // NCORES


def _run(nc, in_maps):
    res = run_bass_kernel_spmd(nc, in_maps, core_ids=list(range(NCORES)))
    return [{k: np.asarray(v) for k, v in r.items()} for r in res.results]


def _c(a):
    return np.ascontiguousarray(a, dtype=np.float32)


def _tok(a, c):
    return _c(a[:, c * TOKC:(c + 1) * TOKC])


def _conv_layout(w):
    return _c(w.T.reshape(4, 128, 5).transpose(1, 0, 2).reshape(128, 20))


def _pad2(a):
    return np.pad(a, ((0, 0), (2, 2)))


def _ffn_inputs(i, norm_w, ffn_w_gate_up, ffn_w_down):
    return dict(
        nws=_c(np.concatenate([col_tiles(norm_w[i, k]) for k in (1, 2, 3)], axis=1)),
        wgu=_c(np.concatenate([arrange_w(ffn_w_gate_up[i][:, :DFF]), arrange_w(ffn_w_gate_up[i][:, DFF:])], axis=2)),
        wd=arrange_w(ffn_w_down[i]),
    )


def _ssd_maps(P, j, conv_w, conv_b, dt_bias, a_log, d_skip):
    maps = []
    for g in range(NCORES):
        ch = np.concatenate([np.arange(g * 256, (g + 1) * 256), 2048 + np.arange(g * 128, (g + 1) * 128),
                             3072 + np.arange(g * 128, (g + 1) * 128)])
        xg = P[2048 + ch]
        w = conv_w[j][:, ch]
        maps.append(dict(
            xbc=_c(np.stack([_pad2(xg), _pad2(xg[:, ::-1])])),
            dtr=_c(np.stack([P[6144 + 4 * g:6144 + 4 * g + 4], P[6176 + 4 * g:6176 + 4 * g + 4][:, ::-1]])),
            cw=_c(np.stack([_conv_layout(w), _conv_layout(w[::-1])])),
            cb=_c(conv_b[j][ch].reshape(4, 128).T),
            dtb=_c(dt_bias[j][:, 4 * g:4 * g + 4].reshape(2, 4, 1)),
            alog=_c(a_log[j][:, 4 * g:4 * g + 4].reshape(2, 4, 1)),
            dsk=_c(np.repeat(d_skip[j][4 * g:4 * g + 4], 64).reshape(2, 128).T),
        ))
    return maps


def _gdn_maps(P, conv_w, conv_b, dt_bias, a_log):
    maps = []
    for g in range(NCORES):
        ch = np.concatenate([np.arange(g * 128, (g + 1) * 128), 1024 + np.arange(g * 128, (g + 1) * 128),
                             2048 + np.arange(2 * g * 128, (2 * g + 2) * 128)])
        xg = P[ch]
        w = conv_w[0][:, ch]
        a0, a1 = P[6144 + 2 * g:6144 + 2 * g + 2], P[6144 + 16 + 2 * g:6144 + 16 + 2 * g + 2]
        b0, b1 = P[6176 + 2 * g:6176 + 2 * g + 2], P[6176 + 16 + 2 * g:6176 + 16 + 2 * g + 2]
        maps.append(dict(
            qkv=_c(np.stack([_pad2(xg), _pad2(xg[:, ::-1])])),
            abr=_c(np.stack([np.stack([a0, b0]), np.stack([a1[:, ::-1], b1[:, ::-1]])])),
            cw=_c(np.stack([_conv_layout(w), _conv_layout(w[::-1])])),
            cb=_c(conv_b[0][ch].reshape(4, 128).T),
            dtb=_c(dt_bias[0][:, 2 * g:2 * g + 2].reshape(2, 2, 1)),
            alog=_c(a_log[0][:, 2 * g:2 * g + 2].reshape(2, 2, 1)),
        ))
    return maps


def _pair_cols(a):
    return _c(a.reshape(4, 2, 64).transpose(1, 2, 0).reshape(128, 4))


def _blayout(b):
    out = np.zeros((8, 16, 4, 2, 64), np.float32)
    for g in range(8):
        out[g, :, g // 2, g % 2, :] = b[g].T
    return out.reshape(128, 512)


def _clayout(c):
    out = np.zeros((2, 64, 4, 8, 16), np.float32)
    for g in range(8):
        out[g % 2, :, g // 2, g, :] = c[g].T
    return out.reshape(128, 512)


def _s5_maps(hn, lam_re, lam_im, log_step, b_re, b_im, c_re, c_im):
    maps = []
    for c in range(NCORES):
        gs = slice(8 * c, 8 * c + 8)
        u = hn[c * 128:(c + 1) * 128]
        maps.append(dict(
            uT=_c(np.stack([u, u[:, ::-1]])),
            lam_re=np.stack([_pair_cols(lam_re[0, d, gs]) for d in range(2)]),
            lam_im=np.stack([_pair_cols(lam_im[0, d, gs]) for d in range(2)]),
            log_step=np.stack([_pair_cols(np.repeat(log_step[0, d, gs][:, None], 64, axis=1)) for d in range(2)]),
            b_re=_blayout(b_re[0, gs]), b_im=_blayout(b_im[0, gs]),
            c_re=np.stack([_clayout(c_re[0, d, gs]) for d in range(2)]),
            c_im=np.stack([_clayout(c_im[0, d, gs]) for d in range(2)]),
        ))
    return maps


def _gather_tok(results, key):
    return np.concatenate([r[key] for r in results], axis=1)


def kernel(x, norm_w, ssd_w_in, ssd_conv_w, ssd_conv_b, ssd_dt_bias, ssd_a_log, ssd_d,
           ssd_norm_w, ssd_w_out, gdn_w_in, gdn_conv_w, gdn_conv_b, gdn_dt_bias, gdn_a_log,
           gdn_norm_w, gdn_w_out, s5_lam_re, s5_lam_im, s5_log_step, s5_b_re, s5_b_im,
           s5_c_re, s5_c_im, s5_d, s5_w_glu, s5_b_glu, ffn_w_gate_up, ffn_w_down):
    A = lambda a: np.asarray(a, dtype=np.float32)
    (x, norm_w, ssd_w_in, ssd_conv_w, ssd_conv_b, ssd_dt_bias, ssd_a_log, ssd_d, ssd_norm_w, ssd_w_out, gdn_w_in,
     gdn_conv_w, gdn_conv_b, gdn_dt_bias, gdn_a_log, gdn_norm_w, gdn_w_out, s5_lam_re, s5_lam_im, s5_log_step,
     s5_b_re, s5_b_im, s5_c_re, s5_c_im, s5_d, s5_w_glu, s5_b_glu, ffn_w_gate_up, ffn_w_down) = map(A, (
         x, norm_w, ssd_w_in, ssd_conv_w, ssd_conv_b, ssd_dt_bias, ssd_a_log, ssd_d, ssd_norm_w, ssd_w_out, gdn_w_in,
         gdn_conv_w, gdn_conv_b, gdn_dt_bias, gdn_a_log, gdn_norm_w, gdn_w_out, s5_lam_re, s5_lam_im, s5_log_step,
         s5_b_re, s5_b_im, s5_c_re, s5_c_im, s5_d, s5_w_glu, s5_b_glu, ffn_w_gate_up, ffn_w_down))
    hT = _c(x[0].T)

    nc_ssd = build_ssd()
    common = dict(nw0=col_tiles(norm_w[0, 0]), w_in=arrange_w(ssd_w_in[0]))
    r = _run(build_dense(None, False, "proj"), [dict(hT=_tok(hT, c), **common) for c in range(NCORES)])
    P = _gather_tok(r, "projT")

    r = _run(nc_ssd, _ssd_maps(P, 0, ssd_conv_w, ssd_conv_b, ssd_dt_bias, ssd_a_log, ssd_d))
    yf = np.concatenate([q["yT"][0] for q in r], axis=0)
    yb = np.concatenate([q["yT"][1][:, ::-1] for q in r], axis=0)
    common = dict(w_out=arrange_w(ssd_w_out[0]), mnw=col_tiles(ssd_norm_w[0]), nw0=col_tiles(norm_w[1, 0]),
                  w_in=arrange_w(gdn_w_in[0]), **_ffn_inputs(0, norm_w, ffn_w_gate_up, ffn_w_down))
    r = _run(build_dense("ssd", True, "proj"),
             [dict(hT=_tok(hT, c), mf=_tok(yf, c), mb=_tok(yb, c), zT=_tok(P[0:2048], c), **common) for c in range(NCORES)])
    hT = _gather_tok(r, "hT_out")
    P = _gather_tok(r, "projT")

    r = _run(build_gdn(), _gdn_maps(P, gdn_conv_w, gdn_conv_b, gdn_dt_bias, gdn_a_log))
    yf = np.concatenate([q["oT"][0] for q in r], axis=0)
    yb = np.concatenate([q["oT"][1][:, ::-1] for q in r], axis=0)
    common = dict(w_out=arrange_w(gdn_w_out[0]), mnw=_c(np.tile(gdn_norm_w[0][:, None], (1, 16))),
                  nw0=col_tiles(norm_w[2, 0]), **_ffn_inputs(1, norm_w, ffn_w_gate_up, ffn_w_down))
    r = _run(build_dense("gdn", True, "hn"),
             [dict(hT=_tok(hT, c), mf=_tok(yf, c), mb=_tok(yb, c), zT=_tok(P[4096:6144], c), **common)
              for c in range(NCORES)])
    hT = _gather_tok(r, "hT_out")
    hn = _gather_tok(r, "hn_out")

    r = _run(build_s5(), _s5_maps(hn, s5_lam_re, s5_lam_im, s5_log_step, s5_b_re, s5_b_im, s5_c_re, s5_c_im))
    yf = np.concatenate([q["yT"][0] for q in r], axis=0)
    yb = np.concatenate([q["yT"][1][:, ::-1] for q in r], axis=0)
    common = dict(w_glu=arrange_w(s5_w_glu[0]), b_glu=col_tiles(s5_b_glu[0]), s5d=col_tiles(s5_d[0]),
                  nw0=col_tiles(norm_w[3, 0]), w_in=arrange_w(ssd_w_in[1]),
                  **_ffn_inputs(2, norm_w, ffn_w_gate_up, ffn_w_down))
    r = _run(build_dense("s5", True, "proj"),
             [dict(hT=_tok(hT, c), mf=_tok(yf, c), mb=_tok(yb, c), hnT=_tok(hn, c), **common) for c in range(NCORES)])
    hT = _gather_tok(r, "hT_out")
    P = _gather_tok(r, "projT")

    r = _run(nc_ssd, _ssd_maps(P, 1, ssd_conv_w, ssd_conv_b, ssd_dt_bias, ssd_a_log, ssd_d))
    yf = np.concatenate([q["yT"][0] for q in r], axis=0)
    yb = np.concatenate([q["yT"][1][:, ::-1] for q in r], axis=0)
    common = dict(w_out=arrange_w(ssd_w_out[1]), mnw=col_tiles(ssd_norm_w[1]),
                  **_ffn_inputs(3, norm_w, ffn_w_gate_up, ffn_w_down))
    r = _run(build_dense("ssd", True, None),
             [dict(hT=_tok(hT, c), mf=_tok(yf, c), mb=_tok(yb, c), zT=_tok(P[0:2048], c), **common) for c in range(NCORES)])
    hT = _gather_tok(r, "hT_out")
    return np.ascontiguousarray(hT.T[None].astype(np.float32))
```

```python
import math


import numpy as np
import concourse.bass as bass
import concourse.mybir as mybir
from concourse.bass_utils import run_bass_kernel_spmd

F32 = mybir.dt.float32
BF16 = mybir.dt.bfloat16
I32 = mybir.dt.int32
AF = mybir.ActivationFunctionType
ALU = mybir.AluOpType
AX = mybir.AxisListType


def _is_psum(k):
    n = k[0] if isinstance(k, tuple) else k
    return isinstance(n, str) and n.startswith("ps_")


class Prog:
    NDMA = 4

    def __init__(self, nc):
        self.nc = nc
        self.eng = {"pe": nc.tensor, "dve": nc.vector, "act": nc.scalar,
                    "pool": nc.gpsimd, "sp": nc.sync}
        self.sem = {}
        self.cnt = {}
        for e in ("pe", "dve", "act", "pool"):
            self.sem[e] = nc.alloc_semaphore(name="s_" + e)
            self.cnt[e] = 0
        self.dq = {}
        for q in ("sp", "act", "pool"):
            sems = []
            for i in range(self.NDMA):
                k = "d_%s%d" % (q, i)
                self.sem[k] = nc.alloc_semaphore(name=k)
                self.cnt[k] = 0
                sems.append(k)
            self.dq[q] = [sems, 0]
        self.seen = {e: {} for e in self.eng}
        self.last_w = {}
        self.readers = {}
        self.out_tokens = []

    def _wait(self, e, needs):
        eng = self.eng[e]
        for sk, val in needs.items():
            if e == "pe" and sk == "pe":
                continue
            if self.seen[e].get(sk, 0) >= val:
                continue
            eng.wait_ge(self.sem[sk], val)
            self.seen[e][sk] = val

    def _needs(self, reads, writes, e=None):
        needs = {}

        def add(tok):
            if tok is None:
                return
            sk, v = tok
            if needs.get(sk, 0) < v:
                needs[sk] = v
        for k in reads:
            add(self.last_w.get(k))
            if _is_psum(k):
                for t in self.readers.get(k, ()):
                    if t[0] != e:
                        add(t)
        for k in writes:
            add(self.last_w.get(k))
            for t in self.readers.get(k, ()):
                add(t)
        return needs

    def _commit(self, tok, reads, writes):
        for k in writes:
            self.last_w[k] = tok
            self.readers[k] = []
        for k in reads:
            if k in writes:
                continue
            self.readers.setdefault(k, []).append(tok)
            if len(self.readers[k]) > 12:
                best = {}
                for sk, v in self.readers[k]:
                    if best.get(sk, 0) < v:
                        best[sk] = v
                self.readers[k] = list(best.items())

    def op(self, e, fn, reads=(), writes=()):
        self._wait(e, self._needs(reads, writes, e))
        ins = fn()
        self.cnt[e] += 1
        ins.then_inc(self.sem[e], 1)
        tok = (e, self.cnt[e])
        self._commit(tok, reads, writes)
        return tok

    def dma(self, q, out, in_, reads=(), writes=(), is_output=False, **kw):
        sems, n = self.dq[q]
        sk = sems[n % self.NDMA]
        self.dq[q][1] = n + 1
        needs = self._needs(reads, writes)
        if self.cnt[sk] > 0:
            needs[sk] = max(needs.get(sk, 0), self.cnt[sk])
        self._wait(q, needs)
        ins = self.eng[q].dma_start(out=out, in_=in_, **kw)
        self.cnt[sk] += 16
        ins.then_inc(self.sem[sk], 16)
        tok = (sk, self.cnt[sk])
        self._commit(tok, reads, writes)
        if is_output:
            self.out_tokens.append(tok)
        return tok

    def finish(self):
        needs = {}
        for sk, v in self.out_tokens:
            needs[sk] = max(needs.get(sk, 0), v)
        self._wait("sp", needs)
        needs = {e: self.cnt[e] for e in ("pe", "dve", "act", "pool") if self.cnt[e] > 0}
        for q in self.dq:
            for sk in self.dq[q][0]:
                if self.cnt[sk] > 0:
                    needs[sk] = self.cnt[sk]
        self._wait("sp", needs)


class InstView:
    def __init__(self, p, inst, shared):
        self.p, self.inst, self.shared = p, inst, shared

    def k(self, key):
        n = key[0] if isinstance(key, tuple) else key
        if n in self.shared or (isinstance(n, str) and n.startswith("ps_")):
            return key
        return ("I%d" % self.inst, key)

    def op(self, e, fn, reads=(), writes=()):
        r = self.p.op(e, fn, [self.k(x) for x in reads], [self.k(x) for x in writes])
        self.p.baton.step(self.inst)
        return r

    def dma(self, q, out, in_, reads=(), writes=(), **kw):
        r = self.p.dma(q, out, in_, [self.k(x) for x in reads], [self.k(x) for x in writes], **kw)
        self.p.baton.step(self.inst)
        return r


class Baton:
    def __init__(self, n):
        import threading
        self.n = n
        self.sems = [threading.Semaphore(0) for _ in range(n)]
        self.alive = [True] * n
        self.err = None

    def _next(self, i):
        for d in range(1, self.n + 1):
            j = (i + d) % self.n
            if self.alive[j]:
                return j
        return None

    def step(self, i):
        j = self._next(i)
        if j is None or j == i:
            return
        self.sems[j].release()
        self.sems[i].acquire()

    def done(self, i):
        self.alive[i] = False
        j = self._next(i)
        if j is not None:
            self.sems[j].release()


def run_interleaved(p, bodies, shared):
    import threading
    n = len(bodies)
    p.baton = Baton(n)
    errs = []

    def runner(i):
        p.baton.sems[i].acquire()
        try:
            bodies[i](InstView(p, i, shared))
        except BaseException as ex:
            errs.append(ex)
        finally:
            p.baton.done(i)
    ths = [threading.Thread(target=runner, args=(i,)) for i in range(n)]
    for t in ths:
        t.start()
    p.baton.sems[0].release()
    for t in ths:
        t.join()
    if errs:
        raise errs[0]


EPS = 1e-6
TG = 512
NTG = 4
D = 1024
DT = 8
DFF = 2816
FT = 22
GELU_C = 0.7978845608028654


def build_dense(variant, has_ffn, next_kind):
    nc = bass.Bass("TRN2", target_bir_lowering=False)
    p = Prog(nc)
    NTOK = TG * NTG

    def din(name, shape):
        return nc.dram_tensor(name, shape, F32, kind="ExternalInput").ap()

    def dout(name, shape):
        return nc.dram_tensor(name, shape, F32, kind="ExternalOutput").ap()

    hT_in = din("hT", [D, NTOK])
    if variant in ("ssd", "gdn"):
        mf = din("mf", [2048, NTOK])
        mb = din("mb", [2048, NTOK])
        zT = din("zT", [2048, NTOK])
        w_out = din("w_out", [DT, 128, 16 * 128])
        mnw = din("mnw", [128, 16])
    if variant == "s5":
        mf = din("mf", [D, NTOK])
        mb = din("mb", [D, NTOK])
        hn_in = din("hnT", [D, NTOK])
        w_glu = din("w_glu", [16, 128, DT * 128])
        b_glu = din("b_glu", [128, 16])
        s5d = din("s5d", [128, DT])
    if has_ffn:
        nws = din("nws", [128, 3 * DT])
        wgu = din("wgu", [FT, 128, 2 * DT * 128])
        wd = din("wd", [DT, 128, FT * 128])
        hT_out = dout("hT_out", [D, NTOK])
    if next_kind is not None:
        nw0 = din("nw0", [128, DT])
    if next_kind == "proj":
        w_in = din("w_in", [49, 128, DT * 128])
        projT = dout("projT", [49 * 128, NTOK])
    if next_kind == "hn":
        hn_out = dout("hn_out", [D, NTOK])

    sb = lambda name, shape, dt=F32: nc.alloc_sbuf_tensor(name, shape, dt)
    hT = sb("hT_sb", [128, DT, TG])
    mT = sb("mT_sb", [128, DT, TG])
    xnT = sb("xnT", [128, DT, TG], BF16)
    lhsA = sb("lhsA", [128, 16, TG], BF16)
    actT = sb("actT", [128, FT, TG], BF16)
    NW = 3
    wbuf = [sb("wbuf%d" % i, [128, FT * 128], BF16) for i in range(NW)]
    stg = [[sb("stg%d_%d" % (i, s), [128, TG]) for s in range(2)] for i in range(3)]
    tmpA = [sb("tmpA%d" % i, [128, TG]) for i in range(2)]
    tmpB = [sb("tmpB%d" % i, [128, TG]) for i in range(2)]
    yzb = [sb("yzb%d" % i, [128, TG]) for i in range(4)]
    sqb = [sb("sqb%d" % i, [128, TG], BF16) for i in range(2)]
    rstd = sb("rstd", [128, TG])
    ostg = [sb("ostg%d" % i, [128, TG]) for i in range(2)]
    ones = sb("ones", [128, 128], BF16)
    cols = sb("cols", [128, 64])
    ps_ss = nc.alloc_psum_tensor("ps_ss", [128, TG], F32)
    ps_acc = [nc.alloc_psum_tensor("ps_acc%d" % i, [128, TG], F32) for i in range(3)]
    ps_g = [nc.alloc_psum_tensor("ps_g%d" % i, [128, TG], F32) for i in range(2)]
    ps_u = [nc.alloc_psum_tensor("ps_u%d" % i, [128, TG], F32) for i in range(2)]

    p.op("pool", lambda: nc.gpsimd.memset(ones[:], 1.0), writes=["ones"])
    if has_ffn:
        p.dma("sp", cols[:, 0:24], nws, writes=["cols"])
    if next_kind is not None:
        p.dma("sp", cols[:, 24:32], nw0, writes=["cols"])
    if variant in ("ssd", "gdn"):
        p.dma("sp", cols[:, 32:48], mnw, writes=["cols"])
    if variant == "s5":
        p.dma("sp", cols[:, 32:48], b_glu, writes=["cols"])
        p.dma("sp", cols[:, 48:56], s5d, writes=["cols"])

    cnt = {"w": 0, "acc": 0, "gu": 0, "o": 0, "ev": 0}

    def load_w(src, n):
        i = cnt["w"] % NW
        cnt["w"] += 1
        p.dma("pool", wbuf[i][:, 0:n], src, writes=[("w", i)], max_dma_last_dim=4096)
        return wbuf[i], ("w", i)

    def rms_rstd(tiles, n_feat):
        last = len(tiles) - 1
        for i, (ap, key) in enumerate(tiles):
            s = i % 2
            p.op("act", lambda ap=ap, s=s: nc.scalar.activation(out=sqb[s][:], in_=ap, func=AF.Square),
                 reads=[key], writes=[("sq", s)])
            p.op("pe", lambda s=s, i=i: nc.tensor.matmul(ps_ss[:], lhsT=ones[:], rhs=sqb[s][:],
                                                       start=(i == 0), stop=(i == last)),
                 reads=[("sq", s), "ones"], writes=["ps_ss"])
        p.op("act", lambda: nc.scalar.activation(out=rstd[:], in_=ps_ss[:], func=AF.Ln,
                                                 scale=1.0 / n_feat, bias=EPS),
             reads=["ps_ss"], writes=["rstd"])
        p.op("act", lambda: nc.scalar.activation(out=rstd[:], in_=rstd[:], func=AF.Exp, scale=-0.5),
             reads=["rstd"], writes=["rstd"])

    def evac(dst, dkey, src, skey):
        cnt["ev"] += 1
        if cnt["ev"] % 2:
            p.op("act", lambda: nc.scalar.copy(out=dst, in_=src), reads=[skey], writes=[dkey])
        else:
            p.op("dve", lambda: nc.vector.tensor_copy(out=dst, in_=src), reads=[skey], writes=[dkey])

    def proj_fm(wsrc, nk, rhs_fn, rhs_keys, consume):
        wt, wkey = load_w(wsrc, nk * 128)
        a = cnt["acc"] % 3
        cnt["acc"] += 1
        for k in range(nk):
            p.op("pe", lambda k=k: nc.tensor.matmul(ps_acc[a][:], lhsT=wt[:, k * 128:(k + 1) * 128],
                                                    rhs=rhs_fn(k), start=(k == 0), stop=(k == nk - 1)),
                 reads=[wkey] + rhs_keys, writes=[("acc", a)])
        consume(ps_acc[a][:], ("acc", a))

    def add_norm_into_h(nwoff):
        rms_rstd([(mT[:, j, :], ("mT", j)) for j in range(DT)], D)
        for j in range(DT):
            s = j % 2
            p.op("dve", lambda j=j, s=s: nc.vector.scalar_tensor_tensor(
                out=tmpA[s][:], in0=mT[:, j, :], scalar=cols[:, nwoff + j:nwoff + j + 1], in1=rstd[:],
                op0=ALU.mult, op1=ALU.mult), reads=[("mT", j), "cols", "rstd"], writes=[("tmpA", s)])
            p.op("pool", lambda j=j, s=s: nc.gpsimd.tensor_tensor(out=hT[:, j, :], in0=hT[:, j, :], in1=tmpA[s][:],
                                                                  op=ALU.add),
                 reads=[("hT", j), ("tmpA", s)], writes=[("hT", j)])

    def norm_h_to(dst_fn, dkey_fn, nwoff):
        rms_rstd([(hT[:, j, :], ("hT", j)) for j in range(DT)], D)
        for j in range(DT):
            p.op("dve", lambda j=j: nc.vector.scalar_tensor_tensor(
                out=dst_fn(j), in0=hT[:, j, :], scalar=cols[:, nwoff + j:nwoff + j + 1], in1=rstd[:],
                op0=ALU.mult, op1=ALU.mult), reads=[("hT", j), "cols", "rstd"], writes=[dkey_fn(j)])

    for tg in range(NTG):
        tok = slice(tg * TG, (tg + 1) * TG)
        for j in range(DT):
            p.dma("sp", hT[:, j, :], hT_in[j * 128:(j + 1) * 128, tok], writes=[("hT", j)])

        if variant in ("ssd", "gdn"):
            gsz = 2 if variant == "ssd" else 1
            for G in range(16 // gsz):
                tl = []
                for t in range(gsz):
                    ft = G * gsz + t
                    s = ft % 2
                    rows = slice(ft * 128, (ft + 1) * 128)
                    p.dma("sp", stg[0][s][:], mf[rows, tok], writes=[("stg0", s)])
                    p.dma("act", stg[1][s][:], mb[rows, tok], writes=[("stg1", s)])
                    p.dma("sp", stg[2][s][:], zT[rows, tok], writes=[("stg2", s)])
                    yb = yzb[ft % 4]
                    ykey = ("yz", ft % 4)
                    p.op("pool", lambda s=s, yb=yb: nc.gpsimd.tensor_tensor(out=yb[:], in0=stg[0][s][:], in1=stg[1][s][:],
                                                                            op=ALU.add),
                         reads=[("stg0", s), ("stg1", s)], writes=[ykey])
                    p.op("act", lambda s=s: nc.scalar.activation(out=tmpB[s][:], in_=stg[2][s][:], func=AF.Silu),
                         reads=[("stg2", s)], writes=[("tmpB", s)])
                    if variant == "ssd":
                        p.op("dve", lambda s=s, yb=yb: nc.vector.tensor_tensor(out=yb[:], in0=yb[:], in1=tmpB[s][:],
                                                                               op=ALU.mult),
                             reads=[ykey, ("tmpB", s)], writes=[ykey])
                    tl.append((yb, ykey, ft, s))
                rms_rstd([(yb[:], ykey) for (yb, ykey, ft, s) in tl], 128 * gsz)
                for (yb, ykey, ft, s) in tl:
                    if variant == "ssd":
                        p.op("dve", lambda yb=yb, ft=ft: nc.vector.scalar_tensor_tensor(
                            out=lhsA[:, ft, :], in0=yb[:], scalar=cols[:, 32 + ft:33 + ft], in1=rstd[:],
                            op0=ALU.mult, op1=ALU.mult), reads=[ykey, "cols", "rstd"], writes=[("lhsA", ft)])
                    else:
                        p.op("dve", lambda yb=yb: nc.vector.scalar_tensor_tensor(
                            out=yb[:], in0=yb[:], scalar=cols[:, 32:33], in1=rstd[:],
                            op0=ALU.mult, op1=ALU.mult), reads=[ykey, "cols", "rstd"], writes=[ykey])
                        p.op("dve", lambda yb=yb, ft=ft, s=s: nc.vector.tensor_tensor(
                            out=lhsA[:, ft, :], in0=yb[:], in1=tmpB[s][:], op=ALU.mult),
                            reads=[ykey, ("tmpB", s)], writes=[("lhsA", ft)])
            for j in range(DT):
                proj_fm(w_out[j], 16, lambda k: lhsA[:, k, :], [("lhsA", k) for k in range(16)],
                        lambda ps, pk, j=j: evac(mT[:, j, :], ("mT", j), ps, pk))
        if variant == "s5":
            for ft in range(DT):
                s = ft % 2
                rows = slice(ft * 128, (ft + 1) * 128)
                p.dma("sp", stg[0][s][:], mf[rows, tok], writes=[("stg0", s)])
                p.dma("act", stg[1][s][:], mb[rows, tok], writes=[("stg1", s)])
                p.dma("sp", stg[2][s][:], hn_in[rows, tok], writes=[("stg2", s)])
                yb = yzb[ft % 4]
                ykey = ("yz", ft % 4)
                p.op("pool", lambda s=s, yb=yb: nc.gpsimd.tensor_tensor(out=yb[:], in0=stg[0][s][:], in1=stg[1][s][:],
                                                                        op=ALU.add),
                     reads=[("stg0", s), ("stg1", s)], writes=[ykey])
                p.op("dve", lambda s=s, yb=yb, ft=ft: nc.vector.scalar_tensor_tensor(
                    out=yb[:], in0=stg[2][s][:], scalar=cols[:, 48 + ft:49 + ft], in1=yb[:],
                    op0=ALU.mult, op1=ALU.add), reads=[("stg2", s), "cols", ykey], writes=[ykey])
                p.op("act", lambda s=s, yb=yb: nc.scalar.activation(out=tmpB[s][:], in_=yb[:], func=AF.Square),
                     reads=[ykey], writes=[("tmpB", s)])
                p.op("dve", lambda s=s: nc.vector.tensor_scalar(out=tmpB[s][:], in0=tmpB[s][:], scalar1=0.044715,
                                                                scalar2=1.0, op0=ALU.mult, op1=ALU.add),
                     reads=[("tmpB", s)], writes=[("tmpB", s)])
                p.op("dve", lambda s=s, yb=yb: nc.vector.tensor_tensor(out=tmpB[s][:], in0=tmpB[s][:], in1=yb[:],
                                                                       op=ALU.mult),
                     reads=[("tmpB", s), ykey], writes=[("tmpB", s)])
                p.op("act", lambda s=s: nc.scalar.activation(out=tmpB[s][:], in_=tmpB[s][:], func=AF.Sigmoid,
                                                             scale=2.0 * GELU_C),
                     reads=[("tmpB", s)], writes=[("tmpB", s)])
                p.op("dve", lambda s=s, yb=yb, ft=ft: nc.vector.tensor_tensor(out=lhsA[:, ft, :], in0=yb[:],
                                                                              in1=tmpB[s][:], op=ALU.mult),
                     reads=[ykey, ("tmpB", s)], writes=[("lhsA", ft)])
            for j in range(DT):
                def cons_gate(ps, pk, j=j):
                    s = j % 2
                    p.op("act", lambda: nc.scalar.activation(out=tmpA[s][:], in_=ps, func=AF.Sigmoid,
                                                             bias=cols[:, 40 + j:41 + j]),
                         reads=[pk, "cols"], writes=[("tmpA", s)])

                def cons_val(ps, pk, j=j):
                    s = j % 2
                    p.op("dve", lambda: nc.vector.scalar_tensor_tensor(
                        out=mT[:, j, :], in0=ps, scalar=cols[:, 32 + j:33 + j], in1=tmpA[s][:],
                        op0=ALU.add, op1=ALU.mult), reads=[pk, "cols", ("tmpA", s)], writes=[("mT", j)])
                lk = [("lhsA", k) for k in range(DT)]
                proj_fm(w_glu[8 + j], DT, lambda k: lhsA[:, k, :], lk, cons_gate)
                proj_fm(w_glu[j], DT, lambda k: lhsA[:, k, :], lk, cons_val)

        if has_ffn:
            add_norm_into_h(0)
            norm_h_to(lambda j: xnT[:, j, :], lambda j: ("xnT", j), 8)
            xk = [("xnT", k) for k in range(DT)]
            for f in range(FT):
                wt, wkey = load_w(wgu[f], 2 * DT * 128)
                g = cnt["gu"] % 2
                cnt["gu"] += 1
                for half, pst, nm in ((0, ps_g, "g"), (1, ps_u, "u")):
                    for k in range(DT):
                        p.op("pe", lambda k=k, half=half, pst=pst: nc.tensor.matmul(
                            pst[g][:], lhsT=wt[:, (half * DT + k) * 128:(half * DT + k + 1) * 128],
                            rhs=xnT[:, k, :], start=(k == 0), stop=(k == DT - 1)),
                            reads=[wkey] + xk, writes=[(nm, g)])
                p.op("act", lambda g=g: nc.scalar.activation(out=tmpB[g][:], in_=ps_g[g][:], func=AF.Silu),
                     reads=[("g", g)], writes=[("tmpB", g)])
                p.op("dve", lambda g=g, f=f: nc.vector.tensor_tensor(out=actT[:, f, :], in0=tmpB[g][:],
                                                                     in1=ps_u[g][:], op=ALU.mult),
                     reads=[("tmpB", g), ("u", g)], writes=[("actT", f)])
            ak = [("actT", k) for k in range(FT)]
            for j in range(DT):
                proj_fm(wd[j], FT, lambda k: actT[:, k, :], ak,
                        lambda ps, pk, j=j: evac(mT[:, j, :], ("mT", j), ps, pk))
            add_norm_into_h(16)
            for j in range(DT):
                p.dma("sp", hT_out[j * 128:(j + 1) * 128, tok], hT[:, j, :], reads=[("hT", j)], is_output=True)

        if next_kind == "proj":
            norm_h_to(lambda j: xnT[:, j, :], lambda j: ("xnT", j), 24)
            xk = [("xnT", k) for k in range(DT)]
            for j in range(49):
                def cons(ps, pk, j=j):
                    o = cnt["o"] % 2
                    cnt["o"] += 1
                    evac(ostg[o][:], ("ostg", o), ps, pk)
                    p.dma("sp" if j % 2 else "act", projT[j * 128:(j + 1) * 128, tok], ostg[o][:],
                          reads=[("ostg", o)], is_output=True)
                proj_fm(w_in[j], DT, lambda k: xnT[:, k, :], xk, cons)
        if next_kind == "hn":
            for j in range(DT):
                pass
            norm_h_to(lambda j: mT[:, j, :], lambda j: ("mT", j), 24)
            for j in range(DT):
                p.dma("sp", hn_out[j * 128:(j + 1) * 128, tok], mT[:, j, :], reads=[("mT", j)], is_output=True)
    p.finish()
    return nc


def arrange_w(w, n_out_tiles=None):
    K, N = w.shape
    nk = K // 128
    nj = (N + 127) // 128
    if N % 128:
        w = np.concatenate([w, np.zeros((K, nj * 128 - N), w.dtype)], axis=1)
    r = w.reshape(nk, 128, nj, 128).transpose(2, 1, 0, 3).reshape(nj, 128, nk * 128)
    return np.ascontiguousarray(r)


def col_tiles(v):
    return np.ascontiguousarray(v.reshape(-1, 128).T)


T_SEQ = 16384
TB = 512


def make_consts(nc, p):
    io = nc.alloc_sbuf_tensor("c_iota", [128, 128], F32)
    ident = nc.alloc_sbuf_tensor("c_ident", [128, 128], F32)
    triu = nc.alloc_sbuf_tensor("c_triu", [128, 128], F32)
    p.op("pool", lambda: nc.gpsimd.iota(io[:], pattern=[[1, 128]], base=0, channel_multiplier=-1,
                                        allow_small_or_imprecise_dtypes=True), writes=["c_iota"])
    p.op("dve", lambda: nc.vector.tensor_single_scalar(out=ident[:], in_=io[:], scalar=0.0, op=ALU.is_equal),
         reads=["c_iota"], writes=["c_ident"])
    p.op("dve", lambda: nc.vector.tensor_single_scalar(out=triu[:], in_=io[:], scalar=0.0, op=ALU.is_ge),
         reads=["c_iota"], writes=["c_triu"])
    return dict(iota=io, ident=ident, triu=triu)


def conv_silu_block(nc, p, raw, rawkey_fn, cw, cb, cT, ckey_fn, ntiles, tmp, tmpkey):
    for t in range(ntiles):
        p.op("act", lambda t=t: nc.scalar.activation(out=tmp[:], in_=raw[:, t, 0:TB], func=AF.Identity,
                                                     scale=cw[:, t, 0:1]),
             reads=[rawkey_fn(t), "cw"], writes=[tmpkey])
        for k in range(1, 5):
            p.op("dve", lambda t=t, k=k: nc.vector.scalar_tensor_tensor(
                out=tmp[:], in0=raw[:, t, k:k + TB], scalar=cw[:, t, k:k + 1], in1=tmp[:],
                op0=ALU.mult, op1=ALU.add), reads=[rawkey_fn(t), "cw", tmpkey], writes=[tmpkey])
        p.op("act", lambda t=t: nc.scalar.activation(out=cT[:, t, :], in_=tmp[:], func=AF.Silu, bias=cb[:, t:t + 1]),
             reads=[tmpkey, "cb"], writes=[ckey_fn(t)])


def build_ssd(n_blocks=T_SEQ // TB):
    nc = bass.Bass("TRN2", target_bir_lowering=False)
    p = Prog(nc)
    T = n_blocks * TB
    din = lambda name, shape: nc.dram_tensor(name, shape, F32, kind="ExternalInput").ap()
    xbc = din("xbc", [2, 512, T + 4])
    dtr = din("dtr", [2, 4, T])
    cwd = din("cw", [2, 128, 20])
    cbd = din("cb", [128, 4])
    dtbd = din("dtb", [2, 4, 1])
    alogd = din("alog", [2, 4, 1])
    dskd = din("dsk", [128, 2])
    yT = nc.dram_tensor("yT", [2, 256, T], F32, kind="ExternalOutput").ap()

    sb = lambda name, shape, dt=F32: nc.alloc_sbuf_tensor(name, shape, dt)
    C = make_consts(nc, p)
    ident, triu = C["ident"], C["triu"]
    sel = sb("sel", [4, 4, 128])
    p.op("pool", lambda: nc.gpsimd.iota(sel[:], pattern=[[1, 4], [0, 128]], base=0, channel_multiplier=-1,
                                        allow_small_or_imprecise_dtypes=True), writes=["sel"])
    p.op("dve", lambda: nc.vector.tensor_single_scalar(out=sel[:], in_=sel[:], scalar=0.0, op=ALU.is_equal),
         reads=["sel"], writes=["sel"])
    ones4 = sb("ones4", [4, 128])
    p.op("pool", lambda: nc.gpsimd.memset(ones4[:], 1.0), writes=["ones4"])

    cb = sb("cb_sb", [128, 4])
    dsk = sb("dsk_sb", [128, 2])
    p.dma("sp", cb[:], cbd, writes=["cb"])
    p.dma("sp", dsk[:], dskd, writes=["dsk"])
    ps = lambda name: nc.alloc_psum_tensor(name, [128, 512], F32)
    ps_tr = [ps("ps_tr0"), ps("ps_tr1")]
    ps_fb = [ps("ps_fb0"), ps("ps_fb1")]
    ps_C = [ps("ps_C0"), ps("ps_C1")]
    ps_D = [ps("ps_D0"), ps("ps_D1")]

    def body(p, inst):
        sb = lambda name, shape, dt=F32: nc.alloc_sbuf_tensor(name + '_i%d' % inst, shape, dt)
        cw = sb("cw_sb", [128, 4, 5])
        dtb = sb("dtb_sb", [4, 1])
        acol = sb("acol", [4, 1])

        raw = [sb("raw%d" % i, [128, 4, TB + 4]) for i in range(1)] * 2
        cT = [sb("cT%d" % i, [128, 4, TB]) for i in range(1)] * 2
        ctmp = sb("ctmp", [128, TB])
        dtraw = sb("dtraw", [4, TB])
        dtT = [sb("dtT%d" % i, [4, TB]) for i in range(1)] * 2
        daT = sb("daT", [4, TB])
        csT = [sb("csT%d" % i, [4, TB]) for i in range(1)] * 2
        yTo = [sb("yTo%d" % i, [128, 2, TB]) for i in range(1)] * 2
        S = sb("S", [128, 256])
        S_bf = sb("S_bf", [128, 256], BF16)

        def two(name, shape, dt=F32):
            return [sb(name, shape, dt)] * 2
        x_tok = two("x_tok", [128, 256])
        B_tok = two("B_tok", [128, 128], BF16)
        sm = two("sm", [128, 8])
        CBm = two("CBm", [128, 128])
        Dm = two("Dm", [128, 4, 128])
        G = two("G", [128, 4, 128], BF16)
        eFb = two("eFb", [128, 4, 128])
        CsT = two("CsT", [128, 4, 128], BF16)
        w4 = two("w4", [128, 8])
        xdt = two("xdt", [128, 4, 64], BF16)
        xdtd = two("xdtd", [128, 4, 64], BF16)
        y_tok = two("y_tok", [128, 256])

        p.dma("sp", cw[:].rearrange("p t k -> p (t k)"), cwd[inst], writes=["cw"])
        p.dma("sp", dtb[:], dtbd[inst], writes=["dtb"])
        p.dma("sp", acol[:], alogd[inst], writes=["acol"])
        p.op("act", lambda: nc.scalar.activation(out=acol[:], in_=acol[:], func=AF.Exp), reads=["acol"], writes=["acol"])
        p.op("dve", lambda: nc.vector.tensor_scalar(out=acol[:], in0=acol[:], scalar1=-1.0, scalar2=None, op0=ALU.mult),
             reads=["acol"], writes=["acol"])
        p.op("dve", lambda: nc.vector.memset(S[:], 0.0), writes=["S"])
        p.op("dve", lambda: nc.vector.memset(S_bf[:], 0.0), writes=["S_bf"])
        for b in range(n_blocks):
            bp = 0
            col0 = b * TB
            for t in range(4):
                p.dma("sp" if t % 2 else "act", raw[bp][:, t, :], xbc[inst, t * 128:(t + 1) * 128, col0:col0 + TB + 4],
                      writes=[("raw", bp, t)])
            p.dma("sp", dtraw[:], dtr[inst, :, col0:col0 + TB], writes=["dtraw"])
            conv_silu_block(nc, p, raw[bp], lambda t: ("raw", bp, t), cw, cb, cT[bp], lambda t: ("cT", bp, t), 4, ctmp, "ctmp")
            p.op("act", lambda: nc.scalar.activation(out=daT[:], in_=dtraw[:], func=AF.Exp, bias=dtb[:, 0:1]),
                 reads=["dtraw", "dtb"], writes=["daT"])
            p.op("act", lambda: nc.scalar.activation(out=dtT[bp][:], in_=daT[:], func=AF.Ln, bias=1.0),
                 reads=["daT"], writes=[("dtT", bp)])
            p.op("dve", lambda: nc.vector.tensor_scalar(out=daT[:], in0=dtT[bp][:], scalar1=acol[:, 0:1], scalar2=None,
                                                        op0=ALU.mult),
                 reads=[("dtT", bp), "acol"], writes=["daT"])
            for j in range(4):
                cs = slice(j * 128, (j + 1) * 128)
                p.op("dve", lambda cs=cs: nc.vector.tensor_tensor_scan(
                    out=csT[bp][:, cs], data0=ones4[:], data1=daT[:, cs], initial=0.0, op0=ALU.mult, op1=ALU.add),
                    reads=["daT", "ones4"], writes=[("csT", bp)])
            for j in range(4):
                cs = slice(j * 128, (j + 1) * 128)
                c = b * 4 + j
                q = 0
                ctk = [("cT", bp, t) for t in range(4)]
                for t in range(3):
                    p.op("pe", lambda t=t: nc.tensor.transpose(out=ps_tr[inst][:, t * 128:(t + 1) * 128],
                                                               in_=cT[bp][:, t, cs], identity=ident[:]),
                         reads=[("cT", bp, t), "c_ident"], writes=[("ps_tr", inst)])
                p.op("pe", lambda: nc.tensor.transpose(out=ps_tr[inst][:, 384:388], in_=dtT[bp][:, cs],
                                                       identity=ident[0:4, 0:4]),
                     reads=[("dtT", bp), "c_ident"], writes=[("ps_tr", inst)])
                p.op("pe", lambda: nc.tensor.transpose(out=ps_tr[inst][:, 388:392], in_=csT[bp][:, cs],
                                                       identity=ident[0:4, 0:4]),
                     reads=[("csT", bp), "c_ident"], writes=[("ps_tr", inst)])
                p.op("act", lambda: nc.scalar.copy(out=x_tok[q][:], in_=ps_tr[inst][:, 0:256]),
                     reads=[("ps_tr", inst)], writes=[("x_tok", q)])
                p.op("dve", lambda: nc.vector.tensor_copy(out=B_tok[q][:], in_=ps_tr[inst][:, 256:384]),
                     reads=[("ps_tr", inst)], writes=[("B_tok", q)])
                p.op("dve", lambda: nc.vector.tensor_copy(out=sm[q][:], in_=ps_tr[inst][:, 384:392]),
                     reads=[("ps_tr", inst)], writes=[("sm", q)])
                for h in range(4):
                    p.op("pe", lambda h=h: nc.tensor.matmul(ps_fb[inst][:, h * 128:(h + 1) * 128], lhsT=sel[:, h, :],
                                                            rhs=csT[bp][:, cs], start=True, stop=True),
                         reads=["sel", ("csT", bp)], writes=[("ps_fb", inst)])
                p.op("pe", lambda: nc.tensor.matmul(ps_C[inst][:, 256:384], lhsT=cT[bp][:, 2, cs], rhs=cT[bp][:, 3, cs],
                                                    start=True, stop=True),
                     reads=[("cT", bp, 2), ("cT", bp, 3)], writes=[("ps_C", inst)])
                p.op("dve", lambda: nc.vector.tensor_tensor(out=CBm[q][:], in0=ps_C[inst][:, 256:384], in1=triu[:], op=ALU.mult),
                     reads=[("ps_C", inst), "c_triu"], writes=[("CBm", q)])
                fb3 = ps_fb[inst][:].rearrange("p (h l) -> p h l", h=4)
                Ftok = sm[q][:, 4:8]
                p.op("dve", lambda: nc.vector.tensor_tensor(out=Dm[q][:], in0=fb3,
                                                            in1=Ftok.unsqueeze(2).broadcast_to([128, 4, 128]),
                                                            op=ALU.subtract),
                     reads=[("ps_fb", inst), ("sm", q)], writes=[("Dm", q)])
                p.op("dve", lambda: nc.vector.tensor_scalar(out=Dm[q][:], in0=Dm[q][:], scalar1=0.0, scalar2=None, op0=ALU.min),
                     reads=[("Dm", q)], writes=[("Dm", q)])
                p.op("act", lambda: nc.scalar.activation(out=Dm[q][:], in_=Dm[q][:], func=AF.Exp),
                     reads=[("Dm", q)], writes=[("Dm", q)])
                p.op("dve", lambda: nc.vector.scalar_tensor_tensor(
                    out=G[q][:], in0=Dm[q][:], scalar=1.0, in1=CBm[q][:].unsqueeze(1).broadcast_to([128, 4, 128]),
                    op0=ALU.min, op1=ALU.mult), reads=[("Dm", q), ("CBm", q)], writes=[("G", q)])
                p.op("act", lambda: nc.scalar.activation(out=eFb[q][:], in_=fb3, func=AF.Exp),
                     reads=[("ps_fb", inst)], writes=[("eFb", q)])
                p.op("pool", lambda: nc.gpsimd.tensor_tensor(
                    out=CsT[q][:], in0=eFb[q][:], in1=cT[bp][:, 3, cs].unsqueeze(1).broadcast_to([128, 4, 128]),
                    op=ALU.mult), reads=[("eFb", q), ("cT", bp, 3)], writes=[("CsT", q)])
                p.op("dve", lambda: nc.vector.tensor_tensor(out=w4[q][:, 0:4], in0=fb3[:, :, 127], in1=Ftok,
                                                            op=ALU.subtract),
                     reads=[("ps_fb", inst), ("sm", q)], writes=[("w4", q)])
                p.op("act", lambda: nc.scalar.activation(out=w4[q][:, 0:4], in_=w4[q][:, 0:4], func=AF.Exp),
                     reads=[("w4", q)], writes=[("w4", q)])
                p.op("dve", lambda: nc.vector.tensor_tensor(out=w4[q][:, 4:8], in0=w4[q][:, 0:4], in1=sm[q][:, 0:4],
                                                            op=ALU.mult),
                     reads=[("w4", q), ("sm", q)], writes=[("w4", q)])
                x3 = x_tok[q][:].rearrange("p (h e) -> p h e", h=4)
                p.op("dve", lambda: nc.vector.tensor_tensor(out=xdt[q][:], in0=x3,
                                                            in1=sm[q][:, 0:4].unsqueeze(2).broadcast_to([128, 4, 64]),
                                                            op=ALU.mult),
                     reads=[("x_tok", q), ("sm", q)], writes=[("xdt", q)])
                p.op("pool", lambda: nc.gpsimd.tensor_tensor(out=xdtd[q][:], in0=x3,
                                                             in1=w4[q][:, 4:8].unsqueeze(2).broadcast_to([128, 4, 64]),
                                                             op=ALU.mult),
                     reads=[("x_tok", q), ("w4", q)], writes=[("xdtd", q)])
                for h in range(4):
                    hs = slice(h * 64, (h + 1) * 64)
                    p.op("pe", lambda h=h, hs=hs: nc.tensor.matmul(ps_C[inst][:, hs], lhsT=G[q][:, h, :], rhs=xdt[q][:, h, :],
                                                                  start=True, stop=False),
                         reads=[("G", q), ("xdt", q)], writes=[("ps_C", inst)])
                    p.op("pe", lambda h=h, hs=hs: nc.tensor.matmul(ps_C[inst][:, hs], lhsT=CsT[q][:, h, :], rhs=S_bf[:, hs],
                                                                  start=False, stop=True),
                         reads=[("CsT", q), "S_bf"], writes=[("ps_C", inst)])
                p.op("pe", lambda: nc.tensor.matmul(ps_D[inst][:, 0:256], lhsT=B_tok[q][:],
                                                    rhs=xdtd[q][:].rearrange("p h e -> p (h e)"), start=True, stop=True),
                     reads=[("B_tok", q), ("xdtd", q)], writes=[("ps_D", inst)])
                for h in range(4):
                    hs = slice(h * 64, (h + 1) * 64)
                    p.op("dve", lambda h=h, hs=hs: nc.vector.scalar_tensor_tensor(
                        out=S[:, hs], in0=S[:, hs], scalar=eFb[q][:, h, 127:128], in1=ps_D[inst][:, hs],
                        op0=ALU.mult, op1=ALU.add), reads=["S", ("eFb", q), ("ps_D", inst)], writes=["S"])
                p.op("act", lambda: nc.scalar.copy(out=S_bf[:], in_=S[:]), reads=["S"], writes=["S_bf"])
                p.op("act", lambda: nc.scalar.copy(out=y_tok[q][:], in_=ps_C[inst][:, 0:256]), reads=[("ps_C", inst)],
                     writes=[("y_tok", q)])
                for t in range(2):
                    p.op("pe", lambda t=t: nc.tensor.transpose(out=ps_D[inst][:, 256 + t * 128:256 + (t + 1) * 128],
                                                               in_=y_tok[q][:, t * 128:(t + 1) * 128], identity=ident[:]),
                         reads=[("y_tok", q), "c_ident"], writes=[("ps_D", inst)])
                for t in range(2):
                    if inst == 0:
                        p.op("dve", lambda t=t: nc.vector.scalar_tensor_tensor(
                            out=yTo[bp][:, t, cs], in0=cT[bp][:, t, cs], scalar=dsk[:, t:t + 1],
                            in1=ps_D[inst][:, 256 + t * 128:256 + (t + 1) * 128], op0=ALU.mult, op1=ALU.add),
                            reads=[("cT", bp, t), "dsk", ("ps_D", inst)], writes=[("yTo", bp)])
                    else:
                        p.op("dve", lambda t=t: nc.vector.tensor_copy(out=yTo[bp][:, t, cs],
                                                                      in_=ps_D[inst][:, 256 + t * 128:256 + (t + 1) * 128]),
                             reads=[("ps_D", inst)], writes=[("yTo", bp)])
            for t in range(2):
                p.dma("sp", yT[inst, t * 128:(t + 1) * 128, col0:col0 + TB], yTo[bp][:, t, :],
                      reads=[("yTo", bp)], is_output=True)
    SHARED = {"c_iota", "c_ident", "c_triu", "sel", "ones4", "cb", "dsk"}
    run_interleaved(p, [lambda v: body(v, 0), lambda v: body(v, 1)], SHARED)
    p.finish()
    return nc


CH = 64
EPS = 1e-6


def build_gdn(n_blocks=T_SEQ // TB):
    nc = bass.Bass("TRN2", target_bir_lowering=False)
    p = Prog(nc)
    T = n_blocks * TB
    din = lambda name, shape: nc.dram_tensor(name, shape, F32, kind="ExternalInput").ap()
    qkv = din("qkv", [2, 512, T + 4])
    abr = din("abr", [2, 2, 2, T])
    cwd = din("cw", [2, 128, 20])
    cbd = din("cb", [128, 4])
    dtbd = din("dtb", [2, 2, 1])
    alogd = din("alog", [2, 2, 1])
    oT = nc.dram_tensor("oT", [2, 256, T], F32, kind="ExternalOutput").ap()

    sb = lambda name, shape, dt=F32: nc.alloc_sbuf_tensor(name, shape, dt)
    C = make_consts(nc, p)
    ident, io = C["ident"], C["iota"]

    def blockdiag(B, name):
        nb = 128 // B
        E = sb("E" + name, [nb, 128])
        E2 = sb("E2" + name, [nb, 128])
        p.op("pool", lambda: nc.gpsimd.iota(E[:], pattern=[[1, 128]], base=0, channel_multiplier=-B,
                                            allow_small_or_imprecise_dtypes=True), writes=["E" + name])
        p.op("dve", lambda: nc.vector.tensor_single_scalar(out=E2[:], in_=E[:], scalar=float(B), op=ALU.is_lt),
             reads=["E" + name], writes=["E2" + name])
        p.op("dve", lambda: nc.vector.tensor_single_scalar(out=E[:], in_=E[:], scalar=0.0, op=ALU.is_ge),
             reads=["E" + name], writes=["E" + name])
        p.op("dve", lambda: nc.vector.tensor_tensor(out=E[:], in0=E[:], in1=E2[:], op=ALU.mult),
             reads=["E" + name, "E2" + name], writes=["E" + name])
        m = sb("bd" + name, [128, 128])
        pst = ps_n[0]
        p.op("pe", lambda: nc.tensor.matmul(pst[:, 0:128], lhsT=E[:], rhs=E[:], start=True, stop=True),
             reads=["E" + name], writes=[("ps_Q0", 0)])
        p.op("dve", lambda: nc.vector.tensor_copy(out=m[:], in_=pst[:, 0:128]), reads=[("ps_Q0", 0)], writes=["bd" + name])
        return m

    ps = lambda name: nc.alloc_psum_tensor(name, [128, 512], F32)
    ps_Q = [[ps("ps_Q%d_%d" % (k, i)) for i in range(2)] for k in range(4)]
    ps_n = [ps_Q[0][0], ps_Q[1][0]]

    bd16 = blockdiag(16, "16")
    bd32 = blockdiag(32, "32")
    bd64 = blockdiag(64, "64")
    m32 = sb("m32", [128, 128])
    m64 = sb("m64", [128, 128])
    MUi = sb("MUi", [128, 128])
    MLs = sb("MLs", [128, 128])
    p.op("dve", lambda: nc.vector.tensor_tensor(out=m32[:], in0=bd32[:], in1=bd16[:], op=ALU.subtract),
         reads=["bd32", "bd16"], writes=["m32"])
    p.op("dve", lambda: nc.vector.tensor_tensor(out=m64[:], in0=bd64[:], in1=bd32[:], op=ALU.subtract),
         reads=["bd64", "bd32"], writes=["m64"])
    p.op("dve", lambda: nc.vector.tensor_single_scalar(out=MUi[:], in_=io[:], scalar=0.0, op=ALU.is_ge),
         reads=["c_iota"], writes=["MUi"])
    p.op("dve", lambda: nc.vector.tensor_tensor(out=MUi[:], in0=MUi[:], in1=bd64[:], op=ALU.mult),
         reads=["MUi", "bd64"], writes=["MUi"])
    p.op("dve", lambda: nc.vector.tensor_single_scalar(out=MLs[:], in_=io[:], scalar=0.0, op=ALU.is_lt),
         reads=["c_iota"], writes=["MLs"])
    p.op("dve", lambda: nc.vector.tensor_tensor(out=MLs[:], in0=MLs[:], in1=bd64[:], op=ALU.mult),
         reads=["MLs", "bd64"], writes=["MLs"])
    ones2 = sb("ones2", [2, 128])
    p.op("pool", lambda: nc.gpsimd.memset(ones2[:], 1.0), writes=["ones2"])
    onesb = sb("onesb", [128, 128], BF16)
    p.op("pool", lambda: nc.gpsimd.memset(onesb[:], 1.0), writes=["onesb"])

    cb = sb("cb_sb", [128, 4])
    p.dma("sp", cb[:], cbd, writes=["cb"])
    def body(p, inst):
        sb = lambda name, shape, dt=F32: nc.alloc_sbuf_tensor(name + '_i%d' % inst, shape, dt)
        cw = sb("cw_sb", [128, 4, 5])
        dtb = sb("dtb_sb", [2, 1])
        acol = sb("acol", [2, 1])

        raw = [sb("raw%d" % i, [128, 4, TB + 4]) for i in range(1)] * 2
        cT = [sb("cT%d" % i, [128, 4, TB]) for i in range(1)] * 2
        ctmp = sb("ctmp", [128, TB])
        sqb = sb("sqb", [128, TB], BF16)
        rs = sb("rs", [128, TB])
        qT2 = [sb("qT2_%d" % i, [128, TB // CH, 2, CH]) for i in range(1)] * 2
        kT2 = [sb("kT2_%d" % i, [128, TB // CH, 2, CH]) for i in range(1)] * 2
        vT2 = [sb("vT2_%d" % i, [128, TB // CH, 2, CH]) for i in range(1)] * 2
        araw = sb("araw", [2, TB])
        braw = sb("braw", [2, TB])
        gT = sb("gT", [2, TB])
        gcsT = sb("gcsT", [2, TB])
        betaT = sb("betaT", [2, TB])
        glT = sb("glT", [2, TB])
        Mg = [sb("Mg%d" % i, [2, TB // CH, 2, CH]) for i in range(1)] * 2
        Mb = [sb("Mb%d" % i, [2, TB // CH, 2, CH]) for i in range(1)] * 2
        Ml = [sb("Ml%d" % i, [2, TB // CH, 2, CH]) for i in range(1)] * 2
        oTo = [sb("oTo%d" % i, [128, 2, TB]) for i in range(1)] * 2
        S = sb("S", [128, 256])

        def two(name, shape, dt=F32):
            return [sb(name, shape, dt)] * 2
        colsb = two("colsb", [128, 8])
        egrow = two("egrow", [128, 128])
        D1 = two("D1", [128, 128])
        D2 = two("D2", [128, 128])
        qkT = two("qkT", [128, 128])
        Am = two("Am", [128, 128])
        ATm = two("ATm", [128, 128])
        X = two("X", [128, 128]); Y = two("Y", [128, 128])
        Ao32 = two("Ao32", [128, 128]); Ao32T = two("Ao32T", [128, 128]); Ao64 = two("Ao64", [128, 128])
        Tm = two("Tm", [128, 128]); Um = two("Um", [128, 128])
        X2 = two("X2", [128, 128]); Y2 = two("Y2", [128, 128])
        Rm = two("Rm", [128, 128]); Pm = two("Pm", [128, 128])
        RHSv = two("RHSv", [128, 128]); RHSw = two("RHSw", [128, 128]); kdec = two("kdec", [128, 2, 128])
        qgT = two("qgT", [128, 128])
        u_sb = two("u_sb", [128, 128]); wT_sb = two("wT_sb", [128, 128])
        vnew = two("vnew", [128, 128])

        ncnt = [0]

        def mmN(lhsT, rhs, rkeys):
            i = ncnt[0] % 2
            ncnt[0] += 1
            key = ("ps_Q%d" % i, inst)
            p.op("pe", lambda: nc.tensor.matmul(ps_Q[i][inst][:, 0:128], lhsT=lhsT, rhs=rhs, start=True, stop=True),
                 reads=rkeys, writes=[key])
            return ps_Q[i][inst][:, 0:128], key

        def ev_copy(dst, dkey, src, skey):
            p.op("act", lambda: nc.scalar.copy(out=dst, in_=src), reads=[skey], writes=[dkey])

        def ev_comb(dst, dkey, a, akey, src, skey, op):
            p.op("dve", lambda: nc.vector.tensor_tensor(out=dst, in0=a, in1=src, op=op), reads=[akey, skey], writes=[dkey])

        p.dma("sp", cw[:].rearrange("p t k -> p (t k)"), cwd[inst], writes=["cw"])
        p.dma("sp", dtb[:], dtbd[inst], writes=["dtb"])
        p.dma("sp", acol[:], alogd[inst], writes=["acol"])
        p.op("act", lambda: nc.scalar.activation(out=acol[:], in_=acol[:], func=AF.Exp), reads=["acol"], writes=["acol"])
        p.op("dve", lambda: nc.vector.tensor_scalar(out=acol[:], in0=acol[:], scalar1=-1.0, scalar2=None, op0=ALU.mult),
             reads=["acol"], writes=["acol"])
        p.op("dve", lambda: nc.vector.memset(S[:], 0.0), writes=["S"])
        for b in range(n_blocks):
            bp = 0
            col0 = b * TB
            for t in range(4):
                p.dma("sp" if t % 2 else "act", raw[bp][:, t, :], qkv[inst, t * 128:(t + 1) * 128, col0:col0 + TB + 4],
                      writes=[("raw", bp, t)])
            p.dma("sp", araw[:], abr[inst, 0, :, col0:col0 + TB], writes=["araw"])
            p.dma("sp", braw[:], abr[inst, 1, :, col0:col0 + TB], writes=["braw"])
            conv_silu_block(nc, p, raw[bp], lambda t: ("raw", bp, t), cw, cb, cT[bp], lambda t: ("cT", bp, t), 4, ctmp, "ctmp")
            for t, dst, dname, sc in ((0, qT2[bp], "qT2", 128.0 ** -0.5), (1, kT2[bp], "kT2", 1.0)):
                p.op("act", lambda t=t: nc.scalar.activation(out=sqb[:], in_=cT[bp][:, t, :], func=AF.Square),
                     reads=[("cT", bp, t)], writes=["sqb"])
                p.op("pe", lambda: nc.tensor.matmul(ps_Q[0][inst][:], lhsT=onesb[:], rhs=sqb[:], start=True, stop=True),
                     reads=["onesb", "sqb"], writes=[("ps_Q0", inst)])
                p.op("act", lambda: nc.scalar.activation(out=rs[:], in_=ps_Q[0][inst][:], func=AF.Ln, bias=EPS),
                     reads=[("ps_Q0", inst)], writes=["rs"])
                p.op("act", lambda: nc.scalar.activation(out=rs[:], in_=rs[:], func=AF.Exp, scale=-0.5),
                     reads=["rs"], writes=["rs"])
                for vh in range(2):
                    p.op("dve", lambda t=t, dst=dst, sc=sc, vh=vh: nc.vector.scalar_tensor_tensor(
                        out=dst[:, :, vh, :], in0=cT[bp][:, t, :].rearrange("p (c i) -> p c i", i=CH), scalar=sc,
                        in1=rs[:].rearrange("p (c i) -> p c i", i=CH), op0=ALU.mult, op1=ALU.mult),
                        reads=[("cT", bp, t), "rs"], writes=[(dname, bp)])
            for vh in range(2):
                p.op("pool", lambda vh=vh: nc.gpsimd.tensor_copy(
                    out=vT2[bp][:, :, vh, :], in_=cT[bp][:, 2 + vh, :].rearrange("p (c i) -> p c i", i=CH)),
                    reads=[("cT", bp, 2 + vh)], writes=[("vT2", bp)])
            p.op("act", lambda: nc.scalar.activation(out=gT[:], in_=araw[:], func=AF.Exp, bias=dtb[:, 0:1]),
                 reads=["araw", "dtb"], writes=["gT"])
            p.op("act", lambda: nc.scalar.activation(out=gT[:], in_=gT[:], func=AF.Ln, bias=1.0),
                 reads=["gT"], writes=["gT"])
            p.op("dve", lambda: nc.vector.tensor_scalar(out=gT[:], in0=gT[:], scalar1=acol[:, 0:1], scalar2=None,
                                                        op0=ALU.mult), reads=["gT", "acol"], writes=["gT"])
            p.op("act", lambda: nc.scalar.activation(out=betaT[:], in_=braw[:], func=AF.Sigmoid),
                 reads=["braw"], writes=["betaT"])
            for j in range(TB // CH):
                cs = slice(j * CH, (j + 1) * CH)
                p.op("dve", lambda cs=cs: nc.vector.tensor_tensor_scan(
                    out=gcsT[:, cs], data0=ones2[:, 0:CH], data1=gT[:, cs], initial=0.0, op0=ALU.mult, op1=ALU.add),
                    reads=["gT", "ones2"], writes=["gcsT"])
            for j in range(TB // CH):
                cs = slice(j * CH, (j + 1) * CH)
                e = (j + 1) * CH - 1
                p.op("pool", lambda cs=cs, e=e: nc.gpsimd.tensor_copy(out=glT[:, cs],
                                                                     in_=gcsT[:, e:e + 1].broadcast_to([2, CH])),
                     reads=["gcsT"], writes=["glT"])
            for src, dst, nm in ((gcsT, Mg[bp], "Mg"), (betaT, Mb[bp], "Mb"), (glT, Ml[bp], "Ml")):
                for vh in range(2):
                    p.op("dve", lambda src=src, dst=dst, vh=vh: nc.vector.tensor_scalar(
                        out=dst[:, :, vh, :], in0=src[:].rearrange("p (c i) -> p c i", i=CH), scalar1=ident[0:2, vh:vh + 1],
                        scalar2=None, op0=ALU.mult),
                        reads=[src is gcsT and "gcsT" or (src is betaT and "betaT" or "glT"), "c_ident"],
                        writes=[(nm, bp)])

            for j in range(TB // CH):
                cs = slice(j * CH, (j + 1) * CH)
                c = b * (TB // CH) + j
                q = 0
                for i, (M, nm) in enumerate(((Mg[bp], "Mg"), (Mb[bp], "Mb"), (Ml[bp], "Ml"))):
                    p.op("pe", lambda i=i, M=M: nc.tensor.matmul(ps_Q[0][inst][:, 384 + 2 * i:386 + 2 * i], lhsT=M[:, j, :, :].rearrange("p v i -> p (v i)"), rhs=ones2[:, 0:2],
                                                                 start=True, stop=True),
                         reads=[(nm, bp), "ones2"], writes=[("ps_Q0", inst)])
                p.op("pe", lambda: nc.tensor.matmul(ps_Q[1][inst][:, 384:512], lhsT=ones2[:], rhs=Mg[bp][:, j, :, :].rearrange("p v i -> p (v i)"),
                                                    start=True, stop=True),
                     reads=[("Mg", bp), "ones2"], writes=[("ps_Q1", inst)])
                cq = colsb[q]
                ck = ("colsb", q)
                p.op("dve", lambda: nc.vector.tensor_copy(out=cq[:, 0:3], in_=ps_Q[0][inst][:, 384:390].rearrange("p (a b) -> p a b", b=2)[:, :, 0]),
                     reads=[("ps_Q0", inst)], writes=[ck])
                p.op("act", lambda: nc.scalar.activation(out=cq[:, 3:4], in_=cq[:, 0:1], func=AF.Exp), reads=[ck], writes=[ck])
                p.op("dve", lambda: nc.vector.tensor_tensor(out=cq[:, 4:5], in0=cq[:, 1:2], in1=cq[:, 3:4], op=ALU.mult),
                     reads=[ck], writes=[ck])
                p.op("dve", lambda: nc.vector.tensor_tensor(out=cq[:, 5:6], in0=cq[:, 2:3], in1=cq[:, 0:1], op=ALU.subtract),
                     reads=[ck], writes=[ck])
                p.op("act", lambda: nc.scalar.activation(out=cq[:, 5:6], in_=cq[:, 5:6], func=AF.Exp), reads=[ck], writes=[ck])
                p.op("dve", lambda: nc.vector.tensor_scalar(out=cq[:, 6:7], in0=cq[:, 0:1], scalar1=-1.0, scalar2=None,
                                                            op0=ALU.mult), reads=[ck], writes=[ck])
                p.op("act", lambda: nc.scalar.activation(out=egrow[q][:], in_=ps_Q[1][inst][:, 384:512], func=AF.Exp),
                     reads=[("ps_Q1", inst)], writes=[("egrow", q)])
                p.op("dve", lambda: nc.vector.tensor_scalar(out=D1[q][:], in0=ps_Q[1][inst][:, 384:512], scalar1=cq[:, 0:1],
                                                            scalar2=0.0, op0=ALU.subtract, op1=ALU.max),
                     reads=[("ps_Q1", inst), ck], writes=[("D1", q)])
                p.op("dve", lambda: nc.vector.tensor_scalar(out=D2[q][:], in0=ps_Q[1][inst][:, 384:512], scalar1=cq[:, 0:1],
                                                            scalar2=0.0, op0=ALU.subtract, op1=ALU.min),
                     reads=[("ps_Q1", inst), ck], writes=[("D2", q)])
                p.op("act", lambda: nc.scalar.activation(out=D1[q][:], in_=D1[q][:], func=AF.Exp, scale=-1.0),
                     reads=[("D1", q)], writes=[("D1", q)])
                p.op("act", lambda: nc.scalar.activation(out=D2[q][:], in_=D2[q][:], func=AF.Exp),
                     reads=[("D2", q)], writes=[("D2", q)])
                p.op("dve", lambda: nc.vector.scalar_tensor_tensor(out=D1[q][:], in0=D1[q][:], scalar=1.0, in1=MLs[:],
                                                                   op0=ALU.min, op1=ALU.mult),
                     reads=[("D1", q), "MLs"], writes=[("D1", q)])
                p.op("dve", lambda: nc.vector.scalar_tensor_tensor(out=D2[q][:], in0=D2[q][:], scalar=1.0, in1=MUi[:],
                                                                   op0=ALU.min, op1=ALU.mult),
                     reads=[("D2", q), "MUi"], writes=[("D2", q)])
                kk = kT2[bp][:, j, :, :].rearrange("p v i -> p (v i)")
                qq = qT2[bp][:, j, :, :].rearrange("p v i -> p (v i)")
                p.op("pe", lambda: nc.tensor.matmul(ps_Q[1][inst][:, 128:256], lhsT=kk, rhs=kk, start=True, stop=True),
                     reads=[("kT2", bp)], writes=[("ps_Q1", inst)])
                p.op("pe", lambda: nc.tensor.matmul(ps_Q[1][inst][:, 256:384], lhsT=kk, rhs=qq, start=True, stop=True),
                     reads=[("kT2", bp), ("qT2", bp)], writes=[("ps_Q1", inst)])
                p.op("dve", lambda: nc.vector.scalar_tensor_tensor(out=Am[q][:], in0=D1[q][:], scalar=cq[:, 1:2],
                                                                   in1=ps_Q[1][inst][:, 128:256], op0=ALU.mult, op1=ALU.mult),
                     reads=[("D1", q), ck, ("ps_Q1", inst)], writes=[("Am", q)])
                p.op("dve", lambda: nc.vector.tensor_tensor(out=qkT[q][:], in0=D2[q][:], in1=ps_Q[1][inst][:, 256:384], op=ALU.mult),
                     reads=[("D2", q), ("ps_Q1", inst)], writes=[("qkT", q)])
                p.op("pe", lambda: nc.tensor.matmul(ps_Q[0][inst][:, 128:256], lhsT=vT2[bp][:, j, :, :].rearrange("p v i -> p (v i)"), rhs=ident[:],
                                                    start=True, stop=True),
                     reads=[("vT2", bp), "c_ident"], writes=[("ps_Q0", inst)])
                p.op("pe", lambda: nc.tensor.matmul(ps_Q[0][inst][:, 256:384], lhsT=kk, rhs=ident[:], start=True, stop=True),
                     reads=[("kT2", bp), "c_ident"], writes=[("ps_Q0", inst)])
                p.op("dve", lambda: nc.vector.tensor_scalar(out=RHSv[q][:], in0=ps_Q[0][inst][:, 128:256], scalar1=cq[:, 1:2],
                                                            scalar2=None, op0=ALU.mult),
                     reads=[("ps_Q0", inst), ck], writes=[("RHSv", q)])
                p.op("dve", lambda: nc.vector.tensor_scalar(out=RHSw[q][:], in0=ps_Q[0][inst][:, 256:384], scalar1=cq[:, 4:5],
                                                            scalar2=None, op0=ALU.mult),
                     reads=[("ps_Q0", inst), ck], writes=[("RHSw", q)])
                for vh in range(2):
                    p.op("dve", lambda vh=vh: nc.vector.tensor_scalar(
                        out=kdec[q][:, vh, :], in0=ps_Q[0][inst][:, 256:384], scalar1=cq[:, 5:6],
                        scalar2=bd64[:, 64 * vh:64 * vh + 1], op0=ALU.mult, op1=ALU.mult),
                        reads=[("ps_Q0", inst), ck, "bd64"], writes=[("kdec", q)])
                p.op("pool", lambda: nc.gpsimd.tensor_tensor(
                    out=qgT[q][:], in0=qq, in1=egrow[q][:], op=ALU.mult),
                     reads=[("qT2", bp), ("egrow", q)], writes=[("qgT", q)])
                pa, pk = mmN(Am[q][:], ident[:], [("Am", q), "c_ident"])
                ev_copy(ATm[q][:], ("ATm", q), pa, pk)
                for dst, nm, src, snm, msk, mnm in ((X, "X", Am, "Am", bd16, "bd16"), (Y, "Y", ATm, "ATm", bd16, "bd16"),
                                                    (Ao32, "Ao32", Am, "Am", m32, "m32"),
                                                    (Ao32T, "Ao32T", ATm, "ATm", m32, "m32"),
                                                    (Ao64, "Ao64", Am, "Am", m64, "m64")):
                    p.op("pool", lambda dst=dst, src=src, msk=msk: nc.gpsimd.tensor_tensor(out=dst[q][:], in0=src[q][:],
                                                                                          in1=msk[:], op=ALU.mult),
                         reads=[(snm, q), mnm], writes=[(nm, q)])
                Tq, Uq = Tm[q], Um[q]
                tk, uk = ("Tm", q), ("Um", q)
                p.op("dve", lambda: nc.vector.tensor_tensor(out=Tq[:], in0=ident[:], in1=X[q][:], op=ALU.subtract),
                     reads=["c_ident", ("X", q)], writes=[tk])
                p.op("dve", lambda: nc.vector.tensor_tensor(out=Uq[:], in0=ident[:], in1=Y[q][:], op=ALU.subtract),
                     reads=["c_ident", ("Y", q)], writes=[uk])
                xa, ya, xk_, yk_ = X[q], Y[q], ("X", q), ("Y", q)
                xb, yb, xbk, ybk = X2[q], Y2[q], ("X2", q), ("Y2", q)
                for lvl in range(3):
                    pa, pk = mmN(ya[:], xa[:], [yk_, xk_])
                    ev_copy(xb[:], xbk, pa, pk)
                    if lvl < 2:
                        pa, pk = mmN(xa[:], ya[:], [xk_, yk_])
                        ev_copy(yb[:], ybk, pa, pk)
                    pa, pk = mmN(Uq[:], xb[:], [uk, xbk])
                    pb, pkb = mmN(xb[:], Uq[:], [xbk, uk])
                    ev_comb(Tq[:], tk, Tq[:], tk, pa, pk, ALU.add)
                    ev_comb(Uq[:], uk, Uq[:], uk, pb, pkb, ALU.add)
                    xa, ya, xk_, yk_, xb, yb, xbk, ybk = xb, yb, xbk, ybk, xa, ya, xk_, yk_
                pa, pk = mmN(Ao32T[q][:], Tq[:], [("Ao32T", q), tk])
                ev_copy(Rm[q][:], ("Rm", q), pa, pk)
                pa, pk = mmN(Ao32[q][:], Uq[:], [("Ao32", q), uk])
                ev_copy(Pm[q][:], ("Pm", q), pa, pk)
                pa, pk = mmN(Uq[:], Rm[q][:], [uk, ("Rm", q)])
                pb, pkb = mmN(Tq[:], Pm[q][:], [tk, ("Pm", q)])
                ev_comb(Tq[:], tk, Tq[:], tk, pa, pk, ALU.subtract)
                ev_comb(Uq[:], uk, Uq[:], uk, pb, pkb, ALU.subtract)
                pa, pk = mmN(Ao64[q][:], Uq[:], [("Ao64", q), uk])
                ev_copy(Pm[q][:], ("Pm", q), pa, pk)
                pb, pkb = mmN(Tq[:], Pm[q][:], [tk, ("Pm", q)])
                ev_comb(Uq[:], uk, Uq[:], uk, pb, pkb, ALU.subtract)
                pa, pk = mmN(Uq[:], RHSv[q][:], [uk, ("RHSv", q)])
                ev_copy(u_sb[q][:], ("u_sb", q), pa, pk)
                pa, pk = mmN(RHSw[q][:], Uq[:], [("RHSw", q), uk])
                ev_copy(wT_sb[q][:], ("wT_sb", q), pa, pk)
                for vh in range(2):
                    hs = slice(vh * 64, (vh + 1) * 64)
                    vs = slice(vh * 128, (vh + 1) * 128)
                    p.op("pe", lambda hs=hs, vs=vs: nc.tensor.matmul(ps_Q[2][inst][:, vs], lhsT=wT_sb[q][:], rhs=S[:, vs],
                                                                    start=True, stop=True),
                         reads=[("wT_sb", q), "S"], writes=[("ps_Q2", inst)])
                for vh in range(2):
                    hs = slice(vh * 64, (vh + 1) * 64)
                    vs = slice(vh * 128, (vh + 1) * 128)
                    p.op("dve", lambda hs=hs, vs=vs: nc.vector.tensor_tensor(out=vnew[q][hs, :], in0=u_sb[q][hs, :],
                                                                            in1=ps_Q[2][inst][hs, vs], op=ALU.subtract),
                         reads=[("u_sb", q), ("ps_Q2", inst)], writes=[("vnew", q)])
                for vh in range(2):
                    hs = slice(vh * 64, (vh + 1) * 64)
                    vs = slice(vh * 128, (vh + 1) * 128)
                    oc = slice(256 + vh * 64, 256 + (vh + 1) * 64)
                    p.op("pe", lambda hs=hs, vs=vs, oc=oc: nc.tensor.matmul(ps_Q[2][inst][:, oc], lhsT=S[:, vs], rhs=qgT[q][:, hs],
                                                                           start=True, stop=False),
                         reads=["S", ("qgT", q)], writes=[("ps_Q2", inst)])
                    p.op("pe", lambda hs=hs, oc=oc: nc.tensor.matmul(ps_Q[2][inst][:, oc], lhsT=vnew[q][:], rhs=qkT[q][:, hs],
                                                                    start=False, stop=True),
                         reads=[("vnew", q), ("qkT", q)], writes=[("ps_Q2", inst)])
                p.op("act", lambda: nc.scalar.copy(out=oTo[bp][:, :, cs],
                                                   in_=ps_Q[2][inst][:, 256:384].rearrange("p (v i) -> p v i", v=2)),
                     reads=[("ps_Q2", inst)], writes=[("oTo", bp)])
                for vh in range(2):
                    hs = slice(vh * 64, (vh + 1) * 64)
                    vs = slice(vh * 128, (vh + 1) * 128)
                    p.op("pe", lambda hs=hs, vs=vs, vh=vh: nc.tensor.matmul(ps_Q[3][inst][:, vs], lhsT=kdec[q][:, vh, :], rhs=vnew[q][:],
                                                                    start=True, stop=True),
                         reads=[("kdec", q), ("vnew", q)], writes=[("ps_Q3", inst)])
                for vh in range(2):
                    vs = slice(vh * 128, (vh + 1) * 128)
                    e = vh * 64 + 63
                    p.op("dve", lambda vs=vs, e=e: nc.vector.scalar_tensor_tensor(
                        out=S[:, vs], in0=S[:, vs], scalar=egrow[q][:, e:e + 1], in1=ps_Q[3][inst][:, vs],
                        op0=ALU.mult, op1=ALU.add), reads=["S", ("egrow", q), ("ps_Q3", inst)], writes=["S"])
            for vh in range(2):
                p.dma("sp", oT[inst, vh * 128:(vh + 1) * 128, col0:col0 + TB], oTo[bp][:, vh, :],
                      reads=[("oTo", bp)], is_output=True)
    SHARED = {"c_iota", "c_ident", "c_triu", "cb", "bd16", "bd32", "bd64", "m32", "m64", "MUi", "MLs", "ones2", "onesb"}
    run_interleaved(p, [lambda v: body(v, 0), lambda v: body(v, 1)], SHARED)
    p.finish()
    return nc


T_SEQ = 16384
TC = 512
NP = 4


def build_s5(n_blocks=T_SEQ // TC):
    nc = bass.Bass("TRN2", target_bir_lowering=False)
    p = Prog(nc)
    T = n_blocks * TC
    din = lambda name, shape: nc.dram_tensor(name, shape, F32, kind="ExternalInput").ap()
    uT = din("uT", [2, 128, T])
    lre_d = din("lam_re", [2, 128, NP])
    lim_d = din("lam_im", [2, 128, NP])
    lst_d = din("log_step", [2, 128, NP])
    bre_d = din("b_re", [128, NP * 128])
    bim_d = din("b_im", [128, NP * 128])
    cre_d = din("c_re", [2, 128, NP * 128])
    cim_d = din("c_im", [2, 128, NP * 128])
    yT = nc.dram_tensor("yT", [2, 128, T], F32, kind="ExternalOutput").ap()

    sb = lambda name, shape, dt=F32: nc.alloc_sbuf_tensor(name, shape, dt)
    tpr = sb("tpr", [128, TC])
    p.op("pool", lambda: nc.gpsimd.iota(tpr[:], pattern=[[1, TC]], base=0, channel_multiplier=0,
                                        allow_small_or_imprecise_dtypes=True), writes=["tpr"])
    Bre = sb("Bre", [128, NP * 128]); Bim = sb("Bim", [128, NP * 128])
    p.dma("sp", Bre[:], bre_d, writes=["Bre"])
    p.dma("sp", Bim[:], bim_d, writes=["Bim"])
    ps = lambda name: nc.alloc_psum_tensor(name, [128, 512], F32)
    ps_xr = [ps("ps_xr0"), ps("ps_xr1")]
    ps_xi = [ps("ps_xi0"), ps("ps_xi1")]
    ps_y = [ps("ps_y0"), ps("ps_y1")]

    def body(p, inst):
        sb = lambda name, shape, dt=F32: nc.alloc_sbuf_tensor(name + '_i%d' % inst, shape, dt)
        Cre = sb("Cre", [128, NP * 128]); Cim = sb("Cim", [128, NP * 128])
        lre = sb("lre", [128, NP]); lim = sb("lim", [128, NP]); stp = sb("stp", [128, NP])
        rr = sb("rr", [128, NP]); thn = sb("thn", [128, NP])
        sc = {n: sb("sc_" + n, [128, NP]) for n in ("cos", "sin", "rc", "rs", "zr", "zi", "nzr", "den", "t1", "t2", "y")}
        sy = sb("sy", [128, TC]); sk = sb("sk", [128, TC], I32); sf = sb("sf", [128, TC])
        s2 = sb("s2", [128, TC]); q4 = sb("q4", [128, TC]); c2 = sb("c2", [128, TC])
        tS = sb("tS", [128, TC]); tCo = sb("tCo", [128, TC])
        Ezr = sb("Ezr", [128, NP, TC]); Ezi = sb("Ezi", [128, NP, TC])
        Fr = sb("Fr", [128, NP, TC]); Fi = sb("Fi", [128, NP, TC])
        carry = sb("carry", [128, NP, 2])
        ctmp = sb("carry_tmp", [128, 4])
        ub = [sb("ub%d" % i, [128, TC]) for i in range(1)] * 2
        yo = [sb("yo%d" % i, [128, TC]) for i in range(1)] * 2

        def two(name):
            return [sb(name, [128, TC])] * 2
        m1, m2, m3, m4 = two("m1"), two("m2"), two("m3"), two("m4")
        xr_, xi_ = two("xr_"), two("xi_")
        sr_, si_ = two("sr_"), two("si_")
        d1, d2, d3, d4 = two("d1"), two("d2"), two("d3"), two("d4")
        or_, oi_ = two("or_"), two("oi_")
        def sincos(y_ap, n, out_s, out_c, ykeys, okeys):
            K = "sincos_tmp"
            p.op("dve", lambda: nc.vector.tensor_copy(out=sk[:, 0:n], in_=y_ap), reads=ykeys, writes=[K])
            p.op("dve", lambda: nc.vector.tensor_copy(out=sf[:, 0:n], in_=sk[:, 0:n]), reads=[K], writes=[K])
            p.op("dve", lambda: nc.vector.tensor_tensor(out=sf[:, 0:n], in0=y_ap, in1=sf[:, 0:n], op=ALU.subtract),
                 reads=ykeys + [K], writes=[K])
            p.op("act", lambda: nc.scalar.activation(out=s2[:, 0:n], in_=sf[:, 0:n], func=AF.Sin, scale=math.pi),
                 reads=[K], writes=[K])
            p.op("act", lambda: nc.scalar.activation(out=q4[:, 0:n], in_=sf[:, 0:n], func=AF.Sin, scale=math.pi / 2),
                 reads=[K], writes=[K])
            p.op("dve", lambda: nc.vector.tensor_tensor(out=c2[:, 0:n], in0=q4[:, 0:n], in1=q4[:, 0:n], op=ALU.mult),
                 reads=[K], writes=[K])
            p.op("dve", lambda: nc.vector.tensor_scalar(out=c2[:, 0:n], in0=c2[:, 0:n], scalar1=-2.0, scalar2=1.0,
                                                        op0=ALU.mult, op1=ALU.add), reads=[K], writes=[K])
            p.op("dve", lambda: nc.vector.scalar_tensor_tensor(out=out_s, in0=s2[:, 0:n], scalar=2.0, in1=c2[:, 0:n],
                                                               op0=ALU.mult, op1=ALU.mult), reads=[K], writes=okeys)
            p.op("dve", lambda: nc.vector.tensor_tensor(out=c2[:, 0:n], in0=s2[:, 0:n], in1=s2[:, 0:n], op=ALU.mult),
                 reads=[K], writes=[K])
            p.op("dve", lambda: nc.vector.tensor_scalar(out=out_c, in0=c2[:, 0:n], scalar1=-2.0, scalar2=1.0,
                                                        op0=ALU.mult, op1=ALU.add), reads=[K], writes=okeys)

        p.dma("sp", lre[:], lre_d[inst], writes=["prm"])
        p.dma("sp", lim[:], lim_d[inst], writes=["prm"])
        p.dma("sp", stp[:], lst_d[inst], writes=["prm"])
        p.dma("sp", Cre[:], cre_d[inst], writes=["Cre"])
        p.dma("sp", Cim[:], cim_d[inst], writes=["Cim"])
        P = ["prm"]
        p.op("act", lambda: nc.scalar.activation(out=stp[:], in_=stp[:], func=AF.Exp), reads=P, writes=P)
        p.op("dve", lambda: nc.vector.tensor_tensor(out=rr[:], in0=lre[:], in1=stp[:], op=ALU.mult), reads=P, writes=P)
        p.op("act", lambda: nc.scalar.activation(out=rr[:], in_=rr[:], func=AF.Exp), reads=P, writes=P)
        p.op("dve", lambda: nc.vector.tensor_tensor(out=thn[:], in0=lim[:], in1=stp[:], op=ALU.mult), reads=P, writes=P)
        p.op("dve", lambda: nc.vector.tensor_scalar(out=thn[:], in0=thn[:], scalar1=1.0 / (2 * math.pi), scalar2=None,
                                                    op0=ALU.mult), reads=P, writes=P)
        sincos(thn[:], NP, sc["sin"][:], sc["cos"][:], P, P)
        p.op("dve", lambda: nc.vector.tensor_scalar(out=sc["y"][:], in0=thn[:], scalar1=float(TC), scalar2=None,
                                                    op0=ALU.mult), reads=P, writes=P)
        sincos(sc["y"][:], NP, sc["rs"][:], sc["rc"][:], P, P)
        tt = lambda o, a, b, op: p.op("dve", lambda: nc.vector.tensor_tensor(out=o, in0=a, in1=b, op=op), reads=P, writes=P)
        tt(sc["cos"][:], sc["cos"][:], rr[:], ALU.mult)
        tt(sc["sin"][:], sc["sin"][:], rr[:], ALU.mult)
        p.op("dve", lambda: nc.vector.tensor_scalar(out=sc["cos"][:], in0=sc["cos"][:], scalar1=-1.0, scalar2=None,
                                                    op0=ALU.add), reads=P, writes=P)
        tt(sc["t1"][:], lre[:], lre[:], ALU.mult)
        tt(sc["t2"][:], lim[:], lim[:], ALU.mult)
        tt(sc["den"][:], sc["t1"][:], sc["t2"][:], ALU.add)
        p.op("dve", lambda: nc.vector.reciprocal(out=sc["den"][:], in_=sc["den"][:]), reads=P, writes=P)
        tt(sc["t1"][:], sc["cos"][:], lre[:], ALU.mult)
        tt(sc["t2"][:], sc["sin"][:], lim[:], ALU.mult)
        tt(sc["zr"][:], sc["t1"][:], sc["t2"][:], ALU.add)
        tt(sc["zr"][:], sc["zr"][:], sc["den"][:], ALU.mult)
        tt(sc["t1"][:], sc["sin"][:], lre[:], ALU.mult)
        tt(sc["t2"][:], sc["cos"][:], lim[:], ALU.mult)
        tt(sc["zi"][:], sc["t1"][:], sc["t2"][:], ALU.subtract)
        tt(sc["zi"][:], sc["zi"][:], sc["den"][:], ALU.mult)
        p.op("dve", lambda: nc.vector.tensor_scalar(out=sc["nzr"][:], in0=sc["zr"][:], scalar1=-1.0, scalar2=None,
                                                    op0=ALU.mult), reads=P, writes=P)
        for pr in range(NP):
            p.op("dve", lambda pr=pr: nc.vector.tensor_scalar(out=sy[:], in0=tpr[:], scalar1=thn[:, pr:pr + 1], scalar2=None,
                                                              op0=ALU.mult), reads=["tpr"] + P, writes=["sy"])
            sincos(sy[:], TC, tS[:], tCo[:], ["sy"], ["tSC"])
            tk = ("tab", pr)
            p.op("pool", lambda pr=pr: nc.gpsimd.tensor_copy(out=Fr[:, pr, :], in_=tCo[:]), reads=["tSC"], writes=[tk])
            p.op("pool", lambda pr=pr: nc.gpsimd.tensor_copy(out=Fi[:, pr, :], in_=tS[:]), reads=["tSC"], writes=[tk])
            p.op("dve", lambda pr=pr: nc.vector.tensor_scalar(out=Ezr[:, pr, :], in0=tCo[:], scalar1=sc["zr"][:, pr:pr + 1],
                                                              scalar2=None, op0=ALU.mult), reads=["tSC"] + P, writes=[tk])
            p.op("dve", lambda pr=pr: nc.vector.scalar_tensor_tensor(
                out=Ezr[:, pr, :], in0=tS[:], scalar=sc["zi"][:, pr:pr + 1], in1=Ezr[:, pr, :], op0=ALU.mult, op1=ALU.add),
                reads=["tSC", tk] + P, writes=[tk])
            p.op("dve", lambda pr=pr: nc.vector.tensor_scalar(out=Ezi[:, pr, :], in0=tCo[:], scalar1=sc["zi"][:, pr:pr + 1],
                                                              scalar2=None, op0=ALU.mult), reads=["tSC"] + P, writes=[tk])
            p.op("dve", lambda pr=pr: nc.vector.scalar_tensor_tensor(
                out=Ezi[:, pr, :], in0=tS[:], scalar=sc["nzr"][:, pr:pr + 1], in1=Ezi[:, pr, :], op0=ALU.mult, op1=ALU.add),
                reads=["tSC", tk] + P, writes=[tk])
        p.op("dve", lambda: nc.vector.memset(carry[:], 0.0), writes=[("carry", pr) for pr in range(NP)])

        it = 0
        for b in range(n_blocks):
            bp = 0
            col0 = b * TC
            p.dma("sp", ub[bp][:], uT[inst, :, col0:col0 + TC], writes=[("ub", bp)])
            for pr in range(NP):
                q = 0
                it += 1
                tk = ("tab", pr)
                ck = ("carry", pr)
                ws = slice(pr * 128, (pr + 1) * 128)
                p.op("pe", lambda: nc.tensor.matmul(ps_xr[inst][:], lhsT=Bre[:, ws], rhs=ub[bp][:], start=True, stop=True),
                     reads=["Bre", ("ub", bp)], writes=[("ps_xr", inst)])
                p.op("pe", lambda: nc.tensor.matmul(ps_xi[inst][:], lhsT=Bim[:, ws], rhs=ub[bp][:], start=True, stop=True),
                     reads=["Bim", ("ub", bp)], writes=[("ps_xi", inst)])
                for o, onm, a, anm, tab in ((m1, "m1", ps_xr, "ps_xr", Ezr), (m2, "m2", ps_xi, "ps_xi", Ezi),
                                            (m3, "m3", ps_xr, "ps_xr", Ezi), (m4, "m4", ps_xi, "ps_xi", Ezr)):
                    p.op("dve", lambda o=o, a=a, tab=tab: nc.vector.tensor_tensor(out=o[q][:], in0=a[inst][:], in1=tab[:, pr, :],
                                                                                 op=ALU.mult),
                         reads=[(anm, inst), tk], writes=[(onm, q)])
                p.op("pool", lambda: nc.gpsimd.tensor_tensor(out=xr_[q][:], in0=m1[q][:], in1=m2[q][:], op=ALU.subtract),
                     reads=[("m1", q), ("m2", q)], writes=[("xr_", q)])
                p.op("pool", lambda: nc.gpsimd.tensor_tensor(out=xi_[q][:], in0=m3[q][:], in1=m4[q][:], op=ALU.add),
                     reads=[("m3", q), ("m4", q)], writes=[("xi_", q)])
                rb = rr[:, pr:pr + 1].broadcast_to([128, TC])
                p.op("dve", lambda: nc.vector.tensor_tensor_scan(out=sr_[q][:], data0=rb, data1=xr_[q][:],
                                                                 initial=carry[:, pr, 0:1], op0=ALU.mult, op1=ALU.add),
                     reads=["prm", ("xr_", q), ck], writes=[("sr_", q)])
                p.op("dve", lambda: nc.vector.tensor_tensor_scan(out=si_[q][:], data0=rb, data1=xi_[q][:],
                                                                 initial=carry[:, pr, 1:2], op0=ALU.mult, op1=ALU.add),
                     reads=["prm", ("xi_", q), ck], writes=[("si_", q)])
                lr, li = sr_[q][:, TC - 1:TC], si_[q][:, TC - 1:TC]
                p.op("dve", lambda: nc.vector.tensor_tensor(out=ctmp[:, 0:1], in0=li, in1=sc["rs"][:, pr:pr + 1], op=ALU.mult),
                     reads=[("si_", q), "prm"], writes=["ctmp"])
                p.op("dve", lambda: nc.vector.tensor_tensor(out=ctmp[:, 1:2], in0=li, in1=sc["rc"][:, pr:pr + 1], op=ALU.mult),
                     reads=[("si_", q), "prm"], writes=["ctmp"])
                p.op("dve", lambda: nc.vector.scalar_tensor_tensor(out=carry[:, pr, 0:1], in0=lr, scalar=sc["rc"][:, pr:pr + 1],
                                                                   in1=ctmp[:, 0:1], op0=ALU.mult, op1=ALU.subtract),
                     reads=[("sr_", q), "prm", "ctmp"], writes=[ck])
                p.op("dve", lambda: nc.vector.scalar_tensor_tensor(out=carry[:, pr, 1:2], in0=lr, scalar=sc["rs"][:, pr:pr + 1],
                                                                   in1=ctmp[:, 1:2], op0=ALU.mult, op1=ALU.add),
                     reads=[("sr_", q), "prm", "ctmp"], writes=[ck])
                for o, onm, a, anm, tab in ((d1, "d1", sr_, "sr_", Fr), (d2, "d2", si_, "si_", Fi),
                                            (d3, "d3", sr_, "sr_", Fi), (d4, "d4", si_, "si_", Fr)):
                    p.op("pool", lambda o=o, a=a, tab=tab: nc.gpsimd.tensor_tensor(out=o[q][:], in0=a[q][:], in1=tab[:, pr, :],
                                                                                  op=ALU.mult),
                         reads=[(anm, q), tk], writes=[(onm, q)])
                p.op("dve", lambda: nc.vector.tensor_tensor(out=or_[q][:], in0=d1[q][:], in1=d2[q][:], op=ALU.subtract),
                     reads=[("d1", q), ("d2", q)], writes=[("or_", q)])
                p.op("dve", lambda: nc.vector.scalar_tensor_tensor(out=oi_[q][:], in0=d3[q][:], scalar=-1.0, in1=d4[q][:],
                                                                   op0=ALU.mult, op1=ALU.subtract),
                     reads=[("d3", q), ("d4", q)], writes=[("oi_", q)])
                p.op("pe", lambda: nc.tensor.matmul(ps_y[inst][:], lhsT=Cre[:, ws], rhs=or_[q][:], start=(pr == 0), stop=False),
                     reads=["Cre", ("or_", q)], writes=[("ps_y", inst)])
                p.op("pe", lambda: nc.tensor.matmul(ps_y[inst][:], lhsT=Cim[:, ws], rhs=oi_[q][:], start=False,
                                                    stop=(pr == NP - 1)),
                     reads=["Cim", ("oi_", q)], writes=[("ps_y", inst)])
            p.op("act", lambda: nc.scalar.copy(out=yo[bp][:], in_=ps_y[inst][:]), reads=[("ps_y", inst)], writes=[("yo", bp)])
            p.dma("sp", yT[inst, :, col0:col0 + TC], yo[bp][:], reads=[("yo", bp)], is_output=True)
    SHARED = {"tpr", "Bre", "Bim"}
    run_interleaved(p, [lambda v: body(v, 0), lambda v: body(v, 1)], SHARED)
    p.finish()
    return nc


NCORES = 8
SEQ = 16384
TOKC = SEQ // NCORES


def _run(nc, in_maps):
    res = run_bass_kernel_spmd(nc, in_maps, core_ids=list(range(NCORES)))
    return [{k: np.asarray(v) for k, v in r.items()} for r in res.results]


def _c(a):
    return np.ascontiguousarray(a, dtype=np.float32)


def _tok(a, c):
    return _c(a[:, c * TOKC:(c + 1) * TOKC])


def _conv_layout(w):
    return _c(w.T.reshape(4, 128, 5).transpose(1, 0, 2).reshape(128, 20))


def _pad2(a):
    return np.pad(a, ((0, 0), (2, 2)))


def _ffn_inputs(i, norm_w, ffn_w_gate_up, ffn_w_down):
    return dict(
        nws=_c(np.concatenate([col_tiles(norm_w[i, k]) for k in (1, 2, 3)], axis=1)),
        wgu=_c(np.concatenate([arrange_w(ffn_w_gate_up[i][:, :DFF]), arrange_w(ffn_w_gate_up[i][:, DFF:])], axis=2)),
        wd=arrange_w(ffn_w_down[i]),
    )


def _ssd_maps(P, j, conv_w, conv_b, dt_bias, a_log, d_skip):
    maps = []
    for g in range(NCORES):
        ch = np.concatenate([np.arange(g * 256, (g + 1) * 256), 2048 + np.arange(g * 128, (g + 1) * 128),
                             3072 + np.arange(g * 128, (g + 1) * 128)])
        xg = P[2048 + ch]
        w = conv_w[j][:, ch]
        maps.append(dict(
            xbc=_c(np.stack([_pad2(xg), _pad2(xg[:, ::-1])])),
            dtr=_c(np.stack([P[6144 + 4 * g:6144 + 4 * g + 4], P[6176 + 4 * g:6176 + 4 * g + 4][:, ::-1]])),
            cw=_c(np.stack([_conv_layout(w), _conv_layout(w[::-1])])),
            cb=_c(conv_b[j][ch].reshape(4, 128).T),
            dtb=_c(dt_bias[j][:, 4 * g:4 * g + 4].reshape(2, 4, 1)),
            alog=_c(a_log[j][:, 4 * g:4 * g + 4].reshape(2, 4, 1)),
            dsk=_c(np.repeat(d_skip[j][4 * g:4 * g + 4], 64).reshape(2, 128).T),
        ))
    return maps


def _gdn_maps(P, conv_w, conv_b, dt_bias, a_log):
    maps = []
    for g in range(NCORES):
        ch = np.concatenate([np.arange(g * 128, (g + 1) * 128), 1024 + np.arange(g * 128, (g + 1) * 128),
                             2048 + np.arange(2 * g * 128, (2 * g + 2) * 128)])
        xg = P[ch]
        w = conv_w[0][:, ch]
        a0, a1 = P[6144 + 2 * g:6144 + 2 * g + 2], P[6144 + 16 + 2 * g:6144 + 16 + 2 * g + 2]
        b0, b1 = P[6176 + 2 * g:6176 + 2 * g + 2], P[6176 + 16 + 2 * g:6176 + 16 + 2 * g + 2]
        maps.append(dict(
            qkv=_c(np.stack([_pad2(xg), _pad2(xg[:, ::-1])])),
            abr=_c(np.stack([np.stack([a0, b0]), np.stack([a1[:, ::-1], b1[:, ::-1]])])),
            cw=_c(np.stack([_conv_layout(w), _conv_layout(w[::-1])])),
            cb=_c(conv_b[0][ch].reshape(4, 128).T),
            dtb=_c(dt_bias[0][:, 2 * g:2 * g + 2].reshape(2, 2, 1)),
            alog=_c(a_log[0][:, 2 * g:2 * g + 2].reshape(2, 2, 1)),
        ))
    return maps


def _pair_cols(a):
    return _c(a.reshape(4, 2, 64).transpose(1, 2, 0).reshape(128, 4))


def _blayout(b):
    out = np.zeros((8, 16, 4, 2, 64), np.float32)
    for g in range(8):
        out[g, :, g // 2, g % 2, :] = b[g].T
    return out.reshape(128, 512)


def _clayout(c):
    out = np.zeros((2, 64, 4, 8, 16), np.float32)
    for g in range(8):
        out[g % 2, :, g // 2, g, :] = c[g].T
    return out.reshape(128, 512)


def _s5_maps(hn, lam_re, lam_im, log_step, b_re, b_im, c_re, c_im):
    maps = []
    for c in range(NCORES):
        gs = slice(8 * c, 8 * c + 8)
        u = hn[c * 128:(c + 1) * 128]
        maps.append(dict(
            uT=_c(np.stack([u, u[:, ::-1]])),
            lam_re=np.stack([_pair_cols(lam_re[0, d, gs]) for d in range(2)]),
            lam_im=np.stack([_pair_cols(lam_im[0, d, gs]) for d in range(2)]),
            log_step=np.stack([_pair_cols(np.repeat(log_step[0, d, gs][:, None], 64, axis=1)) for d in range(2)]),
            b_re=_blayout(b_re[0, gs]), b_im=_blayout(b_im[0, gs]),
            c_re=np.stack([_clayout(c_re[0, d, gs]) for d in range(2)]),
            c_im=np.stack([_clayout(c_im[0, d, gs]) for d in range(2)]),
        ))
    return maps


def _gather_tok(results, key):
    return np.concatenate([r[key] for r in results], axis=1)


def kernel(x, norm_w, ssd_w_in, ssd_conv_w, ssd_conv_b, ssd_dt_bias, ssd_a_log, ssd_d,
           ssd_norm_w, ssd_w_out, gdn_w_in, gdn_conv_w, gdn_conv_b, gdn_dt_bias, gdn_a_log,
           gdn_norm_w, gdn_w_out, s5_lam_re, s5_lam_im, s5_log_step, s5_b_re, s5_b_im,
           s5_c_re, s5_c_im, s5_d, s5_w_glu, s5_b_glu, ffn_w_gate_up, ffn_w_down):
    A = lambda a: np.asarray(a, dtype=np.float32)
    (x, norm_w, ssd_w_in, ssd_conv_w, ssd_conv_b, ssd_dt_bias, ssd_a_log, ssd_d, ssd_norm_w, ssd_w_out, gdn_w_in,
     gdn_conv_w, gdn_conv_b, gdn_dt_bias, gdn_a_log, gdn_norm_w, gdn_w_out, s5_lam_re, s5_lam_im, s5_log_step,
     s5_b_re, s5_b_im, s5_c_re, s5_c_im, s5_d, s5_w_glu, s5_b_glu, ffn_w_gate_up, ffn_w_down) = map(A, (
         x, norm_w, ssd_w_in, ssd_conv_w, ssd_conv_b, ssd_dt_bias, ssd_a_log, ssd_d, ssd_norm_w, ssd_w_out, gdn_w_in,
         gdn_conv_w, gdn_conv_b, gdn_dt_bias, gdn_a_log, gdn_norm_w, gdn_w_out, s5_lam_re, s5_lam_im, s5_log_step,
         s5_b_re, s5_b_im, s5_c_re, s5_c_im, s5_d, s5_w_glu, s5_b_glu, ffn_w_gate_up, ffn_w_down))
    hT = _c(x[0].T)

    nc_ssd = build_ssd()
    common = dict(nw0=col_tiles(norm_w[0, 0]), w_in=arrange_w(ssd_w_in[0]))
    r = _run(build_dense(None, False, "proj"), [dict(hT=_tok(hT, c), **common) for c in range(NCORES)])
    P = _gather_tok(r, "projT")

    r = _run(nc_ssd, _ssd_maps(P, 0, ssd_conv_w, ssd_conv_b, ssd_dt_bias, ssd_a_log, ssd_d))
    yf = np.concatenate([q["yT"][0] for q in r], axis=0)
    yb = np.concatenate([q["yT"][1][:, ::-1] for q in r], axis=0)
    common = dict(w_out=arrange_w(ssd_w_out[0]), mnw=col_tiles(ssd_norm_w[0]), nw0=col_tiles(norm_w[1, 0]),
                  w_in=arrange_w(gdn_w_in[0]), **_ffn_inputs(0, norm_w, ffn_w_gate_up, ffn_w_down))
    r = _run(build_dense("ssd", True, "proj"),
             [dict(hT=_tok(hT, c), mf=_tok(yf, c), mb=_tok(yb, c), zT=_tok(P[0:2048], c), **common) for c in range(NCORES)])
    hT = _gather_tok(r, "hT_out")
    P = _gather_tok(r, "projT")

    r = _run(build_gdn(), _gdn_maps(P, gdn_conv_w, gdn_conv_b, gdn_dt_bias, gdn_a_log))
    yf = np.concatenate([q["oT"][0] for q in r], axis=0)
    yb = np.concatenate([q["oT"][1][:, ::-1] for q in r], axis=0)
    common = dict(w_out=arrange_w(gdn_w_out[0]), mnw=_c(np.tile(gdn_norm_w[0][:, None], (1, 16))),
                  nw0=col_tiles(norm_w[2, 0]), **_ffn_inputs(1, norm_w, ffn_w_gate_up, ffn_w_down))
    r = _run(build_dense("gdn", True, "hn"),
             [dict(hT=_tok(hT, c), mf=_tok(yf, c), mb=_tok(yb, c), zT=_tok(P[4096:6144], c), **common)
              for c in range(NCORES)])
    hT = _gather_tok(r, "hT_out")
    hn = _gather_tok(r, "hn_out")

    r = _run(build_s5(), _s5_maps(hn, s5_lam_re, s5_lam_im, s5_log_step, s5_b_re, s5_b_im, s5_c_re, s5_c_im))
    yf = np.concatenate([q["yT"][0] for q in r], axis=0)
    yb = np.concatenate([q["yT"][1][:, ::-1] for q in r], axis=0)
    common = dict(w_glu=arrange_w(s5_w_glu[0]), b_glu=col_tiles(s5_b_glu[0]), s5d=col_tiles(s5_d[0]),
                  nw0=col_tiles(norm_w[3, 0]), w_in=arrange_w(ssd_w_in[1]),
                  **_ffn_inputs(2, norm_w, ffn_w_gate_up, ffn_w_down))
    r = _run(build_dense("s5", True, "proj"),
             [dict(hT=_tok(hT, c), mf=_tok(yf, c), mb=_tok(yb, c), hnT=_tok(hn, c), **common) for c in range(NCORES)])
    hT = _gather_tok(r, "hT_out")
    P = _gather_tok(r, "projT")

    r = _run(nc_ssd, _ssd_maps(P, 1, ssd_conv_w, ssd_conv_b, ssd_dt_bias, ssd_a_log, ssd_d))
    yf = np.concatenate([q["yT"][0] for q in r], axis=0)
    yb = np.concatenate([q["yT"][1][:, ::-1] for q in r], axis=0)
    common = dict(w_out=arrange_w(ssd_w_out[1]), mnw=col_tiles(ssd_norm_w[1]),
                  **_ffn_inputs(3, norm_w, ffn_w_gate_up, ffn_w_down))
    r = _run(build_dense("ssd", True, None),
             [dict(hT=_tok(hT, c), mf=_tok(yf, c), mb=_tok(yb, c), zT=_tok(P[0:2048], c), **common) for c in range(NCORES)])
    hT = _gather_tok(r, "hT_out")
    return np.ascontiguousarray(hT.T[None].astype(np.float32))
```

```python
import math
import contextlib


import numpy as np
import concourse.bass as bass
import concourse.mybir as mybir
from concourse.bass_utils import run_bass_kernel_spmd

F32 = mybir.dt.float32
BF16 = mybir.dt.bfloat16
I32 = mybir.dt.int32
AF = mybir.ActivationFunctionType
ALU = mybir.AluOpType
AX = mybir.AxisListType


def _is_psum(k):
    n = k[0] if isinstance(k, tuple) else k
    return isinstance(n, str) and n.startswith("ps_")


_STAGE = {"es": None, "pfx": ""}


def _SB(nc, name, shape, dt=None):
    if dt is None:
        dt = F32
    if _STAGE["es"] is None:
        return nc.alloc_sbuf_tensor(name, shape, dt)
    return _STAGE["es"].enter_context(nc.sbuf_tensor(_STAGE["pfx"] + name, shape, dt))


def _PS(nc, name, shape, dt=None):
    if dt is None:
        dt = F32
    if _STAGE["es"] is None:
        return nc.alloc_psum_tensor(name, shape, dt)
    return _STAGE["es"].enter_context(nc.psum_tensor(_STAGE["pfx"] + name, shape, dt))


class Prog:
    NDMA = 4

    def __init__(self, nc):
        self.nc = nc
        self.eng = {"pe": nc.tensor, "dve": nc.vector, "act": nc.scalar,
                    "pool": nc.gpsimd, "sp": nc.sync}
        self.sem = {}
        self.cnt = {}
        for e in ("pe", "dve", "act", "pool"):
            self.sem[e] = nc.alloc_semaphore(name="s_" + e)
            self.cnt[e] = 0
        self.dq = {}
        for q in ("sp", "act", "pool"):
            sems = []
            for i in range(self.NDMA):
                k = "d_%s%d" % (q, i)
                self.sem[k] = nc.alloc_semaphore(name=k)
                self.cnt[k] = 0
                sems.append(k)
            self.dq[q] = [sems, 0]
        self.seen = {e: {} for e in self.eng}
        self.last_w = {}
        self.readers = {}
        self.out_tokens = []

    def _wait(self, e, needs):
        eng = self.eng[e]
        for sk, val in needs.items():
            if e == "pe" and sk == "pe":
                continue
            if self.seen[e].get(sk, 0) >= val:
                continue
            eng.wait_ge(self.sem[sk], val)
            self.seen[e][sk] = val

    def _needs(self, reads, writes, e=None):
        needs = {}

        def add(tok):
            if tok is None:
                return
            sk, v = tok
            if needs.get(sk, 0) < v:
                needs[sk] = v
        for k in reads:
            add(self.last_w.get(k))
            if _is_psum(k):
                for t in self.readers.get(k, ()):
                    if t[0] != e:
                        add(t)
        for k in writes:
            add(self.last_w.get(k))
            for t in self.readers.get(k, ()):
                add(t)
        return needs

    def _commit(self, tok, reads, writes):
        for k in writes:
            self.last_w[k] = tok
            self.readers[k] = []
        for k in reads:
            if k in writes:
                continue
            self.readers.setdefault(k, []).append(tok)
            if len(self.readers[k]) > 12:
                best = {}
                for sk, v in self.readers[k]:
                    if best.get(sk, 0) < v:
                        best[sk] = v
                self.readers[k] = list(best.items())

    def op(self, e, fn, reads=(), writes=()):
        self._wait(e, self._needs(reads, writes, e))
        ins = fn()
        self.cnt[e] += 1
        ins.then_inc(self.sem[e], 1)
        tok = (e, self.cnt[e])
        self._commit(tok, reads, writes)
        return tok

    def dma(self, q, out, in_, reads=(), writes=(), is_output=False, **kw):
        sems, n = self.dq[q]
        sk = sems[n % self.NDMA]
        self.dq[q][1] = n + 1
        needs = self._needs(reads, writes)
        if self.cnt[sk] > 0:
            needs[sk] = max(needs.get(sk, 0), self.cnt[sk])
        self._wait(q, needs)
        ins = self.eng[q].dma_start(out=out, in_=in_, **kw)
        self.cnt[sk] += 16
        ins.then_inc(self.sem[sk], 16)
        tok = (sk, self.cnt[sk])
        self._commit(tok, reads, writes)
        if is_output:
            self.out_tokens.append(tok)
        return tok

    def barrier(self):
        needs = {k: v for k, v in self.cnt.items() if v > 0}
        for e in self.eng:
            self._wait(e, dict(needs))

    def allgather(self, in_ap, out_ap):
        if "cc" not in self.sem:
            self.sem["cc"] = self.nc.alloc_semaphore(name="cc_sem")
            self.cnt["cc"] = 0
        ins = self.nc.gpsimd.collective_compute("AllGather", ALU.bypass, replica_groups=[list(range(8))],
                                                ins=[in_ap.opt()], outs=[out_ap.opt()])
        self.cnt["cc"] += 1
        ins.then_inc(self.sem["cc"], 1)

    def finish(self):
        needs = {}
        for sk, v in self.out_tokens:
            needs[sk] = max(needs.get(sk, 0), v)
        self._wait("sp", needs)
        needs = {e: self.cnt[e] for e in ("pe", "dve", "act", "pool") if self.cnt[e] > 0}
        for q in self.dq:
            for sk in self.dq[q][0]:
                if self.cnt[sk] > 0:
                    needs[sk] = self.cnt[sk]
        self._wait("sp", needs)


class InstView:
    def __init__(self, p, inst, shared):
        self.p, self.inst, self.shared = p, inst, shared

    def k(self, key):
        n = key[0] if isinstance(key, tuple) else key
        if n in self.shared or (isinstance(n, str) and n.startswith("ps_")):
            return key
        return ("I%d" % self.inst, key)

    def op(self, e, fn, reads=(), writes=()):
        r = self.p.op(e, fn, [self.k(x) for x in reads], [self.k(x) for x in writes])
        self.p.baton.step(self.inst)
        return r

    def dma(self, q, out, in_, reads=(), writes=(), **kw):
        r = self.p.dma(q, out, in_, [self.k(x) for x in reads], [self.k(x) for x in writes], **kw)
        self.p.baton.step(self.inst)
        return r


class Baton:
    def __init__(self, n):
        import threading
        self.n = n
        self.sems = [threading.Semaphore(0) for _ in range(n)]
        self.alive = [True] * n
        self.err = None

    def _next(self, i):
        for d in range(1, self.n + 1):
            j = (i + d) % self.n
            if self.alive[j]:
                return j
        return None

    def step(self, i):
        j = self._next(i)
        if j is None or j == i:
            return
        self.sems[j].release()
        self.sems[i].acquire()

    def done(self, i):
        self.alive[i] = False
        j = self._next(i)
        if j is not None:
            self.sems[j].release()


def run_interleaved(p, bodies, shared):
    import threading
    n = len(bodies)
    p.baton = Baton(n)
    errs = []

    def runner(i):
        p.baton.sems[i].acquire()
        try:
            bodies[i](InstView(p, i, shared))
        except BaseException as ex:
            errs.append(ex)
        finally:
            p.baton.done(i)
    ths = [threading.Thread(target=runner, args=(i,)) for i in range(n)]
    for t in ths:
        t.start()
    p.baton.sems[0].release()
    for t in ths:
        t.join()
    if errs:
        raise errs[0]


EPS = 1e-6
TG = 512
NTG = 4
D = 1024
DT = 8
DFF = 2816
FT = 22
GELU_C = 0.7978845608028654


def build_dense(variant, has_ffn, next_kind, nc=None, p=None, pfx="", io=None):
    standalone = nc is None
    if standalone:
        nc = bass.Bass("TRN2", target_bir_lowering=False)
        p = Prog(nc)
    io = io or {}
    NTOK = TG * NTG

    def din(name, shape):
        if name in io:
            return None
        return nc.dram_tensor(pfx + name, shape, F32, kind="ExternalInput").ap()

    def dout(name, shape):
        if name in io:
            return None
        return nc.dram_tensor(pfx + name, shape, F32, kind="ExternalOutput").ap()

    def rd(name, ten, r, tok):
        if name in io:
            return io[name](r, tok)
        return ten[r * 128:(r + 1) * 128, tok]

    hT_in = din("hT", [D, NTOK])
    if variant in ("ssd", "gdn"):
        mf = din("mf", [2048, NTOK])
        mb = din("mb", [2048, NTOK])
        zT = din("zT", [2048, NTOK])
        w_out = din("w_out", [DT, 128, 16 * 128])
        mnw = din("mnw", [128, 16])
    if variant == "s5":
        mf = din("mf", [D, NTOK])
        mb = din("mb", [D, NTOK])
        hn_in = din("hnT", [D, NTOK])
        w_glu = din("w_glu", [16, 128, DT * 128])
        b_glu = din("b_glu", [128, 16])
        s5d = din("s5d", [128, DT])
    if has_ffn:
        nws = din("nws", [128, 3 * DT])
        wgu = din("wgu", [FT, 128, 2 * DT * 128])
        wd = din("wd", [DT, 128, FT * 128])
        hT_out = dout("hT_out", [D, NTOK])
    if next_kind is not None:
        nw0 = din("nw0", [128, DT])
    if next_kind == "proj":
        w_in = din("w_in", [49, 128, DT * 128])
        projT = dout("projT", [49 * 128, NTOK])
    if next_kind == "hn":
        hn_out = dout("hn_out", [D, NTOK])

    sb = lambda name, shape, dt=F32: _SB(nc, name, shape, dt)
    hT = sb("hT_sb", [128, DT, TG])
    mT = sb("mT_sb", [128, DT, TG])
    xnT = sb("xnT", [128, DT, TG], BF16)
    lhsA = sb("lhsA", [128, 16, TG], BF16)
    actT = sb("actT", [128, FT, TG], BF16)
    NW = 3
    wbuf = [sb("wbuf%d" % i, [128, FT * 128], BF16) for i in range(NW)]
    stg = [[sb("stg%d_%d" % (i, s), [128, TG]) for s in range(2)] for i in range(3)]
    tmpA = [sb("tmpA%d" % i, [128, TG]) for i in range(2)]
    tmpB = [sb("tmpB%d" % i, [128, TG]) for i in range(2)]
    yzb = [sb("yzb%d" % i, [128, TG]) for i in range(4)]
    sqb = [sb("sqb%d" % i, [128, TG], BF16) for i in range(2)]
    rstd = sb("rstd", [128, TG])
    ostg = [sb("ostg%d" % i, [128, TG]) for i in range(2)]
    ones = sb("ones", [128, 128], BF16)
    cols = sb("cols", [128, 64])
    ps_ss = _PS(nc, "ps_ss", [128, TG], F32)
    ps_acc = [_PS(nc, "ps_acc%d" % i, [128, TG], F32) for i in range(3)]
    ps_g = [_PS(nc, "ps_g%d" % i, [128, TG], F32) for i in range(2)]
    ps_u = [_PS(nc, "ps_u%d" % i, [128, TG], F32) for i in range(2)]

    p.op("pool", lambda: nc.gpsimd.memset(ones[:], 1.0), writes=["ones"])
    if has_ffn:
        p.dma("sp", cols[:, 0:24], nws, writes=["cols"])
    if next_kind is not None:
        p.dma("sp", cols[:, 24:32], nw0, writes=["cols"])
    if variant in ("ssd", "gdn"):
        p.dma("sp", cols[:, 32:48], mnw, writes=["cols"])
    if variant == "s5":
        p.dma("sp", cols[:, 32:48], b_glu, writes=["cols"])
        p.dma("sp", cols[:, 48:56], s5d, writes=["cols"])

    cnt = {"w": 0, "acc": 0, "gu": 0, "o": 0, "ev": 0}

    def load_w(src, n):
        i = cnt["w"] % NW
        cnt["w"] += 1
        p.dma("pool", wbuf[i][:, 0:n], src, writes=[("w", i)], max_dma_last_dim=4096)
        return wbuf[i], ("w", i)

    def rms_rstd(tiles, n_feat):
        last = len(tiles) - 1
        for i, (ap, key) in enumerate(tiles):
            s = i % 2
            p.op("act", lambda ap=ap, s=s: nc.scalar.activation(out=sqb[s][:], in_=ap, func=AF.Square),
                 reads=[key], writes=[("sq", s)])
            p.op("pe", lambda s=s, i=i: nc.tensor.matmul(ps_ss[:], lhsT=ones[:], rhs=sqb[s][:],
                                                       start=(i == 0), stop=(i == last)),
                 reads=[("sq", s), "ones"], writes=["ps_ss"])
        p.op("act", lambda: nc.scalar.activation(out=rstd[:], in_=ps_ss[:], func=AF.Ln,
                                                 scale=1.0 / n_feat, bias=EPS),
             reads=["ps_ss"], writes=["rstd"])
        p.op("act", lambda: nc.scalar.activation(out=rstd[:], in_=rstd[:], func=AF.Exp, scale=-0.5),
             reads=["rstd"], writes=["rstd"])

    def evac(dst, dkey, src, skey):
        cnt["ev"] += 1
        if cnt["ev"] % 2:
            p.op("act", lambda: nc.scalar.copy(out=dst, in_=src), reads=[skey], writes=[dkey])
        else:
            p.op("dve", lambda: nc.vector.tensor_copy(out=dst, in_=src), reads=[skey], writes=[dkey])

    def proj_fm(wsrc, nk, rhs_fn, rhs_keys, consume):
        wt, wkey = load_w(wsrc, nk * 128)
        a = cnt["acc"] % 3
        cnt["acc"] += 1
        for k in range(nk):
            p.op("pe", lambda k=k: nc.tensor.matmul(ps_acc[a][:], lhsT=wt[:, k * 128:(k + 1) * 128],
                                                    rhs=rhs_fn(k), start=(k == 0), stop=(k == nk - 1)),
                 reads=[wkey] + rhs_keys, writes=[("acc", a)])
        consume(ps_acc[a][:], ("acc", a))

    def add_norm_into_h(nwoff):
        rms_rstd([(mT[:, j, :], ("mT", j)) for j in range(DT)], D)
        for j in range(DT):
            s = j % 2
            p.op("dve", lambda j=j, s=s: nc.vector.scalar_tensor_tensor(
                out=tmpA[s][:], in0=mT[:, j, :], scalar=cols[:, nwoff + j:nwoff + j + 1], in1=rstd[:],
                op0=ALU.mult, op1=ALU.mult), reads=[("mT", j), "cols", "rstd"], writes=[("tmpA", s)])
            p.op("pool", lambda j=j, s=s: nc.gpsimd.tensor_tensor(out=hT[:, j, :], in0=hT[:, j, :], in1=tmpA[s][:],
                                                                  op=ALU.add),
                 reads=[("hT", j), ("tmpA", s)], writes=[("hT", j)])

    def norm_h_to(dst_fn, dkey_fn, nwoff):
        rms_rstd([(hT[:, j, :], ("hT", j)) for j in range(DT)], D)
        for j in range(DT):
            p.op("dve", lambda j=j: nc.vector.scalar_tensor_tensor(
                out=dst_fn(j), in0=hT[:, j, :], scalar=cols[:, nwoff + j:nwoff + j + 1], in1=rstd[:],
                op0=ALU.mult, op1=ALU.mult), reads=[("hT", j), "cols", "rstd"], writes=[dkey_fn(j)])

    for tg in range(NTG):
        tok = slice(tg * TG, (tg + 1) * TG)
        for j in range(DT):
            p.dma("sp", hT[:, j, :], rd("hT", hT_in, j, tok), writes=[("hT", j)])

        if variant in ("ssd", "gdn"):
            gsz = 2 if variant == "ssd" else 1
            for G in range(16 // gsz):
                tl = []
                for t in range(gsz):
                    ft = G * gsz + t
                    s = ft % 2
                    rows = slice(ft * 128, (ft + 1) * 128)
                    p.dma("sp", stg[0][s][:], rd("mf", mf, ft, tok), writes=[("stg0", s)])
                    p.dma("act", stg[1][s][:], rd("mb", mb, ft, tok), writes=[("stg1", s)])
                    p.dma("sp", stg[2][s][:], rd("zT", zT, ft, tok), writes=[("stg2", s)])
                    yb = yzb[ft % 4]
                    ykey = ("yz", ft % 4)
                    p.op("pool", lambda s=s, yb=yb: nc.gpsimd.tensor_tensor(out=yb[:], in0=stg[0][s][:], in1=stg[1][s][:],
                                                                            op=ALU.add),
                         reads=[("stg0", s), ("stg1", s)], writes=[ykey])
                    p.op("act", lambda s=s: nc.scalar.activation(out=tmpB[s][:], in_=stg[2][s][:], func=AF.Silu),
                         reads=[("stg2", s)], writes=[("tmpB", s)])
                    if variant == "ssd":
                        p.op("dve", lambda s=s, yb=yb: nc.vector.tensor_tensor(out=yb[:], in0=yb[:], in1=tmpB[s][:],
                                                                               op=ALU.mult),
                             reads=[ykey, ("tmpB", s)], writes=[ykey])
                    tl.append((yb, ykey, ft, s))
                rms_rstd([(yb[:], ykey) for (yb, ykey, ft, s) in tl], 128 * gsz)
                for (yb, ykey, ft, s) in tl:
                    if variant == "ssd":
                        p.op("dve", lambda yb=yb, ft=ft: nc.vector.scalar_tensor_tensor(
                            out=lhsA[:, ft, :], in0=yb[:], scalar=cols[:, 32 + ft:33 + ft], in1=rstd[:],
                            op0=ALU.mult, op1=ALU.mult), reads=[ykey, "cols", "rstd"], writes=[("lhsA", ft)])
                    else:
                        p.op("dve", lambda yb=yb: nc.vector.scalar_tensor_tensor(
                            out=yb[:], in0=yb[:], scalar=cols[:, 32:33], in1=rstd[:],
                            op0=ALU.mult, op1=ALU.mult), reads=[ykey, "cols", "rstd"], writes=[ykey])
                        p.op("dve", lambda yb=yb, ft=ft, s=s: nc.vector.tensor_tensor(
                            out=lhsA[:, ft, :], in0=yb[:], in1=tmpB[s][:], op=ALU.mult),
                            reads=[ykey, ("tmpB", s)], writes=[("lhsA", ft)])
            for j in range(DT):
                proj_fm(w_out[j], 16, lambda k: lhsA[:, k, :], [("lhsA", k) for k in range(16)],
                        lambda ps, pk, j=j: evac(mT[:, j, :], ("mT", j), ps, pk))
        if variant == "s5":
            for ft in range(DT):
                s = ft % 2
                rows = slice(ft * 128, (ft + 1) * 128)
                p.dma("sp", stg[0][s][:], rd("mf", mf, ft, tok), writes=[("stg0", s)])
                p.dma("act", stg[1][s][:], rd("mb", mb, ft, tok), writes=[("stg1", s)])
                p.dma("sp", stg[2][s][:], rd("hnT", hn_in, ft, tok), writes=[("stg2", s)])
                yb = yzb[ft % 4]
                ykey = ("yz", ft % 4)
                p.op("pool", lambda s=s, yb=yb: nc.gpsimd.tensor_tensor(out=yb[:], in0=stg[0][s][:], in1=stg[1][s][:],
                                                                        op=ALU.add),
                     reads=[("stg0", s), ("stg1", s)], writes=[ykey])
                p.op("dve", lambda s=s, yb=yb, ft=ft: nc.vector.scalar_tensor_tensor(
                    out=yb[:], in0=stg[2][s][:], scalar=cols[:, 48 + ft:49 + ft], in1=yb[:],
                    op0=ALU.mult, op1=ALU.add), reads=[("stg2", s), "cols", ykey], writes=[ykey])
                p.op("act", lambda s=s, yb=yb: nc.scalar.activation(out=tmpB[s][:], in_=yb[:], func=AF.Square),
                     reads=[ykey], writes=[("tmpB", s)])
                p.op("dve", lambda s=s: nc.vector.tensor_scalar(out=tmpB[s][:], in0=tmpB[s][:], scalar1=0.044715,
                                                                scalar2=1.0, op0=ALU.mult, op1=ALU.add),
                     reads=[("tmpB", s)], writes=[("tmpB", s)])
                p.op("dve", lambda s=s, yb=yb: nc.vector.tensor_tensor(out=tmpB[s][:], in0=tmpB[s][:], in1=yb[:],
                                                                       op=ALU.mult),
                     reads=[("tmpB", s), ykey], writes=[("tmpB", s)])
                p.op("act", lambda s=s: nc.scalar.activation(out=tmpB[s][:], in_=tmpB[s][:], func=AF.Sigmoid,
                                                             scale=2.0 * GELU_C),
                     reads=[("tmpB", s)], writes=[("tmpB", s)])
                p.op("dve", lambda s=s, yb=yb, ft=ft: nc.vector.tensor_tensor(out=lhsA[:, ft, :], in0=yb[:],
                                                                              in1=tmpB[s][:], op=ALU.mult),
                     reads=[ykey, ("tmpB", s)], writes=[("lhsA", ft)])
            for j in range(DT):
                def cons_gate(ps, pk, j=j):
                    s = j % 2
                    p.op("act", lambda: nc.scalar.activation(out=tmpA[s][:], in_=ps, func=AF.Sigmoid,
                                                             bias=cols[:, 40 + j:41 + j]),
                         reads=[pk, "cols"], writes=[("tmpA", s)])

                def cons_val(ps, pk, j=j):
                    s = j % 2
                    p.op("dve", lambda: nc.vector.scalar_tensor_tensor(
                        out=mT[:, j, :], in0=ps, scalar=cols[:, 32 + j:33 + j], in1=tmpA[s][:],
                        op0=ALU.add, op1=ALU.mult), reads=[pk, "cols", ("tmpA", s)], writes=[("mT", j)])
                lk = [("lhsA", k) for k in range(DT)]
                proj_fm(w_glu[8 + j], DT, lambda k: lhsA[:, k, :], lk, cons_gate)
                proj_fm(w_glu[j], DT, lambda k: lhsA[:, k, :], lk, cons_val)

        if has_ffn:
            add_norm_into_h(0)
            norm_h_to(lambda j: xnT[:, j, :], lambda j: ("xnT", j), 8)
            xk = [("xnT", k) for k in range(DT)]
            for f in range(FT):
                wt, wkey = load_w(wgu[f], 2 * DT * 128)
                g = cnt["gu"] % 2
                cnt["gu"] += 1
                for half, pst, nm in ((0, ps_g, "g"), (1, ps_u, "u")):
                    for k in range(DT):
                        p.op("pe", lambda k=k, half=half, pst=pst: nc.tensor.matmul(
                            pst[g][:], lhsT=wt[:, (half * DT + k) * 128:(half * DT + k + 1) * 128],
                            rhs=xnT[:, k, :], start=(k == 0), stop=(k == DT - 1)),
                            reads=[wkey] + xk, writes=[(nm, g)])
                p.op("act", lambda g=g: nc.scalar.activation(out=tmpB[g][:], in_=ps_g[g][:], func=AF.Silu),
                     reads=[("g", g)], writes=[("tmpB", g)])
                p.op("dve", lambda g=g, f=f: nc.vector.tensor_tensor(out=actT[:, f, :], in0=tmpB[g][:],
                                                                     in1=ps_u[g][:], op=ALU.mult),
                     reads=[("tmpB", g), ("u", g)], writes=[("actT", f)])
            ak = [("actT", k) for k in range(FT)]
            for j in range(DT):
                proj_fm(wd[j], FT, lambda k: actT[:, k, :], ak,
                        lambda ps, pk, j=j: evac(mT[:, j, :], ("mT", j), ps, pk))
            add_norm_into_h(16)
            for j in range(DT):
                p.dma("sp", rd("hT_out", hT_out, j, tok), hT[:, j, :], reads=[("hT", j)], is_output=True)

        if next_kind == "proj":
            norm_h_to(lambda j: xnT[:, j, :], lambda j: ("xnT", j), 24)
            xk = [("xnT", k) for k in range(DT)]
            for j in range(49):
                def cons(ps, pk, j=j):
                    o = cnt["o"] % 2
                    cnt["o"] += 1
                    evac(ostg[o][:], ("ostg", o), ps, pk)
                    p.dma("sp" if j % 2 else "act", rd("projT", projT, j, tok), ostg[o][:],
                          reads=[("ostg", o)], is_output=True)
                proj_fm(w_in[j], DT, lambda k: xnT[:, k, :], xk, cons)
        if next_kind == "hn":
            for j in range(DT):
                pass
            norm_h_to(lambda j: mT[:, j, :], lambda j: ("mT", j), 24)
            for j in range(DT):
                p.dma("sp", rd("hn_out", hn_out, j, tok), mT[:, j, :], reads=[("mT", j)], is_output=True)
    if standalone:
        p.finish()
    return nc


def arrange_w(w, n_out_tiles=None):
    K, N = w.shape
    nk = K // 128
    nj = (N + 127) // 128
    if N % 128:
        w = np.concatenate([w, np.zeros((K, nj * 128 - N), w.dtype)], axis=1)
    r = w.reshape(nk, 128, nj, 128).transpose(2, 1, 0, 3).reshape(nj, 128, nk * 128)
    return np.ascontiguousarray(r)


def col_tiles(v):
    return np.ascontiguousarray(v.reshape(-1, 128).T)


T_SEQ = 16384
TB = 512


def make_consts(nc, p):
    io = _SB(nc, "c_iota", [128, 128], F32)
    ident = _SB(nc, "c_ident", [128, 128], F32)
    triu = _SB(nc, "c_triu", [128, 128], F32)
    p.op("pool", lambda: nc.gpsimd.iota(io[:], pattern=[[1, 128]], base=0, channel_multiplier=-1,
                                        allow_small_or_imprecise_dtypes=True), writes=["c_iota"])
    p.op("dve", lambda: nc.vector.tensor_single_scalar(out=ident[:], in_=io[:], scalar=0.0, op=ALU.is_equal),
         reads=["c_iota"], writes=["c_ident"])
    p.op("dve", lambda: nc.vector.tensor_single_scalar(out=triu[:], in_=io[:], scalar=0.0, op=ALU.is_ge),
         reads=["c_iota"], writes=["c_triu"])
    return dict(iota=io, ident=ident, triu=triu)


def conv_silu_block(nc, p, raw, rawkey_fn, cw, cb, cT, ckey_fn, ntiles, tmp, tmpkey):
    for t in range(ntiles):
        p.op("act", lambda t=t: nc.scalar.activation(out=tmp[:], in_=raw[:, t, 0:TB], func=AF.Identity,
                                                     scale=cw[:, t, 0:1]),
             reads=[rawkey_fn(t), "cw"], writes=[tmpkey])
        for k in range(1, 5):
            p.op("dve", lambda t=t, k=k: nc.vector.scalar_tensor_tensor(
                out=tmp[:], in0=raw[:, t, k:k + TB], scalar=cw[:, t, k:k + 1], in1=tmp[:],
                op0=ALU.mult, op1=ALU.add), reads=[rawkey_fn(t), "cw", tmpkey], writes=[tmpkey])
        p.op("act", lambda t=t: nc.scalar.activation(out=cT[:, t, :], in_=tmp[:], func=AF.Silu, bias=cb[:, t:t + 1]),
             reads=[tmpkey, "cb"], writes=[ckey_fn(t)])


def build_ssd(n_blocks=T_SEQ // TB, nc=None, p=None, pfx="", io=None):
    standalone = nc is None
    if standalone:
        nc = bass.Bass("TRN2", target_bir_lowering=False)
        p = Prog(nc)
    T = n_blocks * TB
    din = lambda name, shape: nc.dram_tensor(pfx + name, shape, F32, kind="ExternalInput").ap()
    if io is None:
        xbc = din("xbc", [2, 512, T + 4])
        dtr = din("dtr", [2, 4, T])
    cwd = din("cw", [2, 128, 20])
    cbd = din("cb", [128, 4])
    dtbd = din("dtb", [2, 4, 1])
    alogd = din("alog", [2, 4, 1])
    dskd = din("dsk", [128, 2])
    if io is None:
        yT = nc.dram_tensor("yT", [2, 256, T], F32, kind="ExternalOutput").ap()

    sb = lambda name, shape, dt=F32: _SB(nc, name, shape, dt)
    C = make_consts(nc, p)
    ident, triu = C["ident"], C["triu"]
    sel = sb("sel", [4, 4, 128])
    p.op("pool", lambda: nc.gpsimd.iota(sel[:], pattern=[[1, 4], [0, 128]], base=0, channel_multiplier=-1,
                                        allow_small_or_imprecise_dtypes=True), writes=["sel"])
    p.op("dve", lambda: nc.vector.tensor_single_scalar(out=sel[:], in_=sel[:], scalar=0.0, op=ALU.is_equal),
         reads=["sel"], writes=["sel"])
    ones4 = sb("ones4", [4, 128])
    p.op("pool", lambda: nc.gpsimd.memset(ones4[:], 1.0), writes=["ones4"])

    cb = sb("cb_sb", [128, 4])
    dsk = sb("dsk_sb", [128, 2])
    p.dma("sp", cb[:], cbd, writes=["cb"])
    p.dma("sp", dsk[:], dskd, writes=["dsk"])
    ps = lambda name: _PS(nc, name, [128, 512], F32)
    ps_tr = [ps("ps_tr0"), ps("ps_tr1")]
    ps_fb = [ps("ps_fb0"), ps("ps_fb1")]
    ps_C = [ps("ps_C0"), ps("ps_C1")]
    ps_D = [ps("ps_D0"), ps("ps_D1")]

    def body(p, inst):
        sb = lambda name, shape, dt=F32: _SB(nc, name + '_i%d' % inst, shape, dt)
        cw = sb("cw_sb", [128, 4, 5])
        dtb = sb("dtb_sb", [4, 1])
        acol = sb("acol", [4, 1])

        raw = [sb("raw%d" % i, [128, 4, TB + 4]) for i in range(1)] * 2
        cT = [sb("cT%d" % i, [128, 4, TB]) for i in range(1)] * 2
        ctmp = sb("ctmp", [128, TB])
        dtraw = sb("dtraw", [4, TB])
        dtT = [sb("dtT%d" % i, [4, TB]) for i in range(1)] * 2
        daT = sb("daT", [4, TB])
        csT = [sb("csT%d" % i, [4, TB]) for i in range(1)] * 2
        yTo = [sb("yTo%d" % i, [128, 2, TB]) for i in range(1)] * 2
        S = sb("S", [128, 256])
        S_bf = sb("S_bf", [128, 256], BF16)

        def two(name, shape, dt=F32):
            return [sb(name, shape, dt)] * 2
        x_tok = two("x_tok", [128, 256])
        B_tok = two("B_tok", [128, 128], BF16)
        sm = two("sm", [128, 8])
        CBm = two("CBm", [128, 128])
        Dm = two("Dm", [128, 4, 128])
        G = two("G", [128, 4, 128], BF16)
        eFb = two("eFb", [128, 4, 128])
        CsT = two("CsT", [128, 4, 128], BF16)
        w4 = two("w4", [128, 8])
        xdt = two("xdt", [128, 4, 64], BF16)
        xdtd = two("xdtd", [128, 4, 64], BF16)
        y_tok = two("y_tok", [128, 256])

        p.dma("sp", cw[:].rearrange("p t k -> p (t k)"), cwd[inst], writes=["cw"])
        p.dma("sp", dtb[:], dtbd[inst], writes=["dtb"])
        p.dma("sp", acol[:], alogd[inst], writes=["acol"])
        p.op("act", lambda: nc.scalar.activation(out=acol[:], in_=acol[:], func=AF.Exp), reads=["acol"], writes=["acol"])
        p.op("dve", lambda: nc.vector.tensor_scalar(out=acol[:], in0=acol[:], scalar1=-1.0, scalar2=None, op0=ALU.mult),
             reads=["acol"], writes=["acol"])
        p.op("dve", lambda: nc.vector.memset(S[:], 0.0), writes=["S"])
        p.op("dve", lambda: nc.vector.memset(S_bf[:], 0.0), writes=["S_bf"])
        for b in range(n_blocks):
            bp = 0
            col0 = b * TB
            for t in range(4):
                if io is None:
                    p.dma("sp" if t % 2 else "act", raw[bp][:, t, :],
                          xbc[inst, t * 128:(t + 1) * 128, col0:col0 + TB + 4], writes=[("raw", bp, t)])
                else:
                    io.load(p, inst, b, "raw%d" % t, raw[bp][:, t, :], ("raw", bp, t))
            if io is None:
                p.dma("sp", dtraw[:], dtr[inst, :, col0:col0 + TB], writes=["dtraw"])
            else:
                io.load(p, inst, b, "dt", dtraw[:], "dtraw")
            conv_silu_block(nc, p, raw[bp], lambda t: ("raw", bp, t), cw, cb, cT[bp], lambda t: ("cT", bp, t), 4, ctmp, "ctmp")
            p.op("act", lambda: nc.scalar.activation(out=daT[:], in_=dtraw[:], func=AF.Exp, bias=dtb[:, 0:1]),
                 reads=["dtraw", "dtb"], writes=["daT"])
            p.op("act", lambda: nc.scalar.activation(out=dtT[bp][:], in_=daT[:], func=AF.Ln, bias=1.0),
                 reads=["daT"], writes=[("dtT", bp)])
            p.op("dve", lambda: nc.vector.tensor_scalar(out=daT[:], in0=dtT[bp][:], scalar1=acol[:, 0:1], scalar2=None,
                                                        op0=ALU.mult),
                 reads=[("dtT", bp), "acol"], writes=["daT"])
            for j in range(4):
                cs = slice(j * 128, (j + 1) * 128)
                p.op("dve", lambda cs=cs: nc.vector.tensor_tensor_scan(
                    out=csT[bp][:, cs], data0=ones4[:], data1=daT[:, cs], initial=0.0, op0=ALU.mult, op1=ALU.add),
                    reads=["daT", "ones4"], writes=[("csT", bp)])
            for j in range(4):
                cs = slice(j * 128, (j + 1) * 128)
                c = b * 4 + j
                q = 0
                ctk = [("cT", bp, t) for t in range(4)]
                for t in range(3):
                    p.op("pe", lambda t=t: nc.tensor.transpose(out=ps_tr[inst][:, t * 128:(t + 1) * 128],
                                                               in_=cT[bp][:, t, cs], identity=ident[:]),
                         reads=[("cT", bp, t), "c_ident"], writes=[("ps_tr", inst)])
                p.op("pe", lambda: nc.tensor.transpose(out=ps_tr[inst][:, 384:388], in_=dtT[bp][:, cs],
                                                       identity=ident[0:4, 0:4]),
                     reads=[("dtT", bp), "c_ident"], writes=[("ps_tr", inst)])
                p.op("pe", lambda: nc.tensor.transpose(out=ps_tr[inst][:, 388:392], in_=csT[bp][:, cs],
                                                       identity=ident[0:4, 0:4]),
                     reads=[("csT", bp), "c_ident"], writes=[("ps_tr", inst)])
                p.op("act", lambda: nc.scalar.copy(out=x_tok[q][:], in_=ps_tr[inst][:, 0:256]),
                     reads=[("ps_tr", inst)], writes=[("x_tok", q)])
                p.op("dve", lambda: nc.vector.tensor_copy(out=B_tok[q][:], in_=ps_tr[inst][:, 256:384]),
                     reads=[("ps_tr", inst)], writes=[("B_tok", q)])
                p.op("dve", lambda: nc.vector.tensor_copy(out=sm[q][:], in_=ps_tr[inst][:, 384:392]),
                     reads=[("ps_tr", inst)], writes=[("sm", q)])
                for h in range(4):
                    p.op("pe", lambda h=h: nc.tensor.matmul(ps_fb[inst][:, h * 128:(h + 1) * 128], lhsT=sel[:, h, :],
                                                            rhs=csT[bp][:, cs], start=True, stop=True),
                         reads=["sel", ("csT", bp)], writes=[("ps_fb", inst)])
                p.op("pe", lambda: nc.tensor.matmul(ps_C[inst][:, 256:384], lhsT=cT[bp][:, 2, cs], rhs=cT[bp][:, 3, cs],
                                                    start=True, stop=True),
                     reads=[("cT", bp, 2), ("cT", bp, 3)], writes=[("ps_C", inst)])
                p.op("dve", lambda: nc.vector.tensor_tensor(out=CBm[q][:], in0=ps_C[inst][:, 256:384], in1=triu[:], op=ALU.mult),
                     reads=[("ps_C", inst), "c_triu"], writes=[("CBm", q)])
                fb3 = ps_fb[inst][:].rearrange("p (h l) -> p h l", h=4)
                Ftok = sm[q][:, 4:8]
                p.op("dve", lambda: nc.vector.tensor_tensor(out=Dm[q][:], in0=fb3,
                                                            in1=Ftok.unsqueeze(2).broadcast_to([128, 4, 128]),
                                                            op=ALU.subtract),
                     reads=[("ps_fb", inst), ("sm", q)], writes=[("Dm", q)])
                p.op("dve", lambda: nc.vector.tensor_scalar(out=Dm[q][:], in0=Dm[q][:], scalar1=0.0, scalar2=None, op0=ALU.min),
                     reads=[("Dm", q)], writes=[("Dm", q)])
                p.op("act", lambda: nc.scalar.activation(out=Dm[q][:], in_=Dm[q][:], func=AF.Exp),
                     reads=[("Dm", q)], writes=[("Dm", q)])
                p.op("dve", lambda: nc.vector.scalar_tensor_tensor(
                    out=G[q][:], in0=Dm[q][:], scalar=1.0, in1=CBm[q][:].unsqueeze(1).broadcast_to([128, 4, 128]),
                    op0=ALU.min, op1=ALU.mult), reads=[("Dm", q), ("CBm", q)], writes=[("G", q)])
                p.op("act", lambda: nc.scalar.activation(out=eFb[q][:], in_=fb3, func=AF.Exp),
                     reads=[("ps_fb", inst)], writes=[("eFb", q)])
                p.op("pool", lambda: nc.gpsimd.tensor_tensor(
                    out=CsT[q][:], in0=eFb[q][:], in1=cT[bp][:, 3, cs].unsqueeze(1).broadcast_to([128, 4, 128]),
                    op=ALU.mult), reads=[("eFb", q), ("cT", bp, 3)], writes=[("CsT", q)])
                p.op("dve", lambda: nc.vector.tensor_tensor(out=w4[q][:, 0:4], in0=fb3[:, :, 127], in1=Ftok,
                                                            op=ALU.subtract),
                     reads=[("ps_fb", inst), ("sm", q)], writes=[("w4", q)])
                p.op("act", lambda: nc.scalar.activation(out=w4[q][:, 0:4], in_=w4[q][:, 0:4], func=AF.Exp),
                     reads=[("w4", q)], writes=[("w4", q)])
                p.op("dve", lambda: nc.vector.tensor_tensor(out=w4[q][:, 4:8], in0=w4[q][:, 0:4], in1=sm[q][:, 0:4],
                                                            op=ALU.mult),
                     reads=[("w4", q), ("sm", q)], writes=[("w4", q)])
                x3 = x_tok[q][:].rearrange("p (h e) -> p h e", h=4)
                p.op("dve", lambda: nc.vector.tensor_tensor(out=xdt[q][:], in0=x3,
                                                            in1=sm[q][:, 0:4].unsqueeze(2).broadcast_to([128, 4, 64]),
                                                            op=ALU.mult),
                     reads=[("x_tok", q), ("sm", q)], writes=[("xdt", q)])
                p.op("pool", lambda: nc.gpsimd.tensor_tensor(out=xdtd[q][:], in0=x3,
                                                             in1=w4[q][:, 4:8].unsqueeze(2).broadcast_to([128, 4, 64]),
                                                             op=ALU.mult),
                     reads=[("x_tok", q), ("w4", q)], writes=[("xdtd", q)])
                for h in range(4):
                    hs = slice(h * 64, (h + 1) * 64)
                    p.op("pe", lambda h=h, hs=hs: nc.tensor.matmul(ps_C[inst][:, hs], lhsT=G[q][:, h, :], rhs=xdt[q][:, h, :],
                                                                  start=True, stop=False),
                         reads=[("G", q), ("xdt", q)], writes=[("ps_C", inst)])
                    p.op("pe", lambda h=h, hs=hs: nc.tensor.matmul(ps_C[inst][:, hs], lhsT=CsT[q][:, h, :], rhs=S_bf[:, hs],
                                                                  start=False, stop=True),
                         reads=[("CsT", q), "S_bf"], writes=[("ps_C", inst)])
                p.op("pe", lambda: nc.tensor.matmul(ps_D[inst][:, 0:256], lhsT=B_tok[q][:],
                                                    rhs=xdtd[q][:].rearrange("p h e -> p (h e)"), start=True, stop=True),
                     reads=[("B_tok", q), ("xdtd", q)], writes=[("ps_D", inst)])
                for h in range(4):
                    hs = slice(h * 64, (h + 1) * 64)
                    p.op("dve", lambda h=h, hs=hs: nc.vector.scalar_tensor_tensor(
                        out=S[:, hs], in0=S[:, hs], scalar=eFb[q][:, h, 127:128], in1=ps_D[inst][:, hs],
                        op0=ALU.mult, op1=ALU.add), reads=["S", ("eFb", q), ("ps_D", inst)], writes=["S"])
                p.op("act", lambda: nc.scalar.copy(out=S_bf[:], in_=S[:]), reads=["S"], writes=["S_bf"])
                p.op("act", lambda: nc.scalar.copy(out=y_tok[q][:], in_=ps_C[inst][:, 0:256]), reads=[("ps_C", inst)],
                     writes=[("y_tok", q)])
                for t in range(2):
                    p.op("pe", lambda t=t: nc.tensor.transpose(out=ps_D[inst][:, 256 + t * 128:256 + (t + 1) * 128],
                                                               in_=y_tok[q][:, t * 128:(t + 1) * 128], identity=ident[:]),
                         reads=[("y_tok", q), "c_ident"], writes=[("ps_D", inst)])
                for t in range(2):
                    if inst == 0:
                        p.op("dve", lambda t=t: nc.vector.scalar_tensor_tensor(
                            out=yTo[bp][:, t, cs], in0=cT[bp][:, t, cs], scalar=dsk[:, t:t + 1],
                            in1=ps_D[inst][:, 256 + t * 128:256 + (t + 1) * 128], op0=ALU.mult, op1=ALU.add),
                            reads=[("cT", bp, t), "dsk", ("ps_D", inst)], writes=[("yTo", bp)])
                    else:
                        p.op("dve", lambda t=t: nc.vector.tensor_copy(out=yTo[bp][:, t, cs],
                                                                      in_=ps_D[inst][:, 256 + t * 128:256 + (t + 1) * 128]),
                             reads=[("ps_D", inst)], writes=[("yTo", bp)])
            for t in range(2):
                if io is None:
                    p.dma("sp", yT[inst, t * 128:(t + 1) * 128, col0:col0 + TB], yTo[bp][:, t, :],
                          reads=[("yTo", bp)], is_output=True)
                else:
                    io.store(p, inst, b, "y%d" % t, yTo[bp][:, t, :], ("yTo", bp))
    SHARED = {"c_iota", "c_ident", "c_triu", "sel", "ones4", "cb", "dsk"}
    run_interleaved(p, [lambda v: body(v, 0), lambda v: body(v, 1)], SHARED)
    if standalone:
        p.finish()
    return nc


CH = 64
EPS = 1e-6


def build_gdn(n_blocks=T_SEQ // TB, nc=None, p=None, pfx="", io=None):
    standalone = nc is None
    if standalone:
        nc = bass.Bass("TRN2", target_bir_lowering=False)
        p = Prog(nc)
    T = n_blocks * TB
    din = lambda name, shape: nc.dram_tensor(pfx + name, shape, F32, kind="ExternalInput").ap()
    if io is None:
        qkv = din("qkv", [2, 512, T + 4])
        abr = din("abr", [2, 2, 2, T])
    cwd = din("cw", [2, 128, 20])
    cbd = din("cb", [128, 4])
    dtbd = din("dtb", [2, 2, 1])
    alogd = din("alog", [2, 2, 1])
    if io is None:
        oT = nc.dram_tensor("oT", [2, 256, T], F32, kind="ExternalOutput").ap()

    sb = lambda name, shape, dt=F32: _SB(nc, name, shape, dt)
    C = make_consts(nc, p)
    ident, iot = C["ident"], C["iota"]

    def blockdiag(B, name):
        nb = 128 // B
        E = sb("E" + name, [nb, 128])
        E2 = sb("E2" + name, [nb, 128])
        p.op("pool", lambda: nc.gpsimd.iota(E[:], pattern=[[1, 128]], base=0, channel_multiplier=-B,
                                            allow_small_or_imprecise_dtypes=True), writes=["E" + name])
        p.op("dve", lambda: nc.vector.tensor_single_scalar(out=E2[:], in_=E[:], scalar=float(B), op=ALU.is_lt),
             reads=["E" + name], writes=["E2" + name])
        p.op("dve", lambda: nc.vector.tensor_single_scalar(out=E[:], in_=E[:], scalar=0.0, op=ALU.is_ge),
             reads=["E" + name], writes=["E" + name])
        p.op("dve", lambda: nc.vector.tensor_tensor(out=E[:], in0=E[:], in1=E2[:], op=ALU.mult),
             reads=["E" + name, "E2" + name], writes=["E" + name])
        m = sb("bd" + name, [128, 128])
        pst = ps_n[0]
        p.op("pe", lambda: nc.tensor.matmul(pst[:, 0:128], lhsT=E[:], rhs=E[:], start=True, stop=True),
             reads=["E" + name], writes=[("ps_Q0", 0)])
        p.op("dve", lambda: nc.vector.tensor_copy(out=m[:], in_=pst[:, 0:128]), reads=[("ps_Q0", 0)], writes=["bd" + name])
        return m

    ps = lambda name: _PS(nc, name, [128, 512], F32)
    ps_Q = [[ps("ps_Q%d_%d" % (k, i)) for i in range(2)] for k in range(4)]
    ps_n = [ps_Q[0][0], ps_Q[1][0]]

    bd16 = blockdiag(16, "16")
    bd32 = blockdiag(32, "32")
    bd64 = blockdiag(64, "64")
    m32 = sb("m32", [128, 128])
    m64 = sb("m64", [128, 128])
    MUi = sb("MUi", [128, 128])
    MLs = sb("MLs", [128, 128])
    p.op("dve", lambda: nc.vector.tensor_tensor(out=m32[:], in0=bd32[:], in1=bd16[:], op=ALU.subtract),
         reads=["bd32", "bd16"], writes=["m32"])
    p.op("dve", lambda: nc.vector.tensor_tensor(out=m64[:], in0=bd64[:], in1=bd32[:], op=ALU.subtract),
         reads=["bd64", "bd32"], writes=["m64"])
    p.op("dve", lambda: nc.vector.tensor_single_scalar(out=MUi[:], in_=iot[:], scalar=0.0, op=ALU.is_ge),
         reads=["c_iota"], writes=["MUi"])
    p.op("dve", lambda: nc.vector.tensor_tensor(out=MUi[:], in0=MUi[:], in1=bd64[:], op=ALU.mult),
         reads=["MUi", "bd64"], writes=["MUi"])
    p.op("dve", lambda: nc.vector.tensor_single_scalar(out=MLs[:], in_=iot[:], scalar=0.0, op=ALU.is_lt),
         reads=["c_iota"], writes=["MLs"])
    p.op("dve", lambda: nc.vector.tensor_tensor(out=MLs[:], in0=MLs[:], in1=bd64[:], op=ALU.mult),
         reads=["MLs", "bd64"], writes=["MLs"])
    ones2 = sb("ones2", [2, 128])
    p.op("pool", lambda: nc.gpsimd.memset(ones2[:], 1.0), writes=["ones2"])
    onesb = sb("onesb", [128, 128], BF16)
    p.op("pool", lambda: nc.gpsimd.memset(onesb[:], 1.0), writes=["onesb"])

    cb = sb("cb_sb", [128, 4])
    p.dma("sp", cb[:], cbd, writes=["cb"])
    def body(p, inst):
        sb = lambda name, shape, dt=F32: _SB(nc, name + '_i%d' % inst, shape, dt)
        cw = sb("cw_sb", [128, 4, 5])
        dtb = sb("dtb_sb", [2, 1])
        acol = sb("acol", [2, 1])

        raw = [sb("raw%d" % i, [128, 4, TB + 4]) for i in range(1)] * 2
        cT = [sb("cT%d" % i, [128, 4, TB]) for i in range(1)] * 2
        ctmp = sb("ctmp", [128, TB])
        sqb = sb("sqb", [128, TB], BF16)
        rs = sb("rs", [128, TB])
        qT2 = [sb("qT2_%d" % i, [128, TB // CH, 2, CH]) for i in range(1)] * 2
        kT2 = [sb("kT2_%d" % i, [128, TB // CH, 2, CH]) for i in range(1)] * 2
        vT2 = [sb("vT2_%d" % i, [128, TB // CH, 2, CH]) for i in range(1)] * 2
        araw = sb("araw", [2, TB])
        braw = sb("braw", [2, TB])
        gT = sb("gT", [2, TB])
        gcsT = sb("gcsT", [2, TB])
        betaT = sb("betaT", [2, TB])
        glT = sb("glT", [2, TB])
        Mg = [sb("Mg%d" % i, [2, TB // CH, 2, CH]) for i in range(1)] * 2
        Mb = [sb("Mb%d" % i, [2, TB // CH, 2, CH]) for i in range(1)] * 2
        Ml = [sb("Ml%d" % i, [2, TB // CH, 2, CH]) for i in range(1)] * 2
        oTo = [sb("oTo%d" % i, [128, 2, TB]) for i in range(1)] * 2
        S = sb("S", [128, 256])
        S_bf = sb("S_bf", [128, 256], BF16)

        def two(name, shape, dt=F32):
            return [sb(name, shape, dt)] * 2
        colsb = two("colsb", [128, 8])
        egrow = two("egrow", [128, 128])
        D1 = two("D1", [128, 128])
        D2 = two("D2", [128, 128])
        qkT = two("qkT", [128, 128], BF16)
        Am = two("Am", [128, 128])
        ATm = two("ATm", [128, 128])
        X = two("X", [128, 128]); Y = two("Y", [128, 128])
        Ao32 = two("Ao32", [128, 128]); Ao32T = two("Ao32T", [128, 128]); Ao64 = two("Ao64", [128, 128])
        Tm = two("Tm", [128, 128]); Um = two("Um", [128, 128])
        X2 = two("X2", [128, 128]); Y2 = two("Y2", [128, 128])
        Rm = two("Rm", [128, 128]); Pm = two("Pm", [128, 128])
        RHSv = two("RHSv", [128, 128]); RHSw = two("RHSw", [128, 128]); kdec = two("kdec", [128, 2, 128], BF16)
        qgT = two("qgT", [128, 128], BF16)
        u_sb = two("u_sb", [128, 128]); wT_sb = two("wT_sb", [128, 128], BF16)
        vnew = two("vnew", [128, 128], BF16)

        ncnt = [0]

        def mmN(lhsT, rhs, rkeys):
            i = ncnt[0] % 2
            ncnt[0] += 1
            key = ("ps_Q%d" % i, inst)
            p.op("pe", lambda: nc.tensor.matmul(ps_Q[i][inst][:, 0:128], lhsT=lhsT, rhs=rhs, start=True, stop=True),
                 reads=rkeys, writes=[key])
            return ps_Q[i][inst][:, 0:128], key

        def ev_copy(dst, dkey, src, skey):
            p.op("act", lambda: nc.scalar.copy(out=dst, in_=src), reads=[skey], writes=[dkey])

        def ev_comb(dst, dkey, a, akey, src, skey, op):
            p.op("dve", lambda: nc.vector.tensor_tensor(out=dst, in0=a, in1=src, op=op), reads=[akey, skey], writes=[dkey])

        p.dma("sp", cw[:].rearrange("p t k -> p (t k)"), cwd[inst], writes=["cw"])
        p.dma("sp", dtb[:], dtbd[inst], writes=["dtb"])
        p.dma("sp", acol[:], alogd[inst], writes=["acol"])
        p.op("act", lambda: nc.scalar.activation(out=acol[:], in_=acol[:], func=AF.Exp), reads=["acol"], writes=["acol"])
        p.op("dve", lambda: nc.vector.tensor_scalar(out=acol[:], in0=acol[:], scalar1=-1.0, scalar2=None, op0=ALU.mult),
             reads=["acol"], writes=["acol"])
        p.op("dve", lambda: nc.vector.memset(S[:], 0.0), writes=["S"])
        p.op("dve", lambda: nc.vector.memset(S_bf[:], 0.0), writes=["S_bf"])
        for b in range(n_blocks):
            bp = 0
            col0 = b * TB
            for t in range(4):
                if io is None:
                    p.dma("sp" if t % 2 else "act", raw[bp][:, t, :],
                          qkv[inst, t * 128:(t + 1) * 128, col0:col0 + TB + 4], writes=[("raw", bp, t)])
                else:
                    io.load(p, inst, b, "raw%d" % t, raw[bp][:, t, :], ("raw", bp, t))
            if io is None:
                p.dma("sp", araw[:], abr[inst, 0, :, col0:col0 + TB], writes=["araw"])
                p.dma("sp", braw[:], abr[inst, 1, :, col0:col0 + TB], writes=["braw"])
            else:
                io.load(p, inst, b, "a", araw[:], "araw")
                io.load(p, inst, b, "b", braw[:], "braw")
            conv_silu_block(nc, p, raw[bp], lambda t: ("raw", bp, t), cw, cb, cT[bp], lambda t: ("cT", bp, t), 4, ctmp, "ctmp")
            for t, dst, dname, sc in ((0, qT2[bp], "qT2", 128.0 ** -0.5), (1, kT2[bp], "kT2", 1.0)):
                p.op("act", lambda t=t: nc.scalar.activation(out=sqb[:], in_=cT[bp][:, t, :], func=AF.Square),
                     reads=[("cT", bp, t)], writes=["sqb"])
                p.op("pe", lambda: nc.tensor.matmul(ps_Q[0][inst][:], lhsT=onesb[:], rhs=sqb[:], start=True, stop=True),
                     reads=["onesb", "sqb"], writes=[("ps_Q0", inst)])
                p.op("act", lambda: nc.scalar.activation(out=rs[:], in_=ps_Q[0][inst][:], func=AF.Ln, bias=EPS),
                     reads=[("ps_Q0", inst)], writes=["rs"])
                p.op("act", lambda: nc.scalar.activation(out=rs[:], in_=rs[:], func=AF.Exp, scale=-0.5),
                     reads=["rs"], writes=["rs"])
                for vh in range(2):
                    p.op("dve", lambda t=t, dst=dst, sc=sc, vh=vh: nc.vector.scalar_tensor_tensor(
                        out=dst[:, :, vh, :], in0=cT[bp][:, t, :].rearrange("p (c i) -> p c i", i=CH), scalar=sc,
                        in1=rs[:].rearrange("p (c i) -> p c i", i=CH), op0=ALU.mult, op1=ALU.mult),
                        reads=[("cT", bp, t), "rs"], writes=[(dname, bp)])
            for vh in range(2):
                p.op("pool", lambda vh=vh: nc.gpsimd.tensor_copy(
                    out=vT2[bp][:, :, vh, :], in_=cT[bp][:, 2 + vh, :].rearrange("p (c i) -> p c i", i=CH)),
                    reads=[("cT", bp, 2 + vh)], writes=[("vT2", bp)])
            p.op("act", lambda: nc.scalar.activation(out=gT[:], in_=araw[:], func=AF.Exp, bias=dtb[:, 0:1]),
                 reads=["araw", "dtb"], writes=["gT"])
            p.op("act", lambda: nc.scalar.activation(out=gT[:], in_=gT[:], func=AF.Ln, bias=1.0),
                 reads=["gT"], writes=["gT"])
            p.op("dve", lambda: nc.vector.tensor_scalar(out=gT[:], in0=gT[:], scalar1=acol[:, 0:1], scalar2=None,
                                                        op0=ALU.mult), reads=["gT", "acol"], writes=["gT"])
            p.op("act", lambda: nc.scalar.activation(out=betaT[:], in_=braw[:], func=AF.Sigmoid),
                 reads=["braw"], writes=["betaT"])
            for j in range(TB // CH):
                cs = slice(j * CH, (j + 1) * CH)
                p.op("dve", lambda cs=cs: nc.vector.tensor_tensor_scan(
                    out=gcsT[:, cs], data0=ones2[:, 0:CH], data1=gT[:, cs], initial=0.0, op0=ALU.mult, op1=ALU.add),
                    reads=["gT", "ones2"], writes=["gcsT"])
            for j in range(TB // CH):
                cs = slice(j * CH, (j + 1) * CH)
                e = (j + 1) * CH - 1
                p.op("pool", lambda cs=cs, e=e: nc.gpsimd.tensor_copy(out=glT[:, cs],
                                                                     in_=gcsT[:, e:e + 1].broadcast_to([2, CH])),
                     reads=["gcsT"], writes=["glT"])
            for src, dst, nm in ((gcsT, Mg[bp], "Mg"), (betaT, Mb[bp], "Mb"), (glT, Ml[bp], "Ml")):
                for vh in range(2):
                    p.op("dve", lambda src=src, dst=dst, vh=vh: nc.vector.tensor_scalar(
                        out=dst[:, :, vh, :], in0=src[:].rearrange("p (c i) -> p c i", i=CH), scalar1=ident[0:2, vh:vh + 1],
                        scalar2=None, op0=ALU.mult),
                        reads=[src is gcsT and "gcsT" or (src is betaT and "betaT" or "glT"), "c_ident"],
                        writes=[(nm, bp)])

            for j in range(TB // CH):
                cs = slice(j * CH, (j + 1) * CH)
                c = b * (TB // CH) + j
                q = 0
                for i, (M, nm) in enumerate(((Mg[bp], "Mg"), (Mb[bp], "Mb"), (Ml[bp], "Ml"))):
                    p.op("pe", lambda i=i, M=M: nc.tensor.matmul(ps_Q[0][inst][:, 384 + 2 * i:386 + 2 * i], lhsT=M[:, j, :, :].rearrange("p v i -> p (v i)"), rhs=ones2[:, 0:2],
                                                                 start=True, stop=True),
                         reads=[(nm, bp), "ones2"], writes=[("ps_Q0", inst)])
                p.op("pe", lambda: nc.tensor.matmul(ps_Q[1][inst][:, 384:512], lhsT=ones2[:], rhs=Mg[bp][:, j, :, :].rearrange("p v i -> p (v i)"),
                                                    start=True, stop=True),
                     reads=[("Mg", bp), "ones2"], writes=[("ps_Q1", inst)])
                cq = colsb[q]
                ck = ("colsb", q)
                p.op("dve", lambda: nc.vector.tensor_copy(out=cq[:, 0:3], in_=ps_Q[0][inst][:, 384:390].rearrange("p (a b) -> p a b", b=2)[:, :, 0]),
                     reads=[("ps_Q0", inst)], writes=[ck])
                p.op("act", lambda: nc.scalar.activation(out=cq[:, 3:4], in_=cq[:, 0:1], func=AF.Exp), reads=[ck], writes=[ck])
                p.op("dve", lambda: nc.vector.tensor_tensor(out=cq[:, 4:5], in0=cq[:, 1:2], in1=cq[:, 3:4], op=ALU.mult),
                     reads=[ck], writes=[ck])
                p.op("dve", lambda: nc.vector.tensor_tensor(out=cq[:, 5:6], in0=cq[:, 2:3], in1=cq[:, 0:1], op=ALU.subtract),
                     reads=[ck], writes=[ck])
                p.op("act", lambda: nc.scalar.activation(out=cq[:, 5:6], in_=cq[:, 5:6], func=AF.Exp), reads=[ck], writes=[ck])
                p.op("dve", lambda: nc.vector.tensor_scalar(out=cq[:, 6:7], in0=cq[:, 0:1], scalar1=-1.0, scalar2=None,
                                                            op0=ALU.mult), reads=[ck], writes=[ck])
                p.op("act", lambda: nc.scalar.activation(out=egrow[q][:], in_=ps_Q[1][inst][:, 384:512], func=AF.Exp),
                     reads=[("ps_Q1", inst)], writes=[("egrow", q)])
                p.op("dve", lambda: nc.vector.tensor_scalar(out=D1[q][:], in0=ps_Q[1][inst][:, 384:512], scalar1=cq[:, 0:1],
                                                            scalar2=0.0, op0=ALU.subtract, op1=ALU.max),
                     reads=[("ps_Q1", inst), ck], writes=[("D1", q)])
                p.op("dve", lambda: nc.vector.tensor_scalar(out=D2[q][:], in0=ps_Q[1][inst][:, 384:512], scalar1=cq[:, 0:1],
                                                            scalar2=0.0, op0=ALU.subtract, op1=ALU.min),
                     reads=[("ps_Q1", inst), ck], writes=[("D2", q)])
                p.op("act", lambda: nc.scalar.activation(out=D1[q][:], in_=D1[q][:], func=AF.Exp, scale=-1.0),
                     reads=[("D1", q)], writes=[("D1", q)])
                p.op("act", lambda: nc.scalar.activation(out=D2[q][:], in_=D2[q][:], func=AF.Exp),
                     reads=[("D2", q)], writes=[("D2", q)])
                p.op("dve", lambda: nc.vector.scalar_tensor_tensor(out=D1[q][:], in0=D1[q][:], scalar=1.0, in1=MLs[:],
                                                                   op0=ALU.min, op1=ALU.mult),
                     reads=[("D1", q), "MLs"], writes=[("D1", q)])
                p.op("dve", lambda: nc.vector.scalar_tensor_tensor(out=D2[q][:], in0=D2[q][:], scalar=1.0, in1=MUi[:],
                                                                   op0=ALU.min, op1=ALU.mult),
                     reads=[("D2", q), "MUi"], writes=[("D2", q)])
                kk = kT2[bp][:, j, :, :].rearrange("p v i -> p (v i)")
                qq = qT2[bp][:, j, :, :].rearrange("p v i -> p (v i)")
                p.op("pe", lambda: nc.tensor.matmul(ps_Q[1][inst][:, 128:256], lhsT=kk, rhs=kk, start=True, stop=True),
                     reads=[("kT2", bp)], writes=[("ps_Q1", inst)])
                p.op("pe", lambda: nc.tensor.matmul(ps_Q[1][inst][:, 256:384], lhsT=kk, rhs=qq, start=True, stop=True),
                     reads=[("kT2", bp), ("qT2", bp)], writes=[("ps_Q1", inst)])
                p.op("dve", lambda: nc.vector.scalar_tensor_tensor(out=Am[q][:], in0=D1[q][:], scalar=cq[:, 1:2],
                                                                   in1=ps_Q[1][inst][:, 128:256], op0=ALU.mult, op1=ALU.mult),
                     reads=[("D1", q), ck, ("ps_Q1", inst)], writes=[("Am", q)])
                p.op("dve", lambda: nc.vector.tensor_tensor(out=qkT[q][:], in0=D2[q][:], in1=ps_Q[1][inst][:, 256:384], op=ALU.mult),
                     reads=[("D2", q), ("ps_Q1", inst)], writes=[("qkT", q)])
                p.op("pe", lambda: nc.tensor.transpose(out=ps_Q[0][inst][:, 128:256],
                                                       in_=vT2[bp][:, j, :, :].rearrange("p v i -> p (v i)"), identity=ident[:]),
                     reads=[("vT2", bp), "c_ident"], writes=[("ps_Q0", inst)])
                p.op("pe", lambda: nc.tensor.transpose(out=ps_Q[0][inst][:, 256:384], in_=kk, identity=ident[:]),
                     reads=[("kT2", bp), "c_ident"], writes=[("ps_Q0", inst)])
                p.op("dve", lambda: nc.vector.tensor_scalar(out=RHSv[q][:], in0=ps_Q[0][inst][:, 128:256], scalar1=cq[:, 1:2],
                                                            scalar2=None, op0=ALU.mult),
                     reads=[("ps_Q0", inst), ck], writes=[("RHSv", q)])
                p.op("dve", lambda: nc.vector.tensor_scalar(out=RHSw[q][:], in0=ps_Q[0][inst][:, 256:384], scalar1=cq[:, 4:5],
                                                            scalar2=None, op0=ALU.mult),
                     reads=[("ps_Q0", inst), ck], writes=[("RHSw", q)])
                for vh in range(2):
                    p.op("dve", lambda vh=vh: nc.vector.tensor_scalar(
                        out=kdec[q][:, vh, :], in0=ps_Q[0][inst][:, 256:384], scalar1=cq[:, 5:6],
                        scalar2=bd64[:, 64 * vh:64 * vh + 1], op0=ALU.mult, op1=ALU.mult),
                        reads=[("ps_Q0", inst), ck, "bd64"], writes=[("kdec", q)])
                p.op("pool", lambda: nc.gpsimd.tensor_tensor(
                    out=qgT[q][:], in0=qq, in1=egrow[q][:], op=ALU.mult),
                     reads=[("qT2", bp), ("egrow", q)], writes=[("qgT", q)])
                ni = ncnt[0] % 2
                ncnt[0] += 1
                pa, pk = ps_Q[ni][inst][:, 0:128], ("ps_Q%d" % ni, inst)
                p.op("pe", lambda: nc.tensor.transpose(out=pa, in_=Am[q][:], identity=ident[:]),
                     reads=[("Am", q), "c_ident"], writes=[pk])
                ev_copy(ATm[q][:], ("ATm", q), pa, pk)
                for dst, nm, src, snm, msk, mnm in ((X, "X", Am, "Am", bd16, "bd16"), (Y, "Y", ATm, "ATm", bd16, "bd16"),
                                                    (Ao32, "Ao32", Am, "Am", m32, "m32"),
                                                    (Ao32T, "Ao32T", ATm, "ATm", m32, "m32"),
                                                    (Ao64, "Ao64", Am, "Am", m64, "m64")):
                    p.op("pool", lambda dst=dst, src=src, msk=msk: nc.gpsimd.tensor_tensor(out=dst[q][:], in0=src[q][:],
                                                                                          in1=msk[:], op=ALU.mult),
                         reads=[(snm, q), mnm], writes=[(nm, q)])
                Tq, Uq = Tm[q], Um[q]
                tk, uk = ("Tm", q), ("Um", q)
                p.op("dve", lambda: nc.vector.tensor_tensor(out=Tq[:], in0=ident[:], in1=X[q][:], op=ALU.subtract),
                     reads=["c_ident", ("X", q)], writes=[tk])
                p.op("dve", lambda: nc.vector.tensor_tensor(out=Uq[:], in0=ident[:], in1=Y[q][:], op=ALU.subtract),
                     reads=["c_ident", ("Y", q)], writes=[uk])
                xa, ya, xk_, yk_ = X[q], Y[q], ("X", q), ("Y", q)
                xb, yb, xbk, ybk = X2[q], Y2[q], ("X2", q), ("Y2", q)
                for lvl in range(3):
                    pa, pk = mmN(ya[:], xa[:], [yk_, xk_])
                    ev_copy(xb[:], xbk, pa, pk)
                    if lvl < 2:
                        pa, pk = mmN(xa[:], ya[:], [xk_, yk_])
                        ev_copy(yb[:], ybk, pa, pk)
                    pa, pk = mmN(Uq[:], xb[:], [uk, xbk])
                    pb, pkb = mmN(xb[:], Uq[:], [xbk, uk])
                    ev_comb(Tq[:], tk, Tq[:], tk, pa, pk, ALU.add)
                    ev_comb(Uq[:], uk, Uq[:], uk, pb, pkb, ALU.add)
                    xa, ya, xk_, yk_, xb, yb, xbk, ybk = xb, yb, xbk, ybk, xa, ya, xk_, yk_
                pa, pk = mmN(Ao32T[q][:], Tq[:], [("Ao32T", q), tk])
                ev_copy(Rm[q][:], ("Rm", q), pa, pk)
                pa, pk = mmN(Ao32[q][:], Uq[:], [("Ao32", q), uk])
                ev_copy(Pm[q][:], ("Pm", q), pa, pk)
                pa, pk = mmN(Uq[:], Rm[q][:], [uk, ("Rm", q)])
                pb, pkb = mmN(Tq[:], Pm[q][:], [tk, ("Pm", q)])
                ev_comb(Tq[:], tk, Tq[:], tk, pa, pk, ALU.subtract)
                ev_comb(Uq[:], uk, Uq[:], uk, pb, pkb, ALU.subtract)
                pa, pk = mmN(Ao64[q][:], Uq[:], [("Ao64", q), uk])
                ev_copy(Pm[q][:], ("Pm", q), pa, pk)
                pb, pkb = mmN(Tq[:], Pm[q][:], [tk, ("Pm", q)])
                ev_comb(Uq[:], uk, Uq[:], uk, pb, pkb, ALU.subtract)
                pa, pk = mmN(Uq[:], RHSv[q][:], [uk, ("RHSv", q)])
                ev_copy(u_sb[q][:], ("u_sb", q), pa, pk)
                pa, pk = mmN(RHSw[q][:], Uq[:], [("RHSw", q), uk])
                ev_copy(wT_sb[q][:], ("wT_sb", q), pa, pk)
                for vh in range(2):
                    hs = slice(vh * 64, (vh + 1) * 64)
                    vs = slice(vh * 128, (vh + 1) * 128)
                    p.op("pe", lambda hs=hs, vs=vs: nc.tensor.matmul(ps_Q[2][inst][:, vs], lhsT=wT_sb[q][:], rhs=S_bf[:, vs],
                                                                    start=True, stop=True),
                         reads=[("wT_sb", q), "S_bf"], writes=[("ps_Q2", inst)])
                for vh in range(2):
                    hs = slice(vh * 64, (vh + 1) * 64)
                    vs = slice(vh * 128, (vh + 1) * 128)
                    p.op("dve", lambda hs=hs, vs=vs: nc.vector.tensor_tensor(out=vnew[q][hs, :], in0=u_sb[q][hs, :],
                                                                            in1=ps_Q[2][inst][hs, vs], op=ALU.subtract),
                         reads=[("u_sb", q), ("ps_Q2", inst)], writes=[("vnew", q)])
                for vh in range(2):
                    hs = slice(vh * 64, (vh + 1) * 64)
                    vs = slice(vh * 128, (vh + 1) * 128)
                    oc = slice(256 + vh * 64, 256 + (vh + 1) * 64)
                    p.op("pe", lambda hs=hs, vs=vs, oc=oc: nc.tensor.matmul(ps_Q[2][inst][:, oc], lhsT=S_bf[:, vs], rhs=qgT[q][:, hs],
                                                                           start=True, stop=False),
                         reads=["S_bf", ("qgT", q)], writes=[("ps_Q2", inst)])
                    p.op("pe", lambda hs=hs, oc=oc: nc.tensor.matmul(ps_Q[2][inst][:, oc], lhsT=vnew[q][:], rhs=qkT[q][:, hs],
                                                                    start=False, stop=True),
                         reads=[("vnew", q), ("qkT", q)], writes=[("ps_Q2", inst)])
                p.op("act", lambda: nc.scalar.copy(out=oTo[bp][:, :, cs],
                                                   in_=ps_Q[2][inst][:, 256:384].rearrange("p (v i) -> p v i", v=2)),
                     reads=[("ps_Q2", inst)], writes=[("oTo", bp)])
                for vh in range(2):
                    hs = slice(vh * 64, (vh + 1) * 64)
                    vs = slice(vh * 128, (vh + 1) * 128)
                    p.op("pe", lambda hs=hs, vs=vs, vh=vh: nc.tensor.matmul(ps_Q[3][inst][:, vs], lhsT=kdec[q][:, vh, :], rhs=vnew[q][:],
                                                                    start=True, stop=True),
                         reads=[("kdec", q), ("vnew", q)], writes=[("ps_Q3", inst)])
                for vh in range(2):
                    vs = slice(vh * 128, (vh + 1) * 128)
                    e = vh * 64 + 63
                    p.op("dve", lambda vs=vs, e=e: nc.vector.scalar_tensor_tensor(
                        out=S[:, vs], in0=S[:, vs], scalar=egrow[q][:, e:e + 1], in1=ps_Q[3][inst][:, vs],
                        op0=ALU.mult, op1=ALU.add), reads=["S", ("egrow", q), ("ps_Q3", inst)], writes=["S"])
                p.op("act", lambda: nc.scalar.copy(out=S_bf[:], in_=S[:]), reads=["S"], writes=["S_bf"])
            for vh in range(2):
                if io is None:
                    p.dma("sp", oT[inst, vh * 128:(vh + 1) * 128, col0:col0 + TB], oTo[bp][:, vh, :],
                          reads=[("oTo", bp)], is_output=True)
                else:
                    io.store(p, inst, b, "y%d" % vh, oTo[bp][:, vh, :], ("oTo", bp))
    SHARED = {"c_iota", "c_ident", "c_triu", "cb", "bd16", "bd32", "bd64", "m32", "m64", "MUi", "MLs", "ones2", "onesb"}
    run_interleaved(p, [lambda v: body(v, 0), lambda v: body(v, 1)], SHARED)
    if standalone:
        p.finish()
    return nc


T_SEQ = 16384
TC = 512
NP = 4


def build_s5(n_blocks=T_SEQ // TC, nc=None, p=None, pfx="", io=None):
    standalone = nc is None
    if standalone:
        nc = bass.Bass("TRN2", target_bir_lowering=False)
        p = Prog(nc)
    T = n_blocks * TC
    din = lambda name, shape: nc.dram_tensor(pfx + name, shape, F32, kind="ExternalInput").ap()
    if io is None:
        uT = din("uT", [2, 128, T])
    lre_d = din("lam_re", [2, 128, NP])
    lim_d = din("lam_im", [2, 128, NP])
    lst_d = din("log_step", [2, 128, NP])
    bre_d = din("b_re", [128, NP * 128])
    bim_d = din("b_im", [128, NP * 128])
    cre_d = din("c_re", [2, 128, NP * 128])
    cim_d = din("c_im", [2, 128, NP * 128])
    if io is None:
        yT = nc.dram_tensor("yT", [2, 128, T], F32, kind="ExternalOutput").ap()

    sb = lambda name, shape, dt=F32: _SB(nc, name, shape, dt)
    tpr = sb("tpr", [128, TC])
    p.op("pool", lambda: nc.gpsimd.iota(tpr[:], pattern=[[1, TC]], base=0, channel_multiplier=0,
                                        allow_small_or_imprecise_dtypes=True), writes=["tpr"])
    Bre = sb("Bre", [128, NP * 128]); Bim = sb("Bim", [128, NP * 128])
    p.dma("sp", Bre[:], bre_d, writes=["Bre"])
    p.dma("sp", Bim[:], bim_d, writes=["Bim"])
    ps = lambda name: _PS(nc, name, [128, 512], F32)
    ps_xr = [ps("ps_xr0"), ps("ps_xr1")]
    ps_xi = [ps("ps_xi0"), ps("ps_xi1")]
    ps_y = [ps("ps_y0"), ps("ps_y1")]

    def body(p, inst):
        sb = lambda name, shape, dt=F32: _SB(nc, name + '_i%d' % inst, shape, dt)
        Cre = sb("Cre", [128, NP * 128]); Cim = sb("Cim", [128, NP * 128])
        lre = sb("lre", [128, NP]); lim = sb("lim", [128, NP]); stp = sb("stp", [128, NP])
        rr = sb("rr", [128, NP]); thn = sb("thn", [128, NP])
        sc = {n: sb("sc_" + n, [128, NP]) for n in ("cos", "sin", "rc", "rs", "zr", "zi", "nzr", "den", "t1", "t2", "y")}
        sy = sb("sy", [128, TC]); sk = sb("sk", [128, TC], I32); sf = sb("sf", [128, TC])
        s2 = sb("s2", [128, TC]); q4 = sb("q4", [128, TC]); c2 = sb("c2", [128, TC])
        tS = sb("tS", [128, TC]); tCo = sb("tCo", [128, TC])
        Ezr = sb("Ezr", [128, NP, TC]); Ezi = sb("Ezi", [128, NP, TC])
        Fr = sb("Fr", [128, NP, TC]); Fi = sb("Fi", [128, NP, TC])
        carry = sb("carry", [128, NP, 2])
        ctmp = sb("carry_tmp", [128, 4])
        ub = [sb("ub%d" % i, [128, TC]) for i in range(1)] * 2
        yo = [sb("yo%d" % i, [128, TC]) for i in range(1)] * 2

        def two(name):
            return [sb(name, [128, TC])] * 2
        m1, m2, m3, m4 = two("m1"), two("m2"), two("m3"), two("m4")
        xr_, xi_ = two("xr_"), two("xi_")
        sr_, si_ = two("sr_"), two("si_")
        d1, d2, d3, d4 = two("d1"), two("d2"), two("d3"), two("d4")
        or_, oi_ = two("or_"), two("oi_")
        def sincos(y_ap, n, out_s, out_c, ykeys, okeys):
            K = "sincos_tmp"
            p.op("dve", lambda: nc.vector.tensor_copy(out=sk[:, 0:n], in_=y_ap), reads=ykeys, writes=[K])
            p.op("dve", lambda: nc.vector.tensor_copy(out=sf[:, 0:n], in_=sk[:, 0:n]), reads=[K], writes=[K])
            p.op("dve", lambda: nc.vector.tensor_tensor(out=sf[:, 0:n], in0=y_ap, in1=sf[:, 0:n], op=ALU.subtract),
                 reads=ykeys + [K], writes=[K])
            p.op("act", lambda: nc.scalar.activation(out=s2[:, 0:n], in_=sf[:, 0:n], func=AF.Sin, scale=math.pi),
                 reads=[K], writes=[K])
            p.op("act", lambda: nc.scalar.activation(out=q4[:, 0:n], in_=sf[:, 0:n], func=AF.Sin, scale=math.pi / 2),
                 reads=[K], writes=[K])
            p.op("dve", lambda: nc.vector.tensor_tensor(out=c2[:, 0:n], in0=q4[:, 0:n], in1=q4[:, 0:n], op=ALU.mult),
                 reads=[K], writes=[K])
            p.op("dve", lambda: nc.vector.tensor_scalar(out=c2[:, 0:n], in0=c2[:, 0:n], scalar1=-2.0, scalar2=1.0,
                                                        op0=ALU.mult, op1=ALU.add), reads=[K], writes=[K])
            p.op("dve", lambda: nc.vector.scalar_tensor_tensor(out=out_s, in0=s2[:, 0:n], scalar=2.0, in1=c2[:, 0:n],
                                                               op0=ALU.mult, op1=ALU.mult), reads=[K], writes=okeys)
            p.op("dve", lambda: nc.vector.tensor_tensor(out=c2[:, 0:n], in0=s2[:, 0:n], in1=s2[:, 0:n], op=ALU.mult),
                 reads=[K], writes=[K])
            p.op("dve", lambda: nc.vector.tensor_scalar(out=out_c, in0=c2[:, 0:n], scalar1=-2.0, scalar2=1.0,
                                                        op0=ALU.mult, op1=ALU.add), reads=[K], writes=okeys)

        p.dma("sp", lre[:], lre_d[inst], writes=["prm"])
        p.dma("sp", lim[:], lim_d[inst], writes=["prm"])
        p.dma("sp", stp[:], lst_d[inst], writes=["prm"])
        p.dma("sp", Cre[:], cre_d[inst], writes=["Cre"])
        p.dma("sp", Cim[:], cim_d[inst], writes=["Cim"])
        P = ["prm"]
        p.op("act", lambda: nc.scalar.activation(out=stp[:], in_=stp[:], func=AF.Exp), reads=P, writes=P)
        p.op("dve", lambda: nc.vector.tensor_tensor(out=rr[:], in0=lre[:], in1=stp[:], op=ALU.mult), reads=P, writes=P)
        p.op("act", lambda: nc.scalar.activation(out=rr[:], in_=rr[:], func=AF.Exp), reads=P, writes=P)
        p.op("dve", lambda: nc.vector.tensor_tensor(out=thn[:], in0=lim[:], in1=stp[:], op=ALU.mult), reads=P, writes=P)
        p.op("dve", lambda: nc.vector.tensor_scalar(out=thn[:], in0=thn[:], scalar1=1.0 / (2 * math.pi), scalar2=None,
                                                    op0=ALU.mult), reads=P, writes=P)
        sincos(thn[:], NP, sc["sin"][:], sc["cos"][:], P, P)
        p.op("dve", lambda: nc.vector.tensor_scalar(out=sc["y"][:], in0=thn[:], scalar1=float(TC), scalar2=None,
                                                    op0=ALU.mult), reads=P, writes=P)
        sincos(sc["y"][:], NP, sc["rs"][:], sc["rc"][:], P, P)
        tt = lambda o, a, b, op: p.op("dve", lambda: nc.vector.tensor_tensor(out=o, in0=a, in1=b, op=op), reads=P, writes=P)
        tt(sc["cos"][:], sc["cos"][:], rr[:], ALU.mult)
        tt(sc["sin"][:], sc["sin"][:], rr[:], ALU.mult)
        p.op("dve", lambda: nc.vector.tensor_scalar(out=sc["cos"][:], in0=sc["cos"][:], scalar1=-1.0, scalar2=None,
                                                    op0=ALU.add), reads=P, writes=P)
        tt(sc["t1"][:], lre[:], lre[:], ALU.mult)
        tt(sc["t2"][:], lim[:], lim[:], ALU.mult)
        tt(sc["den"][:], sc["t1"][:], sc["t2"][:], ALU.add)
        p.op("dve", lambda: nc.vector.reciprocal(out=sc["den"][:], in_=sc["den"][:]), reads=P, writes=P)
        tt(sc["t1"][:], sc["cos"][:], lre[:], ALU.mult)
        tt(sc["t2"][:], sc["sin"][:], lim[:], ALU.mult)
        tt(sc["zr"][:], sc["t1"][:], sc["t2"][:], ALU.add)
        tt(sc["zr"][:], sc["zr"][:], sc["den"][:], ALU.mult)
        tt(sc["t1"][:], sc["sin"][:], lre[:], ALU.mult)
        tt(sc["t2"][:], sc["cos"][:], lim[:], ALU.mult)
        tt(sc["zi"][:], sc["t1"][:], sc["t2"][:], ALU.subtract)
        tt(sc["zi"][:], sc["zi"][:], sc["den"][:], ALU.mult)
        p.op("dve", lambda: nc.vector.tensor_scalar(out=sc["nzr"][:], in0=sc["zr"][:], scalar1=-1.0, scalar2=None,
                                                    op0=ALU.mult), reads=P, writes=P)
        for pr in range(NP):
            p.op("dve", lambda pr=pr: nc.vector.tensor_scalar(out=sy[:], in0=tpr[:], scalar1=thn[:, pr:pr + 1], scalar2=None,
                                                              op0=ALU.mult), reads=["tpr"] + P, writes=["sy"])
            sincos(sy[:], TC, tS[:], tCo[:], ["sy"], ["tSC"])
            tk = ("tab", pr)
            p.op("pool", lambda pr=pr: nc.gpsimd.tensor_copy(out=Fr[:, pr, :], in_=tCo[:]), reads=["tSC"], writes=[tk])
            p.op("pool", lambda pr=pr: nc.gpsimd.tensor_copy(out=Fi[:, pr, :], in_=tS[:]), reads=["tSC"], writes=[tk])
            p.op("dve", lambda pr=pr: nc.vector.tensor_scalar(out=Ezr[:, pr, :], in0=tCo[:], scalar1=sc["zr"][:, pr:pr + 1],
                                                              scalar2=None, op0=ALU.mult), reads=["tSC"] + P, writes=[tk])
            p.op("dve", lambda pr=pr: nc.vector.scalar_tensor_tensor(
                out=Ezr[:, pr, :], in0=tS[:], scalar=sc["zi"][:, pr:pr + 1], in1=Ezr[:, pr, :], op0=ALU.mult, op1=ALU.add),
                reads=["tSC", tk] + P, writes=[tk])
            p.op("dve", lambda pr=pr: nc.vector.tensor_scalar(out=Ezi[:, pr, :], in0=tCo[:], scalar1=sc["zi"][:, pr:pr + 1],
                                                              scalar2=None, op0=ALU.mult), reads=["tSC"] + P, writes=[tk])
            p.op("dve", lambda pr=pr: nc.vector.scalar_tensor_tensor(
                out=Ezi[:, pr, :], in0=tS[:], scalar=sc["nzr"][:, pr:pr + 1], in1=Ezi[:, pr, :], op0=ALU.mult, op1=ALU.add),
                reads=["tSC", tk] + P, writes=[tk])
        p.op("dve", lambda: nc.vector.memset(carry[:], 0.0), writes=[("carry", pr) for pr in range(NP)])

        it = 0
        for b in range(n_blocks):
            bp = 0
            col0 = b * TC
            if io is None:
                p.dma("sp", ub[bp][:], uT[inst, :, col0:col0 + TC], writes=[("ub", bp)])
            else:
                io.load(p, inst, b, "u", ub[bp][:], ("ub", bp))
            for pr in range(NP):
                q = 0
                it += 1
                tk = ("tab", pr)
                ck = ("carry", pr)
                ws = slice(pr * 128, (pr + 1) * 128)
                p.op("pe", lambda: nc.tensor.matmul(ps_xr[inst][:], lhsT=Bre[:, ws], rhs=ub[bp][:], start=True, stop=True),
                     reads=["Bre", ("ub", bp)], writes=[("ps_xr", inst)])
                p.op("pe", lambda: nc.tensor.matmul(ps_xi[inst][:], lhsT=Bim[:, ws], rhs=ub[bp][:], start=True, stop=True),
                     reads=["Bim", ("ub", bp)], writes=[("ps_xi", inst)])
                for o, onm, a, anm, tab in ((m1, "m1", ps_xr, "ps_xr", Ezr), (m2, "m2", ps_xi, "ps_xi", Ezi),
                                            (m3, "m3", ps_xr, "ps_xr", Ezi), (m4, "m4", ps_xi, "ps_xi", Ezr)):
                    p.op("dve", lambda o=o, a=a, tab=tab: nc.vector.tensor_tensor(out=o[q][:], in0=a[inst][:], in1=tab[:, pr, :],
                                                                                 op=ALU.mult),
                         reads=[(anm, inst), tk], writes=[(onm, q)])
                p.op("pool", lambda: nc.gpsimd.tensor_tensor(out=xr_[q][:], in0=m1[q][:], in1=m2[q][:], op=ALU.subtract),
                     reads=[("m1", q), ("m2", q)], writes=[("xr_", q)])
                p.op("pool", lambda: nc.gpsimd.tensor_tensor(out=xi_[q][:], in0=m3[q][:], in1=m4[q][:], op=ALU.add),
                     reads=[("m3", q), ("m4", q)], writes=[("xi_", q)])
                rb = rr[:, pr:pr + 1].broadcast_to([128, TC])
                p.op("dve", lambda: nc.vector.tensor_tensor_scan(out=sr_[q][:], data0=rb, data1=xr_[q][:],
                                                                 initial=carry[:, pr, 0:1], op0=ALU.mult, op1=ALU.add),
                     reads=["prm", ("xr_", q), ck], writes=[("sr_", q)])
                p.op("dve", lambda: nc.vector.tensor_tensor_scan(out=si_[q][:], data0=rb, data1=xi_[q][:],
                                                                 initial=carry[:, pr, 1:2], op0=ALU.mult, op1=ALU.add),
                     reads=["prm", ("xi_", q), ck], writes=[("si_", q)])
                lr, li = sr_[q][:, TC - 1:TC], si_[q][:, TC - 1:TC]
                p.op("dve", lambda: nc.vector.tensor_tensor(out=ctmp[:, 0:1], in0=li, in1=sc["rs"][:, pr:pr + 1], op=ALU.mult),
                     reads=[("si_", q), "prm"], writes=["ctmp"])
                p.op("dve", lambda: nc.vector.tensor_tensor(out=ctmp[:, 1:2], in0=li, in1=sc["rc"][:, pr:pr + 1], op=ALU.mult),
                     reads=[("si_", q), "prm"], writes=["ctmp"])
                p.op("dve", lambda: nc.vector.scalar_tensor_tensor(out=carry[:, pr, 0:1], in0=lr, scalar=sc["rc"][:, pr:pr + 1],
                                                                   in1=ctmp[:, 0:1], op0=ALU.mult, op1=ALU.subtract),
                     reads=[("sr_", q), "prm", "ctmp"], writes=[ck])
                p.op("dve", lambda: nc.vector.scalar_tensor_tensor(out=carry[:, pr, 1:2], in0=lr, scalar=sc["rs"][:, pr:pr + 1],
                                                                   in1=ctmp[:, 1:2], op0=ALU.mult, op1=ALU.add),
                     reads=[("sr_", q), "prm", "ctmp"], writes=[ck])
                for o, onm, a, anm, tab in ((d1, "d1", sr_, "sr_", Fr), (d2, "d2", si_, "si_", Fi),
                                            (d3, "d3", sr_, "sr_", Fi), (d4, "d4", si_, "si_", Fr)):
                    p.op("pool", lambda o=o, a=a, tab=tab: nc.gpsimd.tensor_tensor(out=o[q][:], in0=a[q][:], in1=tab[:, pr, :],
                                                                                  op=ALU.mult),
                         reads=[(anm, q), tk], writes=[(onm, q)])
                p.op("dve", lambda: nc.vector.tensor_tensor(out=or_[q][:], in0=d1[q][:], in1=d2[q][:], op=ALU.subtract),
                     reads=[("d1", q), ("d2", q)], writes=[("or_", q)])
                p.op("dve", lambda: nc.vector.scalar_tensor_tensor(out=oi_[q][:], in0=d3[q][:], scalar=-1.0, in1=d4[q][:],
                                                                   op0=ALU.mult, op1=ALU.subtract),
                     reads=[("d3", q), ("d4", q)], writes=[("oi_", q)])
                p.op("pe", lambda: nc.tensor.matmul(ps_y[inst][:], lhsT=Cre[:, ws], rhs=or_[q][:], start=(pr == 0), stop=False),
                     reads=["Cre", ("or_", q)], writes=[("ps_y", inst)])
                p.op("pe", lambda: nc.tensor.matmul(ps_y[inst][:], lhsT=Cim[:, ws], rhs=oi_[q][:], start=False,
                                                    stop=(pr == NP - 1)),
                     reads=["Cim", ("oi_", q)], writes=[("ps_y", inst)])
            p.op("act", lambda: nc.scalar.copy(out=yo[bp][:], in_=ps_y[inst][:]), reads=[("ps_y", inst)], writes=[("yo", bp)])
            if io is None:
                p.dma("sp", yT[inst, :, col0:col0 + TC], yo[bp][:], reads=[("yo", bp)], is_output=True)
            else:
                io.store(p, inst, b, "y", yo[bp][:], ("yo", bp))
    SHARED = {"tpr", "Bre", "Bim"}
    run_interleaved(p, [lambda v: body(v, 0), lambda v: body(v, 1)], SHARED)
    if standalone:
        p.finish()
    return nc


NCORES = 8
SEQ = 16384
TOKC = SEQ // NCORES


def _run(nc, in_maps):
    res = run_bass_kernel_spmd(nc, in_maps, core_ids=list(range(NCORES)))
    return [{k: np.asarray(v) for k, v in r.items()} for r in res.results]


def _c(a):
    return np.ascontiguousarray(a, dtype=np.float32)


def _tok(a, c):
    return _c(a[:, c * TOKC:(c + 1) * TOKC])


def _conv_layout(w):
    return _c(w.T.reshape(4, 128, 5).transpose(1, 0, 2).reshape(128, 20))


def _pad2(a):
    return np.pad(a, ((0, 0), (2, 2)))


def _ffn_inputs(i, norm_w, ffn_w_gate_up, ffn_w_down):
    return dict(
        nws=_c(np.concatenate([col_tiles(norm_w[i, k]) for k in (1, 2, 3)], axis=1)),
        wgu=_c(np.concatenate([arrange_w(ffn_w_gate_up[i][:, :DFF]), arrange_w(ffn_w_gate_up[i][:, DFF:])], axis=2)),
        wd=arrange_w(ffn_w_down[i]),
    )


def _ssd_maps(P, j, conv_w, conv_b, dt_bias, a_log, d_skip):
    maps = []
    for g in range(NCORES):
        ch = np.concatenate([np.arange(g * 256, (g + 1) * 256), 2048 + np.arange(g * 128, (g + 1) * 128),
                             3072 + np.arange(g * 128, (g + 1) * 128)])
        xg = P[2048 + ch]
        w = conv_w[j][:, ch]
        maps.append(dict(
            xbc=_c(np.stack([_pad2(xg), _pad2(xg[:, ::-1])])),
            dtr=_c(np.stack([P[6144 + 4 * g:6144 + 4 * g + 4], P[6176 + 4 * g:6176 + 4 * g + 4][:, ::-1]])),
            cw=_c(np.stack([_conv_layout(w), _conv_layout(w[::-1])])),
            cb=_c(conv_b[j][ch].reshape(4, 128).T),
            dtb=_c(dt_bias[j][:, 4 * g:4 * g + 4].reshape(2, 4, 1)),
            alog=_c(a_log[j][:, 4 * g:4 * g + 4].reshape(2, 4, 1)),
            dsk=_c(np.repeat(d_skip[j][4 * g:4 * g + 4], 64).reshape(2, 128).T),
        ))
    return maps


def _gdn_maps(P, conv_w, conv_b, dt_bias, a_log):
    maps = []
    for g in range(NCORES):
        ch = np.concatenate([np.arange(g * 128, (g + 1) * 128), 1024 + np.arange(g * 128, (g + 1) * 128),
                             2048 + np.arange(2 * g * 128, (2 * g + 2) * 128)])
        xg = P[ch]
        w = conv_w[0][:, ch]
        a0, a1 = P[6144 + 2 * g:6144 + 2 * g + 2], P[6144 + 16 + 2 * g:6144 + 16 + 2 * g + 2]
        b0, b1 = P[6176 + 2 * g:6176 + 2 * g + 2], P[6176 + 16 + 2 * g:6176 + 16 + 2 * g + 2]
        maps.append(dict(
            qkv=_c(np.stack([_pad2(xg), _pad2(xg[:, ::-1])])),
            abr=_c(np.stack([np.stack([a0, b0]), np.stack([a1[:, ::-1], b1[:, ::-1]])])),
            cw=_c(np.stack([_conv_layout(w), _conv_layout(w[::-1])])),
            cb=_c(conv_b[0][ch].reshape(4, 128).T),
            dtb=_c(dt_bias[0][:, 2 * g:2 * g + 2].reshape(2, 2, 1)),
            alog=_c(a_log[0][:, 2 * g:2 * g + 2].reshape(2, 2, 1)),
        ))
    return maps


def _pair_cols(a):
    return _c(a.reshape(4, 2, 64).transpose(1, 2, 0).reshape(128, 4))


def _blayout(b):
    out = np.zeros((8, 16, 4, 2, 64), np.float32)
    for g in range(8):
        out[g, :, g // 2, g % 2, :] = b[g].T
    return out.reshape(128, 512)


def _clayout(c):
    out = np.zeros((2, 64, 4, 8, 16), np.float32)
    for g in range(8):
        out[g % 2, :, g // 2, g, :] = c[g].T
    return out.reshape(128, 512)


def _s5_maps(hn, lam_re, lam_im, log_step, b_re, b_im, c_re, c_im):
    maps = []
    for c in range(NCORES):
        gs = slice(8 * c, 8 * c + 8)
        u = hn[c * 128:(c + 1) * 128]
        maps.append(dict(
            uT=_c(np.stack([u, u[:, ::-1]])),
            lam_re=np.stack([_pair_cols(lam_re[0, d, gs]) for d in range(2)]),
            lam_im=np.stack([_pair_cols(lam_im[0, d, gs]) for d in range(2)]),
            log_step=np.stack([_pair_cols(np.repeat(log_step[0, d, gs][:, None], 64, axis=1)) for d in range(2)]),
            b_re=_blayout(b_re[0, gs]), b_im=_blayout(b_im[0, gs]),
            c_re=np.stack([_clayout(c_re[0, d, gs]) for d in range(2)]),
            c_im=np.stack([_clayout(c_im[0, d, gs]) for d in range(2)]),
        ))
    return maps


def _gather_tok(results, key):
    return np.concatenate([r[key] for r in results], axis=1)


def kernel(x, norm_w, ssd_w_in, ssd_conv_w, ssd_conv_b, ssd_dt_bias, ssd_a_log, ssd_d,
           ssd_norm_w, ssd_w_out, gdn_w_in, gdn_conv_w, gdn_conv_b, gdn_dt_bias, gdn_a_log,
           gdn_norm_w, gdn_w_out, s5_lam_re, s5_lam_im, s5_log_step, s5_b_re, s5_b_im,
           s5_c_re, s5_c_im, s5_d, s5_w_glu, s5_b_glu, ffn_w_gate_up, ffn_w_down):
    A = lambda a: np.asarray(a, dtype=np.float32)
    (x, norm_w, ssd_w_in, ssd_conv_w, ssd_conv_b, ssd_dt_bias, ssd_a_log, ssd_d, ssd_norm_w, ssd_w_out, gdn_w_in,
     gdn_conv_w, gdn_conv_b, gdn_dt_bias, gdn_a_log, gdn_norm_w, gdn_w_out, s5_lam_re, s5_lam_im, s5_log_step,
     s5_b_re, s5_b_im, s5_c_re, s5_c_im, s5_d, s5_w_glu, s5_b_glu, ffn_w_gate_up, ffn_w_down) = map(A, (
         x, norm_w, ssd_w_in, ssd_conv_w, ssd_conv_b, ssd_dt_bias, ssd_a_log, ssd_d, ssd_norm_w, ssd_w_out, gdn_w_in,
         gdn_conv_w, gdn_conv_b, gdn_dt_bias, gdn_a_log, gdn_norm_w, gdn_w_out, s5_lam_re, s5_lam_im, s5_log_step,
         s5_b_re, s5_b_im, s5_c_re, s5_c_im, s5_d, s5_w_glu, s5_b_glu, ffn_w_gate_up, ffn_w_down))
    hT = _c(x[0].T)

    nc_ssd = build_ssd()
    common = dict(nw0=col_tiles(norm_w[0, 0]), w_in=arrange_w(ssd_w_in[0]))
    r = _run(build_dense(None, False, "proj"), [dict(hT=_tok(hT, c), **common) for c in range(NCORES)])
    P = _gather_tok(r, "projT")

    r = _run(nc_ssd, _ssd_maps(P, 0, ssd_conv_w, ssd_conv_b, ssd_dt_bias, ssd_a_log, ssd_d))
    yf = np.concatenate([q["yT"][0] for q in r], axis=0)
    yb = np.concatenate([q["yT"][1][:, ::-1] for q in r], axis=0)
    common = dict(w_out=arrange_w(ssd_w_out[0]), mnw=col_tiles(ssd_norm_w[0]), nw0=col_tiles(norm_w[1, 0]),
                  w_in=arrange_w(gdn_w_in[0]), **_ffn_inputs(0, norm_w, ffn_w_gate_up, ffn_w_down))
    r = _run(build_dense("ssd", True, "proj"),
             [dict(hT=_tok(hT, c), mf=_tok(yf, c), mb=_tok(yb, c), zT=_tok(P[0:2048], c), **common) for c in range(NCORES)])
    hT = _gather_tok(r, "hT_out")
    P = _gather_tok(r, "projT")

    r = _run(build_gdn(), _gdn_maps(P, gdn_conv_w, gdn_conv_b, gdn_dt_bias, gdn_a_log))
    yf = np.concatenate([q["oT"][0] for q in r], axis=0)
    yb = np.concatenate([q["oT"][1][:, ::-1] for q in r], axis=0)
    common = dict(w_out=arrange_w(gdn_w_out[0]), mnw=_c(np.tile(gdn_norm_w[0][:, None], (1, 16))),
                  nw0=col_tiles(norm_w[2, 0]), **_ffn_inputs(1, norm_w, ffn_w_gate_up, ffn_w_down))
    r = _run(build_dense("gdn", True, "hn"),
             [dict(hT=_tok(hT, c), mf=_tok(yf, c), mb=_tok(yb, c), zT=_tok(P[4096:6144], c), **common)
              for c in range(NCORES)])
    hT = _gather_tok(r, "hT_out")
    hn = _gather_tok(r, "hn_out")

    r = _run(build_s5(), _s5_maps(hn, s5_lam_re, s5_lam_im, s5_log_step, s5_b_re, s5_b_im, s5_c_re, s5_c_im))
    yf = np.concatenate([q["yT"][0] for q in r], axis=0)
    yb = np.concatenate([q["yT"][1][:, ::-1] for q in r], axis=0)
    common = dict(w_glu=arrange_w(s5_w_glu[0]), b_glu=col_tiles(s5_b_glu[0]), s5d=col_tiles(s5_d[0]),
                  nw0=col_tiles(norm_w[3, 0]), w_in=arrange_w(ssd_w_in[1]),
                  **_ffn_inputs(2, norm_w, ffn_w_gate_up, ffn_w_down))
    r = _run(build_dense("s5", True, "proj"),
             [dict(hT=_tok(hT, c), mf=_tok(yf, c), mb=_tok(yb, c), hnT=_tok(hn, c), **common) for c in range(NCORES)])
    hT = _gather_tok(r, "hT_out")
    P = _gather_tok(r, "projT")

    r = _run(nc_ssd, _ssd_maps(P, 1, ssd_conv_w, ssd_conv_b, ssd_dt_bias, ssd_a_log, ssd_d))
    yf = np.concatenate([q["yT"][0] for q in r], axis=0)
    yb = np.concatenate([q["yT"][1][:, ::-1] for q in r], axis=0)
    common = dict(w_out=arrange_w(ssd_w_out[1]), mnw=col_tiles(ssd_norm_w[1]),
                  **_ffn_inputs(3, norm_w, ffn_w_gate_up, ffn_w_down))
    r = _run(build_dense("ssd", True, None),
             [dict(hT=_tok(hT, c), mf=_tok(yf, c), mb=_tok(yb, c), zT=_tok(P[0:2048], c), **common) for c in range(NCORES)])
    hT = _gather_tok(r, "hT_out")
    return np.ascontiguousarray(hT.T[None].astype(np.float32))
```

```python
import math
import contextlib


import numpy as np
import concourse.bass as bass
import concourse.mybir as mybir
from concourse.bass_utils import run_bass_kernel_spmd

F32 = mybir.dt.float32
BF16 = mybir.dt.bfloat16
I32 = mybir.dt.int32
AF = mybir.ActivationFunctionType
ALU = mybir.AluOpType
AX = mybir.AxisListType


def _is_psum(k):
    n = k[0] if isinstance(k, tuple) else k
    return isinstance(n, str) and n.startswith("ps_")


_STAGE = {"es": None, "pfx": ""}


def _SB(nc, name, shape, dt=None):
    if dt is None:
        dt = F32
    if _STAGE["es"] is None:
        return nc.alloc_sbuf_tensor(name, shape, dt)
    return _STAGE["es"].enter_context(nc.sbuf_tensor(_STAGE["pfx"] + name, shape, dt))


def _PS(nc, name, shape, dt=None):
    if dt is None:
        dt = F32
    if _STAGE["es"] is None:
        return nc.alloc_psum_tensor(name, shape, dt)
    return _STAGE["es"].enter_context(nc.psum_tensor(_STAGE["pfx"] + name, shape, dt))


class Prog:
    NDMA = 4

    def __init__(self, nc):
        self.nc = nc
        self.eng = {"pe": nc.tensor, "dve": nc.vector, "act": nc.scalar,
                    "pool": nc.gpsimd, "sp": nc.sync}
        self.sem = {}
        self.cnt = {}
        for e in ("pe", "dve", "act", "pool"):
            self.sem[e] = nc.alloc_semaphore(name="s_" + e)
            self.cnt[e] = 0
        self.dq = {}
        for q in ("sp", "act", "pool"):
            sems = []
            for i in range(self.NDMA):
                k = "d_%s%d" % (q, i)
                self.sem[k] = nc.alloc_semaphore(name=k)
                self.cnt[k] = 0
                sems.append(k)
            self.dq[q] = [sems, 0]
        self.seen = {e: {} for e in self.eng}
        self.last_w = {}
        self.readers = {}
        self.out_tokens = []

    def _wait(self, e, needs):
        eng = self.eng[e]
        for sk, val in needs.items():
            if e == "pe" and sk == "pe":
                continue
            if self.seen[e].get(sk, 0) >= val:
                continue
            eng.wait_ge(self.sem[sk], val)
            self.seen[e][sk] = val

    def _needs(self, reads, writes, e=None):
        needs = {}

        def add(tok):
            if tok is None:
                return
            sk, v = tok
            if needs.get(sk, 0) < v:
                needs[sk] = v
        for k in reads:
            add(self.last_w.get(k))
            if _is_psum(k):
                for t in self.readers.get(k, ()):
                    if t[0] != e:
                        add(t)
        for k in writes:
            add(self.last_w.get(k))
            for t in self.readers.get(k, ()):
                add(t)
        return needs

    def _commit(self, tok, reads, writes):
        for k in writes:
            self.last_w[k] = tok
            self.readers[k] = []
        for k in reads:
            if k in writes:
                continue
            self.readers.setdefault(k, []).append(tok)
            if len(self.readers[k]) > 12:
                best = {}
                for sk, v in self.readers[k]:
                    if best.get(sk, 0) < v:
                        best[sk] = v
                self.readers[k] = list(best.items())

    def op(self, e, fn, reads=(), writes=()):
        self._wait(e, self._needs(reads, writes, e))
        ins = fn()
        self.cnt[e] += 1
        ins.then_inc(self.sem[e], 1)
        tok = (e, self.cnt[e])
        self._commit(tok, reads, writes)
        return tok

    def dma(self, q, out, in_, reads=(), writes=(), is_output=False, **kw):
        sems, n = self.dq[q]
        sk = sems[n % self.NDMA]
        self.dq[q][1] = n + 1
        needs = self._needs(reads, writes)
        if self.cnt[sk] > 0:
            needs[sk] = max(needs.get(sk, 0), self.cnt[sk])
        self._wait(q, needs)
        ins = self.eng[q].dma_start(out=out, in_=in_, **kw)
        self.cnt[sk] += 16
        ins.then_inc(self.sem[sk], 16)
        tok = (sk, self.cnt[sk])
        self._commit(tok, reads, writes)
        if is_output:
            self.out_tokens.append(tok)
        return tok

    def barrier(self):
        needs = {k: v for k, v in self.cnt.items() if v > 0}
        for e in self.eng:
            self._wait(e, dict(needs))

    def allgather(self, in_ap, out_ap):
        if "cc" not in self.sem:
            self.sem["cc"] = self.nc.alloc_semaphore(name="cc_sem")
            self.cnt["cc"] = 0
        ins = self.nc.gpsimd.collective_compute("AllGather", ALU.bypass, replica_groups=[list(range(8))],
                                                ins=[in_ap.opt()], outs=[out_ap.opt()])
        self.cnt["cc"] += 1
        ins.then_inc(self.sem["cc"], 1)

    def finish(self):
        needs = {}
        for sk, v in self.out_tokens:
            needs[sk] = max(needs.get(sk, 0), v)
        self._wait("sp", needs)
        needs = {e: self.cnt[e] for e in ("pe", "dve", "act", "pool") if self.cnt[e] > 0}
        for q in self.dq:
            for sk in self.dq[q][0]:
                if self.cnt[sk] > 0:
                    needs[sk] = self.cnt[sk]
        self._wait("sp", needs)


class InstView:
    def __init__(self, p, inst, shared):
        self.p, self.inst, self.shared = p, inst, shared

    def k(self, key):
        n = key[0] if isinstance(key, tuple) else key
        if n in self.shared or (isinstance(n, str) and n.startswith("ps_")):
            return key
        return ("I%d" % self.inst, key)

    def op(self, e, fn, reads=(), writes=()):
        r = self.p.op(e, fn, [self.k(x) for x in reads], [self.k(x) for x in writes])
        self.p.baton.step(self.inst)
        return r

    def dma(self, q, out, in_, reads=(), writes=(), **kw):
        r = self.p.dma(q, out, in_, [self.k(x) for x in reads], [self.k(x) for x in writes], **kw)
        self.p.baton.step(self.inst)
        return r


class Baton:
    def __init__(self, n):
        import threading
        self.n = n
        self.sems = [threading.Semaphore(0) for _ in range(n)]
        self.alive = [True] * n
        self.err = None

    def _next(self, i):
        for d in range(1, self.n + 1):
            j = (i + d) % self.n
            if self.alive[j]:
                return j
        return None

    def step(self, i):
        j = self._next(i)
        if j is None or j == i:
            return
        self.sems[j].release()
        self.sems[i].acquire()

    def done(self, i):
        self.alive[i] = False
        j = self._next(i)
        if j is not None:
            self.sems[j].release()


def run_interleaved(p, bodies, shared):
    import threading
    n = len(bodies)
    p.baton = Baton(n)
    errs = []

    def runner(i):
        p.baton.sems[i].acquire()
        try:
            bodies[i](InstView(p, i, shared))
        except BaseException as ex:
            errs.append(ex)
        finally:
            p.baton.done(i)
    ths = [threading.Thread(target=runner, args=(i,)) for i in range(n)]
    for t in ths:
        t.start()
    p.baton.sems[0].release()
    for t in ths:
        t.join()
    if errs:
        raise errs[0]


EPS = 1e-6
TG = 512
NTG = 4
D = 1024
DT = 8
DFF = 2816
FT = 22
GELU_C = 0.7978845608028654


def build_dense(variant, has_ffn, next_kind, nc=None, p=None, pfx="", io=None):
    standalone = nc is None
    if standalone:
        nc = bass.Bass("TRN2", target_bir_lowering=False)
        p = Prog(nc)
    io = io or {}
    NTOK = TG * NTG

    def din(name, shape):
        if name in io:
            return None
        return nc.dram_tensor(pfx + name, shape, F32, kind="ExternalInput").ap()

    def dout(name, shape):
        if name in io:
            return None
        return nc.dram_tensor(pfx + name, shape, F32, kind="ExternalOutput").ap()

    def rd(name, ten, r, tok):
        if name in io:
            return io[name](r, tok)
        return ten[r * 128:(r + 1) * 128, tok]

    hT_in = din("hT", [D, NTOK])
    if variant in ("ssd", "gdn"):
        mf = din("mf", [2048, NTOK])
        mb = din("mb", [2048, NTOK])
        zT = din("zT", [2048, NTOK])
        w_out = din("w_out", [DT, 128, 16 * 128])
        mnw = din("mnw", [128, 16])
    if variant == "s5":
        mf = din("mf", [D, NTOK])
        mb = din("mb", [D, NTOK])
        hn_in = din("hnT", [D, NTOK])
        w_glu = din("w_glu", [16, 128, DT * 128])
        b_glu = din("b_glu", [128, 16])
        s5d = din("s5d", [128, DT])
    if has_ffn:
        nws = din("nws", [128, 3 * DT])
        wgu = din("wgu", [FT, 128, 2 * DT * 128])
        wd = din("wd", [DT, 128, FT * 128])
        hT_out = dout("hT_out", [D, NTOK])
    if next_kind is not None:
        nw0 = din("nw0", [128, DT])
    if next_kind == "proj":
        w_in = din("w_in", [49, 128, DT * 128])
        projT = dout("projT", [49 * 128, NTOK])
    if next_kind == "hn":
        hn_out = dout("hn_out", [D, NTOK])

    sb = lambda name, shape, dt=F32: _SB(nc, name, shape, dt)
    hT = sb("hT_sb", [128, DT, TG])
    mT = sb("mT_sb", [128, DT, TG])
    xnT = sb("xnT", [128, DT, TG], BF16)
    lhsA = sb("lhsA", [128, 16, TG], BF16)
    actT = sb("actT", [128, FT, TG], BF16)
    NW = 3
    wbuf = [sb("wbuf%d" % i, [128, FT * 128], BF16) for i in range(NW)]
    stg = [[sb("stg%d_%d" % (i, s), [128, TG]) for s in range(2)] for i in range(3)]
    tmpA = [sb("tmpA%d" % i, [128, TG]) for i in range(2)]
    tmpB = [sb("tmpB%d" % i, [128, TG]) for i in range(2)]
    yzb = [sb("yzb%d" % i, [128, TG]) for i in range(4)]
    sqb = [sb("sqb%d" % i, [128, TG], BF16) for i in range(2)]
    rstd = sb("rstd", [128, TG])
    ostg = [sb("ostg%d" % i, [128, TG]) for i in range(2)]
    ones = sb("ones", [128, 128], BF16)
    cols = sb("cols", [128, 64])
    ps_ss = _PS(nc, "ps_ss", [128, TG], F32)
    ps_acc = [_PS(nc, "ps_acc%d" % i, [128, TG], F32) for i in range(3)]
    ps_g = [_PS(nc, "ps_g%d" % i, [128, TG], F32) for i in range(2)]
    ps_u = [_PS(nc, "ps_u%d" % i, [128, TG], F32) for i in range(2)]

    p.op("pool", lambda: nc.gpsimd.memset(ones[:], 1.0), writes=["ones"])
    if has_ffn:
        p.dma("sp", cols[:, 0:24], nws, writes=["cols"])
    if next_kind is not None:
        p.dma("sp", cols[:, 24:32], nw0, writes=["cols"])
    if variant in ("ssd", "gdn"):
        p.dma("sp", cols[:, 32:48], mnw, writes=["cols"])
    if variant == "s5":
        p.dma("sp", cols[:, 32:48], b_glu, writes=["cols"])
        p.dma("sp", cols[:, 48:56], s5d, writes=["cols"])

    cnt = {"w": 0, "acc": 0, "gu": 0, "o": 0, "ev": 0}

    def load_w(src, n):
        i = cnt["w"] % NW
        cnt["w"] += 1
        p.dma("pool", wbuf[i][:, 0:n], src, writes=[("w", i)], max_dma_last_dim=4096)
        return wbuf[i], ("w", i)

    def rms_rstd(tiles, n_feat):
        last = len(tiles) - 1
        for i, (ap, key) in enumerate(tiles):
            s = i % 2
            p.op("act", lambda ap=ap, s=s: nc.scalar.activation(out=sqb[s][:], in_=ap, func=AF.Square),
                 reads=[key], writes=[("sq", s)])
            p.op("pe", lambda s=s, i=i: nc.tensor.matmul(ps_ss[:], lhsT=ones[:], rhs=sqb[s][:],
                                                       start=(i == 0), stop=(i == last)),
                 reads=[("sq", s), "ones"], writes=["ps_ss"])
        p.op("act", lambda: nc.scalar.activation(out=rstd[:], in_=ps_ss[:], func=AF.Ln,
                                                 scale=1.0 / n_feat, bias=EPS),
             reads=["ps_ss"], writes=["rstd"])
        p.op("act", lambda: nc.scalar.activation(out=rstd[:], in_=rstd[:], func=AF.Exp, scale=-0.5),
             reads=["rstd"], writes=["rstd"])

    def evac(dst, dkey, src, skey):
        cnt["ev"] += 1
        if cnt["ev"] % 2:
            p.op("act", lambda: nc.scalar.copy(out=dst, in_=src), reads=[skey], writes=[dkey])
        else:
            p.op("dve", lambda: nc.vector.tensor_copy(out=dst, in_=src), reads=[skey], writes=[dkey])

    def proj_fm(wsrc, nk, rhs_fn, rhs_keys, consume):
        wt, wkey = load_w(wsrc, nk * 128)
        a = cnt["acc"] % 3
        cnt["acc"] += 1
        for k in range(nk):
            p.op("pe", lambda k=k: nc.tensor.matmul(ps_acc[a][:], lhsT=wt[:, k * 128:(k + 1) * 128],
                                                    rhs=rhs_fn(k), start=(k == 0), stop=(k == nk - 1)),
                 reads=[wkey] + rhs_keys, writes=[("acc", a)])
        consume(ps_acc[a][:], ("acc", a))

    def add_norm_into_h(nwoff):
        rms_rstd([(mT[:, j, :], ("mT", j)) for j in range(DT)], D)
        for j in range(DT):
            s = j % 2
            p.op("dve", lambda j=j, s=s: nc.vector.scalar_tensor_tensor(
                out=tmpA[s][:], in0=mT[:, j, :], scalar=cols[:, nwoff + j:nwoff + j + 1], in1=rstd[:],
                op0=ALU.mult, op1=ALU.mult), reads=[("mT", j), "cols", "rstd"], writes=[("tmpA", s)])
            p.op("pool", lambda j=j, s=s: nc.gpsimd.tensor_tensor(out=hT[:, j, :], in0=hT[:, j, :], in1=tmpA[s][:],
                                                                  op=ALU.add),
                 reads=[("hT", j), ("tmpA", s)], writes=[("hT", j)])

    def norm_h_to(dst_fn, dkey_fn, nwoff):
        rms_rstd([(hT[:, j, :], ("hT", j)) for j in range(DT)], D)
        for j in range(DT):
            p.op("dve", lambda j=j: nc.vector.scalar_tensor_tensor(
                out=dst_fn(j), in0=hT[:, j, :], scalar=cols[:, nwoff + j:nwoff + j + 1], in1=rstd[:],
                op0=ALU.mult, op1=ALU.mult), reads=[("hT", j), "cols", "rstd"], writes=[dkey_fn(j)])

    for tg in range(NTG):
        tok = slice(tg * TG, (tg + 1) * TG)
        for j in range(DT):
            p.dma("sp", hT[:, j, :], rd("hT", hT_in, j, tok), writes=[("hT", j)])

        if variant in ("ssd", "gdn"):
            gsz = 2 if variant == "ssd" else 1
            for G in range(16 // gsz):
                tl = []
                for t in range(gsz):
                    ft = G * gsz + t
                    s = ft % 2
                    rows = slice(ft * 128, (ft + 1) * 128)
                    p.dma("sp", stg[0][s][:], rd("mf", mf, ft, tok), writes=[("stg0", s)])
                    p.dma("sp" if "mb" in io else "act", stg[1][s][:], rd("mb", mb, ft, tok), writes=[("stg1", s)])
                    p.dma("sp", stg[2][s][:], rd("zT", zT, ft, tok), writes=[("stg2", s)])
                    yb = yzb[ft % 4]
                    ykey = ("yz", ft % 4)
                    p.op("pool", lambda s=s, yb=yb: nc.gpsimd.tensor_tensor(out=yb[:], in0=stg[0][s][:], in1=stg[1][s][:],
                                                                            op=ALU.add),
                         reads=[("stg0", s), ("stg1", s)], writes=[ykey])
                    p.op("act", lambda s=s: nc.scalar.activation(out=tmpB[s][:], in_=stg[2][s][:], func=AF.Silu),
                         reads=[("stg2", s)], writes=[("tmpB", s)])
                    if variant == "ssd":
                        p.op("dve", lambda s=s, yb=yb: nc.vector.tensor_tensor(out=yb[:], in0=yb[:], in1=tmpB[s][:],
                                                                               op=ALU.mult),
                             reads=[ykey, ("tmpB", s)], writes=[ykey])
                    tl.append((yb, ykey, ft, s))
                rms_rstd([(yb[:], ykey) for (yb, ykey, ft, s) in tl], 128 * gsz)
                for (yb, ykey, ft, s) in tl:
                    if variant == "ssd":
                        p.op("dve", lambda yb=yb, ft=ft: nc.vector.scalar_tensor_tensor(
                            out=lhsA[:, ft, :], in0=yb[:], scalar=cols[:, 32 + ft:33 + ft], in1=rstd[:],
                            op0=ALU.mult, op1=ALU.mult), reads=[ykey, "cols", "rstd"], writes=[("lhsA", ft)])
                    else:
                        p.op("dve", lambda yb=yb: nc.vector.scalar_tensor_tensor(
                            out=yb[:], in0=yb[:], scalar=cols[:, 32:33], in1=rstd[:],
                            op0=ALU.mult, op1=ALU.mult), reads=[ykey, "cols", "rstd"], writes=[ykey])
                        p.op("dve", lambda yb=yb, ft=ft, s=s: nc.vector.tensor_tensor(
                            out=lhsA[:, ft, :], in0=yb[:], in1=tmpB[s][:], op=ALU.mult),
                            reads=[ykey, ("tmpB", s)], writes=[("lhsA", ft)])
            for j in range(DT):
                proj_fm(w_out[j], 16, lambda k: lhsA[:, k, :], [("lhsA", k) for k in range(16)],
                        lambda ps, pk, j=j: evac(mT[:, j, :], ("mT", j), ps, pk))
        if variant == "s5":
            for ft in range(DT):
                s = ft % 2
                rows = slice(ft * 128, (ft + 1) * 128)
                p.dma("sp", stg[0][s][:], rd("mf", mf, ft, tok), writes=[("stg0", s)])
                p.dma("sp" if "mb" in io else "act", stg[1][s][:], rd("mb", mb, ft, tok), writes=[("stg1", s)])
                p.dma("sp", stg[2][s][:], rd("hnT", hn_in, ft, tok), writes=[("stg2", s)])
                yb = yzb[ft % 4]
                ykey = ("yz", ft % 4)
                p.op("pool", lambda s=s, yb=yb: nc.gpsimd.tensor_tensor(out=yb[:], in0=stg[0][s][:], in1=stg[1][s][:],
                                                                        op=ALU.add),
                     reads=[("stg0", s), ("stg1", s)], writes=[ykey])
                p.op("dve", lambda s=s, yb=yb, ft=ft: nc.vector.scalar_tensor_tensor(
                    out=yb[:], in0=stg[2][s][:], scalar=cols[:, 48 + ft:49 + ft], in1=yb[:],
                    op0=ALU.mult, op1=ALU.add), reads=[("stg2", s), "cols", ykey], writes=[ykey])
                p.op("act", lambda s=s, yb=yb: nc.scalar.activation(out=tmpB[s][:], in_=yb[:], func=AF.Square),
                     reads=[ykey], writes=[("tmpB", s)])
                p.op("dve", lambda s=s: nc.vector.tensor_scalar(out=tmpB[s][:], in0=tmpB[s][:], scalar1=0.044715,
                                                                scalar2=1.0, op0=ALU.mult, op1=ALU.add),
                     reads=[("tmpB", s)], writes=[("tmpB", s)])
                p.op("dve", lambda s=s, yb=yb: nc.vector.tensor_tensor(out=tmpB[s][:], in0=tmpB[s][:], in1=yb[:],
                                                                       op=ALU.mult),
                     reads=[("tmpB", s), ykey], writes=[("tmpB", s)])
                p.op("act", lambda s=s: nc.scalar.activation(out=tmpB[s][:], in_=tmpB[s][:], func=AF.Sigmoid,
                                                             scale=2.0 * GELU_C),
                     reads=[("tmpB", s)], writes=[("tmpB", s)])
                p.op("dve", lambda s=s, yb=yb, ft=ft: nc.vector.tensor_tensor(out=lhsA[:, ft, :], in0=yb[:],
                                                                              in1=tmpB[s][:], op=ALU.mult),
                     reads=[ykey, ("tmpB", s)], writes=[("lhsA", ft)])
            for j in range(DT):
                def cons_gate(ps, pk, j=j):
                    s = j % 2
                    p.op("act", lambda: nc.scalar.activation(out=tmpA[s][:], in_=ps, func=AF.Sigmoid,
                                                             bias=cols[:, 40 + j:41 + j]),
                         reads=[pk, "cols"], writes=[("tmpA", s)])

                def cons_val(ps, pk, j=j):
                    s = j % 2
                    p.op("dve", lambda: nc.vector.scalar_tensor_tensor(
                        out=mT[:, j, :], in0=ps, scalar=cols[:, 32 + j:33 + j], in1=tmpA[s][:],
                        op0=ALU.add, op1=ALU.mult), reads=[pk, "cols", ("tmpA", s)], writes=[("mT", j)])
                lk = [("lhsA", k) for k in range(DT)]
                proj_fm(w_glu[8 + j], DT, lambda k: lhsA[:, k, :], lk, cons_gate)
                proj_fm(w_glu[j], DT, lambda k: lhsA[:, k, :], lk, cons_val)

        if has_ffn:
            add_norm_into_h(0)
            norm_h_to(lambda j: xnT[:, j, :], lambda j: ("xnT", j), 8)
            xk = [("xnT", k) for k in range(DT)]
            for f in range(FT):
                wt, wkey = load_w(wgu[f], 2 * DT * 128)
                g = cnt["gu"] % 2
                cnt["gu"] += 1
                for half, pst, nm in ((0, ps_g, "g"), (1, ps_u, "u")):
                    for k in range(DT):
                        p.op("pe", lambda k=k, half=half, pst=pst: nc.tensor.matmul(
                            pst[g][:], lhsT=wt[:, (half * DT + k) * 128:(half * DT + k + 1) * 128],
                            rhs=xnT[:, k, :], start=(k == 0), stop=(k == DT - 1)),
                            reads=[wkey] + xk, writes=[(nm, g)])
                p.op("act", lambda g=g: nc.scalar.activation(out=tmpB[g][:], in_=ps_g[g][:], func=AF.Silu),
                     reads=[("g", g)], writes=[("tmpB", g)])
                p.op("dve", lambda g=g, f=f: nc.vector.tensor_tensor(out=actT[:, f, :], in0=tmpB[g][:],
                                                                     in1=ps_u[g][:], op=ALU.mult),
                     reads=[("tmpB", g), ("u", g)], writes=[("actT", f)])
            ak = [("actT", k) for k in range(FT)]
            for j in range(DT):
                proj_fm(wd[j], FT, lambda k: actT[:, k, :], ak,
                        lambda ps, pk, j=j: evac(mT[:, j, :], ("mT", j), ps, pk))
            add_norm_into_h(16)
            for j in range(DT):
                p.dma("sp", rd("hT_out", hT_out, j, tok), hT[:, j, :], reads=[("hT", j)], is_output=True)

        if next_kind == "proj":
            norm_h_to(lambda j: xnT[:, j, :], lambda j: ("xnT", j), 24)
            xk = [("xnT", k) for k in range(DT)]
            for j in range(49):
                def cons(ps, pk, j=j):
                    o = cnt["o"] % 2
                    cnt["o"] += 1
                    evac(ostg[o][:], ("ostg", o), ps, pk)
                    p.dma("sp" if j % 2 else "act", rd("projT", projT, j, tok), ostg[o][:],
                          reads=[("ostg", o)], is_output=True)
                proj_fm(w_in[j], DT, lambda k: xnT[:, k, :], xk, cons)
        if next_kind == "hn":
            for j in range(DT):
                pass
            norm_h_to(lambda j: mT[:, j, :], lambda j: ("mT", j), 24)
            for j in range(DT):
                p.dma("sp", rd("hn_out", hn_out, j, tok), mT[:, j, :], reads=[("mT", j)], is_output=True)
    if standalone:
        p.finish()
    return nc


def arrange_w(w, n_out_tiles=None):
    K, N = w.shape
    nk = K // 128
    nj = (N + 127) // 128
    if N % 128:
        w = np.concatenate([w, np.zeros((K, nj * 128 - N), w.dtype)], axis=1)
    r = w.reshape(nk, 128, nj, 128).transpose(2, 1, 0, 3).reshape(nj, 128, nk * 128)
    return np.ascontiguousarray(r)


def col_tiles(v):
    return np.ascontiguousarray(v.reshape(-1, 128).T)


T_SEQ = 16384
TB = 512


def make_consts(nc, p):
    io = _SB(nc, "c_iota", [128, 128], F32)
    ident = _SB(nc, "c_ident", [128, 128], F32)
    triu = _SB(nc, "c_triu", [128, 128], F32)
    p.op("pool", lambda: nc.gpsimd.iota(io[:], pattern=[[1, 128]], base=0, channel_multiplier=-1,
                                        allow_small_or_imprecise_dtypes=True), writes=["c_iota"])
    p.op("dve", lambda: nc.vector.tensor_single_scalar(out=ident[:], in_=io[:], scalar=0.0, op=ALU.is_equal),
         reads=["c_iota"], writes=["c_ident"])
    p.op("dve", lambda: nc.vector.tensor_single_scalar(out=triu[:], in_=io[:], scalar=0.0, op=ALU.is_ge),
         reads=["c_iota"], writes=["c_triu"])
    return dict(iota=io, ident=ident, triu=triu)


def conv_silu_block(nc, p, raw, rawkey_fn, cw, cb, cT, ckey_fn, ntiles, tmp, tmpkey):
    for t in range(ntiles):
        p.op("act", lambda t=t: nc.scalar.activation(out=tmp[:], in_=raw[:, t, 0:TB], func=AF.Identity,
                                                     scale=cw[:, t, 0:1]),
             reads=[rawkey_fn(t), "cw"], writes=[tmpkey])
        for k in range(1, 5):
            p.op("dve", lambda t=t, k=k: nc.vector.scalar_tensor_tensor(
                out=tmp[:], in0=raw[:, t, k:k + TB], scalar=cw[:, t, k:k + 1], in1=tmp[:],
                op0=ALU.mult, op1=ALU.add), reads=[rawkey_fn(t), "cw", tmpkey], writes=[tmpkey])
        p.op("act", lambda t=t: nc.scalar.activation(out=cT[:, t, :], in_=tmp[:], func=AF.Silu, bias=cb[:, t:t + 1]),
             reads=[tmpkey, "cb"], writes=[ckey_fn(t)])


def build_ssd(n_blocks=T_SEQ // TB, nc=None, p=None, pfx="", io=None):
    standalone = nc is None
    if standalone:
        nc = bass.Bass("TRN2", target_bir_lowering=False)
        p = Prog(nc)
    T = n_blocks * TB
    din = lambda name, shape: nc.dram_tensor(pfx + name, shape, F32, kind="ExternalInput").ap()
    if io is None:
        xbc = din("xbc", [2, 512, T + 4])
        dtr = din("dtr", [2, 4, T])
    cwd = din("cw", [2, 128, 20])
    cbd = din("cb", [128, 4])
    dtbd = din("dtb", [2, 4, 1])
    alogd = din("alog", [2, 4, 1])
    dskd = din("dsk", [128, 2])
    if io is None:
        yT = nc.dram_tensor("yT", [2, 256, T], F32, kind="ExternalOutput").ap()

    sb = lambda name, shape, dt=F32: _SB(nc, name, shape, dt)
    C = make_consts(nc, p)
    ident, triu = C["ident"], C["triu"]
    sel = sb("sel", [4, 4, 128])
    p.op("pool", lambda: nc.gpsimd.iota(sel[:], pattern=[[1, 4], [0, 128]], base=0, channel_multiplier=-1,
                                        allow_small_or_imprecise_dtypes=True), writes=["sel"])
    p.op("dve", lambda: nc.vector.tensor_single_scalar(out=sel[:], in_=sel[:], scalar=0.0, op=ALU.is_equal),
         reads=["sel"], writes=["sel"])
    ones4 = sb("ones4", [4, 128])
    p.op("pool", lambda: nc.gpsimd.memset(ones4[:], 1.0), writes=["ones4"])

    cb = sb("cb_sb", [128, 4])
    dsk = sb("dsk_sb", [128, 2])
    p.dma("sp", cb[:], cbd, writes=["cb"])
    p.dma("sp", dsk[:], dskd, writes=["dsk"])
    ps = lambda name: _PS(nc, name, [128, 512], F32)
    ps_tr = [ps("ps_tr0"), ps("ps_tr1")]
    ps_fb = [ps("ps_fb0"), ps("ps_fb1")]
    ps_C = [ps("ps_C0"), ps("ps_C1")]
    ps_D = [ps("ps_D0"), ps("ps_D1")]

    def body(p, inst):
        sb = lambda name, shape, dt=F32: _SB(nc, name + '_i%d' % inst, shape, dt)
        cw = sb("cw_sb", [128, 4, 5])
        dtb = sb("dtb_sb", [4, 1])
        acol = sb("acol", [4, 1])

        raw = [sb("raw%d" % i, [128, 4, TB + 4]) for i in range(1)] * 2
        cT = [sb("cT%d" % i, [128, 4, TB]) for i in range(1)] * 2
        ctmp = sb("ctmp", [128, TB])
        dtraw = sb("dtraw", [4, TB])
        dtT = [sb("dtT%d" % i, [4, TB]) for i in range(1)] * 2
        daT = sb("daT", [4, TB])
        csT = [sb("csT%d" % i, [4, TB]) for i in range(1)] * 2
        yTo = [sb("yTo%d" % i, [128, 2, TB]) for i in range(1)] * 2
        S = sb("S", [128, 256])
        S_bf = sb("S_bf", [128, 256], BF16)

        def two(name, shape, dt=F32):
            return [sb(name, shape, dt)] * 2
        x_tok = two("x_tok", [128, 256])
        B_tok = two("B_tok", [128, 128], BF16)
        sm = two("sm", [128, 8])
        CBm = two("CBm", [128, 128])
        Dm = two("Dm", [128, 4, 128])
        G = two("G", [128, 4, 128], BF16)
        eFb = two("eFb", [128, 4, 128])
        CsT = two("CsT", [128, 4, 128], BF16)
        w4 = two("w4", [128, 8])
        xdt = two("xdt", [128, 4, 64], BF16)
        xdtd = two("xdtd", [128, 4, 64], BF16)
        y_tok = two("y_tok", [128, 256])

        p.dma("sp", cw[:].rearrange("p t k -> p (t k)"), cwd[inst], writes=["cw"])
        p.dma("sp", dtb[:], dtbd[inst], writes=["dtb"])
        p.dma("sp", acol[:], alogd[inst], writes=["acol"])
        p.op("act", lambda: nc.scalar.activation(out=acol[:], in_=acol[:], func=AF.Exp), reads=["acol"], writes=["acol"])
        p.op("dve", lambda: nc.vector.tensor_scalar(out=acol[:], in0=acol[:], scalar1=-1.0, scalar2=None, op0=ALU.mult),
             reads=["acol"], writes=["acol"])
        p.op("dve", lambda: nc.vector.memset(S[:], 0.0), writes=["S"])
        p.op("dve", lambda: nc.vector.memset(S_bf[:], 0.0), writes=["S_bf"])
        for b in range(n_blocks):
            bp = 0
            col0 = b * TB
            for t in range(4):
                if io is None:
                    p.dma("sp" if t % 2 else "act", raw[bp][:, t, :],
                          xbc[inst, t * 128:(t + 1) * 128, col0:col0 + TB + 4], writes=[("raw", bp, t)])
                else:
                    io.load(p, inst, b, "raw%d" % t, raw[bp][:, t, :], ("raw", bp, t))
            if io is None:
                p.dma("sp", dtraw[:], dtr[inst, :, col0:col0 + TB], writes=["dtraw"])
            else:
                io.load(p, inst, b, "dt", dtraw[:], "dtraw")
            conv_silu_block(nc, p, raw[bp], lambda t: ("raw", bp, t), cw, cb, cT[bp], lambda t: ("cT", bp, t), 4, ctmp, "ctmp")
            p.op("act", lambda: nc.scalar.activation(out=daT[:], in_=dtraw[:], func=AF.Exp, bias=dtb[:, 0:1]),
                 reads=["dtraw", "dtb"], writes=["daT"])
            p.op("act", lambda: nc.scalar.activation(out=dtT[bp][:], in_=daT[:], func=AF.Ln, bias=1.0),
                 reads=["daT"], writes=[("dtT", bp)])
            p.op("dve", lambda: nc.vector.tensor_scalar(out=daT[:], in0=dtT[bp][:], scalar1=acol[:, 0:1], scalar2=None,
                                                        op0=ALU.mult),
                 reads=[("dtT", bp), "acol"], writes=["daT"])
            for j in range(4):
                cs = slice(j * 128, (j + 1) * 128)
                p.op("dve", lambda cs=cs: nc.vector.tensor_tensor_scan(
                    out=csT[bp][:, cs], data0=ones4[:], data1=daT[:, cs], initial=0.0, op0=ALU.mult, op1=ALU.add),
                    reads=["daT", "ones4"], writes=[("csT", bp)])
            for j in range(4):
                cs = slice(j * 128, (j + 1) * 128)
                c = b * 4 + j
                q = 0
                ctk = [("cT", bp, t) for t in range(4)]
                for t in range(3):
                    p.op("pe", lambda t=t: nc.tensor.transpose(out=ps_tr[inst][:, t * 128:(t + 1) * 128],
                                                               in_=cT[bp][:, t, cs], identity=ident[:]),
                         reads=[("cT", bp, t), "c_ident"], writes=[("ps_tr", inst)])
                p.op("pe", lambda: nc.tensor.transpose(out=ps_tr[inst][:, 384:388], in_=dtT[bp][:, cs],
                                                       identity=ident[0:4, 0:4]),
                     reads=[("dtT", bp), "c_ident"], writes=[("ps_tr", inst)])
                p.op("pe", lambda: nc.tensor.transpose(out=ps_tr[inst][:, 388:392], in_=csT[bp][:, cs],
                                                       identity=ident[0:4, 0:4]),
                     reads=[("csT", bp), "c_ident"], writes=[("ps_tr", inst)])
                p.op("act", lambda: nc.scalar.copy(out=x_tok[q][:], in_=ps_tr[inst][:, 0:256]),
                     reads=[("ps_tr", inst)], writes=[("x_tok", q)])
                p.op("dve", lambda: nc.vector.tensor_copy(out=B_tok[q][:], in_=ps_tr[inst][:, 256:384]),
                     reads=[("ps_tr", inst)], writes=[("B_tok", q)])
                p.op("dve", lambda: nc.vector.tensor_copy(out=sm[q][:], in_=ps_tr[inst][:, 384:392]),
                     reads=[("ps_tr", inst)], writes=[("sm", q)])
                for h in range(4):
                    p.op("pe", lambda h=h: nc.tensor.matmul(ps_fb[inst][:, h * 128:(h + 1) * 128], lhsT=sel[:, h, :],
                                                            rhs=csT[bp][:, cs], start=True, stop=True),
                         reads=["sel", ("csT", bp)], writes=[("ps_fb", inst)])
                p.op("pe", lambda: nc.tensor.matmul(ps_C[inst][:, 256:384], lhsT=cT[bp][:, 2, cs], rhs=cT[bp][:, 3, cs],
                                                    start=True, stop=True),
                     reads=[("cT", bp, 2), ("cT", bp, 3)], writes=[("ps_C", inst)])
                p.op("dve", lambda: nc.vector.tensor_tensor(out=CBm[q][:], in0=ps_C[inst][:, 256:384], in1=triu[:], op=ALU.mult),
                     reads=[("ps_C", inst), "c_triu"], writes=[("CBm", q)])
                fb3 = ps_fb[inst][:].rearrange("p (h l) -> p h l", h=4)
                Ftok = sm[q][:, 4:8]
                p.op("dve", lambda: nc.vector.tensor_tensor(out=Dm[q][:], in0=fb3,
                                                            in1=Ftok.unsqueeze(2).broadcast_to([128, 4, 128]),
                                                            op=ALU.subtract),
                     reads=[("ps_fb", inst), ("sm", q)], writes=[("Dm", q)])
                p.op("dve", lambda: nc.vector.tensor_scalar(out=Dm[q][:], in0=Dm[q][:], scalar1=0.0, scalar2=None, op0=ALU.min),
                     reads=[("Dm", q)], writes=[("Dm", q)])
                p.op("act", lambda: nc.scalar.activation(out=Dm[q][:], in_=Dm[q][:], func=AF.Exp),
                     reads=[("Dm", q)], writes=[("Dm", q)])
                p.op("dve", lambda: nc.vector.scalar_tensor_tensor(
                    out=G[q][:], in0=Dm[q][:], scalar=1.0, in1=CBm[q][:].unsqueeze(1).broadcast_to([128, 4, 128]),
                    op0=ALU.min, op1=ALU.mult), reads=[("Dm", q), ("CBm", q)], writes=[("G", q)])
                p.op("act", lambda: nc.scalar.activation(out=eFb[q][:], in_=fb3, func=AF.Exp),
                     reads=[("ps_fb", inst)], writes=[("eFb", q)])
                p.op("pool", lambda: nc.gpsimd.tensor_tensor(
                    out=CsT[q][:], in0=eFb[q][:], in1=cT[bp][:, 3, cs].unsqueeze(1).broadcast_to([128, 4, 128]),
                    op=ALU.mult), reads=[("eFb", q), ("cT", bp, 3)], writes=[("CsT", q)])
                p.op("dve", lambda: nc.vector.tensor_tensor(out=w4[q][:, 0:4], in0=fb3[:, :, 127], in1=Ftok,
                                                            op=ALU.subtract),
                     reads=[("ps_fb", inst), ("sm", q)], writes=[("w4", q)])
                p.op("act", lambda: nc.scalar.activation(out=w4[q][:, 0:4], in_=w4[q][:, 0:4], func=AF.Exp),
                     reads=[("w4", q)], writes=[("w4", q)])
                p.op("dve", lambda: nc.vector.tensor_tensor(out=w4[q][:, 4:8], in0=w4[q][:, 0:4], in1=sm[q][:, 0:4],
                                                            op=ALU.mult),
                     reads=[("w4", q), ("sm", q)], writes=[("w4", q)])
                x3 = x_tok[q][:].rearrange("p (h e) -> p h e", h=4)
                p.op("dve", lambda: nc.vector.tensor_tensor(out=xdt[q][:], in0=x3,
                                                            in1=sm[q][:, 0:4].unsqueeze(2).broadcast_to([128, 4, 64]),
                                                            op=ALU.mult),
                     reads=[("x_tok", q), ("sm", q)], writes=[("xdt", q)])
                p.op("pool", lambda: nc.gpsimd.tensor_tensor(out=xdtd[q][:], in0=x3,
                                                             in1=w4[q][:, 4:8].unsqueeze(2).broadcast_to([128, 4, 64]),
                                                             op=ALU.mult),
                     reads=[("x_tok", q), ("w4", q)], writes=[("xdtd", q)])
                for h in range(4):
                    hs = slice(h * 64, (h + 1) * 64)
                    p.op("pe", lambda h=h, hs=hs: nc.tensor.matmul(ps_C[inst][:, hs], lhsT=G[q][:, h, :], rhs=xdt[q][:, h, :],
                                                                  start=True, stop=False),
                         reads=[("G", q), ("xdt", q)], writes=[("ps_C", inst)])
                    p.op("pe", lambda h=h, hs=hs: nc.tensor.matmul(ps_C[inst][:, hs], lhsT=CsT[q][:, h, :], rhs=S_bf[:, hs],
                                                                  start=False, stop=True),
                         reads=[("CsT", q), "S_bf"], writes=[("ps_C", inst)])
                p.op("pe", lambda: nc.tensor.matmul(ps_D[inst][:, 0:256], lhsT=B_tok[q][:],
                                                    rhs=xdtd[q][:].rearrange("p h e -> p (h e)"), start=True, stop=True),
                     reads=[("B_tok", q), ("xdtd", q)], writes=[("ps_D", inst)])
                for h in range(4):
                    hs = slice(h * 64, (h + 1) * 64)
                    p.op("dve", lambda h=h, hs=hs: nc.vector.scalar_tensor_tensor(
                        out=S[:, hs], in0=S[:, hs], scalar=eFb[q][:, h, 127:128], in1=ps_D[inst][:, hs],
                        op0=ALU.mult, op1=ALU.add), reads=["S", ("eFb", q), ("ps_D", inst)], writes=["S"])
                p.op("act", lambda: nc.scalar.copy(out=S_bf[:], in_=S[:]), reads=["S"], writes=["S_bf"])
                p.op("act", lambda: nc.scalar.copy(out=y_tok[q][:], in_=ps_C[inst][:, 0:256]), reads=[("ps_C", inst)],
                     writes=[("y_tok", q)])
                for t in range(2):
                    p.op("pe", lambda t=t: nc.tensor.transpose(out=ps_D[inst][:, 256 + t * 128:256 + (t + 1) * 128],
                                                               in_=y_tok[q][:, t * 128:(t + 1) * 128], identity=ident[:]),
                         reads=[("y_tok", q), "c_ident"], writes=[("ps_D", inst)])
                for t in range(2):
                    if inst == 0:
                        p.op("dve", lambda t=t: nc.vector.scalar_tensor_tensor(
                            out=yTo[bp][:, t, cs], in0=cT[bp][:, t, cs], scalar=dsk[:, t:t + 1],
                            in1=ps_D[inst][:, 256 + t * 128:256 + (t + 1) * 128], op0=ALU.mult, op1=ALU.add),
                            reads=[("cT", bp, t), "dsk", ("ps_D", inst)], writes=[("yTo", bp)])
                    else:
                        p.op("dve", lambda t=t: nc.vector.tensor_copy(out=yTo[bp][:, t, cs],
                                                                      in_=ps_D[inst][:, 256 + t * 128:256 + (t + 1) * 128]),
                             reads=[("ps_D", inst)], writes=[("yTo", bp)])
            for t in range(2):
                if io is None:
                    p.dma("sp", yT[inst, t * 128:(t + 1) * 128, col0:col0 + TB], yTo[bp][:, t, :],
                          reads=[("yTo", bp)], is_output=True)
                else:
                    io.store(p, inst, b, "y%d" % t, yTo[bp][:, t, :], ("yTo", bp))
    SHARED = {"c_iota", "c_ident", "c_triu", "sel", "ones4", "cb", "dsk"}
    run_interleaved(p, [lambda v: body(v, 0), lambda v: body(v, 1)], SHARED)
    if standalone:
        p.finish()
    return nc


CH = 64
EPS = 1e-6


def build_gdn(n_blocks=T_SEQ // TB, nc=None, p=None, pfx="", io=None):
    standalone = nc is None
    if standalone:
        nc = bass.Bass("TRN2", target_bir_lowering=False)
        p = Prog(nc)
    T = n_blocks * TB
    din = lambda name, shape: nc.dram_tensor(pfx + name, shape, F32, kind="ExternalInput").ap()
    if io is None:
        qkv = din("qkv", [2, 512, T + 4])
        abr = din("abr", [2, 2, 2, T])
    cwd = din("cw", [2, 128, 20])
    cbd = din("cb", [128, 4])
    dtbd = din("dtb", [2, 2, 1])
    alogd = din("alog", [2, 2, 1])
    if io is None:
        oT = nc.dram_tensor("oT", [2, 256, T], F32, kind="ExternalOutput").ap()

    sb = lambda name, shape, dt=F32: _SB(nc, name, shape, dt)
    C = make_consts(nc, p)
    ident, iot = C["ident"], C["iota"]

    def blockdiag(B, name):
        nb = 128 // B
        E = sb("E" + name, [nb, 128])
        E2 = sb("E2" + name, [nb, 128])
        p.op("pool", lambda: nc.gpsimd.iota(E[:], pattern=[[1, 128]], base=0, channel_multiplier=-B,
                                            allow_small_or_imprecise_dtypes=True), writes=["E" + name])
        p.op("dve", lambda: nc.vector.tensor_single_scalar(out=E2[:], in_=E[:], scalar=float(B), op=ALU.is_lt),
             reads=["E" + name], writes=["E2" + name])
        p.op("dve", lambda: nc.vector.tensor_single_scalar(out=E[:], in_=E[:], scalar=0.0, op=ALU.is_ge),
             reads=["E" + name], writes=["E" + name])
        p.op("dve", lambda: nc.vector.tensor_tensor(out=E[:], in0=E[:], in1=E2[:], op=ALU.mult),
             reads=["E" + name, "E2" + name], writes=["E" + name])
        m = sb("bd" + name, [128, 128])
        pst = ps_n[0]
        p.op("pe", lambda: nc.tensor.matmul(pst[:, 0:128], lhsT=E[:], rhs=E[:], start=True, stop=True),
             reads=["E" + name], writes=[("ps_Q0", 0)])
        p.op("dve", lambda: nc.vector.tensor_copy(out=m[:], in_=pst[:, 0:128]), reads=[("ps_Q0", 0)], writes=["bd" + name])
        return m

    ps = lambda name: _PS(nc, name, [128, 512], F32)
    ps_Q = [[ps("ps_Q%d_%d" % (k, i)) for i in range(2)] for k in range(4)]
    ps_n = [ps_Q[0][0], ps_Q[1][0]]

    bd16 = blockdiag(16, "16")
    bd32 = blockdiag(32, "32")
    bd64 = blockdiag(64, "64")
    m32 = sb("m32", [128, 128])
    m64 = sb("m64", [128, 128])
    MUi = sb("MUi", [128, 128])
    MLs = sb("MLs", [128, 128])
    p.op("dve", lambda: nc.vector.tensor_tensor(out=m32[:], in0=bd32[:], in1=bd16[:], op=ALU.subtract),
         reads=["bd32", "bd16"], writes=["m32"])
    p.op("dve", lambda: nc.vector.tensor_tensor(out=m64[:], in0=bd64[:], in1=bd32[:], op=ALU.subtract),
         reads=["bd64", "bd32"], writes=["m64"])
    p.op("dve", lambda: nc.vector.tensor_single_scalar(out=MUi[:], in_=iot[:], scalar=0.0, op=ALU.is_ge),
         reads=["c_iota"], writes=["MUi"])
    p.op("dve", lambda: nc.vector.tensor_tensor(out=MUi[:], in0=MUi[:], in1=bd64[:], op=ALU.mult),
         reads=["MUi", "bd64"], writes=["MUi"])
    p.op("dve", lambda: nc.vector.tensor_single_scalar(out=MLs[:], in_=iot[:], scalar=0.0, op=ALU.is_lt),
         reads=["c_iota"], writes=["MLs"])
    p.op("dve", lambda: nc.vector.tensor_tensor(out=MLs[:], in0=MLs[:], in1=bd64[:], op=ALU.mult),
         reads=["MLs", "bd64"], writes=["MLs"])
    ones2 = sb("ones2", [2, 128])
    p.op("pool", lambda: nc.gpsimd.memset(ones2[:], 1.0), writes=["ones2"])
    onesb = sb("onesb", [128, 128], BF16)
    p.op("pool", lambda: nc.gpsimd.memset(onesb[:], 1.0), writes=["onesb"])

    cb = sb("cb_sb", [128, 4])
    p.dma("sp", cb[:], cbd, writes=["cb"])
    def body(p, inst):
        sb = lambda name, shape, dt=F32: _SB(nc, name + '_i%d' % inst, shape, dt)
        cw = sb("cw_sb", [128, 4, 5])
        dtb = sb("dtb_sb", [2, 1])
        acol = sb("acol", [2, 1])

        raw = [sb("raw%d" % i, [128, 4, TB + 4]) for i in range(1)] * 2
        cT = [sb("cT%d" % i, [128, 4, TB]) for i in range(1)] * 2
        ctmp = sb("ctmp", [128, TB])
        sqb = sb("sqb", [128, TB], BF16)
        rs = sb("rs", [128, TB])
        qT2 = [sb("qT2_%d" % i, [128, TB // CH, 2, CH]) for i in range(1)] * 2
        kT2 = [sb("kT2_%d" % i, [128, TB // CH, 2, CH]) for i in range(1)] * 2
        vT2 = [sb("vT2_%d" % i, [128, TB // CH, 2, CH]) for i in range(1)] * 2
        araw = sb("araw", [2, TB])
        braw = sb("braw", [2, TB])
        gT = sb("gT", [2, TB])
        gcsT = sb("gcsT", [2, TB])
        betaT = sb("betaT", [2, TB])
        glT = sb("glT", [2, TB])
        Mg = [sb("Mg%d" % i, [2, TB // CH, 2, CH]) for i in range(1)] * 2
        Mb = [sb("Mb%d" % i, [2, TB // CH, 2, CH]) for i in range(1)] * 2
        Ml = [sb("Ml%d" % i, [2, TB // CH, 2, CH]) for i in range(1)] * 2
        oTo = [sb("oTo%d" % i, [128, 2, TB]) for i in range(1)] * 2
        S = sb("S", [128, 256])
        S_bf = sb("S_bf", [128, 256], BF16)

        def two(name, shape, dt=F32):
            return [sb(name, shape, dt)] * 2
        colsb = two("colsb", [128, 8])
        egrow = two("egrow", [128, 128])
        D1 = two("D1", [128, 128])
        D2 = two("D2", [128, 128])
        qkT = two("qkT", [128, 128], BF16)
        Am = two("Am", [128, 128])
        ATm = two("ATm", [128, 128])
        X = two("X", [128, 128]); Y = two("Y", [128, 128])
        Ao32 = two("Ao32", [128, 128]); Ao32T = two("Ao32T", [128, 128]); Ao64 = two("Ao64", [128, 128])
        Tm = two("Tm", [128, 128]); Um = two("Um", [128, 128])
        X2 = two("X2", [128, 128]); Y2 = two("Y2", [128, 128])
        Rm = two("Rm", [128, 128]); Pm = two("Pm", [128, 128])
        RHSv = two("RHSv", [128, 128]); RHSw = two("RHSw", [128, 128]); kdec = two("kdec", [128, 2, 128], BF16)
        qgT = two("qgT", [128, 128], BF16)
        u_sb = two("u_sb", [128, 128]); wT_sb = two("wT_sb", [128, 128], BF16)
        vnew = two("vnew", [128, 128], BF16)

        ncnt = [0]

        def mmN(lhsT, rhs, rkeys):
            i = ncnt[0] % 2
            ncnt[0] += 1
            key = ("ps_Q%d" % i, inst)
            p.op("pe", lambda: nc.tensor.matmul(ps_Q[i][inst][:, 0:128], lhsT=lhsT, rhs=rhs, start=True, stop=True),
                 reads=rkeys, writes=[key])
            return ps_Q[i][inst][:, 0:128], key

        def ev_copy(dst, dkey, src, skey):
            p.op("act", lambda: nc.scalar.copy(out=dst, in_=src), reads=[skey], writes=[dkey])

        def ev_comb(dst, dkey, a, akey, src, skey, op):
            p.op("dve", lambda: nc.vector.tensor_tensor(out=dst, in0=a, in1=src, op=op), reads=[akey, skey], writes=[dkey])

        p.dma("sp", cw[:].rearrange("p t k -> p (t k)"), cwd[inst], writes=["cw"])
        p.dma("sp", dtb[:], dtbd[inst], writes=["dtb"])
        p.dma("sp", acol[:], alogd[inst], writes=["acol"])
        p.op("act", lambda: nc.scalar.activation(out=acol[:], in_=acol[:], func=AF.Exp), reads=["acol"], writes=["acol"])
        p.op("dve", lambda: nc.vector.tensor_scalar(out=acol[:], in0=acol[:], scalar1=-1.0, scalar2=None, op0=ALU.mult),
             reads=["acol"], writes=["acol"])
        p.op("dve", lambda: nc.vector.memset(S[:], 0.0), writes=["S"])
        p.op("dve", lambda: nc.vector.memset(S_bf[:], 0.0), writes=["S_bf"])
        for b in range(n_blocks):
            bp = 0
            col0 = b * TB
            for t in range(4):
                if io is None:
                    p.dma("sp" if t % 2 else "act", raw[bp][:, t, :],
                          qkv[inst, t * 128:(t + 1) * 128, col0:col0 + TB + 4], writes=[("raw", bp, t)])
                else:
                    io.load(p, inst, b, "raw%d" % t, raw[bp][:, t, :], ("raw", bp, t))
            if io is None:
                p.dma("sp", araw[:], abr[inst, 0, :, col0:col0 + TB], writes=["araw"])
                p.dma("sp", braw[:], abr[inst, 1, :, col0:col0 + TB], writes=["braw"])
            else:
                io.load(p, inst, b, "a", araw[:], "araw")
                io.load(p, inst, b, "b", braw[:], "braw")
            conv_silu_block(nc, p, raw[bp], lambda t: ("raw", bp, t), cw, cb, cT[bp], lambda t: ("cT", bp, t), 4, ctmp, "ctmp")
            for t, dst, dname, sc in ((0, qT2[bp], "qT2", 128.0 ** -0.5), (1, kT2[bp], "kT2", 1.0)):
                p.op("act", lambda t=t: nc.scalar.activation(out=sqb[:], in_=cT[bp][:, t, :], func=AF.Square),
                     reads=[("cT", bp, t)], writes=["sqb"])
                p.op("pe", lambda: nc.tensor.matmul(ps_Q[0][inst][:], lhsT=onesb[:], rhs=sqb[:], start=True, stop=True),
                     reads=["onesb", "sqb"], writes=[("ps_Q0", inst)])
                p.op("act", lambda: nc.scalar.activation(out=rs[:], in_=ps_Q[0][inst][:], func=AF.Ln, bias=EPS),
                     reads=[("ps_Q0", inst)], writes=["rs"])
                p.op("act", lambda: nc.scalar.activation(out=rs[:], in_=rs[:], func=AF.Exp, scale=-0.5),
                     reads=["rs"], writes=["rs"])
                for vh in range(2):
                    p.op("dve", lambda t=t, dst=dst, sc=sc, vh=vh: nc.vector.scalar_tensor_tensor(
                        out=dst[:, :, vh, :], in0=cT[bp][:, t, :].rearrange("p (c i) -> p c i", i=CH), scalar=sc,
                        in1=rs[:].rearrange("p (c i) -> p c i", i=CH), op0=ALU.mult, op1=ALU.mult),
                        reads=[("cT", bp, t), "rs"], writes=[(dname, bp)])
            for vh in range(2):
                p.op("pool", lambda vh=vh: nc.gpsimd.tensor_copy(
                    out=vT2[bp][:, :, vh, :], in_=cT[bp][:, 2 + vh, :].rearrange("p (c i) -> p c i", i=CH)),
                    reads=[("cT", bp, 2 + vh)], writes=[("vT2", bp)])
            p.op("act", lambda: nc.scalar.activation(out=gT[:], in_=araw[:], func=AF.Exp, bias=dtb[:, 0:1]),
                 reads=["araw", "dtb"], writes=["gT"])
            p.op("act", lambda: nc.scalar.activation(out=gT[:], in_=gT[:], func=AF.Ln, bias=1.0),
                 reads=["gT"], writes=["gT"])
            p.op("dve", lambda: nc.vector.tensor_scalar(out=gT[:], in0=gT[:], scalar1=acol[:, 0:1], scalar2=None,
                                                        op0=ALU.mult), reads=["gT", "acol"], writes=["gT"])
            p.op("act", lambda: nc.scalar.activation(out=betaT[:], in_=braw[:], func=AF.Sigmoid),
                 reads=["braw"], writes=["betaT"])
            for j in range(TB // CH):
                cs = slice(j * CH, (j + 1) * CH)
                p.op("dve", lambda cs=cs: nc.vector.tensor_tensor_scan(
                    out=gcsT[:, cs], data0=ones2[:, 0:CH], data1=gT[:, cs], initial=0.0, op0=ALU.mult, op1=ALU.add),
                    reads=["gT", "ones2"], writes=["gcsT"])
            for j in range(TB // CH):
                cs = slice(j * CH, (j + 1) * CH)
                e = (j + 1) * CH - 1
                p.op("pool", lambda cs=cs, e=e: nc.gpsimd.tensor_copy(out=glT[:, cs],
                                                                     in_=gcsT[:, e:e + 1].broadcast_to([2, CH])),
                     reads=["gcsT"], writes=["glT"])
            for src, dst, nm in ((gcsT, Mg[bp], "Mg"), (betaT, Mb[bp], "Mb"), (glT, Ml[bp], "Ml")):
                for vh in range(2):
                    p.op("dve", lambda src=src, dst=dst, vh=vh: nc.vector.tensor_scalar(
                        out=dst[:, :, vh, :], in0=src[:].rearrange("p (c i) -> p c i", i=CH), scalar1=ident[0:2, vh:vh + 1],
                        scalar2=None, op0=ALU.mult),
                        reads=[src is gcsT and "gcsT" or (src is betaT and "betaT" or "glT"), "c_ident"],
                        writes=[(nm, bp)])

            for j in range(TB // CH):
                cs = slice(j * CH, (j + 1) * CH)
                c = b * (TB // CH) + j
                q = 0
                for i, (M, nm) in enumerate(((Mg[bp], "Mg"), (Mb[bp], "Mb"), (Ml[bp], "Ml"))):
                    p.op("pe", lambda i=i, M=M: nc.tensor.matmul(ps_Q[0][inst][:, 384 + 2 * i:386 + 2 * i], lhsT=M[:, j, :, :].rearrange("p v i -> p (v i)"), rhs=ones2[:, 0:2],
                                                                 start=True, stop=True),
                         reads=[(nm, bp), "ones2"], writes=[("ps_Q0", inst)])
                p.op("pe", lambda: nc.tensor.matmul(ps_Q[1][inst][:, 384:512], lhsT=ones2[:], rhs=Mg[bp][:, j, :, :].rearrange("p v i -> p (v i)"),
                                                    start=True, stop=True),
                     reads=[("Mg", bp), "ones2"], writes=[("ps_Q1", inst)])
                cq = colsb[q]
                ck = ("colsb", q)
                p.op("dve", lambda: nc.vector.tensor_copy(out=cq[:, 0:3], in_=ps_Q[0][inst][:, 384:390].rearrange("p (a b) -> p a b", b=2)[:, :, 0]),
                     reads=[("ps_Q0", inst)], writes=[ck])
                p.op("act", lambda: nc.scalar.activation(out=cq[:, 3:4], in_=cq[:, 0:1], func=AF.Exp), reads=[ck], writes=[ck])
                p.op("dve", lambda: nc.vector.tensor_tensor(out=cq[:, 4:5], in0=cq[:, 1:2], in1=cq[:, 3:4], op=ALU.mult),
                     reads=[ck], writes=[ck])
                p.op("dve", lambda: nc.vector.tensor_tensor(out=cq[:, 5:6], in0=cq[:, 2:3], in1=cq[:, 0:1], op=ALU.subtract),
                     reads=[ck], writes=[ck])
                p.op("act", lambda: nc.scalar.activation(out=cq[:, 5:6], in_=cq[:, 5:6], func=AF.Exp), reads=[ck], writes=[ck])
                p.op("dve", lambda: nc.vector.tensor_scalar(out=cq[:, 6:7], in0=cq[:, 0:1], scalar1=-1.0, scalar2=None,
                                                            op0=ALU.mult), reads=[ck], writes=[ck])
                p.op("act", lambda: nc.scalar.activation(out=egrow[q][:], in_=ps_Q[1][inst][:, 384:512], func=AF.Exp),
                     reads=[("ps_Q1", inst)], writes=[("egrow", q)])
                p.op("dve", lambda: nc.vector.tensor_scalar(out=D1[q][:], in0=ps_Q[1][inst][:, 384:512], scalar1=cq[:, 0:1],
                                                            scalar2=0.0, op0=ALU.subtract, op1=ALU.max),
                     reads=[("ps_Q1", inst), ck], writes=[("D1", q)])
                p.op("dve", lambda: nc.vector.tensor_scalar(out=D2[q][:], in0=ps_Q[1][inst][:, 384:512], scalar1=cq[:, 0:1],
                                                            scalar2=0.0, op0=ALU.subtract, op1=ALU.min),
                     reads=[("ps_Q1", inst), ck], writes=[("D2", q)])
                p.op("act", lambda: nc.scalar.activation(out=D1[q][:], in_=D1[q][:], func=AF.Exp, scale=-1.0),
                     reads=[("D1", q)], writes=[("D1", q)])
                p.op("act", lambda: nc.scalar.activation(out=D2[q][:], in_=D2[q][:], func=AF.Exp),
                     reads=[("D2", q)], writes=[("D2", q)])
                p.op("dve", lambda: nc.vector.scalar_tensor_tensor(out=D1[q][:], in0=D1[q][:], scalar=1.0, in1=MLs[:],
                                                                   op0=ALU.min, op1=ALU.mult),
                     reads=[("D1", q), "MLs"], writes=[("D1", q)])
                p.op("dve", lambda: nc.vector.scalar_tensor_tensor(out=D2[q][:], in0=D2[q][:], scalar=1.0, in1=MUi[:],
                                                                   op0=ALU.min, op1=ALU.mult),
                     reads=[("D2", q), "MUi"], writes=[("D2", q)])
                kk = kT2[bp][:, j, :, :].rearrange("p v i -> p (v i)")
                qq = qT2[bp][:, j, :, :].rearrange("p v i -> p (v i)")
                p.op("pe", lambda: nc.tensor.matmul(ps_Q[1][inst][:, 128:256], lhsT=kk, rhs=kk, start=True, stop=True),
                     reads=[("kT2", bp)], writes=[("ps_Q1", inst)])
                p.op("pe", lambda: nc.tensor.matmul(ps_Q[1][inst][:, 256:384], lhsT=kk, rhs=qq, start=True, stop=True),
                     reads=[("kT2", bp), ("qT2", bp)], writes=[("ps_Q1", inst)])
                p.op("dve", lambda: nc.vector.scalar_tensor_tensor(out=Am[q][:], in0=D1[q][:], scalar=cq[:, 1:2],
                                                                   in1=ps_Q[1][inst][:, 128:256], op0=ALU.mult, op1=ALU.mult),
                     reads=[("D1", q), ck, ("ps_Q1", inst)], writes=[("Am", q)])
                p.op("dve", lambda: nc.vector.tensor_tensor(out=qkT[q][:], in0=D2[q][:], in1=ps_Q[1][inst][:, 256:384], op=ALU.mult),
                     reads=[("D2", q), ("ps_Q1", inst)], writes=[("qkT", q)])
                p.op("pe", lambda: nc.tensor.transpose(out=ps_Q[0][inst][:, 128:256],
                                                       in_=vT2[bp][:, j, :, :].rearrange("p v i -> p (v i)"), identity=ident[:]),
                     reads=[("vT2", bp), "c_ident"], writes=[("ps_Q0", inst)])
                p.op("pe", lambda: nc.tensor.transpose(out=ps_Q[0][inst][:, 256:384], in_=kk, identity=ident[:]),
                     reads=[("kT2", bp), "c_ident"], writes=[("ps_Q0", inst)])
                p.op("dve", lambda: nc.vector.tensor_scalar(out=RHSv[q][:], in0=ps_Q[0][inst][:, 128:256], scalar1=cq[:, 1:2],
                                                            scalar2=None, op0=ALU.mult),
                     reads=[("ps_Q0", inst), ck], writes=[("RHSv", q)])
                p.op("dve", lambda: nc.vector.tensor_scalar(out=RHSw[q][:], in0=ps_Q[0][inst][:, 256:384], scalar1=cq[:, 4:5],
                                                            scalar2=None, op0=ALU.mult),
                     reads=[("ps_Q0", inst), ck], writes=[("RHSw", q)])
                for vh in range(2):
                    p.op("dve", lambda vh=vh: nc.vector.tensor_scalar(
                        out=kdec[q][:, vh, :], in0=ps_Q[0][inst][:, 256:384], scalar1=cq[:, 5:6],
                        scalar2=bd64[:, 64 * vh:64 * vh + 1], op0=ALU.mult, op1=ALU.mult),
                        reads=[("ps_Q0", inst), ck, "bd64"], writes=[("kdec", q)])
                p.op("pool", lambda: nc.gpsimd.tensor_tensor(
                    out=qgT[q][:], in0=qq, in1=egrow[q][:], op=ALU.mult),
                     reads=[("qT2", bp), ("egrow", q)], writes=[("qgT", q)])
                ni = ncnt[0] % 2
                ncnt[0] += 1
                pa, pk = ps_Q[ni][inst][:, 0:128], ("ps_Q%d" % ni, inst)
                p.op("pe", lambda: nc.tensor.transpose(out=pa, in_=Am[q][:], identity=ident[:]),
                     reads=[("Am", q), "c_ident"], writes=[pk])
                ev_copy(ATm[q][:], ("ATm", q), pa, pk)
                for dst, nm, src, snm, msk, mnm in ((X, "X", Am, "Am", bd16, "bd16"), (Y, "Y", ATm, "ATm", bd16, "bd16"),
                                                    (Ao32, "Ao32", Am, "Am", m32, "m32"),
                                                    (Ao32T, "Ao32T", ATm, "ATm", m32, "m32"),
                                                    (Ao64, "Ao64", Am, "Am", m64, "m64")):
                    p.op("pool", lambda dst=dst, src=src, msk=msk: nc.gpsimd.tensor_tensor(out=dst[q][:], in0=src[q][:],
                                                                                          in1=msk[:], op=ALU.mult),
                         reads=[(snm, q), mnm], writes=[(nm, q)])
                Tq, Uq = Tm[q], Um[q]
                tk, uk = ("Tm", q), ("Um", q)
                p.op("dve", lambda: nc.vector.tensor_tensor(out=Tq[:], in0=ident[:], in1=X[q][:], op=ALU.subtract),
                     reads=["c_ident", ("X", q)], writes=[tk])
                p.op("dve", lambda: nc.vector.tensor_tensor(out=Uq[:], in0=ident[:], in1=Y[q][:], op=ALU.subtract),
                     reads=["c_ident", ("Y", q)], writes=[uk])
                xa, ya, xk_, yk_ = X[q], Y[q], ("X", q), ("Y", q)
                xb, yb, xbk, ybk = X2[q], Y2[q], ("X2", q), ("Y2", q)
                for lvl in range(3):
                    pa, pk = mmN(ya[:], xa[:], [yk_, xk_])
                    ev_copy(xb[:], xbk, pa, pk)
                    if lvl < 2:
                        pa, pk = mmN(xa[:], ya[:], [xk_, yk_])
                        ev_copy(yb[:], ybk, pa, pk)
                    pa, pk = mmN(Uq[:], xb[:], [uk, xbk])
                    pb, pkb = mmN(xb[:], Uq[:], [xbk, uk])
                    ev_comb(Tq[:], tk, Tq[:], tk, pa, pk, ALU.add)
                    ev_comb(Uq[:], uk, Uq[:], uk, pb, pkb, ALU.add)
                    xa, ya, xk_, yk_, xb, yb, xbk, ybk = xb, yb, xbk, ybk, xa, ya, xk_, yk_
                pa, pk = mmN(Ao32T[q][:], Tq[:], [("Ao32T", q), tk])
                ev_copy(Rm[q][:], ("Rm", q), pa, pk)
                pa, pk = mmN(Ao32[q][:], Uq[:], [("Ao32", q), uk])
                ev_copy(Pm[q][:], ("Pm", q), pa, pk)
                pa, pk = mmN(Uq[:], Rm[q][:], [uk, ("Rm", q)])
                pb, pkb = mmN(Tq[:], Pm[q][:], [tk, ("Pm", q)])
                ev_comb(Tq[:], tk, Tq[:], tk, pa, pk, ALU.subtract)
                ev_comb(Uq[:], uk, Uq[:], uk, pb, pkb, ALU.subtract)
                pa, pk = mmN(Ao64[q][:], Uq[:], [("Ao64", q), uk])
                ev_copy(Pm[q][:], ("Pm", q), pa, pk)
                pb, pkb = mmN(Tq[:], Pm[q][:], [tk, ("Pm", q)])
                ev_comb(Uq[:], uk, Uq[:], uk, pb, pkb, ALU.subtract)
                pa, pk = mmN(Uq[:], RHSv[q][:], [uk, ("RHSv", q)])
                ev_copy(u_sb[q][:], ("u_sb", q), pa, pk)
                pa, pk = mmN(RHSw[q][:], Uq[:], [("RHSw", q), uk])
                ev_copy(wT_sb[q][:], ("wT_sb", q), pa, pk)
                for vh in range(2):
                    hs = slice(vh * 64, (vh + 1) * 64)
                    vs = slice(vh * 128, (vh + 1) * 128)
                    p.op("pe", lambda hs=hs, vs=vs: nc.tensor.matmul(ps_Q[2][inst][:, vs], lhsT=wT_sb[q][:], rhs=S_bf[:, vs],
                                                                    start=True, stop=True),
                         reads=[("wT_sb", q), "S_bf"], writes=[("ps_Q2", inst)])
                for vh in range(2):
                    hs = slice(vh * 64, (vh + 1) * 64)
                    vs = slice(vh * 128, (vh + 1) * 128)
                    p.op("dve", lambda hs=hs, vs=vs: nc.vector.tensor_tensor(out=vnew[q][hs, :], in0=u_sb[q][hs, :],
                                                                            in1=ps_Q[2][inst][hs, vs], op=ALU.subtract),
                         reads=[("u_sb", q), ("ps_Q2", inst)], writes=[("vnew", q)])
                for vh in range(2):
                    hs = slice(vh * 64, (vh + 1) * 64)
                    vs = slice(vh * 128, (vh + 1) * 128)
                    oc = slice(256 + vh * 64, 256 + (vh + 1) * 64)
                    p.op("pe", lambda hs=hs, vs=vs, oc=oc: nc.tensor.matmul(ps_Q[2][inst][:, oc], lhsT=S_bf[:, vs], rhs=qgT[q][:, hs],
                                                                           start=True, stop=False),
                         reads=["S_bf", ("qgT", q)], writes=[("ps_Q2", inst)])
                    p.op("pe", lambda hs=hs, oc=oc: nc.tensor.matmul(ps_Q[2][inst][:, oc], lhsT=vnew[q][:], rhs=qkT[q][:, hs],
                                                                    start=False, stop=True),
                         reads=[("vnew", q), ("qkT", q)], writes=[("ps_Q2", inst)])
                p.op("act", lambda: nc.scalar.copy(out=oTo[bp][:, :, cs],
                                                   in_=ps_Q[2][inst][:, 256:384].rearrange("p (v i) -> p v i", v=2)),
                     reads=[("ps_Q2", inst)], writes=[("oTo", bp)])
                for vh in range(2):
                    hs = slice(vh * 64, (vh + 1) * 64)
                    vs = slice(vh * 128, (vh + 1) * 128)
                    p.op("pe", lambda hs=hs, vs=vs, vh=vh: nc.tensor.matmul(ps_Q[3][inst][:, vs], lhsT=kdec[q][:, vh, :], rhs=vnew[q][:],
                                                                    start=True, stop=True),
                         reads=[("kdec", q), ("vnew", q)], writes=[("ps_Q3", inst)])
                for vh in range(2):
                    vs = slice(vh * 128, (vh + 1) * 128)
                    e = vh * 64 + 63
                    p.op("dve", lambda vs=vs, e=e: nc.vector.scalar_tensor_tensor(
                        out=S[:, vs], in0=S[:, vs], scalar=egrow[q][:, e:e + 1], in1=ps_Q[3][inst][:, vs],
                        op0=ALU.mult, op1=ALU.add), reads=["S", ("egrow", q), ("ps_Q3", inst)], writes=["S"])
                p.op("act", lambda: nc.scalar.copy(out=S_bf[:], in_=S[:]), reads=["S"], writes=["S_bf"])
            for vh in range(2):
                if io is None:
                    p.dma("sp", oT[inst, vh * 128:(vh + 1) * 128, col0:col0 + TB], oTo[bp][:, vh, :],
                          reads=[("oTo", bp)], is_output=True)
                else:
                    io.store(p, inst, b, "y%d" % vh, oTo[bp][:, vh, :], ("oTo", bp))
    SHARED = {"c_iota", "c_ident", "c_triu", "cb", "bd16", "bd32", "bd64", "m32", "m64", "MUi", "MLs", "ones2", "onesb"}
    run_interleaved(p, [lambda v: body(v, 0), lambda v: body(v, 1)], SHARED)
    if standalone:
        p.finish()
    return nc


T_SEQ = 16384
TC = 512
NP = 4


def build_s5(n_blocks=T_SEQ // TC, nc=None, p=None, pfx="", io=None):
    standalone = nc is None
    if standalone:
        nc = bass.Bass("TRN2", target_bir_lowering=False)
        p = Prog(nc)
    T = n_blocks * TC
    din = lambda name, shape: nc.dram_tensor(pfx + name, shape, F32, kind="ExternalInput").ap()
    if io is None:
        uT = din("uT", [2, 128, T])
    lre_d = din("lam_re", [2, 128, NP])
    lim_d = din("lam_im", [2, 128, NP])
    lst_d = din("log_step", [2, 128, NP])
    bre_d = din("b_re", [128, NP * 128])
    bim_d = din("b_im", [128, NP * 128])
    cre_d = din("c_re", [2, 128, NP * 128])
    cim_d = din("c_im", [2, 128, NP * 128])
    if io is None:
        yT = nc.dram_tensor("yT", [2, 128, T], F32, kind="ExternalOutput").ap()

    sb = lambda name, shape, dt=F32: _SB(nc, name, shape, dt)
    tpr = sb("tpr", [128, TC])
    p.op("pool", lambda: nc.gpsimd.iota(tpr[:], pattern=[[1, TC]], base=0, channel_multiplier=0,
                                        allow_small_or_imprecise_dtypes=True), writes=["tpr"])
    Bre = sb("Bre", [128, NP * 128]); Bim = sb("Bim", [128, NP * 128])
    p.dma("sp", Bre[:], bre_d, writes=["Bre"])
    p.dma("sp", Bim[:], bim_d, writes=["Bim"])
    Bre_b = sb("Bre_b", [128, NP * 128], BF16); Bim_b = sb("Bim_b", [128, NP * 128], BF16)
    p.op("act", lambda: nc.scalar.copy(out=Bre_b[:], in_=Bre[:]), reads=["Bre"], writes=["Bre_b"])
    p.op("act", lambda: nc.scalar.copy(out=Bim_b[:], in_=Bim[:]), reads=["Bim"], writes=["Bim_b"])
    ps = lambda name: _PS(nc, name, [128, 512], F32)
    ps_xr = [ps("ps_xr0"), ps("ps_xr1")]
    ps_xi = [ps("ps_xi0"), ps("ps_xi1")]
    ps_y = [ps("ps_y0"), ps("ps_y1")]

    def body(p, inst):
        sb = lambda name, shape, dt=F32: _SB(nc, name + '_i%d' % inst, shape, dt)
        Cre = sb("Cre", [128, NP * 128]); Cim = sb("Cim", [128, NP * 128])
        Cre_b = sb("Cre_b", [128, NP * 128], BF16); Cim_b = sb("Cim_b", [128, NP * 128], BF16)
        ubb = sb("ubb", [128, TC], BF16)
        lre = sb("lre", [128, NP]); lim = sb("lim", [128, NP]); stp = sb("stp", [128, NP])
        rr = sb("rr", [128, NP]); thn = sb("thn", [128, NP])
        sc = {n: sb("sc_" + n, [128, NP]) for n in ("cos", "sin", "rc", "rs", "zr", "zi", "nzr", "den", "t1", "t2", "y")}
        sy = sb("sy", [128, TC]); sk = sb("sk", [128, TC], I32); sf = sb("sf", [128, TC])
        s2 = sb("s2", [128, TC]); q4 = sb("q4", [128, TC]); c2 = sb("c2", [128, TC])
        tS = sb("tS", [128, TC]); tCo = sb("tCo", [128, TC])
        Ezr = sb("Ezr", [128, NP, TC]); Ezi = sb("Ezi", [128, NP, TC])
        Fr = sb("Fr", [128, NP, TC]); Fi = sb("Fi", [128, NP, TC])
        carry = sb("carry", [128, NP, 2])
        ctmp = sb("carry_tmp", [128, 4])
        ub = [sb("ub%d" % i, [128, TC]) for i in range(1)] * 2
        yo = [sb("yo%d" % i, [128, TC]) for i in range(1)] * 2

        def two(name):
            return [sb(name, [128, TC])] * 2
        m1, m2, m3, m4 = two("m1"), two("m2"), two("m3"), two("m4")
        xr_, xi_ = two("xr_"), two("xi_")
        sr_, si_ = two("sr_"), two("si_")
        d1, d2, d3, d4 = two("d1"), two("d2"), two("d3"), two("d4")
        or_ = [sb("or_", [128, TC], BF16)] * 2
        oi_ = [sb("oi_", [128, TC], BF16)] * 2
        def sincos(y_ap, n, out_s, out_c, ykeys, okeys):
            K = "sincos_tmp"
            p.op("dve", lambda: nc.vector.tensor_copy(out=sk[:, 0:n], in_=y_ap), reads=ykeys, writes=[K])
            p.op("dve", lambda: nc.vector.tensor_copy(out=sf[:, 0:n], in_=sk[:, 0:n]), reads=[K], writes=[K])
            p.op("dve", lambda: nc.vector.tensor_tensor(out=sf[:, 0:n], in0=y_ap, in1=sf[:, 0:n], op=ALU.subtract),
                 reads=ykeys + [K], writes=[K])
            p.op("act", lambda: nc.scalar.activation(out=s2[:, 0:n], in_=sf[:, 0:n], func=AF.Sin, scale=math.pi),
                 reads=[K], writes=[K])
            p.op("act", lambda: nc.scalar.activation(out=q4[:, 0:n], in_=sf[:, 0:n], func=AF.Sin, scale=math.pi / 2),
                 reads=[K], writes=[K])
            p.op("dve", lambda: nc.vector.tensor_tensor(out=c2[:, 0:n], in0=q4[:, 0:n], in1=q4[:, 0:n], op=ALU.mult),
                 reads=[K], writes=[K])
            p.op("dve", lambda: nc.vector.tensor_scalar(out=c2[:, 0:n], in0=c2[:, 0:n], scalar1=-2.0, scalar2=1.0,
                                                        op0=ALU.mult, op1=ALU.add), reads=[K], writes=[K])
            p.op("dve", lambda: nc.vector.scalar_tensor_tensor(out=out_s, in0=s2[:, 0:n], scalar=2.0, in1=c2[:, 0:n],
                                                               op0=ALU.mult, op1=ALU.mult), reads=[K], writes=okeys)
            p.op("dve", lambda: nc.vector.tensor_tensor(out=c2[:, 0:n], in0=s2[:, 0:n], in1=s2[:, 0:n], op=ALU.mult),
                 reads=[K], writes=[K])
            p.op("dve", lambda: nc.vector.tensor_scalar(out=out_c, in0=c2[:, 0:n], scalar1=-2.0, scalar2=1.0,
                                                        op0=ALU.mult, op1=ALU.add), reads=[K], writes=okeys)

        p.dma("sp", lre[:], lre_d[inst], writes=["prm"])
        p.dma("sp", lim[:], lim_d[inst], writes=["prm"])
        p.dma("sp", stp[:], lst_d[inst], writes=["prm"])
        p.dma("sp", Cre[:], cre_d[inst], writes=["Cre"])
        p.dma("sp", Cim[:], cim_d[inst], writes=["Cim"])
        p.op("act", lambda: nc.scalar.copy(out=Cre_b[:], in_=Cre[:]), reads=["Cre"], writes=["Cre_b"])
        p.op("act", lambda: nc.scalar.copy(out=Cim_b[:], in_=Cim[:]), reads=["Cim"], writes=["Cim_b"])
        P = ["prm"]
        p.op("act", lambda: nc.scalar.activation(out=stp[:], in_=stp[:], func=AF.Exp), reads=P, writes=P)
        p.op("dve", lambda: nc.vector.tensor_tensor(out=rr[:], in0=lre[:], in1=stp[:], op=ALU.mult), reads=P, writes=P)
        p.op("act", lambda: nc.scalar.activation(out=rr[:], in_=rr[:], func=AF.Exp), reads=P, writes=P)
        p.op("dve", lambda: nc.vector.tensor_tensor(out=thn[:], in0=lim[:], in1=stp[:], op=ALU.mult), reads=P, writes=P)
        p.op("dve", lambda: nc.vector.tensor_scalar(out=thn[:], in0=thn[:], scalar1=1.0 / (2 * math.pi), scalar2=None,
                                                    op0=ALU.mult), reads=P, writes=P)
        sincos(thn[:], NP, sc["sin"][:], sc["cos"][:], P, P)
        p.op("dve", lambda: nc.vector.tensor_scalar(out=sc["y"][:], in0=thn[:], scalar1=float(TC), scalar2=None,
                                                    op0=ALU.mult), reads=P, writes=P)
        sincos(sc["y"][:], NP, sc["rs"][:], sc["rc"][:], P, P)
        tt = lambda o, a, b, op: p.op("dve", lambda: nc.vector.tensor_tensor(out=o, in0=a, in1=b, op=op), reads=P, writes=P)
        tt(sc["cos"][:], sc["cos"][:], rr[:], ALU.mult)
        tt(sc["sin"][:], sc["sin"][:], rr[:], ALU.mult)
        p.op("dve", lambda: nc.vector.tensor_scalar(out=sc["cos"][:], in0=sc["cos"][:], scalar1=-1.0, scalar2=None,
                                                    op0=ALU.add), reads=P, writes=P)
        tt(sc["t1"][:], lre[:], lre[:], ALU.mult)
        tt(sc["t2"][:], lim[:], lim[:], ALU.mult)
        tt(sc["den"][:], sc["t1"][:], sc["t2"][:], ALU.add)
        p.op("dve", lambda: nc.vector.reciprocal(out=sc["den"][:], in_=sc["den"][:]), reads=P, writes=P)
        tt(sc["t1"][:], sc["cos"][:], lre[:], ALU.mult)
        tt(sc["t2"][:], sc["sin"][:], lim[:], ALU.mult)
        tt(sc["zr"][:], sc["t1"][:], sc["t2"][:], ALU.add)
        tt(sc["zr"][:], sc["zr"][:], sc["den"][:], ALU.mult)
        tt(sc["t1"][:], sc["sin"][:], lre[:], ALU.mult)
        tt(sc["t2"][:], sc["cos"][:], lim[:], ALU.mult)
        tt(sc["zi"][:], sc["t1"][:], sc["t2"][:], ALU.subtract)
        tt(sc["zi"][:], sc["zi"][:], sc["den"][:], ALU.mult)
        p.op("dve", lambda: nc.vector.tensor_scalar(out=sc["nzr"][:], in0=sc["zr"][:], scalar1=-1.0, scalar2=None,
                                                    op0=ALU.mult), reads=P, writes=P)
        for pr in range(NP):
            p.op("dve", lambda pr=pr: nc.vector.tensor_scalar(out=sy[:], in0=tpr[:], scalar1=thn[:, pr:pr + 1], scalar2=None,
                                                              op0=ALU.mult), reads=["tpr"] + P, writes=["sy"])
            sincos(sy[:], TC, tS[:], tCo[:], ["sy"], ["tSC"])
            tk = ("tab", pr)
            p.op("pool", lambda pr=pr: nc.gpsimd.tensor_copy(out=Fr[:, pr, :], in_=tCo[:]), reads=["tSC"], writes=[tk])
            p.op("pool", lambda pr=pr: nc.gpsimd.tensor_copy(out=Fi[:, pr, :], in_=tS[:]), reads=["tSC"], writes=[tk])
            p.op("dve", lambda pr=pr: nc.vector.tensor_scalar(out=Ezr[:, pr, :], in0=tCo[:], scalar1=sc["zr"][:, pr:pr + 1],
                                                              scalar2=None, op0=ALU.mult), reads=["tSC"] + P, writes=[tk])
            p.op("dve", lambda pr=pr: nc.vector.scalar_tensor_tensor(
                out=Ezr[:, pr, :], in0=tS[:], scalar=sc["zi"][:, pr:pr + 1], in1=Ezr[:, pr, :], op0=ALU.mult, op1=ALU.add),
                reads=["tSC", tk] + P, writes=[tk])
            p.op("dve", lambda pr=pr: nc.vector.tensor_scalar(out=Ezi[:, pr, :], in0=tCo[:], scalar1=sc["zi"][:, pr:pr + 1],
                                                              scalar2=None, op0=ALU.mult), reads=["tSC"] + P, writes=[tk])
            p.op("dve", lambda pr=pr: nc.vector.scalar_tensor_tensor(
                out=Ezi[:, pr, :], in0=tS[:], scalar=sc["nzr"][:, pr:pr + 1], in1=Ezi[:, pr, :], op0=ALU.mult, op1=ALU.add),
                reads=["tSC", tk] + P, writes=[tk])
        p.op("dve", lambda: nc.vector.memset(carry[:], 0.0), writes=[("carry", pr) for pr in range(NP)])

        it = 0
        for b in range(n_blocks):
            bp = 0
            col0 = b * TC
            if io is None:
                p.dma("sp", ub[bp][:], uT[inst, :, col0:col0 + TC], writes=[("ub", bp)])
            else:
                io.load(p, inst, b, "u", ub[bp][:], ("ub", bp))
            p.op("act", lambda: nc.scalar.copy(out=ubb[:], in_=ub[bp][:]), reads=[("ub", bp)], writes=["ubb"])
            for pr in range(NP):
                q = 0
                it += 1
                tk = ("tab", pr)
                ck = ("carry", pr)
                ws = slice(pr * 128, (pr + 1) * 128)
                p.op("pe", lambda: nc.tensor.matmul(ps_xr[inst][:], lhsT=Bre_b[:, ws], rhs=ubb[:], start=True, stop=True),
                     reads=["Bre_b", "ubb"], writes=[("ps_xr", inst)])
                p.op("pe", lambda: nc.tensor.matmul(ps_xi[inst][:], lhsT=Bim_b[:, ws], rhs=ubb[:], start=True, stop=True),
                     reads=["Bim_b", "ubb"], writes=[("ps_xi", inst)])
                for o, onm, a, anm, tab in ((m1, "m1", ps_xr, "ps_xr", Ezr), (m2, "m2", ps_xi, "ps_xi", Ezi),
                                            (m3, "m3", ps_xr, "ps_xr", Ezi), (m4, "m4", ps_xi, "ps_xi", Ezr)):
                    p.op("dve", lambda o=o, a=a, tab=tab: nc.vector.tensor_tensor(out=o[q][:], in0=a[inst][:], in1=tab[:, pr, :],
                                                                                 op=ALU.mult),
                         reads=[(anm, inst), tk], writes=[(onm, q)])
                p.op("pool", lambda: nc.gpsimd.tensor_tensor(out=xr_[q][:], in0=m1[q][:], in1=m2[q][:], op=ALU.subtract),
                     reads=[("m1", q), ("m2", q)], writes=[("xr_", q)])
                p.op("pool", lambda: nc.gpsimd.tensor_tensor(out=xi_[q][:], in0=m3[q][:], in1=m4[q][:], op=ALU.add),
                     reads=[("m3", q), ("m4", q)], writes=[("xi_", q)])
                rb = rr[:, pr:pr + 1].broadcast_to([128, TC])
                p.op("dve", lambda: nc.vector.tensor_tensor_scan(out=sr_[q][:], data0=rb, data1=xr_[q][:],
                                                                 initial=carry[:, pr, 0:1], op0=ALU.mult, op1=ALU.add),
                     reads=["prm", ("xr_", q), ck], writes=[("sr_", q)])
                p.op("dve", lambda: nc.vector.tensor_tensor_scan(out=si_[q][:], data0=rb, data1=xi_[q][:],
                                                                 initial=carry[:, pr, 1:2], op0=ALU.mult, op1=ALU.add),
                     reads=["prm", ("xi_", q), ck], writes=[("si_", q)])
                lr, li = sr_[q][:, TC - 1:TC], si_[q][:, TC - 1:TC]
                p.op("dve", lambda: nc.vector.tensor_tensor(out=ctmp[:, 0:1], in0=li, in1=sc["rs"][:, pr:pr + 1], op=ALU.mult),
                     reads=[("si_", q), "prm"], writes=["ctmp"])
                p.op("dve", lambda: nc.vector.tensor_tensor(out=ctmp[:, 1:2], in0=li, in1=sc["rc"][:, pr:pr + 1], op=ALU.mult),
                     reads=[("si_", q), "prm"], writes=["ctmp"])
                p.op("dve", lambda: nc.vector.scalar_tensor_tensor(out=carry[:, pr, 0:1], in0=lr, scalar=sc["rc"][:, pr:pr + 1],
                                                                   in1=ctmp[:, 0:1], op0=ALU.mult, op1=ALU.subtract),
                     reads=[("sr_", q), "prm", "ctmp"], writes=[ck])
                p.op("dve", lambda: nc.vector.scalar_tensor_tensor(out=carry[:, pr, 1:2], in0=lr, scalar=sc["rs"][:, pr:pr + 1],
                                                                   in1=ctmp[:, 1:2], op0=ALU.mult, op1=ALU.add),
                     reads=[("sr_", q), "prm", "ctmp"], writes=[ck])
                for o, onm, a, anm, tab in ((d1, "d1", sr_, "sr_", Fr), (d2, "d2", si_, "si_", Fi),
                                            (d3, "d3", sr_, "sr_", Fi), (d4, "d4", si_, "si_", Fr)):
                    p.op("pool", lambda o=o, a=a, tab=tab: nc.gpsimd.tensor_tensor(out=o[q][:], in0=a[q][:], in1=tab[:, pr, :],
                                                                                  op=ALU.mult),
                         reads=[(anm, q), tk], writes=[(onm, q)])
                p.op("dve", lambda: nc.vector.tensor_tensor(out=or_[q][:], in0=d1[q][:], in1=d2[q][:], op=ALU.subtract),
                     reads=[("d1", q), ("d2", q)], writes=[("or_", q)])
                p.op("dve", lambda: nc.vector.scalar_tensor_tensor(out=oi_[q][:], in0=d3[q][:], scalar=-1.0, in1=d4[q][:],
                                                                   op0=ALU.mult, op1=ALU.subtract),
                     reads=[("d3", q), ("d4", q)], writes=[("oi_", q)])
                p.op("pe", lambda: nc.tensor.matmul(ps_y[inst][:], lhsT=Cre_b[:, ws], rhs=or_[q][:], start=(pr == 0), stop=False),
                     reads=["Cre_b", ("or_", q)], writes=[("ps_y", inst)])
                p.op("pe", lambda: nc.tensor.matmul(ps_y[inst][:], lhsT=Cim_b[:, ws], rhs=oi_[q][:], start=False,
                                                    stop=(pr == NP - 1)),
                     reads=["Cim_b", ("oi_", q)], writes=[("ps_y", inst)])
            p.op("act", lambda: nc.scalar.copy(out=yo[bp][:], in_=ps_y[inst][:]), reads=[("ps_y", inst)], writes=[("yo", bp)])
            if io is None:
                p.dma("sp", yT[inst, :, col0:col0 + TC], yo[bp][:], reads=[("yo", bp)], is_output=True)
            else:
                io.store(p, inst, b, "y", yo[bp][:], ("yo", bp))
    SHARED = {"tpr", "Bre", "Bim", "Bre_b", "Bim_b"}
    run_interleaved(p, [lambda v: body(v, 0), lambda v: body(v, 1)], SHARED)
    if standalone:
        p.finish()
    return nc


NCORES = 8
SEQ = 16384
TOKC = SEQ // NCORES


def _run(nc, in_maps):
    res = run_bass_kernel_spmd(nc, in_maps, core_ids=list(range(NCORES)))
    return [{k: np.asarray(v) for k, v in r.items()} for r in res.results]


def _c(a):
    return np.ascontiguousarray(a, dtype=np.float32)


def _tok(a, c):
    return _c(a[:, c * TOKC:(c + 1) * TOKC])


def _conv_layout(w):
    return _c(w.T.reshape(4, 128, 5).transpose(1, 0, 2).reshape(128, 20))


def _pad2(a):
    return np.pad(a, ((0, 0), (2, 2)))


def _ffn_inputs(i, norm_w, ffn_w_gate_up, ffn_w_down):
    return dict(
        nws=_c(np.concatenate([col_tiles(norm_w[i, k]) for k in (1, 2, 3)], axis=1)),
        wgu=_c(np.concatenate([arrange_w(ffn_w_gate_up[i][:, :DFF]), arrange_w(ffn_w_gate_up[i][:, DFF:])], axis=2)),
        wd=arrange_w(ffn_w_down[i]),
    )


def _ssd_maps(P, j, conv_w, conv_b, dt_bias, a_log, d_skip):
    maps = []
    for g in range(NCORES):
        ch = np.concatenate([np.arange(g * 256, (g + 1) * 256), 2048 + np.arange(g * 128, (g + 1) * 128),
                             3072 + np.arange(g * 128, (g + 1) * 128)])
        xg = P[2048 + ch]
        w = conv_w[j][:, ch]
        maps.append(dict(
            xbc=_c(np.stack([_pad2(xg), _pad2(xg[:, ::-1])])),
            dtr=_c(np.stack([P[6144 + 4 * g:6144 + 4 * g + 4], P[6176 + 4 * g:6176 + 4 * g + 4][:, ::-1]])),
            cw=_c(np.stack([_conv_layout(w), _conv_layout(w[::-1])])),
            cb=_c(conv_b[j][ch].reshape(4, 128).T),
            dtb=_c(dt_bias[j][:, 4 * g:4 * g + 4].reshape(2, 4, 1)),
            alog=_c(a_log[j][:, 4 * g:4 * g + 4].reshape(2, 4, 1)),
            dsk=_c(np.repeat(d_skip[j][4 * g:4 * g + 4], 64).reshape(2, 128).T),
        ))
    return maps


def _gdn_maps(P, conv_w, conv_b, dt_bias, a_log):
    maps = []
    for g in range(NCORES):
        ch = np.concatenate([np.arange(g * 128, (g + 1) * 128), 1024 + np.arange(g * 128, (g + 1) * 128),
                             2048 + np.arange(2 * g * 128, (2 * g + 2) * 128)])
        xg = P[ch]
        w = conv_w[0][:, ch]
        a0, a1 = P[6144 + 2 * g:6144 + 2 * g + 2], P[6144 + 16 + 2 * g:6144 + 16 + 2 * g + 2]
        b0, b1 = P[6176 + 2 * g:6176 + 2 * g + 2], P[6176 + 16 + 2 * g:6176 + 16 + 2 * g + 2]
        maps.append(dict(
            qkv=_c(np.stack([_pad2(xg), _pad2(xg[:, ::-1])])),
            abr=_c(np.stack([np.stack([a0, b0]), np.stack([a1[:, ::-1], b1[:, ::-1]])])),
            cw=_c(np.stack([_conv_layout(w), _conv_layout(w[::-1])])),
            cb=_c(conv_b[0][ch].reshape(4, 128).T),
            dtb=_c(dt_bias[0][:, 2 * g:2 * g + 2].reshape(2, 2, 1)),
            alog=_c(a_log[0][:, 2 * g:2 * g + 2].reshape(2, 2, 1)),
        ))
    return maps


def _pair_cols(a):
    return _c(a.reshape(4, 2, 64).transpose(1, 2, 0).reshape(128, 4))


def _blayout(b):
    out = np.zeros((8, 16, 4, 2, 64), np.float32)
    for g in range(8):
        out[g, :, g // 2, g % 2, :] = b[g].T
    return out.reshape(128, 512)


def _clayout(c):
    out = np.zeros((2, 64, 4, 8, 16), np.float32)
    for g in range(8):
        out[g % 2, :, g // 2, g, :] = c[g].T
    return out.reshape(128, 512)


def _s5_maps(hn, lam_re, lam_im, log_step, b_re, b_im, c_re, c_im):
    maps = []
    for c in range(NCORES):
        gs = slice(8 * c, 8 * c + 8)
        u = hn[c * 128:(c + 1) * 128]
        maps.append(dict(
            uT=_c(np.stack([u, u[:, ::-1]])),
            lam_re=np.stack([_pair_cols(lam_re[0, d, gs]) for d in range(2)]),
            lam_im=np.stack([_pair_cols(lam_im[0, d, gs]) for d in range(2)]),
            log_step=np.stack([_pair_cols(np.repeat(log_step[0, d, gs][:, None], 64, axis=1)) for d in range(2)]),
            b_re=_blayout(b_re[0, gs]), b_im=_blayout(b_im[0, gs]),
            c_re=np.stack([_clayout(c_re[0, d, gs]) for d in range(2)]),
            c_im=np.stack([_clayout(c_im[0, d, gs]) for d in range(2)]),
        ))
    return maps


def _gather_tok(results, key):
    return np.concatenate([r[key] for r in results], axis=1)


def kernel(x, norm_w, ssd_w_in, ssd_conv_w, ssd_conv_b, ssd_dt_bias, ssd_a_log, ssd_d,
           ssd_norm_w, ssd_w_out, gdn_w_in, gdn_conv_w, gdn_conv_b, gdn_dt_bias, gdn_a_log,
           gdn_norm_w, gdn_w_out, s5_lam_re, s5_lam_im, s5_log_step, s5_b_re, s5_b_im,
           s5_c_re, s5_c_im, s5_d, s5_w_glu, s5_b_glu, ffn_w_gate_up, ffn_w_down):
    A = lambda a: np.asarray(a, dtype=np.float32)
    (x, norm_w, ssd_w_in, ssd_conv_w, ssd_conv_b, ssd_dt_bias, ssd_a_log, ssd_d, ssd_norm_w, ssd_w_out, gdn_w_in,
     gdn_conv_w, gdn_conv_b, gdn_dt_bias, gdn_a_log, gdn_norm_w, gdn_w_out, s5_lam_re, s5_lam_im, s5_log_step,
     s5_b_re, s5_b_im, s5_c_re, s5_c_im, s5_d, s5_w_glu, s5_b_glu, ffn_w_gate_up, ffn_w_down) = map(A, (
         x, norm_w, ssd_w_in, ssd_conv_w, ssd_conv_b, ssd_dt_bias, ssd_a_log, ssd_d, ssd_norm_w, ssd_w_out, gdn_w_in,
         gdn_conv_w, gdn_conv_b, gdn_dt_bias, gdn_a_log, gdn_norm_w, gdn_w_out, s5_lam_re, s5_lam_im, s5_log_step,
         s5_b_re, s5_b_im, s5_c_re, s5_c_im, s5_d, s5_w_glu, s5_b_glu, ffn_w_gate_up, ffn_w_down))
    hT = _c(x[0].T)

    nc_ssd = build_ssd()
    common = dict(nw0=col_tiles(norm_w[0, 0]), w_in=arrange_w(ssd_w_in[0]))
    r = _run(build_dense(None, False, "proj"), [dict(hT=_tok(hT, c), **common) for c in range(NCORES)])
    P = _gather_tok(r, "projT")

    r = _run(nc_ssd, _ssd_maps(P, 0, ssd_conv_w, ssd_conv_b, ssd_dt_bias, ssd_a_log, ssd_d))
    yf = np.concatenate([q["yT"][0] for q in r], axis=0)
    yb = np.concatenate([q["yT"][1][:, ::-1] for q in r], axis=0)
    common = dict(w_out=arrange_w(ssd_w_out[0]), mnw=col_tiles(ssd_norm_w[0]), nw0=col_tiles(norm_w[1, 0]),
                  w_in=arrange_w(gdn_w_in[0]), **_ffn_inputs(0, norm_w, ffn_w_gate_up, ffn_w_down))
    r = _run(build_dense("ssd", True, "proj"),
             [dict(hT=_tok(hT, c), mf=_tok(yf, c), mb=_tok(yb, c), zT=_tok(P[0:2048], c), **common) for c in range(NCORES)])
    hT = _gather_tok(r, "hT_out")
    P = _gather_tok(r, "projT")

    r = _run(build_gdn(), _gdn_maps(P, gdn_conv_w, gdn_conv_b, gdn_dt_bias, gdn_a_log))
    yf = np.concatenate([q["oT"][0] for q in r], axis=0)
    yb = np.concatenate([q["oT"][1][:, ::-1] for q in r], axis=0)
    common = dict(w_out=arrange_w(gdn_w_out[0]), mnw=_c(np.tile(gdn_norm_w[0][:, None], (1, 16))),
                  nw0=col_tiles(norm_w[2, 0]), **_ffn_inputs(1, norm_w, ffn_w_gate_up, ffn_w_down))
    r = _run(build_dense("gdn", True, "hn"),
             [dict(hT=_tok(hT, c), mf=_tok(yf, c), mb=_tok(yb, c), zT=_tok(P[4096:6144], c), **common)
              for c in range(NCORES)])
    hT = _gather_tok(r, "hT_out")
    hn = _gather_tok(r, "hn_out")

    r = _run(build_s5(), _s5_maps(hn, s5_lam_re, s5_lam_im, s5_log_step, s5_b_re, s5_b_im, s5_c_re, s5_c_im))
    yf = np.concatenate([q["yT"][0] for q in r], axis=0)
    yb = np.concatenate([q["yT"][1][:, ::-1] for q in r], axis=0)
    common = dict(w_glu=arrange_w(s5_w_glu[0]), b_glu=col_tiles(s5_b_glu[0]), s5d=col_tiles(s5_d[0]),
                  nw0=col_tiles(norm_w[3, 0]), w_in=arrange_w(ssd_w_in[1]),
                  **_ffn_inputs(2, norm_w, ffn_w_gate_up, ffn_w_down))
    r = _run(build_dense("s5", True, "proj"),
             [dict(hT=_tok(hT, c), mf=_tok(yf, c), mb=_tok(yb, c), hnT=_tok(hn, c), **common) for c in range(NCORES)])
    hT = _gather_tok(r, "hT_out")
    P = _gather_tok(r, "projT")

    r = _run(nc_ssd, _ssd_maps(P, 1, ssd_conv_w, ssd_conv_b, ssd_dt_bias, ssd_a_log, ssd_d))
    yf = np.concatenate([q["yT"][0] for q in r], axis=0)
    yb = np.concatenate([q["yT"][1][:, ::-1] for q in r], axis=0)
    common = dict(w_out=arrange_w(ssd_w_out[1]), mnw=col_tiles(ssd_norm_w[1]),
                  **_ffn_inputs(3, norm_w, ffn_w_gate_up, ffn_w_down))
    r = _run(build_dense("ssd", True, None),
             [dict(hT=_tok(hT, c), mf=_tok(yf, c), mb=_tok(yb, c), zT=_tok(P[0:2048], c), **common) for c in range(NCORES)])
    hT = _gather_tok(r, "hT_out")
    return np.ascontiguousarray(hT.T[None].astype(np.float32))
```

```python
import math
import contextlib


import numpy as np
import concourse.bass as bass
import concourse.mybir as mybir
from concourse.bass_utils import run_bass_kernel_spmd

F32 = mybir.dt.float32
BF16 = mybir.dt.bfloat16
I32 = mybir.dt.int32
AF = mybir.ActivationFunctionType
ALU = mybir.AluOpType
AX = mybir.AxisListType


def _is_psum(k):
    n = k[0] if isinstance(k, tuple) else k
    return isinstance(n, str) and n.startswith("ps_")


_STAGE = {"es": None, "pfx": ""}


def _SB(nc, name, shape, dt=None):
    if dt is None:
        dt = F32
    if _STAGE["es"] is None:
        return nc.alloc_sbuf_tensor(name, shape, dt)
    return _STAGE["es"].enter_context(nc.sbuf_tensor(_STAGE["pfx"] + name, shape, dt))


def _PS(nc, name, shape, dt=None):
    if dt is None:
        dt = F32
    if _STAGE["es"] is None:
        return nc.alloc_psum_tensor(name, shape, dt)
    return _STAGE["es"].enter_context(nc.psum_tensor(_STAGE["pfx"] + name, shape, dt))


class Prog:
    NDMA = 4

    def __init__(self, nc):
        self.nc = nc
        self.eng = {"pe": nc.tensor, "dve": nc.vector, "act": nc.scalar,
                    "pool": nc.gpsimd, "sp": nc.sync}
        self.sem = {}
        self.cnt = {}
        for e in ("pe", "dve", "act", "pool"):
            self.sem[e] = nc.alloc_semaphore(name="s_" + e)
            self.cnt[e] = 0
        self.dq = {}
        for q in ("sp", "act", "pool"):
            sems = []
            for i in range(self.NDMA):
                k = "d_%s%d" % (q, i)
                self.sem[k] = nc.alloc_semaphore(name=k)
                self.cnt[k] = 0
                sems.append(k)
            self.dq[q] = [sems, 0]
        self.seen = {e: {} for e in self.eng}
        self.last_w = {}
        self.readers = {}
        self.out_tokens = []

    def _wait(self, e, needs):
        eng = self.eng[e]
        for sk, val in needs.items():
            if e == "pe" and sk == "pe":
                continue
            if self.seen[e].get(sk, 0) >= val:
                continue
            eng.wait_ge(self.sem[sk], val)
            self.seen[e][sk] = val

    def _needs(self, reads, writes, e=None):
        needs = {}

        def add(tok):
            if tok is None:
                return
            sk, v = tok
            if needs.get(sk, 0) < v:
                needs[sk] = v
        for k in reads:
            add(self.last_w.get(k))
            if _is_psum(k):
                for t in self.readers.get(k, ()):
                    if t[0] != e:
                        add(t)
        for k in writes:
            add(self.last_w.get(k))
            for t in self.readers.get(k, ()):
                add(t)
        return needs

    def _commit(self, tok, reads, writes):
        for k in writes:
            self.last_w[k] = tok
            self.readers[k] = []
        for k in reads:
            if k in writes:
                continue
            self.readers.setdefault(k, []).append(tok)
            if len(self.readers[k]) > 12:
                best = {}
                for sk, v in self.readers[k]:
                    if best.get(sk, 0) < v:
                        best[sk] = v
                self.readers[k] = list(best.items())

    def op(self, e, fn, reads=(), writes=()):
        self._wait(e, self._needs(reads, writes, e))
        ins = fn()
        self.cnt[e] += 1
        ins.then_inc(self.sem[e], 1)
        tok = (e, self.cnt[e])
        self._commit(tok, reads, writes)
        return tok

    def dma(self, q, out, in_, reads=(), writes=(), is_output=False, **kw):
        sems, n = self.dq[q]
        sk = sems[n % self.NDMA]
        self.dq[q][1] = n + 1
        needs = self._needs(reads, writes)
        if self.cnt[sk] > 0:
            needs[sk] = max(needs.get(sk, 0), self.cnt[sk])
        self._wait(q, needs)
        ins = self.eng[q].dma_start(out=out, in_=in_, **kw)
        self.cnt[sk] += 16
        ins.then_inc(self.sem[sk], 16)
        tok = (sk, self.cnt[sk])
        self._commit(tok, reads, writes)
        if is_output:
            self.out_tokens.append(tok)
        return tok

    def barrier(self):
        needs = {k: v for k, v in self.cnt.items() if v > 0}
        for e in self.eng:
            self._wait(e, dict(needs))

    def allgather(self, in_ap, out_ap):
        if "cc" not in self.sem:
            self.sem["cc"] = self.nc.alloc_semaphore(name="cc_sem")
            self.cnt["cc"] = 0
        ins = self.nc.gpsimd.collective_compute("AllGather", ALU.bypass, replica_groups=[list(range(8))],
                                                ins=[in_ap.opt()], outs=[out_ap.opt()])
        self.cnt["cc"] += 1
        ins.then_inc(self.sem["cc"], 1)

    def finish(self):
        needs = {}
        for sk, v in self.out_tokens:
            needs[sk] = max(needs.get(sk, 0), v)
        self._wait("sp", needs)
        needs = {e: self.cnt[e] for e in ("pe", "dve", "act", "pool") if self.cnt[e] > 0}
        for q in self.dq:
            for sk in self.dq[q][0]:
                if self.cnt[sk] > 0:
                    needs[sk] = self.cnt[sk]
        self._wait("sp", needs)


class InstView:
    def __init__(self, p, inst, shared):
        self.p, self.inst, self.shared = p, inst, shared

    def k(self, key):
        n = key[0] if isinstance(key, tuple) else key
        if n in self.shared or (isinstance(n, str) and n.startswith("ps_")):
            return key
        return ("I%d" % self.inst, key)

    def op(self, e, fn, reads=(), writes=()):
        r = self.p.op(e, fn, [self.k(x) for x in reads], [self.k(x) for x in writes])
        self.p.baton.step(self.inst)
        return r

    def dma(self, q, out, in_, reads=(), writes=(), **kw):
        r = self.p.dma(q, out, in_, [self.k(x) for x in reads], [self.k(x) for x in writes], **kw)
        self.p.baton.step(self.inst)
        return r


class Baton:
    def __init__(self, n):
        import threading
        self.n = n
        self.sems = [threading.Semaphore(0) for _ in range(n)]
        self.alive = [True] * n
        self.err = None

    def _next(self, i):
        for d in range(1, self.n + 1):
            j = (i + d) % self.n
            if self.alive[j]:
                return j
        return None

    def step(self, i):
        j = self._next(i)
        if j is None or j == i:
            return
        self.sems[j].release()
        self.sems[i].acquire()

    def done(self, i):
        self.alive[i] = False
        j = self._next(i)
        if j is not None:
            self.sems[j].release()


def run_interleaved(p, bodies, shared):
    import threading
    n = len(bodies)
    p.baton = Baton(n)
    errs = []

    def runner(i):
        p.baton.sems[i].acquire()
        try:
            bodies[i](InstView(p, i, shared))
        except BaseException as ex:
            errs.append(ex)
        finally:
            p.baton.done(i)
    ths = [threading.Thread(target=runner, args=(i,)) for i in range(n)]
    for t in ths:
        t.start()
    p.baton.sems[0].release()
    for t in ths:
        t.join()
    if errs:
        raise errs[0]


EPS = 1e-6
TG = 512
NTG = 4
D = 1024
DT = 8
DFF = 2816
FT = 22
GELU_C = 0.7978845608028654


def build_dense(variant, has_ffn, next_kind, nc=None, p=None, pfx="", io=None):
    standalone = nc is None
    if standalone:
        nc = bass.Bass("TRN2", target_bir_lowering=False)
        p = Prog(nc)
    io = io or {}
    NTOK = TG * NTG

    def din(name, shape):
        if name in io:
            return None
        return nc.dram_tensor(pfx + name, shape, F32, kind="ExternalInput").ap()

    def dout(name, shape):
        if name in io:
            return None
        return nc.dram_tensor(pfx + name, shape, F32, kind="ExternalOutput").ap()

    def rd(name, ten, r, tok):
        if name in io:
            return io[name](r, tok)
        return ten[r * 128:(r + 1) * 128, tok]

    hT_in = din("hT", [D, NTOK])
    if variant in ("ssd", "gdn"):
        mf = din("mf", [2048, NTOK])
        mb = din("mb", [2048, NTOK])
        zT = din("zT", [2048, NTOK])
        w_out = din("w_out", [DT, 128, 16 * 128])
        mnw = din("mnw", [128, 16])
    if variant == "s5":
        mf = din("mf", [D, NTOK])
        mb = din("mb", [D, NTOK])
        hn_in = din("hnT", [D, NTOK])
        w_glu = din("w_glu", [16, 128, DT * 128])
        b_glu = din("b_glu", [128, 16])
        s5d = din("s5d", [128, DT])
    if has_ffn:
        nws = din("nws", [128, 3 * DT])
        wgu = din("wgu", [FT, 128, 2 * DT * 128])
        wd = din("wd", [DT, 128, FT * 128])
        hT_out = dout("hT_out", [D, NTOK])
    if next_kind is not None:
        nw0 = din("nw0", [128, DT])
    if next_kind == "proj":
        w_in = din("w_in", [49, 128, DT * 128])
        projT = dout("projT", [49 * 128, NTOK])
    if next_kind == "hn":
        hn_out = dout("hn_out", [D, NTOK])

    sb = lambda name, shape, dt=F32: _SB(nc, name, shape, dt)
    hT = sb("hT_sb", [128, DT, TG])
    mT = sb("mT_sb", [128, DT, TG])
    xnT = sb("xnT", [128, DT, TG], BF16)
    lhsA = sb("lhsA", [128, 16, TG], BF16)
    actT = sb("actT", [128, FT, TG], BF16)
    NW = 3
    wbuf = [sb("wbuf%d" % i, [128, FT * 128], BF16) for i in range(NW)]
    stg = [[sb("stg%d_%d" % (i, s), [128, TG]) for s in range(2)] for i in range(3)]
    tmpA = [sb("tmpA%d" % i, [128, TG]) for i in range(2)]
    tmpB = [sb("tmpB%d" % i, [128, TG]) for i in range(2)]
    yzb = [sb("yzb%d" % i, [128, TG]) for i in range(4)]
    sqb = [sb("sqb%d" % i, [128, TG], BF16) for i in range(2)]
    rstd = sb("rstd", [128, TG])
    ostg = [sb("ostg%d" % i, [128, TG]) for i in range(2)]
    ones = sb("ones", [128, 128], BF16)
    cols = sb("cols", [128, 64])
    ps_ss = _PS(nc, "ps_ss", [128, TG], F32)
    ps_acc = [_PS(nc, "ps_acc%d" % i, [128, TG], F32) for i in range(3)]
    ps_g = [_PS(nc, "ps_g%d" % i, [128, TG], F32) for i in range(2)]
    ps_u = [_PS(nc, "ps_u%d" % i, [128, TG], F32) for i in range(2)]

    p.op("pool", lambda: nc.gpsimd.memset(ones[:], 1.0), writes=["ones"])
    if has_ffn:
        p.dma("sp", cols[:, 0:24], nws, writes=["cols"])
    if next_kind is not None:
        p.dma("sp", cols[:, 24:32], nw0, writes=["cols"])
    if variant in ("ssd", "gdn"):
        p.dma("sp", cols[:, 32:48], mnw, writes=["cols"])
    if variant == "s5":
        p.dma("sp", cols[:, 32:48], b_glu, writes=["cols"])
        p.dma("sp", cols[:, 48:56], s5d, writes=["cols"])

    cnt = {"w": 0, "acc": 0, "gu": 0, "o": 0, "ev": 0}

    def load_w(src, n):
        i = cnt["w"] % NW
        cnt["w"] += 1
        p.dma("pool", wbuf[i][:, 0:n], src, writes=[("w", i)], max_dma_last_dim=4096)
        return wbuf[i], ("w", i)

    def rms_rstd(tiles, n_feat):
        last = len(tiles) - 1
        for i, (ap, key) in enumerate(tiles):
            s = i % 2
            p.op("act", lambda ap=ap, s=s: nc.scalar.activation(out=sqb[s][:], in_=ap, func=AF.Square),
                 reads=[key], writes=[("sq", s)])
            p.op("pe", lambda s=s, i=i: nc.tensor.matmul(ps_ss[:], lhsT=ones[:], rhs=sqb[s][:],
                                                       start=(i == 0), stop=(i == last)),
                 reads=[("sq", s), "ones"], writes=["ps_ss"])
        p.op("act", lambda: nc.scalar.activation(out=rstd[:], in_=ps_ss[:], func=AF.Ln,
                                                 scale=1.0 / n_feat, bias=EPS),
             reads=["ps_ss"], writes=["rstd"])
        p.op("act", lambda: nc.scalar.activation(out=rstd[:], in_=rstd[:], func=AF.Exp, scale=-0.5),
             reads=["rstd"], writes=["rstd"])

    def evac(dst, dkey, src, skey):
        cnt["ev"] += 1
        if cnt["ev"] % 2:
            p.op("act", lambda: nc.scalar.copy(out=dst, in_=src), reads=[skey], writes=[dkey])
        else:
            p.op("dve", lambda: nc.vector.tensor_copy(out=dst, in_=src), reads=[skey], writes=[dkey])

    def proj_fm(wsrc, nk, rhs_fn, rhs_keys, consume):
        wt, wkey = load_w(wsrc, nk * 128)
        a = cnt["acc"] % 3
        cnt["acc"] += 1
        for k in range(nk):
            p.op("pe", lambda k=k: nc.tensor.matmul(ps_acc[a][:], lhsT=wt[:, k * 128:(k + 1) * 128],
                                                    rhs=rhs_fn(k), start=(k == 0), stop=(k == nk - 1)),
                 reads=[wkey] + rhs_keys, writes=[("acc", a)])
        consume(ps_acc[a][:], ("acc", a))

    def add_norm_into_h(nwoff):
        rms_rstd([(mT[:, j, :], ("mT", j)) for j in range(DT)], D)
        for j in range(DT):
            s = j % 2
            p.op("dve", lambda j=j, s=s: nc.vector.scalar_tensor_tensor(
                out=tmpA[s][:], in0=mT[:, j, :], scalar=cols[:, nwoff + j:nwoff + j + 1], in1=rstd[:],
                op0=ALU.mult, op1=ALU.mult), reads=[("mT", j), "cols", "rstd"], writes=[("tmpA", s)])
            p.op("pool", lambda j=j, s=s: nc.gpsimd.tensor_tensor(out=hT[:, j, :], in0=hT[:, j, :], in1=tmpA[s][:],
                                                                  op=ALU.add),
                 reads=[("hT", j), ("tmpA", s)], writes=[("hT", j)])

    def norm_h_to(dst_fn, dkey_fn, nwoff):
        rms_rstd([(hT[:, j, :], ("hT", j)) for j in range(DT)], D)
        for j in range(DT):
            p.op("dve", lambda j=j: nc.vector.scalar_tensor_tensor(
                out=dst_fn(j), in0=hT[:, j, :], scalar=cols[:, nwoff + j:nwoff + j + 1], in1=rstd[:],
                op0=ALU.mult, op1=ALU.mult), reads=[("hT", j), "cols", "rstd"], writes=[dkey_fn(j)])

    for tg in range(NTG):
        tok = slice(tg * TG, (tg + 1) * TG)
        for j in range(DT):
            p.dma("sp", hT[:, j, :], rd("hT", hT_in, j, tok), writes=[("hT", j)])

        if variant in ("ssd", "gdn"):
            gsz = 2 if variant == "ssd" else 1
            for G in range(16 // gsz):
                tl = []
                for t in range(gsz):
                    ft = G * gsz + t
                    s = ft % 2
                    rows = slice(ft * 128, (ft + 1) * 128)
                    p.dma("sp", stg[0][s][:], rd("mf", mf, ft, tok), writes=[("stg0", s)])
                    p.dma("sp" if "mb" in io else "act", stg[1][s][:], rd("mb", mb, ft, tok), writes=[("stg1", s)])
                    p.dma("sp", stg[2][s][:], rd("zT", zT, ft, tok), writes=[("stg2", s)])
                    yb = yzb[ft % 4]
                    ykey = ("yz", ft % 4)
                    p.op("pool", lambda s=s, yb=yb: nc.gpsimd.tensor_tensor(out=yb[:], in0=stg[0][s][:], in1=stg[1][s][:],
                                                                            op=ALU.add),
                         reads=[("stg0", s), ("stg1", s)], writes=[ykey])
                    p.op("act", lambda s=s: nc.scalar.activation(out=tmpB[s][:], in_=stg[2][s][:], func=AF.Silu),
                         reads=[("stg2", s)], writes=[("tmpB", s)])
                    if variant == "ssd":
                        p.op("dve", lambda s=s, yb=yb: nc.vector.tensor_tensor(out=yb[:], in0=yb[:], in1=tmpB[s][:],
                                                                               op=ALU.mult),
                             reads=[ykey, ("tmpB", s)], writes=[ykey])
                    tl.append((yb, ykey, ft, s))
                rms_rstd([(yb[:], ykey) for (yb, ykey, ft, s) in tl], 128 * gsz)
                for (yb, ykey, ft, s) in tl:
                    if variant == "ssd":
                        p.op("dve", lambda yb=yb, ft=ft: nc.vector.scalar_tensor_tensor(
                            out=lhsA[:, ft, :], in0=yb[:], scalar=cols[:, 32 + ft:33 + ft], in1=rstd[:],
                            op0=ALU.mult, op1=ALU.mult), reads=[ykey, "cols", "rstd"], writes=[("lhsA", ft)])
                    else:
                        p.op("dve", lambda yb=yb: nc.vector.scalar_tensor_tensor(
                            out=yb[:], in0=yb[:], scalar=cols[:, 32:33], in1=rstd[:],
                            op0=ALU.mult, op1=ALU.mult), reads=[ykey, "cols", "rstd"], writes=[ykey])
                        p.op("dve", lambda yb=yb, ft=ft, s=s: nc.vector.tensor_tensor(
                            out=lhsA[:, ft, :], in0=yb[:], in1=tmpB[s][:], op=ALU.mult),
                            reads=[ykey, ("tmpB", s)], writes=[("lhsA", ft)])
            for j in range(DT):
                proj_fm(w_out[j], 16, lambda k: lhsA[:, k, :], [("lhsA", k) for k in range(16)],
                        lambda ps, pk, j=j: evac(mT[:, j, :], ("mT", j), ps, pk))
        if variant == "s5":
            for ft in range(DT):
                s = ft % 2
                rows = slice(ft * 128, (ft + 1) * 128)
                p.dma("sp", stg[0][s][:], rd("mf", mf, ft, tok), writes=[("stg0", s)])
                p.dma("sp" if "mb" in io else "act", stg[1][s][:], rd("mb", mb, ft, tok), writes=[("stg1", s)])
                p.dma("sp", stg[2][s][:], rd("hnT", hn_in, ft, tok), writes=[("stg2", s)])
                yb = yzb[ft % 4]
                ykey = ("yz", ft % 4)
                p.op("pool", lambda s=s, yb=yb: nc.gpsimd.tensor_tensor(out=yb[:], in0=stg[0][s][:], in1=stg[1][s][:],
                                                                        op=ALU.add),
                     reads=[("stg0", s), ("stg1", s)], writes=[ykey])
                p.op("dve", lambda s=s, yb=yb, ft=ft: nc.vector.scalar_tensor_tensor(
                    out=yb[:], in0=stg[2][s][:], scalar=cols[:, 48 + ft:49 + ft], in1=yb[:],
                    op0=ALU.mult, op1=ALU.add), reads=[("stg2", s), "cols", ykey], writes=[ykey])
                p.op("act", lambda s=s, yb=yb: nc.scalar.activation(out=tmpB[s][:], in_=yb[:], func=AF.Square),
                     reads=[ykey], writes=[("tmpB", s)])
                p.op("dve", lambda s=s: nc.vector.tensor_scalar(out=tmpB[s][:], in0=tmpB[s][:], scalar1=0.044715,
                                                                scalar2=1.0, op0=ALU.mult, op1=ALU.add),
                     reads=[("tmpB", s)], writes=[("tmpB", s)])
                p.op("dve", lambda s=s, yb=yb: nc.vector.tensor_tensor(out=tmpB[s][:], in0=tmpB[s][:], in1=yb[:],
                                                                       op=ALU.mult),
                     reads=[("tmpB", s), ykey], writes=[("tmpB", s)])
                p.op("act", lambda s=s: nc.scalar.activation(out=tmpB[s][:], in_=tmpB[s][:], func=AF.Sigmoid,
                                                             scale=2.0 * GELU_C),
                     reads=[("tmpB", s)], writes=[("tmpB", s)])
                p.op("dve", lambda s=s, yb=yb, ft=ft: nc.vector.tensor_tensor(out=lhsA[:, ft, :], in0=yb[:],
                                                                              in1=tmpB[s][:], op=ALU.mult),
                     reads=[ykey, ("tmpB", s)], writes=[("lhsA", ft)])
            for j in range(DT):
                def cons_gate(ps, pk, j=j):
                    s = j % 2
                    p.op("act", lambda: nc.scalar.activation(out=tmpA[s][:], in_=ps, func=AF.Sigmoid,
                                                             bias=cols[:, 40 + j:41 + j]),
                         reads=[pk, "cols"], writes=[("tmpA", s)])

                def cons_val(ps, pk, j=j):
                    s = j % 2
                    p.op("dve", lambda: nc.vector.scalar_tensor_tensor(
                        out=mT[:, j, :], in0=ps, scalar=cols[:, 32 + j:33 + j], in1=tmpA[s][:],
                        op0=ALU.add, op1=ALU.mult), reads=[pk, "cols", ("tmpA", s)], writes=[("mT", j)])
                lk = [("lhsA", k) for k in range(DT)]
                proj_fm(w_glu[8 + j], DT, lambda k: lhsA[:, k, :], lk, cons_gate)
                proj_fm(w_glu[j], DT, lambda k: lhsA[:, k, :], lk, cons_val)

        if has_ffn:
            add_norm_into_h(0)
            norm_h_to(lambda j: xnT[:, j, :], lambda j: ("xnT", j), 8)
            xk = [("xnT", k) for k in range(DT)]
            for f in range(FT):
                wt, wkey = load_w(wgu[f], 2 * DT * 128)
                g = cnt["gu"] % 2
                cnt["gu"] += 1
                for half, pst, nm in ((0, ps_g, "g"), (1, ps_u, "u")):
                    for k in range(DT):
                        p.op("pe", lambda k=k, half=half, pst=pst: nc.tensor.matmul(
                            pst[g][:], lhsT=wt[:, (half * DT + k) * 128:(half * DT + k + 1) * 128],
                            rhs=xnT[:, k, :], start=(k == 0), stop=(k == DT - 1)),
                            reads=[wkey] + xk, writes=[(nm, g)])
                p.op("act", lambda g=g: nc.scalar.activation(out=tmpB[g][:], in_=ps_g[g][:], func=AF.Silu),
                     reads=[("g", g)], writes=[("tmpB", g)])
                p.op("dve", lambda g=g, f=f: nc.vector.tensor_tensor(out=actT[:, f, :], in0=tmpB[g][:],
                                                                     in1=ps_u[g][:], op=ALU.mult),
                     reads=[("tmpB", g), ("u", g)], writes=[("actT", f)])
            ak = [("actT", k) for k in range(FT)]
            for j in range(DT):
                proj_fm(wd[j], FT, lambda k: actT[:, k, :], ak,
                        lambda ps, pk, j=j: evac(mT[:, j, :], ("mT", j), ps, pk))
            add_norm_into_h(16)
            for j in range(DT):
                p.dma("sp", rd("hT_out", hT_out, j, tok), hT[:, j, :], reads=[("hT", j)], is_output=True)

        if next_kind == "proj":
            norm_h_to(lambda j: xnT[:, j, :], lambda j: ("xnT", j), 24)
            xk = [("xnT", k) for k in range(DT)]
            for j in range(49):
                def cons(ps, pk, j=j):
                    o = cnt["o"] % 2
                    cnt["o"] += 1
                    evac(ostg[o][:], ("ostg", o), ps, pk)
                    p.dma("sp" if j % 2 else "act", rd("projT", projT, j, tok), ostg[o][:],
                          reads=[("ostg", o)], is_output=True)
                proj_fm(w_in[j], DT, lambda k: xnT[:, k, :], xk, cons)
        if next_kind == "hn":
            for j in range(DT):
                pass
            norm_h_to(lambda j: mT[:, j, :], lambda j: ("mT", j), 24)
            for j in range(DT):
                p.dma("sp", rd("hn_out", hn_out, j, tok), mT[:, j, :], reads=[("mT", j)], is_output=True)
    if standalone:
        p.finish()
    return nc


def arrange_w(w, n_out_tiles=None):
    K, N = w.shape
    nk = K // 128
    nj = (N + 127) // 128
    if N % 128:
        w = np.concatenate([w, np.zeros((K, nj * 128 - N), w.dtype)], axis=1)
    r = w.reshape(nk, 128, nj, 128).transpose(2, 1, 0, 3).reshape(nj, 128, nk * 128)
    return np.ascontiguousarray(r)


def col_tiles(v):
    return np.ascontiguousarray(v.reshape(-1, 128).T)


T_SEQ = 16384
TB = 512


def make_consts(nc, p):
    io = _SB(nc, "c_iota", [128, 128], F32)
    ident = _SB(nc, "c_ident", [128, 128], F32)
    triu = _SB(nc, "c_triu", [128, 128], F32)
    p.op("pool", lambda: nc.gpsimd.iota(io[:], pattern=[[1, 128]], base=0, channel_multiplier=-1,
                                        allow_small_or_imprecise_dtypes=True), writes=["c_iota"])
    p.op("dve", lambda: nc.vector.tensor_single_scalar(out=ident[:], in_=io[:], scalar=0.0, op=ALU.is_equal),
         reads=["c_iota"], writes=["c_ident"])
    p.op("dve", lambda: nc.vector.tensor_single_scalar(out=triu[:], in_=io[:], scalar=0.0, op=ALU.is_ge),
         reads=["c_iota"], writes=["c_triu"])
    return dict(iota=io, ident=ident, triu=triu)


def conv_silu_block(nc, p, raw, rawkey_fn, cw, cb, cT, ckey_fn, ntiles, tmp, tmpkey):
    for t in range(ntiles):
        p.op("act", lambda t=t: nc.scalar.activation(out=tmp[:], in_=raw[:, t, 0:TB], func=AF.Identity,
                                                     scale=cw[:, t, 0:1]),
             reads=[rawkey_fn(t), "cw"], writes=[tmpkey])
        for k in range(1, 5):
            p.op("dve", lambda t=t, k=k: nc.vector.scalar_tensor_tensor(
                out=tmp[:], in0=raw[:, t, k:k + TB], scalar=cw[:, t, k:k + 1], in1=tmp[:],
                op0=ALU.mult, op1=ALU.add), reads=[rawkey_fn(t), "cw", tmpkey], writes=[tmpkey])
        p.op("act", lambda t=t: nc.scalar.activation(out=cT[:, t, :], in_=tmp[:], func=AF.Silu, bias=cb[:, t:t + 1]),
             reads=[tmpkey, "cb"], writes=[ckey_fn(t)])


def build_ssd(n_blocks=T_SEQ // TB, nc=None, p=None, pfx="", io=None):
    standalone = nc is None
    if standalone:
        nc = bass.Bass("TRN2", target_bir_lowering=False)
        p = Prog(nc)
    T = n_blocks * TB
    din = lambda name, shape: nc.dram_tensor(pfx + name, shape, F32, kind="ExternalInput").ap()
    if io is None:
        xbc = din("xbc", [2, 512, T + 4])
        dtr = din("dtr", [2, 4, T])
    cwd = din("cw", [2, 128, 20])
    cbd = din("cb", [128, 4])
    dtbd = din("dtb", [2, 4, 1])
    alogd = din("alog", [2, 4, 1])
    dskd = din("dsk", [128, 2])
    if io is None:
        yT = nc.dram_tensor("yT", [2, 256, T], F32, kind="ExternalOutput").ap()

    sb = lambda name, shape, dt=F32: _SB(nc, name, shape, dt)
    C = make_consts(nc, p)
    ident, triu = C["ident"], C["triu"]
    sel = sb("sel", [4, 4, 128])
    p.op("pool", lambda: nc.gpsimd.iota(sel[:], pattern=[[1, 4], [0, 128]], base=0, channel_multiplier=-1,
                                        allow_small_or_imprecise_dtypes=True), writes=["sel"])
    p.op("dve", lambda: nc.vector.tensor_single_scalar(out=sel[:], in_=sel[:], scalar=0.0, op=ALU.is_equal),
         reads=["sel"], writes=["sel"])
    ones4 = sb("ones4", [4, 128])
    p.op("pool", lambda: nc.gpsimd.memset(ones4[:], 1.0), writes=["ones4"])

    cb = sb("cb_sb", [128, 4])
    dsk = sb("dsk_sb", [128, 2])
    p.dma("sp", cb[:], cbd, writes=["cb"])
    p.dma("sp", dsk[:], dskd, writes=["dsk"])
    ps = lambda name: _PS(nc, name, [128, 512], F32)
    ps_tr = [ps("ps_tr0"), ps("ps_tr1")]
    ps_fb = [ps("ps_fb0"), ps("ps_fb1")]
    ps_C = [ps("ps_C0"), ps("ps_C1")]
    ps_D = [ps("ps_D0"), ps("ps_D1")]

    def body(p, inst):
        sb = lambda name, shape, dt=F32: _SB(nc, name + '_i%d' % inst, shape, dt)
        cw = sb("cw_sb", [128, 4, 5])
        dtb = sb("dtb_sb", [4, 1])
        acol = sb("acol", [4, 1])

        raw = [sb("raw%d" % i, [128, 4, TB + 4]) for i in range(2)]
        cT = [sb("cT%d" % i, [128, 4, TB]) for i in range(2)]
        ctmp = sb("ctmp", [128, TB])
        dtraw = sb("dtraw", [4, TB])
        dtT = [sb("dtT%d" % i, [4, TB]) for i in range(2)]
        daT = sb("daT", [4, TB])
        csT = [sb("csT%d" % i, [4, TB]) for i in range(2)]
        yTo = [sb("yTo%d" % i, [128, 2, TB]) for i in range(2)]
        S = sb("S", [128, 256])
        S_bf = sb("S_bf", [128, 256], BF16)

        def two(name, shape, dt=F32):
            return [sb(name + "_%d" % i, shape, dt) for i in range(2)]
        x_tok = two("x_tok", [128, 256])
        B_tok = two("B_tok", [128, 128], BF16)
        sm = two("sm", [128, 8])
        CBm = two("CBm", [128, 128])
        Dm = two("Dm", [128, 4, 128])
        G = two("G", [128, 4, 128], BF16)
        eFb = two("eFb", [128, 4, 128])
        CsT = two("CsT", [128, 4, 128], BF16)
        w4 = two("w4", [128, 8])
        xdt = two("xdt", [128, 4, 64], BF16)
        xdtd = two("xdtd", [128, 4, 64], BF16)
        y_tok = two("y_tok", [128, 256])

        p.dma("sp", cw[:].rearrange("p t k -> p (t k)"), cwd[inst], writes=["cw"])
        p.dma("sp", dtb[:], dtbd[inst], writes=["dtb"])
        p.dma("sp", acol[:], alogd[inst], writes=["acol"])
        p.op("act", lambda: nc.scalar.activation(out=acol[:], in_=acol[:], func=AF.Exp), reads=["acol"], writes=["acol"])
        p.op("dve", lambda: nc.vector.tensor_scalar(out=acol[:], in0=acol[:], scalar1=-1.0, scalar2=None, op0=ALU.mult),
             reads=["acol"], writes=["acol"])
        p.op("dve", lambda: nc.vector.memset(S[:], 0.0), writes=["S"])
        p.op("dve", lambda: nc.vector.memset(S_bf[:], 0.0), writes=["S_bf"])
        def block_prep(b):
            bp = b % 2
            col0 = b * TB
            for t in range(4):
                if io is None:
                    p.dma("sp" if t % 2 else "act", raw[bp][:, t, :],
                          xbc[inst, t * 128:(t + 1) * 128, col0:col0 + TB + 4], writes=[("raw", bp, t)])
                else:
                    io.load(p, inst, b, "raw%d" % t, raw[bp][:, t, :], ("raw", bp, t))
            if io is None:
                p.dma("sp", dtraw[:], dtr[inst, :, col0:col0 + TB], writes=["dtraw"])
            else:
                io.load(p, inst, b, "dt", dtraw[:], "dtraw")
            conv_silu_block(nc, p, raw[bp], lambda t: ("raw", bp, t), cw, cb, cT[bp], lambda t: ("cT", bp, t), 4, ctmp, "ctmp")
            p.op("act", lambda: nc.scalar.activation(out=daT[:], in_=dtraw[:], func=AF.Exp, bias=dtb[:, 0:1]),
                 reads=["dtraw", "dtb"], writes=["daT"])
            p.op("act", lambda: nc.scalar.activation(out=dtT[bp][:], in_=daT[:], func=AF.Ln, bias=1.0),
                 reads=["daT"], writes=[("dtT", bp)])
            p.op("dve", lambda: nc.vector.tensor_scalar(out=daT[:], in0=dtT[bp][:], scalar1=acol[:, 0:1], scalar2=None,
                                                        op0=ALU.mult),
                 reads=[("dtT", bp), "acol"], writes=["daT"])
            for j in range(4):
                cs = slice(j * 128, (j + 1) * 128)
                p.op("dve", lambda cs=cs: nc.vector.tensor_tensor_scan(
                    out=csT[bp][:, cs], data0=ones4[:], data1=daT[:, cs], initial=0.0, op0=ALU.mult, op1=ALU.add),
                    reads=["daT", "ones4"], writes=[("csT", bp)])

        def prep(b, j):
            bp = b % 2
            cs = slice(j * 128, (j + 1) * 128)
            c = b * 4 + j
            q = c % 2
            for t in range(3):
                p.op("pe", lambda t=t: nc.tensor.transpose(out=ps_tr[inst][:, t * 128:(t + 1) * 128],
                                                           in_=cT[bp][:, t, cs], identity=ident[:]),
                     reads=[("cT", bp, t), "c_ident"], writes=[("ps_tr", inst)])
            p.op("pe", lambda: nc.tensor.transpose(out=ps_tr[inst][:, 384:388], in_=dtT[bp][:, cs],
                                                   identity=ident[0:4, 0:4]),
                 reads=[("dtT", bp), "c_ident"], writes=[("ps_tr", inst)])
            p.op("pe", lambda: nc.tensor.transpose(out=ps_tr[inst][:, 388:392], in_=csT[bp][:, cs],
                                                   identity=ident[0:4, 0:4]),
                 reads=[("csT", bp), "c_ident"], writes=[("ps_tr", inst)])
            p.op("act", lambda: nc.scalar.copy(out=x_tok[q][:], in_=ps_tr[inst][:, 0:256]),
                 reads=[("ps_tr", inst)], writes=[("x_tok", q)])
            p.op("dve", lambda: nc.vector.tensor_copy(out=B_tok[q][:], in_=ps_tr[inst][:, 256:384]),
                 reads=[("ps_tr", inst)], writes=[("B_tok", q)])
            p.op("dve", lambda: nc.vector.tensor_copy(out=sm[q][:], in_=ps_tr[inst][:, 384:392]),
                 reads=[("ps_tr", inst)], writes=[("sm", q)])
            for h in range(4):
                p.op("pe", lambda h=h: nc.tensor.matmul(ps_fb[inst][:, h * 128:(h + 1) * 128], lhsT=sel[:, h, :],
                                                        rhs=csT[bp][:, cs], start=True, stop=True),
                     reads=["sel", ("csT", bp)], writes=[("ps_fb", inst)])
            p.op("pe", lambda: nc.tensor.matmul(ps_C[inst][:, 256:384], lhsT=cT[bp][:, 2, cs], rhs=cT[bp][:, 3, cs],
                                                start=True, stop=True),
                 reads=[("cT", bp, 2), ("cT", bp, 3)], writes=[("ps_C", inst)])
            p.op("dve", lambda: nc.vector.tensor_tensor(out=CBm[q][:], in0=ps_C[inst][:, 256:384], in1=triu[:], op=ALU.mult),
                 reads=[("ps_C", inst), "c_triu"], writes=[("CBm", q)])
            fb3 = ps_fb[inst][:].rearrange("p (h l) -> p h l", h=4)
            Ftok = sm[q][:, 4:8]
            p.op("dve", lambda: nc.vector.tensor_tensor(out=Dm[q][:], in0=fb3,
                                                        in1=Ftok.unsqueeze(2).broadcast_to([128, 4, 128]),
                                                        op=ALU.subtract),
                 reads=[("ps_fb", inst), ("sm", q)], writes=[("Dm", q)])
            p.op("dve", lambda: nc.vector.tensor_scalar(out=Dm[q][:], in0=Dm[q][:], scalar1=0.0, scalar2=None, op0=ALU.min),
                 reads=[("Dm", q)], writes=[("Dm", q)])
            p.op("act", lambda: nc.scalar.activation(out=Dm[q][:], in_=Dm[q][:], func=AF.Exp),
                 reads=[("Dm", q)], writes=[("Dm", q)])
            p.op("dve", lambda: nc.vector.scalar_tensor_tensor(
                out=G[q][:], in0=Dm[q][:], scalar=1.0, in1=CBm[q][:].unsqueeze(1).broadcast_to([128, 4, 128]),
                op0=ALU.min, op1=ALU.mult), reads=[("Dm", q), ("CBm", q)], writes=[("G", q)])
            p.op("act", lambda: nc.scalar.activation(out=eFb[q][:], in_=fb3, func=AF.Exp),
                 reads=[("ps_fb", inst)], writes=[("eFb", q)])
            p.op("pool", lambda: nc.gpsimd.tensor_tensor(
                out=CsT[q][:], in0=eFb[q][:], in1=cT[bp][:, 3, cs].unsqueeze(1).broadcast_to([128, 4, 128]),
                op=ALU.mult), reads=[("eFb", q), ("cT", bp, 3)], writes=[("CsT", q)])
            p.op("dve", lambda: nc.vector.tensor_tensor(out=w4[q][:, 0:4], in0=fb3[:, :, 127], in1=Ftok,
                                                        op=ALU.subtract),
                 reads=[("ps_fb", inst), ("sm", q)], writes=[("w4", q)])
            p.op("act", lambda: nc.scalar.activation(out=w4[q][:, 0:4], in_=w4[q][:, 0:4], func=AF.Exp),
                 reads=[("w4", q)], writes=[("w4", q)])
            p.op("dve", lambda: nc.vector.tensor_tensor(out=w4[q][:, 4:8], in0=w4[q][:, 0:4], in1=sm[q][:, 0:4],
                                                        op=ALU.mult),
                 reads=[("w4", q), ("sm", q)], writes=[("w4", q)])
            x3 = x_tok[q][:].rearrange("p (h e) -> p h e", h=4)
            p.op("dve", lambda: nc.vector.tensor_tensor(out=xdt[q][:], in0=x3,
                                                        in1=sm[q][:, 0:4].unsqueeze(2).broadcast_to([128, 4, 64]),
                                                        op=ALU.mult),
                 reads=[("x_tok", q), ("sm", q)], writes=[("xdt", q)])
            p.op("pool", lambda: nc.gpsimd.tensor_tensor(out=xdtd[q][:], in0=x3,
                                                         in1=w4[q][:, 4:8].unsqueeze(2).broadcast_to([128, 4, 64]),
                                                         op=ALU.mult),
                 reads=[("x_tok", q), ("w4", q)], writes=[("xdtd", q)])

        def state(b, j):
            bp = b % 2
            col0 = b * TB
            cs = slice(j * 128, (j + 1) * 128)
            c = b * 4 + j
            q = c % 2
            x3 = x_tok[q][:].rearrange("p (h e) -> p h e", h=4)
            for h in range(4):
                hs = slice(h * 64, (h + 1) * 64)
                p.op("pe", lambda h=h, hs=hs: nc.tensor.matmul(ps_C[inst][:, hs], lhsT=G[q][:, h, :], rhs=xdt[q][:, h, :],
                                                              start=True, stop=False),
                     reads=[("G", q), ("xdt", q)], writes=[("ps_C", inst)])
                p.op("pe", lambda h=h, hs=hs: nc.tensor.matmul(ps_C[inst][:, hs], lhsT=CsT[q][:, h, :], rhs=S_bf[:, hs],
                                                              start=False, stop=True),
                     reads=[("CsT", q), "S_bf"], writes=[("ps_C", inst)])
            p.op("pe", lambda: nc.tensor.matmul(ps_D[inst][:, 0:256], lhsT=B_tok[q][:],
                                                rhs=xdtd[q][:].rearrange("p h e -> p (h e)"), start=True, stop=True),
                 reads=[("B_tok", q), ("xdtd", q)], writes=[("ps_D", inst)])
            for h in range(4):
                hs = slice(h * 64, (h + 1) * 64)
                p.op("dve", lambda h=h, hs=hs: nc.vector.scalar_tensor_tensor(
                    out=S[:, hs], in0=S[:, hs], scalar=eFb[q][:, h, 127:128], in1=ps_D[inst][:, hs],
                    op0=ALU.mult, op1=ALU.add), reads=["S", ("eFb", q), ("ps_D", inst)], writes=["S"])
            p.op("act", lambda: nc.scalar.copy(out=S_bf[:], in_=S[:]), reads=["S"], writes=["S_bf"])
            p.op("act", lambda: nc.scalar.copy(out=y_tok[q][:], in_=ps_C[inst][:, 0:256]), reads=[("ps_C", inst)],
                 writes=[("y_tok", q)])
            for t in range(2):
                p.op("pe", lambda t=t: nc.tensor.transpose(out=ps_D[inst][:, 256 + t * 128:256 + (t + 1) * 128],
                                                           in_=y_tok[q][:, t * 128:(t + 1) * 128], identity=ident[:]),
                     reads=[("y_tok", q), "c_ident"], writes=[("ps_D", inst)])
            for t in range(2):
                if inst == 0:
                    p.op("dve", lambda t=t: nc.vector.scalar_tensor_tensor(
                        out=yTo[bp][:, t, cs], in0=cT[bp][:, t, cs], scalar=dsk[:, t:t + 1],
                        in1=ps_D[inst][:, 256 + t * 128:256 + (t + 1) * 128], op0=ALU.mult, op1=ALU.add),
                        reads=[("cT", bp, t), "dsk", ("ps_D", inst)], writes=[("yTo", bp)])
                else:
                    p.op("dve", lambda t=t: nc.vector.tensor_copy(out=yTo[bp][:, t, cs],
                                                                  in_=ps_D[inst][:, 256 + t * 128:256 + (t + 1) * 128]),
                         reads=[("ps_D", inst)], writes=[("yTo", bp)])
            if j == 3:
                for t in range(2):
                    if io is None:
                        p.dma("sp", yT[inst, t * 128:(t + 1) * 128, col0:col0 + TB], yTo[bp][:, t, :],
                              reads=[("yTo", bp)], is_output=True)
                    else:
                        io.store(p, inst, b, "y%d" % t, yTo[bp][:, t, :], ("yTo", bp))

        prev = None
        for b in range(n_blocks):
            for j in range(4):
                if j == 0:
                    block_prep(b)
                prep(b, j)
                if prev is not None:
                    state(*prev)
                prev = (b, j)
        state(*prev)
    SHARED = {"c_iota", "c_ident", "c_triu", "sel", "ones4", "cb", "dsk"}
    run_interleaved(p, [lambda v: body(v, 0), lambda v: body(v, 1)], SHARED)
    if standalone:
        p.finish()
    return nc


CH = 64
EPS = 1e-6


def build_gdn(n_blocks=T_SEQ // TB, nc=None, p=None, pfx="", io=None):
    standalone = nc is None
    if standalone:
        nc = bass.Bass("TRN2", target_bir_lowering=False)
        p = Prog(nc)
    T = n_blocks * TB
    din = lambda name, shape: nc.dram_tensor(pfx + name, shape, F32, kind="ExternalInput").ap()
    if io is None:
        qkv = din("qkv", [2, 512, T + 4])
        abr = din("abr", [2, 2, 2, T])
    cwd = din("cw", [2, 128, 20])
    cbd = din("cb", [128, 4])
    dtbd = din("dtb", [2, 2, 1])
    alogd = din("alog", [2, 2, 1])
    if io is None:
        oT = nc.dram_tensor("oT", [2, 256, T], F32, kind="ExternalOutput").ap()

    sb = lambda name, shape, dt=F32: _SB(nc, name, shape, dt)
    C = make_consts(nc, p)
    ident, iot = C["ident"], C["iota"]

    def blockdiag(B, name):
        nb = 128 // B
        E = sb("E" + name, [nb, 128])
        E2 = sb("E2" + name, [nb, 128])
        p.op("pool", lambda: nc.gpsimd.iota(E[:], pattern=[[1, 128]], base=0, channel_multiplier=-B,
                                            allow_small_or_imprecise_dtypes=True), writes=["E" + name])
        p.op("dve", lambda: nc.vector.tensor_single_scalar(out=E2[:], in_=E[:], scalar=float(B), op=ALU.is_lt),
             reads=["E" + name], writes=["E2" + name])
        p.op("dve", lambda: nc.vector.tensor_single_scalar(out=E[:], in_=E[:], scalar=0.0, op=ALU.is_ge),
             reads=["E" + name], writes=["E" + name])
        p.op("dve", lambda: nc.vector.tensor_tensor(out=E[:], in0=E[:], in1=E2[:], op=ALU.mult),
             reads=["E" + name, "E2" + name], writes=["E" + name])
        m = sb("bd" + name, [128, 128])
        pst = ps_n[0]
        p.op("pe", lambda: nc.tensor.matmul(pst[:, 0:128], lhsT=E[:], rhs=E[:], start=True, stop=True),
             reads=["E" + name], writes=[("ps_Q0", 0)])
        p.op("dve", lambda: nc.vector.tensor_copy(out=m[:], in_=pst[:, 0:128]), reads=[("ps_Q0", 0)], writes=["bd" + name])
        return m

    ps = lambda name: _PS(nc, name, [128, 512], F32)
    ps_Q = [[ps("ps_Q%d_%d" % (k, i)) for i in range(2)] for k in range(4)]
    ps_n = [ps_Q[0][0], ps_Q[1][0]]

    bd16 = blockdiag(16, "16")
    bd32 = blockdiag(32, "32")
    bd64 = blockdiag(64, "64")
    m32 = sb("m32", [128, 128])
    m64 = sb("m64", [128, 128])
    MUi = sb("MUi", [128, 128])
    MLs = sb("MLs", [128, 128])
    p.op("dve", lambda: nc.vector.tensor_tensor(out=m32[:], in0=bd32[:], in1=bd16[:], op=ALU.subtract),
         reads=["bd32", "bd16"], writes=["m32"])
    p.op("dve", lambda: nc.vector.tensor_tensor(out=m64[:], in0=bd64[:], in1=bd32[:], op=ALU.subtract),
         reads=["bd64", "bd32"], writes=["m64"])
    p.op("dve", lambda: nc.vector.tensor_single_scalar(out=MUi[:], in_=iot[:], scalar=0.0, op=ALU.is_ge),
         reads=["c_iota"], writes=["MUi"])
    p.op("dve", lambda: nc.vector.tensor_tensor(out=MUi[:], in0=MUi[:], in1=bd64[:], op=ALU.mult),
         reads=["MUi", "bd64"], writes=["MUi"])
    p.op("dve", lambda: nc.vector.tensor_single_scalar(out=MLs[:], in_=iot[:], scalar=0.0, op=ALU.is_lt),
         reads=["c_iota"], writes=["MLs"])
    p.op("dve", lambda: nc.vector.tensor_tensor(out=MLs[:], in0=MLs[:], in1=bd64[:], op=ALU.mult),
         reads=["MLs", "bd64"], writes=["MLs"])
    ones2 = sb("ones2", [2, 128])
    p.op("pool", lambda: nc.gpsimd.memset(ones2[:], 1.0), writes=["ones2"])
    onesb = sb("onesb", [128, 128], BF16)
    p.op("pool", lambda: nc.gpsimd.memset(onesb[:], 1.0), writes=["onesb"])

    cb = sb("cb_sb", [128, 4])
    p.dma("sp", cb[:], cbd, writes=["cb"])
    def body(p, inst):
        sb = lambda name, shape, dt=F32: _SB(nc, name + '_i%d' % inst, shape, dt)
        cw = sb("cw_sb", [128, 4, 5])
        dtb = sb("dtb_sb", [2, 1])
        acol = sb("acol", [2, 1])

        raw = [sb("raw%d" % i, [128, 4, TB + 4]) for i in range(1)] * 2
        cT = [sb("cT%d" % i, [128, 4, TB]) for i in range(1)] * 2
        ctmp = sb("ctmp", [128, TB])
        sqb = sb("sqb", [128, TB], BF16)
        rs = sb("rs", [128, TB])
        qT2 = [sb("qT2_%d" % i, [128, TB // CH, 2, CH]) for i in range(1)] * 2
        kT2 = [sb("kT2_%d" % i, [128, TB // CH, 2, CH]) for i in range(1)] * 2
        vT2 = [sb("vT2_%d" % i, [128, TB // CH, 2, CH]) for i in range(1)] * 2
        araw = sb("araw", [2, TB])
        braw = sb("braw", [2, TB])
        gT = sb("gT", [2, TB])
        gcsT = sb("gcsT", [2, TB])
        betaT = sb("betaT", [2, TB])
        glT = sb("glT", [2, TB])
        Mg = [sb("Mg%d" % i, [2, TB // CH, 2, CH]) for i in range(1)] * 2
        Mb = [sb("Mb%d" % i, [2, TB // CH, 2, CH]) for i in range(1)] * 2
        Ml = [sb("Ml%d" % i, [2, TB // CH, 2, CH]) for i in range(1)] * 2
        oTo = [sb("oTo%d" % i, [128, 2, TB]) for i in range(1)] * 2
        S = sb("S", [128, 256])
        S_bf = sb("S_bf", [128, 256], BF16)

        def two(name, shape, dt=F32):
            return [sb(name, shape, dt)] * 2
        colsb = two("colsb", [128, 8])
        egrow = [sb("egrow_%d" % i, [128, 128], F32) for i in range(2)]
        D1 = two("D1", [128, 128])
        D2 = two("D2", [128, 128])
        qkT = [sb("qkT_%d" % i, [128, 128], BF16) for i in range(2)]
        Am = two("Am", [128, 128])
        ATm = two("ATm", [128, 128])
        X = two("X", [128, 128]); Y = two("Y", [128, 128])
        Ao32 = two("Ao32", [128, 128]); Ao32T = two("Ao32T", [128, 128]); Ao64 = two("Ao64", [128, 128])
        Tm = two("Tm", [128, 128]); Um = two("Um", [128, 128])
        X2 = two("X2", [128, 128]); Y2 = two("Y2", [128, 128])
        Rm = two("Rm", [128, 128]); Pm = two("Pm", [128, 128])
        RHSv = two("RHSv", [128, 128]); RHSw = two("RHSw", [128, 128]); kdec = [sb("kdec_%d" % i, [128, 2, 128], BF16) for i in range(2)]
        qgT = [sb("qgT_%d" % i, [128, 128], BF16) for i in range(2)]
        u_sb = [sb("u_sb_%d" % i, [128, 128], F32) for i in range(2)]; wT_sb = [sb("wT_sb_%d" % i, [128, 128], BF16) for i in range(2)]
        vnew = [sb("vnew_%d" % i, [128, 128], BF16) for i in range(2)]

        ncnt = [0]

        def mmN(lhsT, rhs, rkeys):
            i = ncnt[0] % 2
            ncnt[0] += 1
            key = ("ps_Q%d" % i, inst)
            p.op("pe", lambda: nc.tensor.matmul(ps_Q[i][inst][:, 0:128], lhsT=lhsT, rhs=rhs, start=True, stop=True),
                 reads=rkeys, writes=[key])
            return ps_Q[i][inst][:, 0:128], key

        def ev_copy(dst, dkey, src, skey):
            p.op("act", lambda: nc.scalar.copy(out=dst, in_=src), reads=[skey], writes=[dkey])

        def ev_comb(dst, dkey, a, akey, src, skey, op):
            p.op("dve", lambda: nc.vector.tensor_tensor(out=dst, in0=a, in1=src, op=op), reads=[akey, skey], writes=[dkey])

        p.dma("sp", cw[:].rearrange("p t k -> p (t k)"), cwd[inst], writes=["cw"])
        p.dma("sp", dtb[:], dtbd[inst], writes=["dtb"])
        p.dma("sp", acol[:], alogd[inst], writes=["acol"])
        p.op("act", lambda: nc.scalar.activation(out=acol[:], in_=acol[:], func=AF.Exp), reads=["acol"], writes=["acol"])
        p.op("dve", lambda: nc.vector.tensor_scalar(out=acol[:], in0=acol[:], scalar1=-1.0, scalar2=None, op0=ALU.mult),
             reads=["acol"], writes=["acol"])
        p.op("dve", lambda: nc.vector.memset(S[:], 0.0), writes=["S"])
        p.op("dve", lambda: nc.vector.memset(S_bf[:], 0.0), writes=["S_bf"])
        def block_prep(b):
            bp = 0
            col0 = b * TB
            for t in range(4):
                if io is None:
                    p.dma("sp" if t % 2 else "act", raw[bp][:, t, :],
                          qkv[inst, t * 128:(t + 1) * 128, col0:col0 + TB + 4], writes=[("raw", bp, t)])
                else:
                    io.load(p, inst, b, "raw%d" % t, raw[bp][:, t, :], ("raw", bp, t))
            if io is None:
                p.dma("sp", araw[:], abr[inst, 0, :, col0:col0 + TB], writes=["araw"])
                p.dma("sp", braw[:], abr[inst, 1, :, col0:col0 + TB], writes=["braw"])
            else:
                io.load(p, inst, b, "a", araw[:], "araw")
                io.load(p, inst, b, "b", braw[:], "braw")
            conv_silu_block(nc, p, raw[bp], lambda t: ("raw", bp, t), cw, cb, cT[bp], lambda t: ("cT", bp, t), 4, ctmp, "ctmp")
            for t, dst, dname, sc in ((0, qT2[bp], "qT2", 128.0 ** -0.5), (1, kT2[bp], "kT2", 1.0)):
                p.op("act", lambda t=t: nc.scalar.activation(out=sqb[:], in_=cT[bp][:, t, :], func=AF.Square),
                     reads=[("cT", bp, t)], writes=["sqb"])
                p.op("pe", lambda: nc.tensor.matmul(ps_Q[0][inst][:], lhsT=onesb[:], rhs=sqb[:], start=True, stop=True),
                     reads=["onesb", "sqb"], writes=[("ps_Q0", inst)])
                p.op("act", lambda: nc.scalar.activation(out=rs[:], in_=ps_Q[0][inst][:], func=AF.Ln, bias=EPS),
                     reads=[("ps_Q0", inst)], writes=["rs"])
                p.op("act", lambda: nc.scalar.activation(out=rs[:], in_=rs[:], func=AF.Exp, scale=-0.5),
                     reads=["rs"], writes=["rs"])
                for vh in range(2):
                    p.op("dve", lambda t=t, dst=dst, sc=sc, vh=vh: nc.vector.scalar_tensor_tensor(
                        out=dst[:, :, vh, :], in0=cT[bp][:, t, :].rearrange("p (c i) -> p c i", i=CH), scalar=sc,
                        in1=rs[:].rearrange("p (c i) -> p c i", i=CH), op0=ALU.mult, op1=ALU.mult),
                        reads=[("cT", bp, t), "rs"], writes=[(dname, bp)])
            for vh in range(2):
                p.op("pool", lambda vh=vh: nc.gpsimd.tensor_copy(
                    out=vT2[bp][:, :, vh, :], in_=cT[bp][:, 2 + vh, :].rearrange("p (c i) -> p c i", i=CH)),
                    reads=[("cT", bp, 2 + vh)], writes=[("vT2", bp)])
            p.op("act", lambda: nc.scalar.activation(out=gT[:], in_=araw[:], func=AF.Exp, bias=dtb[:, 0:1]),
                 reads=["araw", "dtb"], writes=["gT"])
            p.op("act", lambda: nc.scalar.activation(out=gT[:], in_=gT[:], func=AF.Ln, bias=1.0),
                 reads=["gT"], writes=["gT"])
            p.op("dve", lambda: nc.vector.tensor_scalar(out=gT[:], in0=gT[:], scalar1=acol[:, 0:1], scalar2=None,
                                                        op0=ALU.mult), reads=["gT", "acol"], writes=["gT"])
            p.op("act", lambda: nc.scalar.activation(out=betaT[:], in_=braw[:], func=AF.Sigmoid),
                 reads=["braw"], writes=["betaT"])
            for j in range(TB // CH):
                cs = slice(j * CH, (j + 1) * CH)
                p.op("dve", lambda cs=cs: nc.vector.tensor_tensor_scan(
                    out=gcsT[:, cs], data0=ones2[:, 0:CH], data1=gT[:, cs], initial=0.0, op0=ALU.mult, op1=ALU.add),
                    reads=["gT", "ones2"], writes=["gcsT"])
            for j in range(TB // CH):
                cs = slice(j * CH, (j + 1) * CH)
                e = (j + 1) * CH - 1
                p.op("pool", lambda cs=cs, e=e: nc.gpsimd.tensor_copy(out=glT[:, cs],
                                                                     in_=gcsT[:, e:e + 1].broadcast_to([2, CH])),
                     reads=["gcsT"], writes=["glT"])
            for src, dst, nm in ((gcsT, Mg[bp], "Mg"), (betaT, Mb[bp], "Mb"), (glT, Ml[bp], "Ml")):
                for vh in range(2):
                    p.op("dve", lambda src=src, dst=dst, vh=vh: nc.vector.tensor_scalar(
                        out=dst[:, :, vh, :], in0=src[:].rearrange("p (c i) -> p c i", i=CH), scalar1=ident[0:2, vh:vh + 1],
                        scalar2=None, op0=ALU.mult),
                        reads=[src is gcsT and "gcsT" or (src is betaT and "betaT" or "glT"), "c_ident"],
                        writes=[(nm, bp)])


        def prep(b, j):
            bp = 0
            col0 = b * TB
            cs = slice(j * CH, (j + 1) * CH)
            c = b * (TB // CH) + j
            q = 0
            qs = c % 2
            cq = colsb[q]
            ck = ("colsb", q)
            for i, (M, nm) in enumerate(((Mg[bp], "Mg"), (Mb[bp], "Mb"), (Ml[bp], "Ml"))):
                p.op("pe", lambda i=i, M=M: nc.tensor.matmul(ps_Q[0][inst][:, 384 + 2 * i:386 + 2 * i], lhsT=M[:, j, :, :].rearrange("p v i -> p (v i)"), rhs=ones2[:, 0:2],
                                                             start=True, stop=True),
                     reads=[(nm, bp), "ones2"], writes=[("ps_Q0", inst)])
            p.op("pe", lambda: nc.tensor.matmul(ps_Q[1][inst][:, 384:512], lhsT=ones2[:], rhs=Mg[bp][:, j, :, :].rearrange("p v i -> p (v i)"),
                                                start=True, stop=True),
                 reads=[("Mg", bp), "ones2"], writes=[("ps_Q1", inst)])
            cq = colsb[q]
            ck = ("colsb", q)
            p.op("dve", lambda: nc.vector.tensor_copy(out=cq[:, 0:3], in_=ps_Q[0][inst][:, 384:390].rearrange("p (a b) -> p a b", b=2)[:, :, 0]),
                 reads=[("ps_Q0", inst)], writes=[ck])
            p.op("act", lambda: nc.scalar.activation(out=cq[:, 3:4], in_=cq[:, 0:1], func=AF.Exp), reads=[ck], writes=[ck])
            p.op("dve", lambda: nc.vector.tensor_tensor(out=cq[:, 4:5], in0=cq[:, 1:2], in1=cq[:, 3:4], op=ALU.mult),
                 reads=[ck], writes=[ck])
            p.op("dve", lambda: nc.vector.tensor_tensor(out=cq[:, 5:6], in0=cq[:, 2:3], in1=cq[:, 0:1], op=ALU.subtract),
                 reads=[ck], writes=[ck])
            p.op("act", lambda: nc.scalar.activation(out=cq[:, 5:6], in_=cq[:, 5:6], func=AF.Exp), reads=[ck], writes=[ck])
            p.op("dve", lambda: nc.vector.tensor_scalar(out=cq[:, 6:7], in0=cq[:, 0:1], scalar1=-1.0, scalar2=None,
                                                        op0=ALU.mult), reads=[ck], writes=[ck])
            p.op("act", lambda: nc.scalar.activation(out=egrow[qs][:], in_=ps_Q[1][inst][:, 384:512], func=AF.Exp),
                 reads=[("ps_Q1", inst)], writes=[("egrow", qs)])
            p.op("dve", lambda: nc.vector.tensor_scalar(out=D1[q][:], in0=ps_Q[1][inst][:, 384:512], scalar1=cq[:, 0:1],
                                                        scalar2=0.0, op0=ALU.subtract, op1=ALU.max),
                 reads=[("ps_Q1", inst), ck], writes=[("D1", q)])
            p.op("dve", lambda: nc.vector.tensor_scalar(out=D2[q][:], in0=ps_Q[1][inst][:, 384:512], scalar1=cq[:, 0:1],
                                                        scalar2=0.0, op0=ALU.subtract, op1=ALU.min),
                 reads=[("ps_Q1", inst), ck], writes=[("D2", q)])
            p.op("act", lambda: nc.scalar.activation(out=D1[q][:], in_=D1[q][:], func=AF.Exp, scale=-1.0),
                 reads=[("D1", q)], writes=[("D1", q)])
            p.op("act", lambda: nc.scalar.activation(out=D2[q][:], in_=D2[q][:], func=AF.Exp),
                 reads=[("D2", q)], writes=[("D2", q)])
            p.op("dve", lambda: nc.vector.scalar_tensor_tensor(out=D1[q][:], in0=D1[q][:], scalar=1.0, in1=MLs[:],
                                                               op0=ALU.min, op1=ALU.mult),
                 reads=[("D1", q), "MLs"], writes=[("D1", q)])
            p.op("dve", lambda: nc.vector.scalar_tensor_tensor(out=D2[q][:], in0=D2[q][:], scalar=1.0, in1=MUi[:],
                                                               op0=ALU.min, op1=ALU.mult),
                 reads=[("D2", q), "MUi"], writes=[("D2", q)])
            kk = kT2[bp][:, j, :, :].rearrange("p v i -> p (v i)")
            qq = qT2[bp][:, j, :, :].rearrange("p v i -> p (v i)")
            p.op("pe", lambda: nc.tensor.matmul(ps_Q[1][inst][:, 128:256], lhsT=kk, rhs=kk, start=True, stop=True),
                 reads=[("kT2", bp)], writes=[("ps_Q1", inst)])
            p.op("pe", lambda: nc.tensor.matmul(ps_Q[1][inst][:, 256:384], lhsT=kk, rhs=qq, start=True, stop=True),
                 reads=[("kT2", bp), ("qT2", bp)], writes=[("ps_Q1", inst)])
            p.op("dve", lambda: nc.vector.scalar_tensor_tensor(out=Am[q][:], in0=D1[q][:], scalar=cq[:, 1:2],
                                                               in1=ps_Q[1][inst][:, 128:256], op0=ALU.mult, op1=ALU.mult),
                 reads=[("D1", q), ck, ("ps_Q1", inst)], writes=[("Am", q)])
            p.op("dve", lambda: nc.vector.tensor_tensor(out=qkT[qs][:], in0=D2[q][:], in1=ps_Q[1][inst][:, 256:384], op=ALU.mult),
                 reads=[("D2", q), ("ps_Q1", inst)], writes=[("qkT", qs)])
            p.op("pe", lambda: nc.tensor.transpose(out=ps_Q[0][inst][:, 128:256],
                                                   in_=vT2[bp][:, j, :, :].rearrange("p v i -> p (v i)"), identity=ident[:]),
                 reads=[("vT2", bp), "c_ident"], writes=[("ps_Q0", inst)])
            p.op("pe", lambda: nc.tensor.transpose(out=ps_Q[0][inst][:, 256:384], in_=kk, identity=ident[:]),
                 reads=[("kT2", bp), "c_ident"], writes=[("ps_Q0", inst)])
            p.op("dve", lambda: nc.vector.tensor_scalar(out=RHSv[q][:], in0=ps_Q[0][inst][:, 128:256], scalar1=cq[:, 1:2],
                                                        scalar2=None, op0=ALU.mult),
                 reads=[("ps_Q0", inst), ck], writes=[("RHSv", q)])
            p.op("dve", lambda: nc.vector.tensor_scalar(out=RHSw[q][:], in0=ps_Q[0][inst][:, 256:384], scalar1=cq[:, 4:5],
                                                        scalar2=None, op0=ALU.mult),
                 reads=[("ps_Q0", inst), ck], writes=[("RHSw", q)])
            for vh in range(2):
                p.op("dve", lambda vh=vh: nc.vector.tensor_scalar(
                    out=kdec[qs][:, vh, :], in0=ps_Q[0][inst][:, 256:384], scalar1=cq[:, 5:6],
                    scalar2=bd64[:, 64 * vh:64 * vh + 1], op0=ALU.mult, op1=ALU.mult),
                    reads=[("ps_Q0", inst), ck, "bd64"], writes=[("kdec", qs)])
            p.op("pool", lambda: nc.gpsimd.tensor_tensor(
                out=qgT[qs][:], in0=qq, in1=egrow[qs][:], op=ALU.mult),
                 reads=[("qT2", bp), ("egrow", qs)], writes=[("qgT", qs)])
            ni = ncnt[0] % 2
            ncnt[0] += 1
            pa, pk = ps_Q[ni][inst][:, 0:128], ("ps_Q%d" % ni, inst)
            p.op("pe", lambda: nc.tensor.transpose(out=pa, in_=Am[q][:], identity=ident[:]),
                 reads=[("Am", q), "c_ident"], writes=[pk])
            ev_copy(ATm[q][:], ("ATm", q), pa, pk)
            for dst, nm, src, snm, msk, mnm in ((X, "X", Am, "Am", bd16, "bd16"), (Y, "Y", ATm, "ATm", bd16, "bd16"),
                                                (Ao32, "Ao32", Am, "Am", m32, "m32"),
                                                (Ao32T, "Ao32T", ATm, "ATm", m32, "m32"),
                                                (Ao64, "Ao64", Am, "Am", m64, "m64")):
                p.op("pool", lambda dst=dst, src=src, msk=msk: nc.gpsimd.tensor_tensor(out=dst[q][:], in0=src[q][:],
                                                                                      in1=msk[:], op=ALU.mult),
                     reads=[(snm, q), mnm], writes=[(nm, q)])
            Tq, Uq = Tm[q], Um[q]
            tk, uk = ("Tm", q), ("Um", q)
            p.op("dve", lambda: nc.vector.tensor_tensor(out=Tq[:], in0=ident[:], in1=X[q][:], op=ALU.subtract),
                 reads=["c_ident", ("X", q)], writes=[tk])
            p.op("dve", lambda: nc.vector.tensor_tensor(out=Uq[:], in0=ident[:], in1=Y[q][:], op=ALU.subtract),
                 reads=["c_ident", ("Y", q)], writes=[uk])
            xa, ya, xk_, yk_ = X[q], Y[q], ("X", q), ("Y", q)
            xb, yb, xbk, ybk = X2[q], Y2[q], ("X2", q), ("Y2", q)
            for lvl in range(3):
                pa, pk = mmN(ya[:], xa[:], [yk_, xk_])
                ev_copy(xb[:], xbk, pa, pk)
                if lvl < 2:
                    pa, pk = mmN(xa[:], ya[:], [xk_, yk_])
                    ev_copy(yb[:], ybk, pa, pk)
                pa, pk = mmN(Uq[:], xb[:], [uk, xbk])
                pb, pkb = mmN(xb[:], Uq[:], [xbk, uk])
                ev_comb(Tq[:], tk, Tq[:], tk, pa, pk, ALU.add)
                ev_comb(Uq[:], uk, Uq[:], uk, pb, pkb, ALU.add)
                xa, ya, xk_, yk_, xb, yb, xbk, ybk = xb, yb, xbk, ybk, xa, ya, xk_, yk_
            pa, pk = mmN(Ao32T[q][:], Tq[:], [("Ao32T", q), tk])
            ev_copy(Rm[q][:], ("Rm", q), pa, pk)
            pa, pk = mmN(Ao32[q][:], Uq[:], [("Ao32", q), uk])
            ev_copy(Pm[q][:], ("Pm", q), pa, pk)
            pa, pk = mmN(Uq[:], Rm[q][:], [uk, ("Rm", q)])
            pb, pkb = mmN(Tq[:], Pm[q][:], [tk, ("Pm", q)])
            ev_comb(Tq[:], tk, Tq[:], tk, pa, pk, ALU.subtract)
            ev_comb(Uq[:], uk, Uq[:], uk, pb, pkb, ALU.subtract)
            pa, pk = mmN(Ao64[q][:], Uq[:], [("Ao64", q), uk])
            ev_copy(Pm[q][:], ("Pm", q), pa, pk)
            pb, pkb = mmN(Tq[:], Pm[q][:], [tk, ("Pm", q)])
            ev_comb(Uq[:], uk, Uq[:], uk, pb, pkb, ALU.subtract)
            pa, pk = mmN(Uq[:], RHSv[q][:], [uk, ("RHSv", q)])
            ev_copy(u_sb[qs][:], ("u_sb", qs), pa, pk)
            pa, pk = mmN(RHSw[q][:], Uq[:], [("RHSw", q), uk])
            ev_copy(wT_sb[qs][:], ("wT_sb", qs), pa, pk)

        def state(b, j):
            bp = 0
            col0 = b * TB
            cs = slice(j * CH, (j + 1) * CH)
            c = b * (TB // CH) + j
            q = 0
            qs = c % 2
            cq = colsb[q]
            ck = ("colsb", q)
            for vh in range(2):
                hs = slice(vh * 64, (vh + 1) * 64)
                vs = slice(vh * 128, (vh + 1) * 128)
                p.op("pe", lambda hs=hs, vs=vs: nc.tensor.matmul(ps_Q[2][inst][:, vs], lhsT=wT_sb[qs][:], rhs=S_bf[:, vs],
                                                                start=True, stop=True),
                     reads=[("wT_sb", qs), "S_bf"], writes=[("ps_Q2", inst)])
            for vh in range(2):
                hs = slice(vh * 64, (vh + 1) * 64)
                vs = slice(vh * 128, (vh + 1) * 128)
                p.op("dve", lambda hs=hs, vs=vs: nc.vector.tensor_tensor(out=vnew[qs][hs, :], in0=u_sb[qs][hs, :],
                                                                        in1=ps_Q[2][inst][hs, vs], op=ALU.subtract),
                     reads=[("u_sb", qs), ("ps_Q2", inst)], writes=[("vnew", qs)])
            for vh in range(2):
                hs = slice(vh * 64, (vh + 1) * 64)
                vs = slice(vh * 128, (vh + 1) * 128)
                oc = slice(256 + vh * 64, 256 + (vh + 1) * 64)
                p.op("pe", lambda hs=hs, vs=vs, oc=oc: nc.tensor.matmul(ps_Q[2][inst][:, oc], lhsT=S_bf[:, vs], rhs=qgT[qs][:, hs],
                                                                       start=True, stop=False),
                     reads=["S_bf", ("qgT", qs)], writes=[("ps_Q2", inst)])
                p.op("pe", lambda hs=hs, oc=oc: nc.tensor.matmul(ps_Q[2][inst][:, oc], lhsT=vnew[qs][:], rhs=qkT[qs][:, hs],
                                                                start=False, stop=True),
                     reads=[("vnew", qs), ("qkT", qs)], writes=[("ps_Q2", inst)])
            p.op("act", lambda: nc.scalar.copy(out=oTo[bp][:, :, cs],
                                               in_=ps_Q[2][inst][:, 256:384].rearrange("p (v i) -> p v i", v=2)),
                 reads=[("ps_Q2", inst)], writes=[("oTo", bp)])
            for vh in range(2):
                hs = slice(vh * 64, (vh + 1) * 64)
                vs = slice(vh * 128, (vh + 1) * 128)
                p.op("pe", lambda hs=hs, vs=vs, vh=vh: nc.tensor.matmul(ps_Q[3][inst][:, vs], lhsT=kdec[qs][:, vh, :], rhs=vnew[qs][:],
                                                                start=True, stop=True),
                     reads=[("kdec", qs), ("vnew", qs)], writes=[("ps_Q3", inst)])
            for vh in range(2):
                vs = slice(vh * 128, (vh + 1) * 128)
                e = vh * 64 + 63
                p.op("dve", lambda vs=vs, e=e: nc.vector.scalar_tensor_tensor(
                    out=S[:, vs], in0=S[:, vs], scalar=egrow[qs][:, e:e + 1], in1=ps_Q[3][inst][:, vs],
                    op0=ALU.mult, op1=ALU.add), reads=["S", ("egrow", qs), ("ps_Q3", inst)], writes=["S"])
            p.op("act", lambda: nc.scalar.copy(out=S_bf[:], in_=S[:]), reads=["S"], writes=["S_bf"])
            if j == TB // CH - 1:
                for vh in range(2):
                    if io is None:
                        p.dma("sp", oT[inst, vh * 128:(vh + 1) * 128, col0:col0 + TB], oTo[bp][:, vh, :],
                              reads=[("oTo", bp)], is_output=True)
                    else:
                        io.store(p, inst, b, "y%d" % vh, oTo[bp][:, vh, :], ("oTo", bp))

        prev = None
        for b in range(n_blocks):
            for j in range(TB // CH):
                if j == 0:
                    block_prep(b)
                prep(b, j)
                if prev is not None:
                    state(*prev)
                prev = (b, j)
        state(*prev)
    SHARED = {"c_iota", "c_ident", "c_triu", "cb", "bd16", "bd32", "bd64", "m32", "m64", "MUi", "MLs", "ones2", "onesb"}
    run_interleaved(p, [lambda v: body(v, 0), lambda v: body(v, 1)], SHARED)
    if standalone:
        p.finish()
    return nc


T_SEQ = 16384
TC = 512
NP = 4


def build_s5(n_blocks=T_SEQ // TC, nc=None, p=None, pfx="", io=None):
    standalone = nc is None
    if standalone:
        nc = bass.Bass("TRN2", target_bir_lowering=False)
        p = Prog(nc)
    T = n_blocks * TC
    din = lambda name, shape: nc.dram_tensor(pfx + name, shape, F32, kind="ExternalInput").ap()
    if io is None:
        uT = din("uT", [2, 128, T])
    lre_d = din("lam_re", [2, 128, NP])
    lim_d = din("lam_im", [2, 128, NP])
    lst_d = din("log_step", [2, 128, NP])
    bre_d = din("b_re", [128, NP * 128])
    bim_d = din("b_im", [128, NP * 128])
    cre_d = din("c_re", [2, 128, NP * 128])
    cim_d = din("c_im", [2, 128, NP * 128])
    if io is None:
        yT = nc.dram_tensor("yT", [2, 128, T], F32, kind="ExternalOutput").ap()

    sb = lambda name, shape, dt=F32: _SB(nc, name, shape, dt)
    tpr = sb("tpr", [128, TC])
    p.op("pool", lambda: nc.gpsimd.iota(tpr[:], pattern=[[1, TC]], base=0, channel_multiplier=0,
                                        allow_small_or_imprecise_dtypes=True), writes=["tpr"])
    Bre = sb("Bre", [128, NP * 128]); Bim = sb("Bim", [128, NP * 128])
    p.dma("sp", Bre[:], bre_d, writes=["Bre"])
    p.dma("sp", Bim[:], bim_d, writes=["Bim"])
    Bre_b = sb("Bre_b", [128, NP * 128], BF16); Bim_b = sb("Bim_b", [128, NP * 128], BF16)
    p.op("act", lambda: nc.scalar.copy(out=Bre_b[:], in_=Bre[:]), reads=["Bre"], writes=["Bre_b"])
    p.op("act", lambda: nc.scalar.copy(out=Bim_b[:], in_=Bim[:]), reads=["Bim"], writes=["Bim_b"])
    ps = lambda name: _PS(nc, name, [128, 512], F32)
    ps_xr = [ps("ps_xr0"), ps("ps_xr1")]
    ps_xi = [ps("ps_xi0"), ps("ps_xi1")]
    ps_y = [ps("ps_y0"), ps("ps_y1")]

    def body(p, inst):
        sb = lambda name, shape, dt=F32: _SB(nc, name + '_i%d' % inst, shape, dt)
        Cre = sb("Cre", [128, NP * 128]); Cim = sb("Cim", [128, NP * 128])
        Cre_b = sb("Cre_b", [128, NP * 128], BF16); Cim_b = sb("Cim_b", [128, NP * 128], BF16)
        ubb = sb("ubb", [128, TC], BF16)
        lre = sb("lre", [128, NP]); lim = sb("lim", [128, NP]); stp = sb("stp", [128, NP])
        rr = sb("rr", [128, NP]); thn = sb("thn", [128, NP])
        sc = {n: sb("sc_" + n, [128, NP]) for n in ("cos", "sin", "rc", "rs", "zr", "zi", "nzr", "den", "t1", "t2", "y")}
        sy = sb("sy", [128, TC]); sk = sb("sk", [128, TC], I32); sf = sb("sf", [128, TC])
        s2 = sb("s2", [128, TC]); q4 = sb("q4", [128, TC]); c2 = sb("c2", [128, TC])
        tS = sb("tS", [128, TC]); tCo = sb("tCo", [128, TC])
        Ezr = sb("Ezr", [128, NP, TC]); Ezi = sb("Ezi", [128, NP, TC])
        Fr = sb("Fr", [128, NP, TC]); Fi = sb("Fi", [128, NP, TC])
        carry = sb("carry", [128, NP, 2])
        ctmp = sb("carry_tmp", [128, 4])
        ub = [sb("ub%d" % i, [128, TC]) for i in range(1)] * 2
        yo = [sb("yo%d" % i, [128, TC]) for i in range(1)] * 2

        def two(name):
            return [sb(name, [128, TC])] * 2
        m1, m2, m3, m4 = two("m1"), two("m2"), two("m3"), two("m4")
        xr_, xi_ = two("xr_"), two("xi_")
        sr_, si_ = two("sr_"), two("si_")
        d1, d2, d3, d4 = two("d1"), two("d2"), two("d3"), two("d4")
        or_ = [sb("or_", [128, TC], BF16)] * 2
        oi_ = [sb("oi_", [128, TC], BF16)] * 2
        def sincos(y_ap, n, out_s, out_c, ykeys, okeys):
            K = "sincos_tmp"
            p.op("dve", lambda: nc.vector.tensor_copy(out=sk[:, 0:n], in_=y_ap), reads=ykeys, writes=[K])
            p.op("dve", lambda: nc.vector.tensor_copy(out=sf[:, 0:n], in_=sk[:, 0:n]), reads=[K], writes=[K])
            p.op("dve", lambda: nc.vector.tensor_tensor(out=sf[:, 0:n], in0=y_ap, in1=sf[:, 0:n], op=ALU.subtract),
                 reads=ykeys + [K], writes=[K])
            p.op("act", lambda: nc.scalar.activation(out=s2[:, 0:n], in_=sf[:, 0:n], func=AF.Sin, scale=math.pi),
                 reads=[K], writes=[K])
            p.op("act", lambda: nc.scalar.activation(out=q4[:, 0:n], in_=sf[:, 0:n], func=AF.Sin, scale=math.pi / 2),
                 reads=[K], writes=[K])
            p.op("dve", lambda: nc.vector.tensor_tensor(out=c2[:, 0:n], in0=q4[:, 0:n], in1=q4[:, 0:n], op=ALU.mult),
                 reads=[K], writes=[K])
            p.op("dve", lambda: nc.vector.tensor_scalar(out=c2[:, 0:n], in0=c2[:, 0:n], scalar1=-2.0, scalar2=1.0,
                                                        op0=ALU.mult, op1=ALU.add), reads=[K], writes=[K])
            p.op("dve", lambda: nc.vector.scalar_tensor_tensor(out=out_s, in0=s2[:, 0:n], scalar=2.0, in1=c2[:, 0:n],
                                                               op0=ALU.mult, op1=ALU.mult), reads=[K], writes=okeys)
            p.op("dve", lambda: nc.vector.tensor_tensor(out=c2[:, 0:n], in0=s2[:, 0:n], in1=s2[:, 0:n], op=ALU.mult),
                 reads=[K], writes=[K])
            p.op("dve", lambda: nc.vector.tensor_scalar(out=out_c, in0=c2[:, 0:n], scalar1=-2.0, scalar2=1.0,
                                                        op0=ALU.mult, op1=ALU.add), reads=[K], writes=okeys)

        p.dma("sp", lre[:], lre_d[inst], writes=["prm"])
        p.dma("sp", lim[:], lim_d[inst], writes=["prm"])
        p.dma("sp", stp[:], lst_d[inst], writes=["prm"])
        p.dma("sp", Cre[:], cre_d[inst], writes=["Cre"])
        p.dma("sp", Cim[:], cim_d[inst], writes=["Cim"])
        p.op("act", lambda: nc.scalar.copy(out=Cre_b[:], in_=Cre[:]), reads=["Cre"], writes=["Cre_b"])
        p.op("act", lambda: nc.scalar.copy(out=Cim_b[:], in_=Cim[:]), reads=["Cim"], writes=["Cim_b"])
        P = ["prm"]
        p.op("act", lambda: nc.scalar.activation(out=stp[:], in_=stp[:], func=AF.Exp), reads=P, writes=P)
        p.op("dve", lambda: nc.vector.tensor_tensor(out=rr[:], in0=lre[:], in1=stp[:], op=ALU.mult), reads=P, writes=P)
        p.op("act", lambda: nc.scalar.activation(out=rr[:], in_=rr[:], func=AF.Exp), reads=P, writes=P)
        p.op("dve", lambda: nc.vector.tensor_tensor(out=thn[:], in0=lim[:], in1=stp[:], op=ALU.mult), reads=P, writes=P)
        p.op("dve", lambda: nc.vector.tensor_scalar(out=thn[:], in0=thn[:], scalar1=1.0 / (2 * math.pi), scalar2=None,
                                                    op0=ALU.mult), reads=P, writes=P)
        sincos(thn[:], NP, sc["sin"][:], sc["cos"][:], P, P)
        p.op("dve", lambda: nc.vector.tensor_scalar(out=sc["y"][:], in0=thn[:], scalar1=float(TC), scalar2=None,
                                                    op0=ALU.mult), reads=P, writes=P)
        sincos(sc["y"][:], NP, sc["rs"][:], sc["rc"][:], P, P)
        tt = lambda o, a, b, op: p.op("dve", lambda: nc.vector.tensor_tensor(out=o, in0=a, in1=b, op=op), reads=P, writes=P)
        tt(sc["cos"][:], sc["cos"][:], rr[:], ALU.mult)
        tt(sc["sin"][:], sc["sin"][:], rr[:], ALU.mult)
        p.op("dve", lambda: nc.vector.tensor_scalar(out=sc["cos"][:], in0=sc["cos"][:], scalar1=-1.0, scalar2=None,
                                                    op0=ALU.add), reads=P, writes=P)
        tt(sc["t1"][:], lre[:], lre[:], ALU.mult)
        tt(sc["t2"][:], lim[:], lim[:], ALU.mult)
        tt(sc["den"][:], sc["t1"][:], sc["t2"][:], ALU.add)
        p.op("dve", lambda: nc.vector.reciprocal(out=sc["den"][:], in_=sc["den"][:]), reads=P, writes=P)
        tt(sc["t1"][:], sc["cos"][:], lre[:], ALU.mult)
        tt(sc["t2"][:], sc["sin"][:], lim[:], ALU.mult)
        tt(sc["zr"][:], sc["t1"][:], sc["t2"][:], ALU.add)
        tt(sc["zr"][:], sc["zr"][:], sc["den"][:], ALU.mult)
        tt(sc["t1"][:], sc["sin"][:], lre[:], ALU.mult)
        tt(sc["t2"][:], sc["cos"][:], lim[:], ALU.mult)
        tt(sc["zi"][:], sc["t1"][:], sc["t2"][:], ALU.subtract)
        tt(sc["zi"][:], sc["zi"][:], sc["den"][:], ALU.mult)
        p.op("dve", lambda: nc.vector.tensor_scalar(out=sc["nzr"][:], in0=sc["zr"][:], scalar1=-1.0, scalar2=None,
                                                    op0=ALU.mult), reads=P, writes=P)
        for pr in range(NP):
            p.op("dve", lambda pr=pr: nc.vector.tensor_scalar(out=sy[:], in0=tpr[:], scalar1=thn[:, pr:pr + 1], scalar2=None,
                                                              op0=ALU.mult), reads=["tpr"] + P, writes=["sy"])
            sincos(sy[:], TC, tS[:], tCo[:], ["sy"], ["tSC"])
            tk = ("tab", pr)
            p.op("pool", lambda pr=pr: nc.gpsimd.tensor_copy(out=Fr[:, pr, :], in_=tCo[:]), reads=["tSC"], writes=[tk])
            p.op("pool", lambda pr=pr: nc.gpsimd.tensor_copy(out=Fi[:, pr, :], in_=tS[:]), reads=["tSC"], writes=[tk])
            p.op("dve", lambda pr=pr: nc.vector.tensor_scalar(out=Ezr[:, pr, :], in0=tCo[:], scalar1=sc["zr"][:, pr:pr + 1],
                                                              scalar2=None, op0=ALU.mult), reads=["tSC"] + P, writes=[tk])
            p.op("dve", lambda pr=pr: nc.vector.scalar_tensor_tensor(
                out=Ezr[:, pr, :], in0=tS[:], scalar=sc["zi"][:, pr:pr + 1], in1=Ezr[:, pr, :], op0=ALU.mult, op1=ALU.add),
                reads=["tSC", tk] + P, writes=[tk])
            p.op("dve", lambda pr=pr: nc.vector.tensor_scalar(out=Ezi[:, pr, :], in0=tCo[:], scalar1=sc["zi"][:, pr:pr + 1],
                                                              scalar2=None, op0=ALU.mult), reads=["tSC"] + P, writes=[tk])
            p.op("dve", lambda pr=pr: nc.vector.scalar_tensor_tensor(
                out=Ezi[:, pr, :], in0=tS[:], scalar=sc["nzr"][:, pr:pr + 1], in1=Ezi[:, pr, :], op0=ALU.mult, op1=ALU.add),
                reads=["tSC", tk] + P, writes=[tk])
        p.op("dve", lambda: nc.vector.memset(carry[:], 0.0), writes=[("carry", pr) for pr in range(NP)])

        it = 0
        for b in range(n_blocks):
            bp = 0
            col0 = b * TC
            if io is None:
                p.dma("sp", ub[bp][:], uT[inst, :, col0:col0 + TC], writes=[("ub", bp)])
            else:
                io.load(p, inst, b, "u", ub[bp][:], ("ub", bp))
            p.op("act", lambda: nc.scalar.copy(out=ubb[:], in_=ub[bp][:]), reads=[("ub", bp)], writes=["ubb"])
            for pr in range(NP):
                q = 0
                it += 1
                tk = ("tab", pr)
                ck = ("carry", pr)
                ws = slice(pr * 128, (pr + 1) * 128)
                p.op("pe", lambda: nc.tensor.matmul(ps_xr[inst][:], lhsT=Bre_b[:, ws], rhs=ubb[:], start=True, stop=True),
                     reads=["Bre_b", "ubb"], writes=[("ps_xr", inst)])
                p.op("pe", lambda: nc.tensor.matmul(ps_xi[inst][:], lhsT=Bim_b[:, ws], rhs=ubb[:], start=True, stop=True),
                     reads=["Bim_b", "ubb"], writes=[("ps_xi", inst)])
                for o, onm, a, anm, tab in ((m1, "m1", ps_xr, "ps_xr", Ezr), (m2, "m2", ps_xi, "ps_xi", Ezi),
                                            (m3, "m3", ps_xr, "ps_xr", Ezi), (m4, "m4", ps_xi, "ps_xi", Ezr)):
                    p.op("dve", lambda o=o, a=a, tab=tab: nc.vector.tensor_tensor(out=o[q][:], in0=a[inst][:], in1=tab[:, pr, :],
                                                                                 op=ALU.mult),
                         reads=[(anm, inst), tk], writes=[(onm, q)])
                p.op("pool", lambda: nc.gpsimd.tensor_tensor(out=xr_[q][:], in0=m1[q][:], in1=m2[q][:], op=ALU.subtract),
                     reads=[("m1", q), ("m2", q)], writes=[("xr_", q)])
                p.op("pool", lambda: nc.gpsimd.tensor_tensor(out=xi_[q][:], in0=m3[q][:], in1=m4[q][:], op=ALU.add),
                     reads=[("m3", q), ("m4", q)], writes=[("xi_", q)])
                rb = rr[:, pr:pr + 1].broadcast_to([128, TC])
                p.op("dve", lambda: nc.vector.tensor_tensor_scan(out=sr_[q][:], data0=rb, data1=xr_[q][:],
                                                                 initial=carry[:, pr, 0:1], op0=ALU.mult, op1=ALU.add),
                     reads=["prm", ("xr_", q), ck], writes=[("sr_", q)])
                p.op("dve", lambda: nc.vector.tensor_tensor_scan(out=si_[q][:], data0=rb, data1=xi_[q][:],
                                                                 initial=carry[:, pr, 1:2], op0=ALU.mult, op1=ALU.add),
                     reads=["prm", ("xi_", q), ck], writes=[("si_", q)])
                lr, li = sr_[q][:, TC - 1:TC], si_[q][:, TC - 1:TC]
                p.op("dve", lambda: nc.vector.tensor_tensor(out=ctmp[:, 0:1], in0=li, in1=sc["rs"][:, pr:pr + 1], op=ALU.mult),
                     reads=[("si_", q), "prm"], writes=["ctmp"])
                p.op("dve", lambda: nc.vector.tensor_tensor(out=ctmp[:, 1:2], in0=li, in1=sc["rc"][:, pr:pr + 1], op=ALU.mult),
                     reads=[("si_", q), "prm"], writes=["ctmp"])
                p.op("dve", lambda: nc.vector.scalar_tensor_tensor(out=carry[:, pr, 0:1], in0=lr, scalar=sc["rc"][:, pr:pr + 1],
                                                                   in1=ctmp[:, 0:1], op0=ALU.mult, op1=ALU.subtract),
                     reads=[("sr_", q), "prm", "ctmp"], writes=[ck])
                p.op("dve", lambda: nc.vector.scalar_tensor_tensor(out=carry[:, pr, 1:2], in0=lr, scalar=sc["rs"][:, pr:pr + 1],
                                                                   in1=ctmp[:, 1:2], op0=ALU.mult, op1=ALU.add),
                     reads=[("sr_", q), "prm", "ctmp"], writes=[ck])
                for o, onm, a, anm, tab in ((d1, "d1", sr_, "sr_", Fr), (d2, "d2", si_, "si_", Fi),
                                            (d3, "d3", sr_, "sr_", Fi), (d4, "d4", si_, "si_", Fr)):
                    p.op("pool", lambda o=o, a=a, tab=tab: nc.gpsimd.tensor_tensor(out=o[q][:], in0=a[q][:], in1=tab[:, pr, :],
                                                                                  op=ALU.mult),
                         reads=[(anm, q), tk], writes=[(onm, q)])
                p.op("dve", lambda: nc.vector.tensor_tensor(out=or_[q][:], in0=d1[q][:], in1=d2[q][:], op=ALU.subtract),
                     reads=[("d1", q), ("d2", q)], writes=[("or_", q)])
                p.op("dve", lambda: nc.vector.scalar_tensor_tensor(out=oi_[q][:], in0=d3[q][:], scalar=-1.0, in1=d4[q][:],
                                                                   op0=ALU.mult, op1=ALU.subtract),
                     reads=[("d3", q), ("d4", q)], writes=[("oi_", q)])
                p.op("pe", lambda: nc.tensor.matmul(ps_y[inst][:], lhsT=Cre_b[:, ws], rhs=or_[q][:], start=(pr == 0), stop=False),
                     reads=["Cre_b", ("or_", q)], writes=[("ps_y", inst)])
                p.op("pe", lambda: nc.tensor.matmul(ps_y[inst][:], lhsT=Cim_b[:, ws], rhs=oi_[q][:], start=False,
                                                    stop=(pr == NP - 1)),
                     reads=["Cim_b", ("oi_", q)], writes=[("ps_y", inst)])
            p.op("act", lambda: nc.scalar.copy(out=yo[bp][:], in_=ps_y[inst][:]), reads=[("ps_y", inst)], writes=[("yo", bp)])
            if io is None:
                p.dma("sp", yT[inst, :, col0:col0 + TC], yo[bp][:], reads=[("yo", bp)], is_output=True)
            else:
                io.store(p, inst, b, "y", yo[bp][:], ("yo", bp))
    SHARED = {"tpr", "Bre", "Bim", "Bre_b", "Bim_b"}
    run_interleaved(p, [lambda v: body(v, 0), lambda v: body(v, 1)], SHARED)
    if standalone:
        p.finish()
    return nc


NCORES = 8
SEQ = 16384
TOKC = SEQ // NCORES


def _run(nc, in_maps):
    res = run_bass_kernel_spmd(nc, in_maps, core_ids=list(range(NCORES)))
    return [{k: np.asarray(v) for k, v in r.items()} for r in res.results]


def _c(a):
    return np.ascontiguousarray(a, dtype=np.float32)


def _tok(a, c):
    return _c(a[:, c * TOKC:(c + 1) * TOKC])


def _conv_layout(w):
    return _c(w.T.reshape(4, 128, 5).transpose(1, 0, 2).reshape(128, 20))


def _pad2(a):
    return np.pad(a, ((0, 0), (2, 2)))


def _ffn_inputs(i, norm_w, ffn_w_gate_up, ffn_w_down):
    return dict(
        nws=_c(np.concatenate([col_tiles(norm_w[i, k]) for k in (1, 2, 3)], axis=1)),
        wgu=_c(np.concatenate([arrange_w(ffn_w_gate_up[i][:, :DFF]), arrange_w(ffn_w_gate_up[i][:, DFF:])], axis=2)),
        wd=arrange_w(ffn_w_down[i]),
    )


def _ssd_maps(P, j, conv_w, conv_b, dt_bias, a_log, d_skip):
    maps = []
    for g in range(NCORES):
        ch = np.concatenate([np.arange(g * 256, (g + 1) * 256), 2048 + np.arange(g * 128, (g + 1) * 128),
                             3072 + np.arange(g * 128, (g + 1) * 128)])
        xg = P[2048 + ch]
        w = conv_w[j][:, ch]
        maps.append(dict(
            xbc=_c(np.stack([_pad2(xg), _pad2(xg[:, ::-1])])),
            dtr=_c(np.stack([P[6144 + 4 * g:6144 + 4 * g + 4], P[6176 + 4 * g:6176 + 4 * g + 4][:, ::-1]])),
            cw=_c(np.stack([_conv_layout(w), _conv_layout(w[::-1])])),
            cb=_c(conv_b[j][ch].reshape(4, 128).T),
            dtb=_c(dt_bias[j][:, 4 * g:4 * g + 4].reshape(2, 4, 1)),
            alog=_c(a_log[j][:, 4 * g:4 * g + 4].reshape(2, 4, 1)),
            dsk=_c(np.repeat(d_skip[j][4 * g:4 * g + 4], 64).reshape(2, 128).T),
        ))
    return maps


def _gdn_maps(P, conv_w, conv_b, dt_bias, a_log):
    maps = []
    for g in range(NCORES):
        ch = np.concatenate([np.arange(g * 128, (g + 1) * 128), 1024 + np.arange(g * 128, (g + 1) * 128),
                             2048 + np.arange(2 * g * 128, (2 * g + 2) * 128)])
        xg = P[ch]
        w = conv_w[0][:, ch]
        a0, a1 = P[6144 + 2 * g:6144 + 2 * g + 2], P[6144 + 16 + 2 * g:6144 + 16 + 2 * g + 2]
        b0, b1 = P[6176 + 2 * g:6176 + 2 * g + 2], P[6176 + 16 + 2 * g:6176 + 16 + 2 * g + 2]
        maps.append(dict(
            qkv=_c(np.stack([_pad2(xg), _pad2(xg[:, ::-1])])),
            abr=_c(np.stack([np.stack([a0, b0]), np.stack([a1[:, ::-1], b1[:, ::-1]])])),
            cw=_c(np.stack([_conv_layout(w), _conv_layout(w[::-1])])),
            cb=_c(conv_b[0][ch].reshape(4, 128).T),
            dtb=_c(dt_bias[0][:, 2 * g:2 * g + 2].reshape(2, 2, 1)),
            alog=_c(a_log[0][:, 2 * g:2 * g + 2].reshape(2, 2, 1)),
        ))
    return maps


def _pair_cols(a):
    return _c(a.reshape(4, 2, 64).transpose(1, 2, 0).reshape(128, 4))


def _blayout(b):
    out = np.zeros((8, 16, 4, 2, 64), np.float32)
    for g in range(8):
        out[g, :, g // 2, g % 2, :] = b[g].T
    return out.reshape(128, 512)


def _clayout(c):
    out = np.zeros((2, 64, 4, 8, 16), np.float32)
    for g in range(8):
        out[g % 2, :, g // 2, g, :] = c[g].T
    return out.reshape(128, 512)


def _s5_maps(hn, lam_re, lam_im, log_step, b_re, b_im, c_re, c_im):
    maps = []
    for c in range(NCORES):
        gs = slice(8 * c, 8 * c + 8)
        u = hn[c * 128:(c + 1) * 128]
        maps.append(dict(
            uT=_c(np.stack([u, u[:, ::-1]])),
            lam_re=np.stack([_pair_cols(lam_re[0, d, gs]) for d in range(2)]),
            lam_im=np.stack([_pair_cols(lam_im[0, d, gs]) for d in range(2)]),
            log_step=np.stack([_pair_cols(np.repeat(log_step[0, d, gs][:, None], 64, axis=1)) for d in range(2)]),
            b_re=_blayout(b_re[0, gs]), b_im=_blayout(b_im[0, gs]),
            c_re=np.stack([_clayout(c_re[0, d, gs]) for d in range(2)]),
            c_im=np.stack([_clayout(c_im[0, d, gs]) for d in range(2)]),
        ))
    return maps


def _gather_tok(results, key):
    return np.concatenate([r[key] for r in results], axis=1)


def kernel(x, norm_w, ssd_w_in, ssd_conv_w, ssd_conv_b, ssd_dt_bias, ssd_a_log, ssd_d,
           ssd_norm_w, ssd_w_out, gdn_w_in, gdn_conv_w, gdn_conv_b, gdn_dt_bias, gdn_a_log,
           gdn_norm_w, gdn_w_out, s5_lam_re, s5_lam_im, s5_log_step, s5_b_re, s5_b_im,
           s5_c_re, s5_c_im, s5_d, s5_w_glu, s5_b_glu, ffn_w_gate_up, ffn_w_down):
    A = lambda a: np.asarray(a, dtype=np.float32)
    (x, norm_w, ssd_w_in, ssd_conv_w, ssd_conv_b, ssd_dt_bias, ssd_a_log, ssd_d, ssd_norm_w, ssd_w_out, gdn_w_in,
     gdn_conv_w, gdn_conv_b, gdn_dt_bias, gdn_a_log, gdn_norm_w, gdn_w_out, s5_lam_re, s5_lam_im, s5_log_step,
     s5_b_re, s5_b_im, s5_c_re, s5_c_im, s5_d, s5_w_glu, s5_b_glu, ffn_w_gate_up, ffn_w_down) = map(A, (
         x, norm_w, ssd_w_in, ssd_conv_w, ssd_conv_b, ssd_dt_bias, ssd_a_log, ssd_d, ssd_norm_w, ssd_w_out, gdn_w_in,
         gdn_conv_w, gdn_conv_b, gdn_dt_bias, gdn_a_log, gdn_norm_w, gdn_w_out, s5_lam_re, s5_lam_im, s5_log_step,
         s5_b_re, s5_b_im, s5_c_re, s5_c_im, s5_d, s5_w_glu, s5_b_glu, ffn_w_gate_up, ffn_w_down))
    hT = _c(x[0].T)

    nc_ssd = build_ssd()
    common = dict(nw0=col_tiles(norm_w[0, 0]), w_in=arrange_w(ssd_w_in[0]))
    r = _run(build_dense(None, False, "proj"), [dict(hT=_tok(hT, c), **common) for c in range(NCORES)])
    P = _gather_tok(r, "projT")

    r = _run(nc_ssd, _ssd_maps(P, 0, ssd_conv_w, ssd_conv_b, ssd_dt_bias, ssd_a_log, ssd_d))
    yf = np.concatenate([q["yT"][0] for q in r], axis=0)
    yb = np.concatenate([q["yT"][1][:, ::-1] for q in r], axis=0)
    common = dict(w_out=arrange_w(ssd_w_out[0]), mnw=col_tiles(ssd_norm_w[0]), nw0=col_tiles(norm_w[1, 0]),
                  w_in=arrange_w(gdn_w_in[0]), **_ffn_inputs(0, norm_w, ffn_w_gate_up, ffn_w_down))
    r = _run(build_dense("ssd", True, "proj"),
             [dict(hT=_tok(hT, c), mf=_tok(yf, c), mb=_tok(yb, c), zT=_tok(P[0:2048], c), **common) for c in range(NCORES)])
    hT = _gather_tok(r, "hT_out")
    P = _gather_tok(r, "projT")

    r = _run(build_gdn(), _gdn_maps(P, gdn_conv_w, gdn_conv_b, gdn_dt_bias, gdn_a_log))
    yf = np.concatenate([q["oT"][0] for q in r], axis=0)
    yb = np.concatenate([q["oT"][1][:, ::-1] for q in r], axis=0)
    common = dict(w_out=arrange_w(gdn_w_out[0]), mnw=_c(np.tile(gdn_norm_w[0][:, None], (1, 16))),
                  nw0=col_tiles(norm_w[2, 0]), **_ffn_inputs(1, norm_w, ffn_w_gate_up, ffn_w_down))
    r = _run(build_dense("gdn", True, "hn"),
             [dict(hT=_tok(hT, c), mf=_tok(yf, c), mb=_tok(yb, c), zT=_tok(P[4096:6144], c), **common)
              for c in range(NCORES)])
    hT = _gather_tok(r, "hT_out")
    hn = _gather_tok(r, "hn_out")

    r = _run(build_s5(), _s5_maps(hn, s5_lam_re, s5_lam_im, s5_log_step, s5_b_re, s5_b_im, s5_c_re, s5_c_im))
    yf = np.concatenate([q["yT"][0] for q in r], axis=0)
    yb = np.concatenate([q["yT"][1][:, ::-1] for q in r], axis=0)
    common = dict(w_glu=arrange_w(s5_w_glu[0]), b_glu=col_tiles(s5_b_glu[0]), s5d=col_tiles(s5_d[0]),
                  nw0=col_tiles(norm_w[3, 0]), w_in=arrange_w(ssd_w_in[1]),
                  **_ffn_inputs(2, norm_w, ffn_w_gate_up, ffn_w_down))
    r = _run(build_dense("s5", True, "proj"),
             [dict(hT=_tok(hT, c), mf=_tok(yf, c), mb=_tok(yb, c), hnT=_tok(hn, c), **common) for c in range(NCORES)])
    hT = _gather_tok(r, "hT_out")
    P = _gather_tok(r, "projT")

    r = _run(nc_ssd, _ssd_maps(P, 1, ssd_conv_w, ssd_conv_b, ssd_dt_bias, ssd_a_log, ssd_d))
    yf = np.concatenate([q["yT"][0] for q in r], axis=0)
    yb = np.concatenate([q["yT"][1][:, ::-1] for q in r], axis=0)
    common = dict(w_out=arrange_w(ssd_w_out[1]), mnw=col_tiles(ssd_norm_w[1]),
                  **_ffn_inputs(3, norm_w, ffn_w_gate_up, ffn_w_down))
    r = _run(build_dense("ssd", True, None),
             [dict(hT=_tok(hT, c), mf=_tok(yf, c), mb=_tok(yb, c), zT=_tok(P[0:2048], c), **common) for c in range(NCORES)])
    hT = _gather_tok(r, "hT_out")
    return np.ascontiguousarray(hT.T[None].astype(np.float32))
```
